# Optimizing a Trainium2 kernel written in Bass

```python
import math
import jax, jax.numpy as jnp
from jax import lax
import numpy as np

D_MODEL = 1024
BATCH = 1
SEQ = 16384
DEPTH = 2
DEC_BATCH = 128
DEC_SEQ = 8
PAST_LEN = 16384
PAGE_SIZE = 128

A_HEADS = 8
A_KV_HEADS = 2
A_GROUP = A_HEADS // A_KV_HEADS
A_HEAD_DIM = 64
WINDOW = 128
ATTN_BLOCK = 128
B_HEADS = 4
B_KEY_DIM = 128
B_VAL_DIM = 128
RET_CHUNK = 128
S5_GROUP = 16
S5_GROUPS = D_MODEL // S5_GROUP
S5_STATE = 64
SSM_CHUNK = 128
D_FF = ((8 * D_MODEL + 3 * 256 - 1) // (3 * 256)) * 256
IN_WIDTHS = (A_HEADS * A_HEAD_DIM, A_KV_HEADS * A_HEAD_DIM, A_KV_HEADS * A_HEAD_DIM,
             B_HEADS * B_KEY_DIM, B_HEADS * B_KEY_DIM, B_HEADS * B_VAL_DIM, B_HEADS * B_VAL_DIM)
IN_WIDTH = sum(IN_WIDTHS)
MIX_WIDTH = A_HEADS * A_HEAD_DIM + B_HEADS * B_VAL_DIM
N_EVEN = (DEPTH + 1) // 2
N_ODD = DEPTH // 2
EPS = 1e-6
NEG_INF = -1e30

kernel_name = 'hybrid_swa_sink_retention_s5_decoder_step'


def rmsnorm(x, g):
    xf = x.astype(jnp.float32)
    y = xf * lax.rsqrt(jnp.mean(xf * xf, axis=-1, keepdims=True) + EPS)
    return (y * g.astype(jnp.float32)).astype(x.dtype)


def alibi_slopes():
    h = jnp.arange(1, A_HEADS + 1, dtype=jnp.float32)
    return jnp.exp2(-8.0 * h / A_HEADS).reshape(A_KV_HEADS, A_GROUP)


def split_cols(proj):
    idx, acc = [], 0
    for w in IN_WIDTHS[:-1]:
        acc += w
        idx.append(acc)
    return jnp.split(proj, idx, axis=-1)


def sink_attention(q, k, v, dist, valid, sinks):
    s = jnp.einsum('...qkgd,...skd->...kgqs', q.astype(jnp.float32), k.astype(jnp.float32)) * (A_HEAD_DIM ** -0.5)
    s = s - alibi_slopes()[:, :, None, None] * dist[..., None, None, :, :].astype(jnp.float32)
    s = jnp.where(valid[..., None, None, :, :], s, NEG_INF)
    sink = jnp.broadcast_to(sinks.astype(jnp.float32).reshape(A_KV_HEADS, A_GROUP)[:, :, None, None],
                            s.shape[:-1] + (1,))
    p = jax.nn.softmax(jnp.concatenate([s, sink], axis=-1), axis=-1)[..., :-1]
    return jnp.einsum('...kgqs,...skd->...qkgd', p, v.astype(jnp.float32))


def swa_prompt(q, k, v, sinks):
    bsz, L = q.shape[:2]
    nb = L // ATTN_BLOCK
    qb = q.reshape(bsz, nb, ATTN_BLOCK, A_KV_HEADS, A_GROUP, A_HEAD_DIM)

    def with_prev(t):
        tb = t.reshape(bsz, nb, ATTN_BLOCK, A_KV_HEADS, A_HEAD_DIM)
        prev = jnp.pad(tb, ((0, 0), (1, 0), (0, 0), (0, 0), (0, 0)))[:, :-1]
        return jnp.concatenate([prev, tb], axis=2)

    kb, vb = with_prev(k), with_prev(v)
    blk = jnp.arange(nb)[:, None] * ATTN_BLOCK
    q_pos = blk + jnp.arange(ATTN_BLOCK)[None, :]
    k_pos = blk - ATTN_BLOCK + jnp.arange(2 * ATTN_BLOCK)[None, :]
    dist = q_pos[:, :, None] - k_pos[:, None, :]
    valid = (dist >= 0) & (dist < WINDOW) & (k_pos[:, None, :] >= 0)
    o = sink_attention(qb, kb, vb, dist, valid, sinks)
    return o.reshape(bsz, L, A_HEADS * A_HEAD_DIM)


def swa_sample(q, k, v, cache_k, cache_v, sinks):
    bsz, L = q.shape[:2]
    W = cache_k.shape[1]
    k_all = jnp.concatenate([cache_k.astype(k.dtype), k], axis=1)
    v_all = jnp.concatenate([cache_v.astype(v.dtype), v], axis=1)
    q_pos = PAST_LEN + jnp.arange(L)
    k_pos = PAST_LEN - W + jnp.arange(W + L)
    dist = q_pos[:, None] - k_pos[None, :]
    valid = (dist >= 0) & (dist < WINDOW)
    o = sink_attention(q, k_all, v_all, dist, valid, sinks)
    return o.reshape(bsz, L, A_HEADS * A_HEAD_DIM), k_all[:, L:], v_all[:, L:]


def retention(q, k, v, S0):
    bsz, L = q.shape[:2]
    blk = min(RET_CHUNK, L)
    nc = L // blk
    log_g = jnp.log1p(-jnp.exp2(-5.0 - jnp.arange(B_HEADS, dtype=jnp.float32)))
    idx = jnp.arange(blk, dtype=jnp.float32)
    diff = idx[:, None] - idx[None, :]
    decay_in = jnp.where(diff >= 0, jnp.exp(log_g[:, None, None] * jnp.maximum(diff, 0.0)), 0.0)
    decay_q = jnp.exp(log_g[None, :] * (idx[:, None] + 1.0))
    decay_k = jnp.exp(log_g[None, :] * (blk - 1.0 - idx[:, None]))
    decay_c = jnp.exp(log_g * blk)

    def chunks(t):
        return t.astype(jnp.float32).reshape(bsz, nc, blk, B_HEADS, t.shape[-1]).transpose(1, 0, 2, 3, 4)

    def step(S, xs):
        qc, kc, vc = xs
        inner = jnp.einsum('bihd,bjhd->bhij', qc, kc) * decay_in
        o = (jnp.einsum('bhij,bjhe->bihe', inner, vc)
             + jnp.einsum('bihd,bhde->bihe', qc, S) * decay_q[None, :, :, None])
        S = S * decay_c[None, :, None, None] + jnp.einsum('bjhd,bjhe->bhde', kc * decay_k[None, :, :, None], vc)
        return S, o

    S, o = lax.scan(step, S0.astype(jnp.float32), (chunks(q), chunks(k), chunks(v)))
    return o.transpose(1, 0, 2, 3, 4).reshape(bsz, L, B_HEADS, B_VAL_DIM), S


def s5_scan(u, h_re, h_im, A_re, A_im, log_dt, B_re, B_im, C_re, C_im, D_skip):
    f32 = jnp.float32
    bsz, L, _ = u.shape
    A_re, A_im = A_re.astype(f32), A_im.astype(f32)
    dt = jnp.exp(log_dt.astype(f32))[:, None]
    mag = jnp.exp(A_re * dt)
    lam_re, lam_im = mag * jnp.cos(A_im * dt), mag * jnp.sin(A_im * dt)
    den = A_re * A_re + A_im * A_im
    n_re, n_im = lam_re - 1.0, lam_im
    f_re = (n_re * A_re + n_im * A_im) / den
    f_im = (n_im * A_re - n_re * A_im) / den
    Br, Bi = B_re.astype(f32), B_im.astype(f32)
    bb_re = f_re[..., None] * Br - f_im[..., None] * Bi
    bb_im = f_re[..., None] * Bi + f_im[..., None] * Br
    Cr, Ci = C_re.astype(f32), C_im.astype(f32)
    blk = min(SSM_CHUNK, L)
    nc = L // blk
    uf = u.astype(f32)
    uc = uf.reshape(bsz, nc, blk, S5_GROUPS, S5_GROUP).transpose(1, 0, 2, 3, 4)

    def combine(e1, e2):
        a1r, a1i, b1r, b1i = e1
        a2r, a2i, b2r, b2i = e2
        return (a2r * a1r - a2i * a1i, a2r * a1i + a2i * a1r,
                a2r * b1r - a2i * b1i + b2r, a2r * b1i + a2i * b1r + b2i)

    def step(carry, x):
        hr, hi = carry
        bu_re = jnp.einsum('bcgi,gpi->bcgp', x, bb_re)
        bu_im = jnp.einsum('bcgi,gpi->bcgp', x, bb_im)
        bu_re = bu_re.at[:, 0].add(lam_re * hr - lam_im * hi)
        bu_im = bu_im.at[:, 0].add(lam_re * hi + lam_im * hr)
        a_re = jnp.broadcast_to(lam_re, bu_re.shape)
        a_im = jnp.broadcast_to(lam_im, bu_im.shape)
        _, _, hs_re, hs_im = lax.associative_scan(combine, (a_re, a_im, bu_re, bu_im), axis=1)
        y = jnp.einsum('bcgp,gip->bcgi', hs_re, Cr) - jnp.einsum('bcgp,gip->bcgi', hs_im, Ci)
        return (hs_re[:, -1], hs_im[:, -1]), y

    (hr, hi), ys = lax.scan(step, (h_re.astype(f32), h_im.astype(f32)), uc)
    y = ys.transpose(1, 0, 2, 3, 4).reshape(bsz, L, D_MODEL) + D_skip.astype(f32) * uf
    return y, hr, hi


def mix_even(h, w_in, q_gain, k_gain, sinks, ret_gain, w_out, cache_k, cache_v, ret_state):
    bsz, L, _ = h.shape
    q_a, k_a, v_a, q_b, k_b, v_b, g_b = split_cols(h @ w_in)
    q_a = rmsnorm(q_a.reshape(bsz, L, A_KV_HEADS, A_GROUP, A_HEAD_DIM), q_gain)
    k_a = rmsnorm(k_a.reshape(bsz, L, A_KV_HEADS, A_HEAD_DIM), k_gain)
    v_a = v_a.reshape(bsz, L, A_KV_HEADS, A_HEAD_DIM)
    if cache_k is None:
        o_a = swa_prompt(q_a, k_a, v_a, sinks)
        keep = min(WINDOW, L)
        new_k, new_v = k_a[:, L - keep:], v_a[:, L - keep:]
        ret_state = jnp.zeros((bsz, B_HEADS, B_KEY_DIM, B_VAL_DIM), jnp.float32)
    else:
        o_a, new_k, new_v = swa_sample(q_a, k_a, v_a, cache_k, cache_v, sinks)
    o_b, new_s = retention(q_b.reshape(bsz, L, B_HEADS, B_KEY_DIM),
                           k_b.reshape(bsz, L, B_HEADS, B_KEY_DIM) * (B_KEY_DIM ** -0.5),
                           v_b.reshape(bsz, L, B_HEADS, B_VAL_DIM), ret_state)
    mu = jnp.mean(o_b, axis=-1, keepdims=True)
    var = jnp.mean(jnp.square(o_b - mu), axis=-1, keepdims=True)
    o_b = (o_b - mu) * lax.rsqrt(var + EPS) * ret_gain.astype(jnp.float32).reshape(B_HEADS, B_VAL_DIM)
    o_b = o_b.reshape(bsz, L, B_HEADS * B_VAL_DIM).astype(h.dtype) * jax.nn.silu(g_b)
    o = jnp.concatenate([o_a.astype(h.dtype), o_b], axis=-1) @ w_out
    return o, new_k, new_v, new_s


def mix_odd(h, A_re, A_im, log_dt, B_re, B_im, C_re, C_im, D_skip, glu_a, glu_b, s_re, s_im):
    bsz = h.shape[0]
    if s_re is None:
        s_re = jnp.zeros((bsz, S5_GROUPS, S5_STATE), jnp.float32)
        s_im = jnp.zeros((bsz, S5_GROUPS, S5_STATE), jnp.float32)
    y, n_re, n_im = s5_scan(h, s_re, s_im, A_re, A_im, log_dt, B_re, B_im, C_re, C_im, D_skip)
    y = jax.nn.gelu(y).astype(h.dtype)
    return (y @ glu_a) * jax.nn.sigmoid(y @ glu_b), n_re, n_im


def swiglu(h, wg, wu, wd):
    return (jax.nn.silu(h @ wg) * (h @ wu)) @ wd


def trunk(x, c, cache_k, cache_v, ret, s5r, s5i, p):
    ks, vs, rets, res, ims = [], [], [], [], []
    for layer in range(DEPTH):
        i = layer // 2
        mod = jax.nn.silu(c) @ p['ada_w'][layer] + p['ada_b'][layer]
        sh1, sc1, g1, sh2, sc2, g2 = [m[:, None, :] for m in jnp.split(mod, 6, axis=-1)]
        h = rmsnorm(x, p['norm_mix'][layer]) * (1.0 + sc1) + sh1
        if layer % 2 == 0:
            out, nk, nv, ns = mix_even(
                h, p['even_w_in'][i], p['even_q_gain'][i], p['even_k_gain'][i], p['even_sinks'][i],
                p['even_ret_gain'][i], p['even_w_out'][i],
                None if cache_k is None else cache_k[i], None if cache_v is None else cache_v[i],
                None if ret is None else ret[i])
            ks.append(nk); vs.append(nv); rets.append(ns)
        else:
            out, nr, ni = mix_odd(
                h, p['odd_A_re'][i], p['odd_A_im'][i], p['odd_log_dt'][i], p['odd_B_re'][i], p['odd_B_im'][i],
                p['odd_C_re'][i], p['odd_C_im'][i], p['odd_D'][i], p['odd_glu_a'][i], p['odd_glu_b'][i],
                None if s5r is None else s5r[i], None if s5i is None else s5i[i])
            res.append(nr); ims.append(ni)
        x = x + g1 * out
        h = rmsnorm(x, p['norm_ffn'][layer]) * (1.0 + sc2) + sh2
        x = x + g2 * swiglu(h, p['ffn_wg'][layer], p['ffn_wu'][layer], p['ffn_wd'][layer])
    return x, jnp.stack(ks), jnp.stack(vs), jnp.stack(rets), jnp.stack(res), jnp.stack(ims)


def setup_inputs(seed: int = 0) -> dict:
    key = jax.random.key(seed)
    k = jax.random.split(key, 40)

    def nrm(kk, shape, scale=1.0):
        return scale * jax.random.normal(kk, shape, jnp.float32)

    W = min(WINDOW, PAST_LEN)
    n = jnp.arange(S5_STATE, dtype=jnp.float32)
    return {
        'x_prompt': nrm(k[0], (BATCH, SEQ, D_MODEL)),
        'x_sample': nrm(k[1], (DEC_BATCH, DEC_SEQ, D_MODEL)),
        'cache_win_k': nrm(k[2], (N_EVEN, DEC_BATCH, W, A_KV_HEADS, A_HEAD_DIM)),
        'cache_win_v': nrm(k[3], (N_EVEN, DEC_BATCH, W, A_KV_HEADS, A_HEAD_DIM)),
        'state_ret': nrm(k[4], (N_EVEN, DEC_BATCH, B_HEADS, B_KEY_DIM, B_VAL_DIM)),
        'state_s5_re': nrm(k[5], (N_ODD, DEC_BATCH, S5_GROUPS, S5_STATE), 0.2),
        'state_s5_im': nrm(k[6], (N_ODD, DEC_BATCH, S5_GROUPS, S5_STATE), 0.2),
        'c_prompt': nrm(k[7], (BATCH, D_MODEL)),
        'c_sample': nrm(k[8], (DEC_BATCH, D_MODEL)),
        'ada_w': nrm(k[9], (DEPTH, D_MODEL, 6 * D_MODEL), 0.5 * D_MODEL ** -0.5),
        'ada_b': nrm(k[10], (DEPTH, 6 * D_MODEL), 0.05),
        'norm_mix': 1.0 + nrm(k[11], (DEPTH, D_MODEL), 0.02),
        'norm_ffn': 1.0 + nrm(k[12], (DEPTH, D_MODEL), 0.02),
        'ffn_wg': nrm(k[13], (DEPTH, D_MODEL, D_FF), D_MODEL ** -0.5),
        'ffn_wu': nrm(k[14], (DEPTH, D_MODEL, D_FF), D_MODEL ** -0.5),
        'ffn_wd': nrm(k[15], (DEPTH, D_FF, D_MODEL), D_FF ** -0.5),
        'even_w_in': nrm(k[16], (N_EVEN, D_MODEL, IN_WIDTH), D_MODEL ** -0.5),
        'even_q_gain': 1.0 + nrm(k[17], (N_EVEN, A_HEAD_DIM), 0.02),
        'even_k_gain': 1.0 + nrm(k[18], (N_EVEN, A_HEAD_DIM), 0.02),
        'even_sinks': nrm(k[19], (N_EVEN, A_HEADS), 0.5),
        'even_ret_gain': 1.0 + nrm(k[20], (N_EVEN, B_HEADS * B_VAL_DIM), 0.02),
        'even_w_out': nrm(k[21], (N_EVEN, MIX_WIDTH, D_MODEL), MIX_WIDTH ** -0.5),
        'odd_A_re': -0.5 + nrm(k[22], (N_ODD, S5_GROUPS, S5_STATE), 0.01),
        'odd_A_im': math.pi * n + nrm(k[23], (N_ODD, S5_GROUPS, S5_STATE), 0.01),
        'odd_log_dt': jax.random.uniform(k[24], (N_ODD, S5_GROUPS), jnp.float32, math.log(0.001), math.log(0.1)),
        'odd_B_re': nrm(k[25], (N_ODD, S5_GROUPS, S5_STATE, S5_GROUP), (2.0 * S5_GROUP) ** -0.5),
        'odd_B_im': nrm(k[26], (N_ODD, S5_GROUPS, S5_STATE, S5_GROUP), (2.0 * S5_GROUP) ** -0.5),
        'odd_C_re': nrm(k[27], (N_ODD, S5_GROUPS, S5_GROUP, S5_STATE), S5_STATE ** -0.5),
        'odd_C_im': nrm(k[28], (N_ODD, S5_GROUPS, S5_GROUP, S5_STATE), S5_STATE ** -0.5),
        'odd_D': nrm(k[29], (N_ODD, D_MODEL), 0.5),
        'odd_glu_a': nrm(k[30], (N_ODD, D_MODEL, D_MODEL), D_MODEL ** -0.5),
        'odd_glu_b': nrm(k[31], (N_ODD, D_MODEL, D_MODEL), D_MODEL ** -0.5),
    }


def reference(x_prompt, x_sample, cache_win_k, cache_win_v, state_ret, state_s5_re, state_s5_im,
              c_prompt, c_sample, ada_w, ada_b, norm_mix, norm_ffn, ffn_wg, ffn_wu, ffn_wd,
              even_w_in, even_q_gain, even_k_gain, even_sinks, even_ret_gain, even_w_out,
              odd_A_re, odd_A_im, odd_log_dt, odd_B_re, odd_B_im, odd_C_re, odd_C_im, odd_D,
              odd_glu_a, odd_glu_b):
    p = dict(ada_w=ada_w, ada_b=ada_b, norm_mix=norm_mix, norm_ffn=norm_ffn,
             ffn_wg=ffn_wg, ffn_wu=ffn_wu, ffn_wd=ffn_wd,
             even_w_in=even_w_in, even_q_gain=even_q_gain, even_k_gain=even_k_gain,
             even_sinks=even_sinks, even_ret_gain=even_ret_gain, even_w_out=even_w_out,
             odd_A_re=odd_A_re, odd_A_im=odd_A_im, odd_log_dt=odd_log_dt,
             odd_B_re=odd_B_re, odd_B_im=odd_B_im, odd_C_re=odd_C_re, odd_C_im=odd_C_im,
             odd_D=odd_D, odd_glu_a=odd_glu_a, odd_glu_b=odd_glu_b)
    y_prompt, p_k, p_v, p_ret, p_re, p_im = trunk(x_prompt, c_prompt, None, None, None, None, None, p)
    y_sample, s_k, s_v, s_ret, s_re, s_im = trunk(x_sample, c_sample, cache_win_k, cache_win_v,
                                                  state_ret, state_s5_re, state_s5_im, p)
    return (y_prompt, y_sample, p_k, p_v, p_ret, p_re, p_im, s_k, s_v, s_ret, s_re, s_im)
```

```python
import os
import numpy as np
from contextlib import ExitStack
import concourse.bass as bass
import concourse.mybir as mybir
from concourse.bass_utils import run_bass_kernel_spmd

F32 = mybir.dt.float32
BF16 = mybir.dt.bfloat16
I32 = mybir.dt.int32
ALU = mybir.AluOpType
AF = mybir.ActivationFunctionType

NCORES = 8
D = 1024
KC = 8
NPT = 16
TOK_P = NPT * 128
NTOK = TOK_P + 128
IN_W = 2816
DFF = 2816
FC = 22
BT = 128
TPB = BT // 128
EPS = 1e-6
ENGS = ("pe", "act", "dve", "pool", "sp")
GAM = [1.0 - 2.0 ** (-5.0 - h) for h in range(4)]
KSTOP = int(os.environ.get('KSTOP', '9'))
KSUB = int(os.environ.get('KSUB', '9'))


class Prog:
    def __init__(self, nc, same_engine_sync=("act", "dve", "pool")):
        self.nc = nc
        self.ins = []
        self.last_w = {}
        self.readers = {}
        self.same_sync = set(same_engine_sync)
        self.final_ids = []
        self.last_eng = {}
        self.last_dma = {}
        self.bar_deps = []
        self.bar_gen = 0
        self.eng_gen = {e: 0 for e in ENGS}

    def barrier(self):
        self.bar_deps = list(self.last_eng.values()) + list(self.last_dma.values())
        self.bar_gen += 1

    def _add(self, eng, fn, reads, writes, dma=False, semkey=None, group=None):
        iid = len(self.ins)
        deps = set()
        pk = tuple(k for k in reads if isinstance(k, str) and len(k) == 3 and k[0] == "Q" and k not in writes)
        writes = tuple(writes) + pk
        if self.eng_gen[eng] < self.bar_gen:
            deps.update(self.bar_deps)
            self.eng_gen[eng] = self.bar_gen
        for k in reads:
            w = self.last_w.get(k)
            if w is not None:
                deps.add(w)
        for k in writes:
            w = self.last_w.get(k)
            if w is not None:
                if group is not None and self.ins[w].get("group") == group:
                    deps.update(self.ins[w]["deps"])
                else:
                    deps.add(w)
            for r in self.readers.get(k, ()):
                deps.add(r)
        self.ins.append(dict(eng=eng, fn=fn, deps=sorted(deps), dma=dma, semkey=semkey, group=group))
        for k in reads:
            self.readers.setdefault(k, []).append(iid)
        for k in writes:
            self.last_w[k] = iid
            self.readers[k] = []
        self.last_eng[eng] = iid
        if dma:
            self.last_dma[semkey] = iid
        return iid

    def op(self, eng, fn, reads=(), writes=()):
        return self._add(eng, fn, tuple(reads), tuple(writes))

    def dma(self, eng, fn, reads=(), writes=(), semkey=None, group=None, final=False):
        assert semkey is not None
        iid = self._add(eng, fn, tuple(reads), tuple(writes), dma=True, semkey=semkey, group=group)
        if final:
            self.final_ids.append(iid)
        return iid

    def build(self):
        nc = self.nc
        ins = self.ins
        n = len(ins)
        needed = [False] * n
        for i, it in enumerate(ins):
            nd = []
            for d in it["deps"]:
                de = ins[d]
                if (not de["dma"]) and (not it["dma"]) and de["eng"] == it["eng"] and it["eng"] not in self.same_sync:
                    continue
                nd.append(d)
            it["deps"] = nd
            for d in nd:
                needed[d] = True
        for f in self.final_ids:
            needed[f] = True
        semkeys = []
        for it in ins:
            if it["dma"] and it["semkey"] not in semkeys:
                semkeys.append(it["semkey"])
        sem_objs = {}
        ctxs = []
        for e in ENGS:
            c = nc.semaphore(f"s_{e}")
            sem_objs[("eng", e)] = c.__enter__()
            ctxs.append(c)
        for j, k in enumerate(semkeys):
            c = nc.semaphore(f"d{j}")
            sem_objs[("dma", k)] = c.__enter__()
            ctxs.append(c)
        cnt = {}
        for i, it in enumerate(ins):
            if it["dma"]:
                key = ("dma", it["semkey"])
                cnt[key] = cnt.get(key, 0) + 16
                it["sig"] = (key, cnt[key])
            elif needed[i]:
                key = ("eng", it["eng"])
                cnt[key] = cnt.get(key, 0) + 1
                it["sig"] = (key, cnt[key])
            else:
                it["sig"] = None
        per = {e: [] for e in ENGS}
        for i, it in enumerate(ins):
            per[it["eng"]].append(i)
        final_waits = {}
        for f in self.final_ids:
            key, val = ins[f]["sig"]
            final_waits[key] = max(final_waits.get(key, 0), val)
        with nc.Block() as block:
            def make(e):
                def body(eng):
                    waited = {}
                    for i in per[e]:
                        it = ins[i]
                        req = {}
                        for d in it["deps"]:
                            key, val = ins[d]["sig"]
                            if waited.get(key, 0) >= val:
                                continue
                            req[key] = max(req.get(key, 0), val)
                        for key, val in req.items():
                            eng.wait_ge(sem_objs[key], val)
                            waited[key] = val
                        r = it["fn"](eng)
                        if it["sig"] is not None:
                            key, val = it["sig"]
                            r.then_inc(sem_objs[key], 16 if it["dma"] else 1)
                    if e == "sp":
                        for key, val in final_waits.items():
                            eng.wait_ge(sem_objs[key], val)
                return body
            block.tensor(make("pe"))
            block.scalar(make("act"))
            block.vector(make("dve"))
            block.gpsimd(make("pool"))
            block.sync(make("sp"))
        for c in reversed(ctxs):
            c.__exit__(None, None, None)
        return dict(n=n, per={e: len(per[e]) for e in ENGS}, sems=len(sem_objs), maxcnt=max(cnt.values()), cnt={k[1]: v for k, v in cnt.items() if k[0] == 'eng'})


def build_program(stage):
    nc = bass.Bass("TRN2", target_bir_lowering=False)
    es = ExitStack()

    def din(name, shape):
        return nc.dram_tensor(name, list(shape), F32, kind="ExternalInput").ap()

    def dout(name, shape):
        return nc.dram_tensor(name, list(shape), F32, kind="ExternalOutput").ap()

    def sb(name, shape, dt=F32):
        return es.enter_context(nc.sbuf_tensor("sb_" + name, list(shape), dt))

    P = Prog(nc)
    nc_allow = nc.allow_non_contiguous_dma(reason="small parameter vectors laid out feature-major")
    nc_allow.__enter__()

    x_in = din("x", [18 * 128, D])
    if stage == "A":
        cvec = din("cvec", [17, D])
        ada_w = din("ada_w", [2, D, 6 * D])
        ada_b = din("ada_b", [2, 6 * D])
        mod_out = dout("modout", [2, 128, 48 * 17])
    else:
        modin_d = din("modin", [128, 48 * 17])
    norm_mix = din("norm_mix", [2, D])
    norm_ffn = din("norm_ffn", [2, D])
    w_in = din("even_w_in", [D, IN_W])
    ident_d = din("ident", [128, 128])
    dk_d = din("dk", [128, 2, 4])
    dc_d = din("dc", [128, 2, 4])
    if stage == "A":
        lret_out = dout("lret", [4, 128, 128])
    if stage in ("C1", "C2"):
        wg_d = din("ffn_wg", [2, D, DFF])
        wu_d = din("ffn_wu", [2, D, DFF])
        wd_d = din("ffn_wd", [2, DFF, D])
        A_re_d = din("odd_A_re", [64, 64])
        A_im_d = din("odd_A_im", [64, 64])
        ldt_d = din("odd_log_dt", [1, 64])
        B_re_d = din("odd_B_re", [64, 64, 16])
        B_im_d = din("odd_B_im", [64, 64, 16])
        C_re_d = din("odd_C_re", [64, 16, 64])
        C_im_d = din("odd_C_im", [64, 16, 64])
        Dsk_d = din("odd_D", [1, D])
        glua_d = din("odd_glu_a", [D, D])
        glub_d = din("odd_glu_b", [D, D])
        s5r_in = din("s5_re", [16, 64, 64])
        s5i_in = din("s5_im", [16, 64, 64])
        fall_d = din("fall", [8, 128, 64])
        wsel_d = din("wsel", [128, 8])
        tal_d = din("tal", [128, 512])
        tbp_d = din("tbp", [128, 512])
        tbs_d = din("tbs", [128, 128])
        amask_d = din("amask", [128, 128])
        maskg_d = din("maskg", [128, 8])
        rott_d = din("rott", [128, 128])
        if stage == "C1":
            floc_out = dout("floc", [128, 64])
        else:
            y_out = dout("y", [17 * 128, D])
            ps5r_out = dout("p_s5r", [64, 64])
            ps5i_out = dout("p_s5i", [64, 64])
            ss5r_out = dout("s_s5r", [16, 64, 64])
            ss5i_out = dout("s_s5i", [16, 64, 64])
    if stage == "B":
        qgain = din("even_q_gain", [1, 64])
        kgain = din("even_k_gain", [1, 64])
        sinks_d = din("even_sinks", [1, 8])
        retg_d = din("even_ret_gain", [1, 512])
        w_out = din("even_w_out", [D, D])
        wg_d = din("ffn_wg", [2, D, DFF])
        wu_d = din("ffn_wu", [2, D, DFF])
        wd_d = din("ffn_wd", [2, DFF, D])
        cache_k = din("cache_k", [16, 128, 128])
        cache_v = din("cache_v", [16, 128, 128])
        sret_in = din("state_ret", [16, 4, 128, 128])
        bd_d = din("bdones", [128, 128])
        dneg_d = din("dneg", [128, 5, 128])
        rmask_d = din("rmask", [128, 2, 4, 128])
        dq_d = din("dq", [128, 2, 4, 128])
        seqmask_d = din("seqmask", [128, 16])
        wret_d = din("wret", [128, 8, 4])
        lall_d = din("lall", [8, 4, 128, 128])
        x1_out = dout("x1", [17 * 128, D])
        pk_out = dout("p_k", [128, 128])
        pv_out = dout("p_v", [128, 128])
        pret_out = dout("p_ret", [4, 128, 128])
        sk_out = dout("s_k", [16, 128, 128])
        sv_out = dout("s_v", [16, 128, 128])
        sret_out = dout("s_ret", [16, 4, 128, 128])

    Q = [es.enter_context(nc.psum_tensor(f"ps_Q{i}", [128, 1024], F32)) for i in range(4)]

    def bank(i, h):
        return Q[i][:, h * 512:(h + 1) * 512]

    def bkey(i, h):
        return f"Q{i}{'ab'[h]}"

    ARENA_W = 33 * 1024
    arena = sb("arena", [128, ARENA_W])
    cur = [0]

    def carve(shape, dt=F32):
        n = int(np.prod(shape[1:]))
        words = n if dt in (F32, I32) else (n + 1) // 2
        words = (words + 7) // 8 * 8
        off = cur[0]
        assert off + words <= ARENA_W, ("arena overflow", off, words)
        cur[0] = off + words
        v = arena[:, off:off + words]
        if dt != F32:
            v = v.bitcast(dt)
        v = v[:, 0:n]
        if len(shape) == 3:
            v = v.rearrange("p (a b) -> p a b", a=shape[1])
        elif len(shape) == 4:
            v = v.rearrange("p (a b c) -> p a b c", a=shape[1], b=shape[2])
        elif len(shape) == 5:
            v = v.rearrange("p (a b c d) -> p a b c d", a=shape[1], b=shape[2], c=shape[3])
        return v

    ident = sb("ident", [128, 128])
    ones_m = sb("ones_m", [128, 128], BF16)
    ones_b = sb("ones_b", [128, 128], BF16)
    ones_g = sb("ones_g", [128, 128], BF16)
    epsb = sb("epsb", [128, 1])
    xT = sb("xT", [128, KC, NTOK])
    cT = sb("cT", [128, KC, 17], BF16)
    adab = sb("adab", [128, 2, 48])
    modT = sb("modT", [128, 48, 17])
    normg = sb("normg", [128, 2, 2, KC])
    G = sb("G", [128, KC, 17])
    dk = sb("dk", [128, 2, 4])
    dc = sb("dc", [128, 2, 4])
    P.dma("sp", lambda e: e.dma_start(out=ident[:], in_=ident_d), writes=["ident"], semkey="ld_ident", group="ld_ident")
    P.dma("sp", lambda e: e.dma_start(out=dk[:], in_=dk_d), writes=["dk"], semkey="ld_dk", group="ld_dk")
    P.dma("sp", lambda e: e.dma_start(out=dc[:], in_=dc_d), writes=["dc"], semkey="ld_dc", group="ld_dc")
    if stage == "A":
        P.dma("sp", lambda e: e.dma_start(out=adab[:], in_=ada_b.rearrange("l (m p) -> p l m", p=128)), writes=["adab"], semkey="ld_adab", group="ld_adab")
    P.dma("sp", lambda e: e.dma_start(out=normg[:, 0], in_=norm_mix.rearrange("l (k p) -> p l k", p=128)), writes=["normg"], semkey="ld_normg", group="ld_normg")
    P.dma("sp", lambda e: e.dma_start(out=normg[:, 1], in_=norm_ffn.rearrange("l (k p) -> p l k", p=128)), writes=["normg"], semkey="ld_normg", group="ld_normg")
    P.op("dve", lambda e: e.memset(ones_m[:], 1.0 / 1024.0), writes=["ones_m"])
    P.op("dve", lambda e: e.memset(ones_b[:], 1.0), writes=["ones_b"])
    P.op("dve", lambda e: e.memset(ones_g[:], 1.0 / 128.0), writes=["ones_g"])
    P.op("dve", lambda e: e.memset(epsb[:], EPS), writes=["epsb"])

    mark0 = cur[0]
    xst = [carve([128, D]) for _ in range(2)]
    c_sb = carve([128, D])
    adaw = [carve([128, KC, 768], BF16) for _ in range(2)]
    pT = [Q[i][:, :].rearrange("p (k n) -> p k n", k=KC) for i in range(2)]

    def load_tile(t, dst_ap, dst_key):
        s = t % 2
        P.dma("sp", lambda e: e.dma_start(out=xst[s], in_=x_in[t * 128:(t + 1) * 128, :]), writes=[f"xst{s}"], semkey=f"xst{s}")
        for k in range(KC):
            P.op("pe", lambda e, k=k: e.transpose(pT[s][:, k, :], xst[s][:, k * 128:(k + 1) * 128], ident[:]),
                 reads=[f"xst{s}", "ident"], writes=[bkey(s, 0), bkey(s, 1)])
        P.op("act", lambda e: e.copy(out=dst_ap, in_=pT[s]), reads=[bkey(s, 0), bkey(s, 1)], writes=[dst_key])

    for t in range(1, 18):
        c0 = (t - 1) * 128
        load_tile(t, xT[:, :, c0:c0 + 128], ("xT", (t - 1) // 4))

    if stage == "A":
        P.dma("sp", lambda e: e.dma_start(out=c_sb[0:17, :], in_=cvec), writes=["c_sb"], semkey="c_sb")
        P.op("act", lambda e: e.activation(out=c_sb[0:17, :], in_=c_sb[0:17, :], func=AF.Silu), reads=["c_sb"], writes=["c_sb"])
        for k in range(KC):
            P.op("pe", lambda e, k=k: e.transpose(pT[0][:, k, 0:17], c_sb[0:17, k * 128:(k + 1) * 128], ident[0:17, 0:17]),
                 reads=["c_sb", "ident"], writes=[bkey(0, 0), bkey(0, 1)])
        P.op("dve", lambda e: e.tensor_copy(out=cT[:], in_=pT[0][:, :, 0:17]), reads=[bkey(0, 0), bkey(0, 1)], writes=["cT"])

    pm = [bank(2, i)[:, 0:408].rearrange("p (m c) -> p m c", c=17) for i in range(2)]

    def modulation(layer):
        for cb in range(8):
            s = cb % 2
            P.dma("pool", lambda e, cb=cb, s=s: e.dma_start(out=adaw[s], in_=ada_w[layer, :, cb * 768:(cb + 1) * 768].rearrange("(k p) n -> p k n", p=128)),
                  writes=[f"adaw{s}"], semkey=f"adaw{s}")
            for mm in range(6):
                m = cb * 6 + mm
                for k in range(KC):
                    P.op("pe", lambda e, k=k, s=s, m=m, mm=mm: e.matmul(pm[m // 24][:, m % 24, :], lhsT=adaw[s][:, k, mm * 128:(mm + 1) * 128],
                                                                 rhs=cT[:, k, :], start=(k == 0), stop=(k == KC - 1)),
                         reads=[f"adaw{s}", "cT"], writes=[bkey(2, m // 24)])
        for h in range(2):
            P.op("dve", lambda e, h=h: e.tensor_tensor(out=modT[:, h * 24:(h + 1) * 24, :], in0=pm[h],
                                                       in1=adab[:, layer, h * 24:(h + 1) * 24].unsqueeze(2).to_broadcast([128, 24, 17]),
                                                       op=ALU.add),
                 reads=[bkey(2, h), "adab"], writes=["modT"])

    def make_G(layer, which):
        r0 = 8 if which == 0 else 32
        P.op("dve", lambda e: e.tensor_scalar(out=G[:], in0=modT[:, r0:r0 + 8, :], scalar1=1.0, scalar2=None, op0=ALU.add),
             reads=["modT"], writes=["G"])
        P.op("dve", lambda e: e.tensor_tensor(out=G[:], in0=G[:], in1=normg[:, which, layer, :].unsqueeze(2).to_broadcast([128, KC, 17]), op=ALU.mult),
             reads=["G", "normg"], writes=["G"])

    LAYER = 1 if stage in ("C1", "C2") else 0
    if stage == "A":
        for lay in (1, 0):
            modulation(lay)
            P.dma("sp", lambda e, lay=lay: e.dma_start(out=mod_out[lay], in_=modT[:].rearrange("p m c -> p (m c)")), reads=["modT"], semkey="modo", final=True)
    else:
        P.dma("sp", lambda e: e.dma_start(out=modT[:].rearrange("p m c -> p (m c)"), in_=modin_d), writes=["modT"], semkey="ld_modT")
    make_G(LAYER, 0)
    P.barrier()
    cur[0] = mark0

    sq = [carve([128, BT], BF16) for _ in range(2)]
    rstd = carve([128, BT])
    tn = [carve([128, BT]) for _ in range(2)]
    mark_ln = cur[0]
    hT = carve([128, KC, BT], BF16)
    p_ss = bank(3, 0)

    def ln_block(src_ap, src_key, ntok, sample, shift_row, out_ap=None, out_key="hT"):
        out_ap = hT if out_ap is None else out_ap
        for k in range(KC):
            s = k % 2
            P.op("act", lambda e, k=k, s=s: e.activation(out=sq[s][:, :ntok], in_=src_ap[:, k, :], func=AF.Square),
                 reads=[src_key], writes=[f"sq{s}"])
            P.op("pe", lambda e, k=k, s=s: e.matmul(p_ss[:, :ntok], lhsT=ones_m[:], rhs=sq[s][:, :ntok], start=(k == 0), stop=(k == KC - 1)),
                 reads=[f"sq{s}", "ones_m"], writes=[bkey(3, 0)])
        P.op("act", lambda e: e.activation(out=rstd[:, :ntok], in_=p_ss[:, :ntok], func=AF.Sqrt, bias=epsb[:], scale=1.0),
             reads=[bkey(3, 0), "epsb"], writes=["rstd"])
        P.op("dve", lambda e: e.reciprocal(out=rstd[:, :ntok], in_=rstd[:, :ntok]), reads=["rstd"], writes=["rstd"])
        for k in range(KC):
            s = k % 2
            P.op("dve", lambda e, k=k, s=s: e.tensor_tensor(out=tn[s][:, :ntok], in0=src_ap[:, k, :], in1=rstd[:, :ntok], op=ALU.mult),
                 reads=[src_key, "rstd"], writes=[f"tn{s}"])
            if not sample:
                P.op("act", lambda e, k=k, s=s: e.activation(out=out_ap[:, k, :ntok], in_=tn[s][:, :ntok], func=AF.Identity,
                                                             scale=G[:, k, 0:1], bias=modT[:, shift_row + k, 0:1]),
                     reads=[f"tn{s}", "G", "modT"], writes=[out_key])
            else:
                tv = tn[s][:, :128].rearrange("p (b t) -> p b t", t=8)
                P.op("dve", lambda e, k=k, tv=tv: e.tensor_tensor(out=tv, in0=tv, in1=G[:, k, 1:17].unsqueeze(2).to_broadcast([128, 16, 8]), op=ALU.mult),
                     reads=[f"tn{s}", "G"], writes=[f"tn{s}"])
                P.op("dve", lambda e, k=k, tv=tv: e.tensor_tensor(out=out_ap[:, k, :128].rearrange("p (b t) -> p b t", t=8), in0=tv,
                                                                  in1=modT[:, shift_row + k, 1:17].unsqueeze(2).to_broadcast([128, 16, 8]), op=ALU.add),
                     reads=[f"tn{s}", "modT"], writes=[out_key])

    win = None
    if stage in ("A", "B"):
        win = carve([128, KC, IN_W], BF16)
        for k in range(KC):
            P.dma("pool", lambda e, k=k: e.dma_start(out=win[:, k, :], in_=w_in[k * 128:(k + 1) * 128, :]), writes=["win"], semkey="win", group="win")

    S = carve([128, 4, 128])
    Sb = carve([128, 4, 128], BF16)
    kd_tok = carve([128, 512], BF16)
    vb_tok = carve([128, 512], BF16)
    p_kb = bank(3, 1)
    p_vb = bank(2, 0)
    p_su = bank(2, 1).rearrange("p (h e) -> p h e", h=4)

    def tok_kv(tc0, grp):
        for k in range(KC):
            P.op("pe", lambda e, k=k: e.matmul(p_kb, lhsT=hT[:, k, tc0:tc0 + 128], rhs=win[:, k, 1280:1792], start=(k == 0), stop=(k == KC - 1)),
                 reads=["hT", "win"], writes=[bkey(3, 1)])
        for k in range(KC):
            P.op("pe", lambda e, k=k: e.matmul(p_vb, lhsT=hT[:, k, tc0:tc0 + 128], rhs=win[:, k, 1792:2304], start=(k == 0), stop=(k == KC - 1)),
                 reads=["hT", "win"], writes=[bkey(2, 0)])
        P.op("dve", lambda e: e.tensor_tensor(out=kd_tok.rearrange("p (h d) -> p h d", h=4), in0=p_kb.rearrange("p (h d) -> p h d", h=4),
                                              in1=dk[:, grp, :].unsqueeze(2).to_broadcast([128, 4, 128]), op=ALU.mult),
             reads=[bkey(3, 1), "dk"], writes=["kd_tok"])
        P.op("act", lambda e: e.copy(out=vb_tok, in_=p_vb), reads=[bkey(2, 0)], writes=["vb_tok"])

    def state_update():
        for h in range(4):
            P.op("pe", lambda e, h=h: e.matmul(p_su[:, h, :], lhsT=kd_tok[:, h * 128:(h + 1) * 128], rhs=vb_tok[:, h * 128:(h + 1) * 128], start=True, stop=True),
                 reads=["kd_tok", "vb_tok"], writes=[bkey(2, 1)])
        P.op("dve", lambda e: e.tensor_tensor(out=S, in0=S, in1=dc[:, 0, :].unsqueeze(2).to_broadcast([128, 4, 128]), op=ALU.mult),
             reads=["S", "dc"], writes=["S"])
        P.op("dve", lambda e: e.tensor_tensor(out=S, in0=S, in1=p_su, op=ALU.add), reads=["S", bkey(2, 1)], writes=["S"])

    def ffn(layer):
        GC = 2
        NG = FC // GC
        P.barrier()
        cur[0] = mark_ln
        make_G(layer, 1)
        hT_all = carve([128, KC, NTOK], BF16)
        wslot = [(carve([128, KC, GC * 128], BF16), carve([128, KC, GC * 128], BF16), carve([128, GC, D], BF16)) for _ in range(3)]
        hid = [carve([128, GC, 512], BF16) for _ in range(2)]
        sgf = [carve([128, 512]) for _ in range(2)]
        gt2 = carve([128, 128])
        print("arena words used (ffn):", cur[0], "of", ARENA_W)
        for t in range(NPT):
            ln_block(xT[:, :, t * 128:(t + 1) * 128], ("xT", t // 4), 128, False, 24, out_ap=hT_all[:, :, t * 128:(t + 1) * 128], out_key="hT_all")
        ln_block(xT[:, :, TOK_P:TOK_P + 128], ("xT", 4), 128, True, 24, out_ap=hT_all[:, :, TOK_P:TOK_P + 128], out_key="hT_all")
        UP = [(bank(0, 0), bkey(0, 0), bank(0, 1), bkey(0, 1)), (bank(1, 0), bkey(1, 0), bank(1, 1), bkey(1, 1))]
        DN = [(bank(2, 0), bkey(2, 0)), (bank(2, 1), bkey(2, 1)), (bank(3, 0), bkey(3, 0)), (bank(3, 1), bkey(3, 1))]
        ui = 0
        di = 0
        blocks = [(b * 512, 512, False) for b in range(4)] + [(TOK_P, 128, True)]
        for g in range(NG):
            ws = g % 3
            wgs, wus, wds = wslot[ws]
            c0h = g * GC * 128
            P.dma("pool", lambda e, wgs=wgs, c0h=c0h: e.dma_start(out=wgs, in_=wg_d[layer, :, c0h:c0h + GC * 128].rearrange("(k p) n -> p k n", p=128)),
                  writes=[f"wg{ws}"], semkey=f"wg{ws}")
            P.dma("pool", lambda e, wus=wus, c0h=c0h: e.dma_start(out=wus, in_=wu_d[layer, :, c0h:c0h + GC * 128].rearrange("(k p) n -> p k n", p=128)),
                  writes=[f"wu{ws}"], semkey=f"wu{ws}")
            P.dma("pool", lambda e, wds=wds, c0h=c0h: e.dma_start(out=wds, in_=wd_d[layer, c0h:c0h + GC * 128, :].rearrange("(c p) n -> p c n", p=128)),
                  writes=[f"wd{ws}"], semkey=f"wd{ws}")
            for (t0, nt, smp) in blocks:
                hs = ui % 2
                for c in range(GC):
                    gps, gkey, ups, ukey = UP[ui % 2]
                    ui += 1
                    for k in range(KC):
                        P.op("pe", lambda e, k=k, c=c, gps=gps, wgs=wgs, nt=nt, t0=t0: e.matmul(gps[:, :nt], lhsT=wgs[:, k, c * 128:(c + 1) * 128], rhs=hT_all[:, k, t0:t0 + nt],
                                                                                  start=(k == 0), stop=(k == KC - 1)),
                             reads=[f"wg{ws}", "hT_all"], writes=[gkey])
                    for k in range(KC):
                        P.op("pe", lambda e, k=k, c=c, ups=ups, wus=wus, nt=nt, t0=t0: e.matmul(ups[:, :nt], lhsT=wus[:, k, c * 128:(c + 1) * 128], rhs=hT_all[:, k, t0:t0 + nt],
                                                                                  start=(k == 0), stop=(k == KC - 1)),
                             reads=[f"wu{ws}", "hT_all"], writes=[ukey])
                    sgs = sgf[c % 2]
                    P.op("act", lambda e, gps=gps, sgs=sgs, nt=nt: e.activation(out=sgs[:, :nt], in_=gps[:, :nt], func=AF.Silu), reads=[gkey], writes=[f"sgf{c % 2}"])
                    P.op("dve", lambda e, ups=ups, sgs=sgs, hs=hs, c=c, nt=nt: e.tensor_tensor(out=hid[hs][:, c, :nt], in0=ups[:, :nt], in1=sgs[:, :nt], op=ALU.mult),
                         reads=[ukey, f"sgf{c % 2}"], writes=[f"hid{hs}"])
                for m in range(KC):
                    dps, dkey = DN[di % 4]
                    di += 1
                    for c in range(GC):
                        P.op("pe", lambda e, m=m, c=c, dps=dps, wds=wds, hs=hs, nt=nt: e.matmul(dps[:, :nt], lhsT=wds[:, c, m * 128:(m + 1) * 128], rhs=hid[hs][:, c, :nt],
                                                                                         start=(c == 0), stop=(c == GC - 1)),
                             reads=[f"wd{ws}", f"hid{hs}"], writes=[dkey])
                    xs = xT[:, m, t0:t0 + nt]
                    xkey = ("xT", t0 // 512)
                    if not smp:
                        P.op("dve", lambda e, m=m, dps=dps, xs=xs, nt=nt: e.scalar_tensor_tensor(out=xs, in0=dps[:, :nt], scalar=modT[:, 40 + m, 0:1], in1=xs,
                                                                                         op0=ALU.mult, op1=ALU.add),
                             reads=[dkey, "modT", xkey], writes=[xkey])
                    else:
                        gv = gt2.rearrange("p (b t) -> p b t", t=8)
                        P.op("dve", lambda e, m=m, dps=dps, gv=gv: e.tensor_tensor(out=gv, in0=dps[:, :128].rearrange("p (b t) -> p b t", t=8),
                                                                                  in1=modT[:, 40 + m, 1:17].unsqueeze(2).to_broadcast([128, 16, 8]), op=ALU.mult),
                             reads=[dkey, "modT"], writes=["gt2"])
                        P.op("dve", lambda e, xs=xs: e.tensor_tensor(out=xs, in0=xs, in1=gt2, op=ALU.add), reads=["gt2", xkey], writes=[xkey])
        return hT_all

    if stage == "A":
        P.op("dve", lambda e: e.memset(S, 0.0), writes=["S"])
        for b in range(TOK_P // BT):
            ln_block(xT[:, :, b * BT:(b + 1) * BT], ("xT", (b * BT) // 512), BT, False, 0)
            for i in range(TPB):
                tok_kv(i * 128, 0)
                state_update()
        P.dma("sp", lambda e: e.dma_start(out=lret_out.rearrange("h d e -> d h e"), in_=S), reads=["S"], semkey="lret", final=True)
        stats = P.build()
        nc_allow.__exit__(None, None, None)
        es.close()
        return nc, stats

    if stage in ("C1", "C2"):
        full = stage == "C2"
        TWO_PI = 2.0 * np.pi
        cur[0] = mark_ln
        hT_all = carve([128, KC, NTOK], BF16)
        for t in range(NPT):
            ln_block(xT[:, :, t * 128:(t + 1) * 128], ("xT", t // 4), 128, False, 0, out_ap=hT_all[:, :, t * 128:(t + 1) * 128], out_key="hT_all")
        ln_block(xT[:, :, TOK_P:TOK_P + 128], ("xT", 4), 128, True, 0, out_ap=hT_all[:, :, TOK_P:TOK_P + 128], out_key="hT_all")

        def t64():
            return carve([128, 64])
        AreT, AimT, dtT, ar, ai, rho, thr, ph64, ph512, sinT, cosT, tmpa, tmpb, lre, lim, fre, fim, rden64 = [t64() for _ in range(18)]
        PHB = carve([128, 64, 4])
        pib = carve([128, 1])
        Glast = carve([128, 64])
        Alast = carve([128, 64])
        Blast = carve([128, 64])
        tal = carve([128, 512])
        tbp = carve([128, 512])
        tbs = carve([128, 128])
        amask = carve([128, 128])
        maskg = carve([128, 8])
        rott = carve([128, 128])
        Dsk = carve([128, KC])
        Mc = carve([128, KC, 128], BF16)
        Mcsw = carve([128, KC, 128], BF16)
        CA = carve([128, 64, 128], BF16)
        CB = carve([128, 64, 128], BF16)
        mark_s5 = cur[0]
        Bn_re = carve([128, 64, 16])
        Bn_im = carve([128, 64, 16])
        tB1 = carve([128, 64, 16])
        tB2 = carve([128, 64, 16])
        Cn = carve([128, 16, 2, 64])
        Cn2 = carve([128, 16, 2, 64])
        CsA = carve([128, 64, 16])
        CsB = carve([128, 64, 16])
        print("arena words used (s5 prep):", cur[0], "of", ARENA_W)
        P.op("dve", lambda e: e.memset(pib, float(np.pi / 2)), writes=["pib"])
        for half in range(2):
            hs_ = slice(half * 64, half * 64 + 64)
            P.dma("sp", lambda e, hs_=hs_: e.dma_start(out=AreT[hs_, :], in_=A_re_d.rearrange("g p -> p g")), writes=["AreT"], semkey="ld_AreT", group="ld_AreT")
            P.dma("sp", lambda e, hs_=hs_: e.dma_start(out=AimT[hs_, :], in_=A_im_d.rearrange("g p -> p g")), writes=["AimT"], semkey="ld_AimT", group="ld_AimT")
        P.dma("sp", lambda e: e.dma_start(out=dtT, in_=ldt_d.partition_broadcast(128)), writes=["dtT"], semkey="ld_dtT")
        for nm, tl, dd in (("tal", tal, tal_d), ("tbp", tbp, tbp_d), ("tbs", tbs, tbs_d), ("amask", amask, amask_d), ("maskg", maskg, maskg_d), ("rott", rott, rott_d)):
            P.dma("sp", lambda e, tl=tl, dd=dd: e.dma_start(out=tl, in_=dd), writes=[nm], semkey="ld_" + nm)
        P.dma("sp", lambda e: e.dma_start(out=Dsk, in_=Dsk_d.rearrange("o (k p) -> p (o k)", p=128)), writes=["Dsk"], semkey="ld_Dsk")
        P.dma("sp", lambda e: e.dma_start(out=Bn_re[0:64], in_=B_re_d.rearrange("g p j -> p g j")), writes=["Bn_re"], semkey="ld_Bn_re")
        P.dma("sp", lambda e: e.dma_start(out=Bn_im[0:64], in_=B_im_d.rearrange("g p j -> p g j")), writes=["Bn_im"], semkey="ld_Bn_im")
        P.dma("sp", lambda e: e.dma_start(out=Cn[0:64, :, 0, :], in_=C_re_d), writes=["Cn"], semkey="ld_Cn", group="ld_Cn")
        P.dma("sp", lambda e: e.dma_start(out=Cn[0:64, :, 1, :], in_=C_im_d), writes=["Cn"], semkey="ld_Cn", group="ld_Cn")
        P.dma("sp", lambda e: e.dma_start(out=Cn2[0:64, :, 0, :], in_=C_im_d), writes=["Cn2"], semkey="ld_Cn2", group="ld_Cn2")
        P.dma("sp", lambda e: e.dma_start(out=Cn2[0:64, :, 1, :], in_=C_re_d), writes=["Cn2"], semkey="ld_Cn2", group="ld_Cn2")

        def V(fn, reads, writes, eng="dve"):
            P.op(eng, fn, reads=reads, writes=writes)

        V(lambda e: e.activation(out=dtT, in_=dtT, func=AF.Exp), ["dtT"], ["dtT"], "act")
        V(lambda e: e.tensor_tensor(out=ar, in0=AreT, in1=dtT, op=ALU.mult), ["AreT", "dtT"], ["ar"])
        V(lambda e: e.tensor_tensor(out=ai, in0=AimT, in1=dtT, op=ALU.mult), ["AimT", "dtT"], ["ai"])
        V(lambda e: e.activation(out=rho, in_=ar, func=AF.Exp), ["ar"], ["rho"], "act")
        ti64 = carve([128, 64], I32)
        aiT = t64()

        def fracr(out, okey, inp, ikey):
            V(lambda e: e.tensor_copy(out=ti64, in_=inp), [ikey], ["ti64"])
            V(lambda e: e.tensor_copy(out=tmpb, in_=ti64), ["ti64"], ["tmpb"])
            V(lambda e: e.tensor_tensor(out=out, in0=inp, in1=tmpb, op=ALU.subtract), [ikey, "tmpb"], [okey])

        V(lambda e: e.tensor_scalar(out=aiT, in0=ai, scalar1=float(1.0 / TWO_PI), scalar2=None, op0=ALU.mult), ["ai"], ["aiT"])
        fracr(thr, "thr", aiT, "aiT")
        V(lambda e: e.tensor_scalar(out=tmpa, in0=aiT, scalar1=64.0, scalar2=None, op0=ALU.mult), ["aiT"], ["tmpa"])
        fracr(ph64, "ph64", tmpa, "tmpa")
        V(lambda e: e.tensor_scalar(out=tmpa, in0=aiT, scalar1=512.0, scalar2=None, op0=ALU.mult), ["aiT"], ["tmpa"])
        fracr(ph512, "ph512", tmpa, "tmpa")
        for blk in range(4):
            V(lambda e, blk=blk: e.tensor_scalar(out=tmpa, in0=ph512, scalar1=float(blk), scalar2=None, op0=ALU.mult), ["ph512"], ["tmpa"])
            fracr(PHB[:, :, blk], "PHB", tmpa, "tmpa")

        def sincos(ang, akey, s_out, skey, c_out, ckey, tmp, tkey):
            V(lambda e: e.activation(out=s_out, in_=ang, func=AF.Sin, scale=TWO_PI), [akey], [skey], "act")
            V(lambda e: e.activation(out=tmp, in_=ang, func=AF.Abs), [akey], [tkey], "act")
            V(lambda e: e.activation(out=c_out, in_=tmp, func=AF.Sin, scale=-TWO_PI, bias=pib[:]), [tkey, "pib"], [ckey], "act")

        sincos(thr, "thr", sinT, "sinT", cosT, "cosT", tmpa, "tmpa")
        V(lambda e: e.tensor_tensor(out=lre, in0=rho, in1=cosT, op=ALU.mult), ["rho", "cosT"], ["lre"])
        V(lambda e: e.tensor_tensor(out=lim, in0=rho, in1=sinT, op=ALU.mult), ["rho", "sinT"], ["lim"])
        V(lambda e: e.tensor_scalar(out=tmpa, in0=lre, scalar1=-1.0, scalar2=None, op0=ALU.add), ["lre"], ["tmpa"])
        V(lambda e: e.tensor_tensor(out=rden64, in0=AreT, in1=AreT, op=ALU.mult), ["AreT"], ["rden64"])
        V(lambda e: e.tensor_tensor(out=tmpb, in0=AimT, in1=AimT, op=ALU.mult), ["AimT"], ["tmpb"])
        V(lambda e: e.tensor_tensor(out=rden64, in0=rden64, in1=tmpb, op=ALU.add), ["rden64", "tmpb"], ["rden64"])
        V(lambda e: e.reciprocal(out=rden64, in_=rden64), ["rden64"], ["rden64"])
        V(lambda e: e.tensor_tensor(out=fre, in0=tmpa, in1=AreT, op=ALU.mult), ["tmpa", "AreT"], ["fre"])
        V(lambda e: e.tensor_tensor(out=tmpb, in0=lim, in1=AimT, op=ALU.mult), ["lim", "AimT"], ["tmpb"])
        V(lambda e: e.tensor_tensor(out=fre, in0=fre, in1=tmpb, op=ALU.add), ["fre", "tmpb"], ["fre"])
        V(lambda e: e.tensor_tensor(out=fre, in0=fre, in1=rden64, op=ALU.mult), ["fre", "rden64"], ["fre"])
        V(lambda e: e.tensor_tensor(out=fim, in0=lim, in1=AreT, op=ALU.mult), ["lim", "AreT"], ["fim"])
        V(lambda e: e.tensor_tensor(out=tmpb, in0=tmpa, in1=AimT, op=ALU.mult), ["tmpa", "AimT"], ["tmpb"])
        V(lambda e: e.tensor_tensor(out=fim, in0=fim, in1=tmpb, op=ALU.subtract), ["fim", "tmpb"], ["fim"])
        V(lambda e: e.tensor_tensor(out=fim, in0=fim, in1=rden64, op=ALU.mult), ["fim", "rden64"], ["fim"])
        h64 = slice(0, 64)
        frb = fre[h64, :].unsqueeze(2).to_broadcast([64, 64, 16])
        fib = fim[h64, :].unsqueeze(2).to_broadcast([64, 64, 16])
        V(lambda e: e.tensor_tensor(out=tB1[h64], in0=Bn_re[h64], in1=frb, op=ALU.mult), ["Bn_re", "fre"], ["tB1"])
        V(lambda e: e.tensor_tensor(out=tB2[h64], in0=Bn_im[h64], in1=fib, op=ALU.mult), ["Bn_im", "fim"], ["tB2"])
        V(lambda e: e.tensor_tensor(out=tB1[h64], in0=tB1[h64], in1=tB2[h64], op=ALU.subtract), ["tB1", "tB2"], ["tB1"])
        V(lambda e: e.tensor_tensor(out=tB2[h64], in0=Bn_im[h64], in1=frb, op=ALU.mult), ["Bn_im", "fre"], ["tB2"])
        V(lambda e: e.tensor_tensor(out=Bn_im[h64], in0=Bn_re[h64], in1=fib, op=ALU.mult), ["Bn_re", "fim", "tB2"], ["Bn_im"])
        V(lambda e: e.tensor_tensor(out=tB2[h64], in0=tB2[h64], in1=Bn_im[h64], op=ALU.add), ["tB2", "Bn_im"], ["tB2"])
        ptr = bank(3, 0)
        for F in range(KC):
            P.op("pe", lambda e, F=F: e.transpose(ptr[:, 0:64], tB1[h64, 8 * F:8 * F + 8, :].rearrange("p a b -> p (a b)"), ident[0:64, 0:64]),
                 reads=["tB1", "ident"], writes=[bkey(3, 0)])
            P.op("pe", lambda e, F=F: e.transpose(ptr[:, 64:128], tB2[h64, 8 * F:8 * F + 8, :].rearrange("p a b -> p (a b)"), ident[0:64, 0:64]),
                 reads=["tB2", "ident"], writes=[bkey(3, 0)])
            V(lambda e, F=F: e.copy(out=Mc[:, F, :], in_=ptr[:, 0:128]), [bkey(3, 0)], ["Mc"], "act")
            V(lambda e, F=F: e.copy(out=Mcsw[:, F, 0:64], in_=ptr[:, 64:128]), [bkey(3, 0)], ["Mcsw"], "act")
            V(lambda e, F=F: e.mul(out=Mcsw[:, F, 64:128], in_=ptr[:, 0:64], mul=-1.0), [bkey(3, 0)], ["Mcsw"], "act")
        for (src, skey, dst, dkey) in ((Cn, "Cn", CsA, "CsA"), (Cn2, "Cn2", CsB, "CsB")):
            for ib in range(2):
                for ii in range(8):
                    i_ = ib * 8 + ii
                    P.op("pe", lambda e, src=src, i_=i_, ii=ii: e.transpose(ptr[:, ii * 64:(ii + 1) * 64], src[h64, i_, :, :].rearrange("p c q -> p (c q)"), ident[0:64, 0:64]),
                         reads=[skey, "ident"], writes=[bkey(3, 0)])
                V(lambda e, dst=dst, ib=ib: e.copy(out=dst[:, :, ib * 8:(ib + 1) * 8].rearrange("p g i -> p i g"), in_=ptr.rearrange("p (i g) -> p i g", i=8)),
                  [bkey(3, 0)], [dkey], "act")
        V(lambda e: e.tensor_scalar(out=CsA[64:128], in0=CsA[64:128], scalar1=-1.0, scalar2=None, op0=ALU.mult), ["CsA"], ["CsA"])
        V(lambda e: e.tensor_scalar(out=CsB, in0=CsB, scalar1=-1.0, scalar2=None, op0=ALU.mult), ["CsB"], ["CsB"])
        V(lambda e: e.memset(CA, 0.0), [], ["CA"], "pool")
        V(lambda e: e.memset(CB, 0.0), [], ["CB"], "pool")
        for gl in range(8):
            for (src, skey, dst, dkey) in ((CsA, "CsA", CA, "CA"), (CsB, "CsB", CB, "CB")):
                V(lambda e, src=src, dst=dst, gl=gl: e.tensor_copy(out=dst.rearrange("p (f g) c -> p f g c", g=8)[:, :, gl, 16 * gl:16 * gl + 16],
                                                                   in_=src.rearrange("p (f g) i -> p f g i", g=8)[:, :, gl, :]),
                  [skey], [dkey])

        hin = carve([128, 64]) if False else None
        P.op("dve", lambda e: e.memset(Glast, 0.0), writes=["Glast"])
        if full:
            ang = tmpa
            V(lambda e: e.tensor_scalar(out=ang, in0=aiT, scalar1=2048.0, scalar2=None, op0=ALU.mult), ["aiT"], ["tmpa"])
            fracr(fim, "fim", ang, "tmpa")
            sincos(fim, "fim", sinT, "sinT", cosT, "cosT", fre, "fre")
            V(lambda e: e.activation(out=lre, in_=ar, func=AF.Exp, scale=2048.0), ["ar"], ["lre"], "act")
            V(lambda e: e.tensor_tensor(out=lim, in0=lre, in1=sinT, op=ALU.mult), ["lre", "sinT"], ["lim"])
            V(lambda e: e.tensor_tensor(out=lre, in0=lre, in1=cosT, op=ALU.mult), ["lre", "cosT"], ["lre"])
            wsel = carve([128, 8])
            fr = [carve([128, 64]) for _ in range(2)]
            P.dma("sp", lambda e: e.dma_start(out=wsel, in_=wsel_d), writes=["wsel"], semkey="ld_wsel")
            prot = bank(3, 1)[:, 0:64]
            for r in range(7):
                s_ = r % 2
                P.dma("sp", lambda e, r=r, s_=s_: e.dma_start(out=fr[s_], in_=fall_d[r]), writes=[f"fr{s_}"], semkey=f"fr{s_}")
                P.op("pe", lambda e: e.matmul(prot, lhsT=rott, rhs=Glast, start=True, stop=True), reads=["rott", "Glast"], writes=[bkey(3, 1)])
                V(lambda e: e.tensor_tensor(out=tmpa, in0=lre, in1=Glast, op=ALU.mult), ["lre", "Glast"], ["tmpa"])
                V(lambda e: e.tensor_tensor(out=tmpb, in0=lim, in1=prot, op=ALU.mult), ["lim", bkey(3, 1)], ["tmpb"])
                V(lambda e: e.tensor_tensor(out=tmpa, in0=tmpa, in1=tmpb, op=ALU.add), ["tmpa", "tmpb"], ["tmpa"])
                V(lambda e, s_=s_: e.tensor_tensor(out=tmpa, in0=tmpa, in1=fr[s_], op=ALU.add), ["tmpa", f"fr{s_}"], ["tmpa"])
                V(lambda e: e.tensor_tensor(out=tmpa, in0=tmpa, in1=Glast, op=ALU.subtract), ["tmpa", "Glast"], ["tmpa"])
                V(lambda e, r=r: e.scalar_tensor_tensor(out=Glast, in0=tmpa, scalar=wsel[:, r:r + 1], in1=Glast, op0=ALU.mult, op1=ALU.add),
                  ["tmpa", "wsel", "Glast"], ["Glast"])
        P.barrier()
        cur[0] = mark_s5
        NW = 2
        um = [carve([128, 512], BF16) for _ in range(NW)]
        xang = [carve([128, 512]) for _ in range(NW)]
        sang = [carve([128, 512]) for _ in range(NW)]
        SINt = [carve([128, 512]) for _ in range(NW)]
        COSt = [carve([128, 512]) for _ in range(NW)]
        Wt = [carve([128, 512]) for _ in range(NW)]
        Ab = [carve([128, 512], BF16) for _ in range(NW)]
        Bb = [carve([128, 512], BF16) for _ in range(NW)]
        aseq = carve([128, 128])
        ysb = carve([128, 512])
        y2 = carve([128, 512])
        H0T = carve([128, 64, 16])
        Asl = carve([128, 64, 16])
        Bsl = carve([128, 64, 16])
        h0n = carve([128, 64, 128]) if False else None
        print("arena words used (s5 main):", cur[0], "of", ARENA_W)
        BU = [(bank(0, 0), bkey(0, 0), bank(0, 1), bkey(0, 1)), (bank(1, 0), bkey(1, 0), bank(1, 1), bkey(1, 1))]
        YP = [(bank(2, 0), bkey(2, 0)), (bank(2, 1), bkey(2, 1))]
        wi = [0]

        def s5_T(F, gl, t0, nt, blk, sample):
            ops = []
            D = lambda fn, reads, writes, eng="dve": ops.append((eng, fn, reads, writes))
            DP = lambda eng, fn, reads=(), writes=(): ops.append((eng, fn, reads, writes))
            g = 8 * F + gl
            w = wi[0] % NW
            wi[0] += 1
            bu, bukey, bus, buskey = BU[w % 2]
            D(lambda e: e.activation(out=um[w][:, :nt], in_=hT_all[:, F, t0:t0 + nt], func=AF.Copy, scale=maskg[:, gl:gl + 1]),
              ["hT_all", "maskg"], [f"um{w}"], "act")
            DP("pe", lambda e: e.matmul(bu[:, :nt], lhsT=Mc[:, F, :], rhs=um[w][:, :nt], start=True, stop=True), reads=["Mc", f"um{w}"], writes=[bukey])
            DP("pe", lambda e: e.matmul(bus[:, :nt], lhsT=Mcsw[:, F, :], rhs=um[w][:, :nt], start=True, stop=True), reads=["Mcsw", f"um{w}"], writes=[buskey])
            MAGIC = 12582912.0
            if not sample:
                D(lambda e: e.activation(out=xang[w], in_=tal, func=AF.Identity, scale=ph64[:, g:g + 1], bias=PHB[:, g, blk:blk + 1]),
                  ["tal", "ph64", "PHB"], [f"xang{w}"], "act")
                D(lambda e: e.scalar_tensor_tensor(out=xang[w], in0=tbp, scalar=thr[:, g:g + 1], in1=xang[w], op0=ALU.mult, op1=ALU.add),
                  ["tbp", "thr", f"xang{w}"], [f"xang{w}"])
            else:
                D(lambda e: e.activation(out=xang[w][:, :nt], in_=tbs, func=AF.Copy, scale=thr[:, g:g + 1]), ["tbs", "thr"], [f"xang{w}"], "act")
            D(lambda e: e.tensor_scalar(out=sang[w][:, :nt], in0=xang[w][:, :nt], scalar1=MAGIC, scalar2=MAGIC, op0=ALU.add, op1=ALU.subtract),
              [f"xang{w}"], [f"sang{w}"])
            D(lambda e: e.tensor_tensor(out=xang[w][:, :nt], in0=xang[w][:, :nt], in1=sang[w][:, :nt], op=ALU.subtract), [f"xang{w}", f"sang{w}"], [f"xang{w}"])
            D(lambda e: e.activation(out=SINt[w][:, :nt], in_=xang[w][:, :nt], func=AF.Sin, scale=TWO_PI), [f"xang{w}"], [f"SINt{w}"], "act")
            D(lambda e: e.activation(out=sang[w][:, :nt], in_=xang[w][:, :nt], func=AF.Abs), [f"xang{w}"], [f"sang{w}"], "act")
            D(lambda e: e.activation(out=COSt[w][:, :nt], in_=sang[w][:, :nt], func=AF.Sin, scale=-TWO_PI, bias=pib[:]), [f"sang{w}", "pib"], [f"COSt{w}"], "act")
            return ops, dict(F=F, gl=gl, g=g, w=w, t0=t0, nt=nt, blk=blk, sample=sample, bu=bu, bukey=bukey, bus=bus, buskey=buskey)

        def s5_S(c, ypk, last_blk):
            F, gl, g, w, t0, nt, blk, sample = c['F'], c['gl'], c['g'], c['w'], c['t0'], c['nt'], c['blk'], c['sample']
            bu, bukey, bus, buskey = c['bu'], c['bukey'], c['bus'], c['buskey']
            yp, ykey = ypk
            ops = []
            D = lambda fn, reads, writes, eng="dve": ops.append((eng, fn, reads, writes))
            DP = lambda eng, fn, reads=(), writes=(): ops.append((eng, fn, reads, writes))
            D(lambda e: e.tensor_tensor(out=sang[w][:, :nt], in0=bu[:, :nt], in1=COSt[w][:, :nt], op=ALU.mult), [bukey, f"COSt{w}"], [f"sang{w}"])
            D(lambda e: e.tensor_tensor(out=Wt[w][:, :nt], in0=bus[:, :nt], in1=SINt[w][:, :nt], op=ALU.mult), [buskey, f"SINt{w}"], [f"Wt{w}"])
            D(lambda e: e.tensor_tensor(out=Wt[w][:, :nt], in0=Wt[w][:, :nt], in1=sang[w][:, :nt], op=ALU.add), [f"Wt{w}", f"sang{w}"], [f"Wt{w}"])
            if not sample:
                D(lambda e: e.tensor_tensor_scan(out=Wt[w], data0=rho[:, g:g + 1].to_broadcast([128, 512]), data1=Wt[w], initial=Glast[:, g:g + 1],
                                                 op0=ALU.mult, op1=ALU.add), [f"Wt{w}", "rho", "Glast"], [f"Wt{w}"])
                D(lambda e: e.copy(out=Glast[:, g:g + 1], in_=Wt[w][:, 511:512]), [f"Wt{w}"], ["Glast"], "act")
                if last_blk:
                    D(lambda e: e.tensor_tensor(out=Alast[:, g:g + 1], in0=Wt[w][:, 511:512], in1=COSt[w][:, 511:512], op=ALU.mult), [f"Wt{w}", f"COSt{w}"], ["Alast"])
                    D(lambda e: e.tensor_tensor(out=Blast[:, g:g + 1], in0=Wt[w][:, 511:512], in1=SINt[w][:, 511:512], op=ALU.mult), [f"Wt{w}", f"SINt{w}"], ["Blast"])
            else:
                wv = Wt[w][:, :128].rearrange("p (b t) -> p b t", t=8)
                D(lambda e: e.scalar_tensor_tensor(out=wv[:, :, 0], in0=H0T[:, g, :], scalar=rho[:, g:g + 1], in1=wv[:, :, 0], op0=ALU.mult, op1=ALU.add),
                  ["H0T", "rho", f"Wt{w}"], [f"Wt{w}"])
                D(lambda e: e.tensor_scalar(out=aseq, in0=amask, scalar1=rho[:, g:g + 1], scalar2=None, op0=ALU.mult), ["amask", "rho"], ["aseq"])
                D(lambda e: e.tensor_tensor_scan(out=Wt[w][:, :128], data0=aseq, data1=Wt[w][:, :128], initial=0.0, op0=ALU.mult, op1=ALU.add),
                  [f"Wt{w}", "aseq"], [f"Wt{w}"])
                cv = COSt[w][:, :128].rearrange("p (b t) -> p b t", t=8)
                sv = SINt[w][:, :128].rearrange("p (b t) -> p b t", t=8)
                D(lambda e: e.tensor_tensor(out=Asl[:, g, :], in0=wv[:, :, 7], in1=cv[:, :, 7], op=ALU.mult), [f"Wt{w}", f"COSt{w}"], ["Asl"])
                D(lambda e: e.tensor_tensor(out=Bsl[:, g, :], in0=wv[:, :, 7], in1=sv[:, :, 7], op=ALU.mult), [f"Wt{w}", f"SINt{w}"], ["Bsl"])
            if full:
                D(lambda e: e.tensor_tensor(out=Ab[w][:, :nt], in0=Wt[w][:, :nt], in1=COSt[w][:, :nt], op=ALU.mult), [f"Wt{w}", f"COSt{w}"], [f"Ab{w}"])
                D(lambda e: e.tensor_tensor(out=Bb[w][:, :nt], in0=Wt[w][:, :nt], in1=SINt[w][:, :nt], op=ALU.mult), [f"Wt{w}", f"SINt{w}"], [f"Bb{w}"], "pool")
                DP("pe", lambda e: e.matmul(yp[:, :nt], lhsT=CA[:, g, :], rhs=Ab[w][:, :nt], start=(gl == 0), stop=False), reads=["CA", f"Ab{w}"], writes=[ykey])
                DP("pe", lambda e: e.matmul(yp[:, :nt], lhsT=CB[:, g, :], rhs=Bb[w][:, :nt], start=False, stop=(gl == 7)), reads=["CB", f"Bb{w}"], writes=[ykey])
            return ops

        def y_finish(F, t0, nt, ypk):
            yp, ykey = ypk
            V(lambda e: e.scalar_tensor_tensor(out=ysb[:, :nt], in0=hT_all[:, F, t0:t0 + nt], scalar=Dsk[:, F:F + 1], in1=yp[:, :nt], op0=ALU.mult, op1=ALU.add),
              ["hT_all", "Dsk", ykey], ["ysb"])
            V(lambda e: e.tensor_tensor(out=y2[:, :nt], in0=ysb[:, :nt], in1=ysb[:, :nt], op=ALU.mult), ["ysb"], ["y2"], "pool")
            V(lambda e: e.tensor_scalar(out=y2[:, :nt], in0=y2[:, :nt], scalar1=0.044715, scalar2=1.0, op0=ALU.mult, op1=ALU.add), ["y2"], ["y2"], "pool")
            V(lambda e: e.tensor_tensor(out=y2[:, :nt], in0=y2[:, :nt], in1=ysb[:, :nt], op=ALU.mult), ["y2", "ysb"], ["y2"], "pool")
            V(lambda e: e.activation(out=y2[:, :nt], in_=y2[:, :nt], func=AF.Tanh, scale=float(np.sqrt(2.0 / np.pi))), ["y2"], ["y2"], "act")
            V(lambda e: e.tensor_scalar(out=y2[:, :nt], in0=y2[:, :nt], scalar1=0.5, scalar2=0.5, op0=ALU.mult, op1=ALU.add), ["y2"], ["y2"])
            V(lambda e: e.tensor_tensor(out=hT_all[:, F, t0:t0 + nt], in0=y2[:, :nt], in1=ysb[:, :nt], op=ALU.mult), ["y2", "ysb"], ["hT_all"])

        if True:
            h0n = um
        s0n = carve([128, 64, 128]) if False else None
        hnat = carve([16, 64 * 128]) if False else None
        stg = xang[0]
        pth = bank(3, 0)
        for gq_ in range(16 if full else 0):
            P.dma("sp", lambda e, gq_=gq_: e.dma_start(out=stg[0:16, :].rearrange("p (g c) -> p g c", g=4)[:, :, 0:64], in_=s5r_in[:, 4 * gq_:4 * gq_ + 4, :]),
                  writes=["xang0"], semkey="stg", group=f"stg{gq_}")
            P.dma("sp", lambda e, gq_=gq_: e.dma_start(out=stg[0:16, :].rearrange("p (g c) -> p g c", g=4)[:, :, 64:128], in_=s5i_in[:, 4 * gq_:4 * gq_ + 4, :]),
                  writes=["xang0"], semkey="stg", group=f"stg{gq_}")
            for gg in range(4):
                P.op("pe", lambda e, gg=gg: e.transpose(pth[:, gg * 16:(gg + 1) * 16], stg[0:16, gg * 128:(gg + 1) * 128], ident[0:16, 0:16]),
                     reads=["xang0", "ident"], writes=[bkey(3, 0)])
            V(lambda e, gq_=gq_: e.copy(out=H0T[:, 4 * gq_:4 * gq_ + 4, :], in_=pth[:, 0:64].rearrange("p (g b) -> p g b", g=4)), [bkey(3, 0)], ["H0T"], "act")

        units = []
        yi = 0
        for F in range(KC):
            for blk in range(4):
                ypk = YP[yi % 2]
                yi += 1
                for gl in range(8):
                    units.append(dict(F=F, gl=gl, t0=blk * 512, nt=512, blk=blk, sample=False, ypk=ypk, last=(blk == 3), fin=(gl == 7)))
            if full:
                ypk = YP[yi % 2]
                yi += 1
                for gl in range(8):
                    units.append(dict(F=F, gl=gl, t0=TOK_P, nt=128, blk=0, sample=True, ypk=ypk, last=False, fin=(gl == 7)))
        def emit(ops):
            for (eng, fn, reads, writes) in ops:
                P.op(eng, fn, reads=reads, writes=writes)

        def interleave(a_, b_):
            out = []
            for i_ in range(max(len(a_), len(b_))):
                if i_ < len(b_):
                    out.append(b_[i_])
                if i_ < len(a_):
                    out.append(a_[i_])
            return out

        u0 = units[0]
        opsT, ctx = s5_T(u0["F"], u0["gl"], u0["t0"], u0["nt"], u0["blk"], u0["sample"])
        emit(opsT)
        for n, u in enumerate(units):
            opsS = s5_S(ctx, u["ypk"], u["last"])
            if n + 1 < len(units):
                v = units[n + 1]
                opsT, nxt = s5_T(v["F"], v["gl"], v["t0"], v["nt"], v["blk"], v["sample"])
            else:
                opsT, nxt = [], None
            emit(interleave(opsT, opsS))
            if full and u["fin"]:
                y_finish(u["F"], u["t0"], u["nt"], u["ypk"])
            ctx = nxt

        pfin = bank(3, 1)[:, 0:64]
        P.op("pe", lambda e: e.matmul(pfin, lhsT=rott, rhs=Blast, start=True, stop=True), reads=["rott", "Blast"], writes=[bkey(3, 1)])
        V(lambda e: e.tensor_tensor(out=Alast, in0=Alast, in1=pfin, op=ALU.add), ["Alast", bkey(3, 1)], ["Alast"])
        if not full:
            P.dma("sp", lambda e: e.dma_start(out=floc_out, in_=Alast), reads=["Alast"], semkey="floc", final=True)
            stats = P.build()
            nc_allow.__exit__(None, None, None)
            es.close()
            return nc, stats
        ptp = bank(3, 0)[0:64, 0:128]
        P.op("pe", lambda e: e.transpose(ptp, Alast, ident[:]), reads=["Alast", "ident"], writes=[bkey(3, 0)])
        V(lambda e: e.copy(out=ysb[0:64, 0:128], in_=ptp), [bkey(3, 0)], ["ysb"], "act")
        P.dma("sp", lambda e: e.dma_start(out=ps5r_out, in_=ysb[0:64, 0:64]), reads=["ysb"], semkey="ps5", final=True)
        P.dma("sp", lambda e: e.dma_start(out=ps5i_out, in_=ysb[0:64, 64:128]), reads=["ysb"], semkey="ps5", final=True)
        for hf in range(2):
            pf2 = bank(3, 1)
            P.op("pe", lambda e, hf=hf: e.matmul(pf2, lhsT=rott, rhs=Bsl[:, 32 * hf:32 * hf + 32, :].rearrange("p g b -> p (g b)"), start=True, stop=True),
                 reads=["rott", "Bsl"], writes=[bkey(3, 1)])
            V(lambda e, hf=hf: e.tensor_tensor(out=Asl[:, 32 * hf:32 * hf + 32, :].rearrange("p g b -> p (g b)"), in0=Asl[:, 32 * hf:32 * hf + 32, :].rearrange("p g b -> p (g b)"),
                                               in1=pf2, op=ALU.add), ["Asl", bkey(3, 1)], ["Asl"])
        for gq_ in range(16):
            pto = bank(3, 0)[0:16, :]
            for gg in range(4):
                P.op("pe", lambda e, gq_=gq_, gg=gg: e.transpose(pto[:, gg * 128:(gg + 1) * 128], Asl[:, 4 * gq_ + gg, :], ident[:]),
                     reads=["Asl", "ident"], writes=[bkey(3, 0)])
            V(lambda e: e.copy(out=stg[0:16, :], in_=pto), [bkey(3, 0)], ["xang0"], "act")
            P.dma("sp", lambda e, gq_=gq_: e.dma_start(out=ss5r_out[:, 4 * gq_:4 * gq_ + 4, :], in_=stg[0:16, :].rearrange("p (g c) -> p g c", g=4)[:, :, 0:64]),
                  reads=["xang0"], semkey="ss5o", final=True)
            P.dma("sp", lambda e, gq_=gq_: e.dma_start(out=ss5i_out[:, 4 * gq_:4 * gq_ + 4, :], in_=stg[0:16, :].rearrange("p (g c) -> p g c", g=4)[:, :, 64:128]),
                  reads=["xang0"], semkey="ss5o", final=True)

        P.barrier()
        cur[0] = mark_ln + NTOK * KC // 2
        glua = carve([128, KC, D], BF16)
        glub = carve([128, KC, D], BF16)
        sgb = [carve([128, 512]) for _ in range(2)]
        prd = [carve([128, 512]) for _ in range(2)]
        gt3 = carve([128, 128])
        P.dma("pool", lambda e: e.dma_start(out=glua, in_=glua_d.rearrange("(k p) n -> p k n", p=128)), writes=["glua"], semkey="glua")
        P.dma("pool", lambda e: e.dma_start(out=glub, in_=glub_d.rearrange("(k p) n -> p k n", p=128)), writes=["glub"], semkey="glub")
        GA = [(bank(0, 0), bkey(0, 0), bank(0, 1), bkey(0, 1)), (bank(1, 0), bkey(1, 0), bank(1, 1), bkey(1, 1))]
        gi = 0
        for (t0, nt, smp) in [(b * 512, 512, False) for b in range(4)] + [(TOK_P, 128, True)]:
            for m in range(KC):
                pa, pak, pb, pbk = GA[gi % 2]
                w_ = gi % 2
                gi += 1
                for k in range(KC):
                    P.op("pe", lambda e, k=k, m=m, pa=pa, t0=t0, nt=nt: e.matmul(pa[:, :nt], lhsT=glua[:, k, m * 128:(m + 1) * 128], rhs=hT_all[:, k, t0:t0 + nt],
                                                                                  start=(k == 0), stop=(k == KC - 1)), reads=["glua", "hT_all"], writes=[pak])
                for k in range(KC):
                    P.op("pe", lambda e, k=k, m=m, pb=pb, t0=t0, nt=nt: e.matmul(pb[:, :nt], lhsT=glub[:, k, m * 128:(m + 1) * 128], rhs=hT_all[:, k, t0:t0 + nt],
                                                                                  start=(k == 0), stop=(k == KC - 1)), reads=["glub", "hT_all"], writes=[pbk])
                V(lambda e, pb=pb, w_=w_, nt=nt: e.activation(out=sgb[w_][:, :nt], in_=pb[:, :nt], func=AF.Sigmoid), [pbk], [f"sgb{w_}"], "act")
                V(lambda e, pa=pa, w_=w_, nt=nt: e.tensor_tensor(out=prd[w_][:, :nt], in0=pa[:, :nt], in1=sgb[w_][:, :nt], op=ALU.mult), [pak, f"sgb{w_}"], [f"prd{w_}"])
                xs = xT[:, m, t0:t0 + nt]
                xkey = ("xT", t0 // 512)
                if not smp:
                    V(lambda e, m=m, w_=w_, xs=xs, nt=nt: e.scalar_tensor_tensor(out=xs, in0=prd[w_][:, :nt], scalar=modT[:, 16 + m, 0:1], in1=xs, op0=ALU.mult, op1=ALU.add),
                      [f"prd{w_}", "modT", xkey], [xkey])
                else:
                    gv = gt3.rearrange("p (b t) -> p b t", t=8)
                    V(lambda e, m=m, w_=w_, gv=gv: e.tensor_tensor(out=gv, in0=prd[w_][:, :128].rearrange("p (b t) -> p b t", t=8),
                                                                  in1=modT[:, 16 + m, 1:17].unsqueeze(2).to_broadcast([128, 16, 8]), op=ALU.mult), [f"prd{w_}", "modT"], ["gt3"])
                    V(lambda e, xs=xs: e.tensor_tensor(out=xs, in0=xs, in1=gt3, op=ALU.add), ["gt3", xkey], [xkey])
        ffn(1)
        P.barrier()
        cur[0] = mark_ln
        yst = [carve([128, D]) for _ in range(2)]
        for t in range(17):
            s = t % 2
            for k in range(KC):
                P.op("pe", lambda e, k=k, t=t, s=s: e.transpose(pT[s][:, k, :], xT[:, k, t * 128:(t + 1) * 128], ident[:]),
                     reads=[("xT", t // 4), "ident"], writes=[bkey(s, 0), bkey(s, 1)])
            P.op("act", lambda e, s=s: e.copy(out=yst[s], in_=Q[s][:, :]), reads=[bkey(s, 0), bkey(s, 1)], writes=[f"yst{s}"])
            P.dma("sp", lambda e, t=t, s=s: e.dma_start(out=y_out[t * 128:(t + 1) * 128, :], in_=yst[s]), reads=[f"yst{s}"], semkey=f"yst{s}", final=True)
        stats = P.build()
        nc_allow.__exit__(None, None, None)
        es.close()
        return nc, stats

    hTh = carve([128, KC, 128], BF16)
    mark1 = cur[0]
    lst = [carve([128, 4, 128]) for _ in range(2)]
    wret = carve([128, 8, 4])
    xTh = carve([128, KC, 128])
    xst[0] = carve([128, D])
    load_tile(0, xTh, "xTh")
    P.dma("sp", lambda e: e.dma_start(out=wret, in_=wret_d), writes=["wret"], semkey="wret")
    P.op("dve", lambda e: e.memset(S, 0.0), writes=["S"])
    for r in range(8):
        s = r % 2
        P.dma("sp", lambda e, r=r, s=s: e.dma_start(out=lst[s], in_=lall_d[r].rearrange("h d e -> d h e")), writes=[f"lst{s}"], semkey=f"lst{s}")
        P.op("dve", lambda e, r=r, s=s: e.tensor_tensor(out=lst[s], in0=lst[s], in1=wret[:, r, :].unsqueeze(2).to_broadcast([128, 4, 128]), op=ALU.mult),
             reads=[f"lst{s}", "wret"], writes=[f"lst{s}"])
        P.op("dve", lambda e, s=s: e.tensor_tensor(out=S, in0=S, in1=lst[s], op=ALU.add), reads=["S", f"lst{s}"], writes=["S"])
    P.op("act", lambda e: e.copy(out=Sb, in_=S), reads=["S"], writes=["Sb"])
    ln_block(xTh, "xTh", 128, False, 0)
    P.op("pool", lambda e: e.tensor_copy(out=hTh, in_=hT[:, :, 0:128]), reads=["hT"], writes=["hTh"])
    P.barrier()
    cur[0] = mark1

    wkd = carve([128, KC, 2, 128], BF16)
    wvd = carve([128, KC, 2, 128], BF16)
    for kv in range(2):
        for half in range(2):
            P.op("pool", lambda e, kv=kv, half=half: e.tensor_copy(out=wkd[:, :, kv, half * 64:(half + 1) * 64], in_=win[:, :, 512 + 64 * kv:576 + 64 * kv]),
                 reads=["win"], writes=["wkd"])
            P.op("pool", lambda e, kv=kv, half=half: e.tensor_copy(out=wvd[:, :, kv, half * 64:(half + 1) * 64], in_=win[:, :, 640 + 64 * kv:704 + 64 * kv]),
                 reads=["win"], writes=["wvd"])
    wout = carve([128, KC, D], BF16)
    P.dma("pool", lambda e: e.dma_start(out=wout, in_=w_out.rearrange("(k p) n -> p k n", p=128)), writes=["wout"], semkey="wout")
    gq = carve([128, 1])
    gk = carve([128, 1])
    bd_f = carve([128, 128])
    bd = carve([128, 128], BF16)
    dneg = carve([128, 5, 128])
    rmask = carve([128, 4, 128])
    dq = carve([128, 4, 128])
    seqmask = carve([128, 16])
    esink = carve([128, 8])
    retg = carve([128, 4])
    for half in range(2):
        P.dma("sp", lambda e, half=half: e.dma_start(out=gq[half * 64:(half + 1) * 64, :], in_=qgain.rearrange("o d -> d o")), writes=["gq"], semkey="ld_gq", group="ld_gq")
        P.dma("sp", lambda e, half=half: e.dma_start(out=gk[half * 64:(half + 1) * 64, :], in_=kgain.rearrange("o d -> d o")), writes=["gk"], semkey="ld_gk", group="ld_gk")
    P.dma("sp", lambda e: e.dma_start(out=bd_f, in_=bd_d), writes=["bd_f"], semkey="ld_bd_f", group="ld_bd_f")
    P.dma("sp", lambda e: e.dma_start(out=dneg, in_=dneg_d), writes=["dneg"], semkey="ld_dneg", group="ld_dneg")
    P.dma("sp", lambda e: e.dma_start(out=rmask, in_=rmask_d[:, 0]), writes=["rmask"], semkey="ld_rmask", group="ld_rmask")
    P.dma("sp", lambda e: e.dma_start(out=dq, in_=dq_d[:, 0]), writes=["dq"], semkey="ld_dq", group="ld_dq")
    P.dma("sp", lambda e: e.dma_start(out=seqmask, in_=seqmask_d), writes=["seqmask"], semkey="ld_seqmask", group="ld_seqmask")
    P.dma("sp", lambda e: e.dma_start(out=esink, in_=sinks_d.partition_broadcast(128)), writes=["esink"], semkey="ld_esink", group="ld_esink")
    P.dma("sp", lambda e: e.dma_start(out=retg, in_=retg_d.rearrange("o (h e) -> e (o h)", h=4)), writes=["retg"], semkey="ld_retg", group="ld_retg")
    P.op("dve", lambda e: e.tensor_copy(out=bd, in_=bd_f), reads=["bd_f"], writes=["bd"])
    P.op("act", lambda e: e.activation(out=esink, in_=esink, func=AF.Exp), reads=["esink"], writes=["esink"])
    P.op("dve", lambda e: e.tensor_scalar(out=gq, in0=gq, scalar1=0.125, scalar2=None, op0=ALU.mult), reads=["gq"], writes=["gq"])

    qnT = carve([128, 4, BT], BF16)
    kdT = carve([128, 2, 128 + BT], BF16)
    qbT = carve([128, 4, BT], BF16)
    qdT = carve([128, 4, BT], BF16)
    kbT = carve([128, 4, BT], BF16)
    sgT = carve([128, 4, BT], BF16)
    vd_tok = carve([128, 1 + TPB, 256], BF16)
    oT = carve([128, KC, BT], BF16)
    qsq = carve([128, max(BT, 256)], BF16)
    qrs = carve([128, max(BT, 256)])
    sc_sb = carve([128, 2, 2, 2, 128])
    pTt = carve([128, 2, 2, 2, 128], BF16)
    rden = carve([128, 2, 2, 128])
    innT = carve([128, 4, 128], BF16)
    o32 = carve([128, 4, 128])
    obf = carve([128, 4, 128], BF16)
    osq = carve([128, 4, 128], BF16)
    t1 = carve([128, 4, 128])
    t2 = o32
    kv32 = carve([128, 2, 128])
    knT = carve([128, 128])
    gtmp = carve([128, 128])
    cacheT = carve([128, 16, 128], BF16)
    vcache = carve([128, 16, 128], BF16)
    kst = carve([128, 4, 128])
    S0 = [carve([128, 4, 128]) for _ in range(2)]
    S0b = [carve([128, 4, 128], BF16) for _ in range(2)]
    kdm = [carve([128, 512], BF16) for _ in range(2)]
    mark2 = cur[0]
    print("arena words used (mixer):", cur[0], "of", ARENA_W)

    PJ = [bank(0, 0), bank(0, 1)]
    PJK = [bkey(0, 0), bkey(0, 1)]
    p_st = bank(1, 0)
    pj_i = [0]

    def proj_fm(lhs_fn, ntok, evac, src=None):
        s = pj_i[0] % 2
        pj_i[0] += 1
        src = hT if src is None else src
        for k in range(KC):
            P.op("pe", lambda e, k=k, s=s: e.matmul(PJ[s][:, :ntok], lhsT=lhs_fn(k), rhs=src[:, k, :ntok], start=(k == 0), stop=(k == KC - 1)),
                 reads=["hT", "hTh", "win", "wkd"], writes=[PJK[s]])
        evac(PJ[s][:, :ntok], PJK[s])

    def qknorm(psum, pkey, ntok, gain, gkey, out_ap, out_key):
        P.op("act", lambda e: e.activation(out=qsq[:, :ntok], in_=psum, func=AF.Square), reads=[pkey], writes=["qsq"])
        P.op("pe", lambda e: e.matmul(p_st[:, :ntok], lhsT=bd, rhs=qsq[:, :ntok], start=True, stop=True), reads=["bd", "qsq"], writes=[bkey(1, 0)])
        P.op("act", lambda e: e.activation(out=qrs[:, :ntok], in_=p_st[:, :ntok], func=AF.Sqrt, bias=epsb[:], scale=1.0), reads=[bkey(1, 0), "epsb"], writes=["qrs"])
        P.op("dve", lambda e: e.reciprocal(out=qrs[:, :ntok], in_=qrs[:, :ntok]), reads=["qrs"], writes=["qrs"])
        P.op("dve", lambda e: e.scalar_tensor_tensor(out=out_ap, in0=psum, scalar=gain[:, 0:1], in1=qrs[:, :ntok], op0=ALU.mult, op1=ALU.mult),
             reads=[pkey, "qrs", gkey], writes=[out_key])

    def project_block(ntok, grp):
        for j in range(4):
            proj_fm(lambda k, j=j: win[:, k, j * 128:(j + 1) * 128], ntok,
                    lambda ps_, key, j=j: qknorm(ps_, key, ntok, gq, "gq", qnT[:, j, :ntok], "qnT"))
        if KSUB < 2:
            return
        for kv in range(2):
            proj_fm(lambda k, kv=kv: wkd[:, k, kv, :], ntok,
                    lambda ps_, key, kv=kv: qknorm(ps_, key, ntok, gk, "gk", kdT[:, kv, 128:128 + ntok], "kdT"))
        if KSUB < 3:
            return
        for h in range(4):
            def ev_q(ps_, key, h=h):
                P.op("act", lambda e: e.copy(out=qbT[:, h, :ntok], in_=ps_), reads=[key], writes=["qbT"])
                P.op("dve", lambda e: e.tensor_tensor(out=qdT[:, h, :ntok].rearrange("p (t i) -> p t i", i=128), in0=ps_.rearrange("p (t i) -> p t i", i=128),
                                                      in1=dq[:, h, :].unsqueeze(1).to_broadcast([128, ntok // 128, 128]), op=ALU.mult),
                     reads=[key, "dq"], writes=["qdT"])
            proj_fm(lambda k, h=h: win[:, k, 768 + h * 128:768 + (h + 1) * 128], ntok, ev_q)
        if KSUB < 4:
            return
        for h in range(4):
            proj_fm(lambda k, h=h: win[:, k, 1280 + h * 128:1280 + (h + 1) * 128], ntok,
                    lambda ps_, key, h=h: P.op("act", lambda e: e.mul(out=kbT[:, h, :ntok], in_=ps_, mul=128.0 ** -0.5), reads=[key], writes=["kbT"]))
        if KSUB < 5:
            return
        for h in range(4):
            proj_fm(lambda k, h=h: win[:, k, 2304 + h * 128:2304 + (h + 1) * 128], ntok,
                    lambda ps_, key, h=h: P.op("act", lambda e: e.activation(out=sgT[:, h, :ntok], in_=ps_, func=AF.Silu), reads=[key], writes=["sgT"]))

    p_vd = bank(1, 1)[:, 0:256]

    def tok_vd(tc0, slot, src=None):
        src = hT if src is None else src
        for k in range(KC):
            P.op("pe", lambda e, k=k: e.matmul(p_vd, lhsT=src[:, k, tc0:tc0 + 128], rhs=wvd[:, k, :, :].rearrange("p a b -> p (a b)"), start=(k == 0), stop=(k == KC - 1)),
                 reads=["hT", "hTh", "wvd"], writes=[bkey(1, 1)])
        P.op("act", lambda e: e.copy(out=vd_tok[:, slot, :], in_=p_vd), reads=[bkey(1, 1)], writes=["vd_tok"])

    SLOPE = [2.0 ** (-(h + 1)) for h in range(8)]
    p_sc = Q[3][:, :].rearrange("p (a c t q) -> p a c t q", a=2, c=2, t=2)
    p_num = bank(2, 0).rearrange("p (a c q) -> p a c q", a=2, c=2)
    p_den = bank(2, 1).rearrange("p (a c q) -> p a c q", a=2, c=2)
    SCK = [bkey(3, 0), bkey(3, 1)]

    def attn_softmax(kv, dn_own, dn_prev):
        for half in range(2):
            for c in range(2):
                h0 = 4 * kv + 2 * c + half
                P.op("dve", lambda e, half=half, c=c, h0=h0: e.scalar_tensor_tensor(out=sc_sb[:, half, c, 0, :], in0=dneg[:, dn_prev, :], scalar=SLOPE[h0],
                                                                                  in1=p_sc[:, half, c, 0, :], op0=ALU.mult, op1=ALU.add),
                     reads=["dneg"] + SCK, writes=["sc_sb"])
                P.op("dve", lambda e, half=half, c=c, h0=h0: e.scalar_tensor_tensor(out=sc_sb[:, half, c, 1, :], in0=dneg[:, dn_own, :], scalar=SLOPE[h0],
                                                                                  in1=p_sc[:, half, c, 1, :], op0=ALU.mult, op1=ALU.add),
                     reads=["dneg"] + SCK, writes=["sc_sb"])
        P.op("act", lambda e: e.activation(out=pTt, in_=sc_sb, func=AF.Exp), reads=["sc_sb"], writes=["pTt"])

    def attn_finish(kv, c0):
        for part in range(2):
            P.op("pe", lambda e, part=part: e.matmul(p_den, lhsT=ones_b[:], rhs=pTt[:, :, :, part, :], start=(part == 0), stop=(part == 1)),
                 reads=["ones_b", "pTt"], writes=[bkey(2, 1)])
        es_v = esink[:, 4 * kv:4 * kv + 4].rearrange("p (c a) -> p a c", a=2)
        P.op("dve", lambda e, es_v=es_v: e.tensor_tensor(out=rden, in0=p_den, in1=es_v.unsqueeze(3).to_broadcast([128, 2, 2, 128]), op=ALU.add),
             reads=[bkey(2, 1), "esink"], writes=["rden"])
        P.op("dve", lambda e: e.reciprocal(out=rden, in_=rden), reads=["rden"], writes=["rden"])
        for half in range(2):
            sl = slice(half * 64, half * 64 + 64)
            P.op("dve", lambda e, half=half, sl=sl, kv=kv: e.tensor_tensor(out=oT[sl, 2 * kv:2 * kv + 2, c0:c0 + 128], in0=p_num[sl, half, :, :],
                                                                         in1=rden[sl, half, :, :], op=ALU.mult),
                 reads=[bkey(2, 0), "rden"], writes=["oT"])

    def attention_tile(i, dn_own, dn_prev):
        c0 = i * 128
        for kv in range(2):
            for half in range(2):
                sl = slice(half * 64, half * 64 + 64)
                P.op("pe", lambda e, kv=kv, half=half, sl=sl: e.matmul(p_sc[:, half, :, 1, :], lhsT=kdT[sl, kv, 128 + c0:256 + c0],
                                                                       rhs=qnT[sl, 2 * kv:2 * kv + 2, c0:c0 + 128], start=True, stop=True),
                     reads=["kdT", "qnT"], writes=SCK)
                P.op("pe", lambda e, kv=kv, half=half, sl=sl: e.matmul(p_sc[:, half, :, 0, :], lhsT=kdT[sl, kv, c0:128 + c0],
                                                                       rhs=qnT[sl, 2 * kv:2 * kv + 2, c0:c0 + 128], start=True, stop=True),
                     reads=["kdT", "qnT"], writes=SCK)
            attn_softmax(kv, dn_own, dn_prev)
            parts = [(0, vd_tok[:, i, kv * 128:(kv + 1) * 128]), (1, vd_tok[:, i + 1, kv * 128:(kv + 1) * 128])]
            for n_, (part, lh) in enumerate(parts):
                P.op("pe", lambda e, part=part, lh=lh, n_=n_: e.matmul(p_num, lhsT=lh, rhs=pTt[:, :, :, part, :], start=(n_ == 0), stop=(n_ == 1)),
                     reads=["vd_tok", "pTt"], writes=[bkey(2, 0)])
            attn_finish(kv, c0)

    def attention_sample():
        p_ct = bank(1, 1)[:, 0:128]
        for kv in range(2):
            for half in range(2):
                P.dma("pool", lambda e, kv=kv, half=half: e.dma_start(out=vcache[:, :, half * 64:(half + 1) * 64],
                                                                      in_=cache_v[:, :, kv * 64:(kv + 1) * 64].rearrange("b w d -> w b d")),
                      writes=["vcache"], semkey="vcache", group=f"vc{kv}")
            for g4 in range(4):
                for half in range(2):
                    P.dma("sp", lambda e, kv=kv, half=half, g4=g4: e.dma_start(out=kst[:, :, half * 64:(half + 1) * 64],
                                                                              in_=cache_k[4 * g4:4 * g4 + 4, :, kv * 64:(kv + 1) * 64].rearrange("b w d -> w b d")),
                          writes=["kst"], semkey="kst", group=f"kst{kv}_{g4}")
                for bb in range(4):
                    b = 4 * g4 + bb
                    P.op("pe", lambda e, bb=bb: e.transpose(p_ct, kst[:, bb, :], ident[:]), reads=["kst", "ident"], writes=[bkey(1, 1)])
                    P.op("act", lambda e, b=b: e.copy(out=cacheT[:, b, :], in_=p_ct), reads=[bkey(1, 1)], writes=["cacheT"])
            for half in range(2):
                sl = slice(half * 64, half * 64 + 64)
                P.op("pe", lambda e, kv=kv, half=half, sl=sl: e.matmul(p_sc[:, half, :, 1, :], lhsT=kdT[sl, kv, 128:256],
                                                                       rhs=qnT[sl, 2 * kv:2 * kv + 2, 0:128], start=True, stop=True),
                     reads=["kdT", "qnT"], writes=SCK)
                for b in range(16):
                    P.op("pe", lambda e, kv=kv, half=half, sl=sl, b=b: e.matmul(p_sc[:, half, :, 0, 8 * b:8 * b + 8], lhsT=cacheT[sl, b, :],
                                                                                 rhs=qnT[sl, 2 * kv:2 * kv + 2, 8 * b:8 * b + 8], start=True, stop=True),
                         reads=["cacheT", "qnT"], writes=SCK)
            attn_softmax(kv, 3, 4)
            P.op("pe", lambda e, kv=kv: e.matmul(p_num, lhsT=vd_tok[:, 1, kv * 128:(kv + 1) * 128], rhs=pTt[:, :, :, 1, :], start=True, stop=False),
                 reads=["vd_tok", "pTt"], writes=[bkey(2, 0)])
            for b in range(16):
                P.op("pe", lambda e, b=b: e.matmul(p_num[:, :, :, 8 * b:8 * b + 8], lhsT=vcache[:, b, :], rhs=pTt[:, :, :, 0, 8 * b:8 * b + 8],
                                                   start=False, stop=(b == 15)),
                     reads=["vcache", "pTt"], writes=[bkey(2, 0)])
            attn_finish(kv, 0)

    p_in = bank(1, 1).rearrange("p (h i) -> p h i", h=4)
    p_o = bank(0, 0).rearrange("p (h i) -> p h i", h=4)
    p_mu = bank(0, 1).rearrange("p (h i) -> p h i", h=4)
    p_e2 = bank(1, 0).rearrange("p (h i) -> p h i", h=4)

    def ret_norm(c0):
        P.op("dve", lambda e: e.tensor_copy(out=obf, in_=o32), reads=["o32"], writes=["obf"])
        P.op("act", lambda e: e.activation(out=osq, in_=o32, func=AF.Square), reads=["o32"], writes=["osq"])
        for h in range(4):
            P.op("pe", lambda e, h=h: e.matmul(p_mu[:, h, :], lhsT=ones_g[:], rhs=obf[:, h, :], start=True, stop=True), reads=["ones_g", "obf"], writes=[bkey(0, 1)])
        for h in range(4):
            P.op("pe", lambda e, h=h: e.matmul(p_e2[:, h, :], lhsT=ones_g[:], rhs=osq[:, h, :], start=True, stop=True), reads=["ones_g", "osq"], writes=[bkey(1, 0)])
        P.op("act", lambda e: e.activation(out=t1, in_=p_mu, func=AF.Square), reads=[bkey(0, 1)], writes=["t1"])
        P.op("dve", lambda e: e.tensor_tensor(out=t1, in0=p_e2, in1=t1, op=ALU.subtract), reads=[bkey(1, 0), "t1"], writes=["t1"])
        P.op("dve", lambda e: e.tensor_scalar(out=t1, in0=t1, scalar1=0.0, scalar2=None, op0=ALU.max), reads=["t1"], writes=["t1"])
        P.op("act", lambda e: e.activation(out=t1, in_=t1, func=AF.Sqrt, bias=epsb[:], scale=1.0), reads=["t1", "epsb"], writes=["t1"])
        P.op("dve", lambda e: e.reciprocal(out=t1, in_=t1), reads=["t1"], writes=["t1"])
        P.op("dve", lambda e: e.tensor_tensor(out=o32, in0=o32, in1=p_mu, op=ALU.subtract), reads=["o32", bkey(0, 1)], writes=["o32"])
        P.op("dve", lambda e: e.tensor_tensor(out=o32, in0=o32, in1=t1, op=ALU.mult), reads=["o32", "t1"], writes=["o32"])
        P.op("dve", lambda e: e.tensor_tensor(out=o32, in0=o32, in1=retg.unsqueeze(2).to_broadcast([128, 4, 128]), op=ALU.mult), reads=["o32", "retg"], writes=["o32"])
        P.op("dve", lambda e: e.tensor_tensor(out=oT[:, 4:8, c0:c0 + 128], in0=o32, in1=sgT[:, :, c0:c0 + 128], op=ALU.mult), reads=["o32", "sgT"], writes=["oT"])

    def ret_inner(c0):
        for h in range(4):
            P.op("pe", lambda e, h=h: e.matmul(p_in[:, h, :], lhsT=kbT[:, h, c0:c0 + 128], rhs=qbT[:, h, c0:c0 + 128], start=True, stop=True),
                 reads=["kbT", "qbT"], writes=[bkey(1, 1)])
        P.op("dve", lambda e: e.tensor_tensor(out=innT, in0=p_in, in1=rmask, op=ALU.mult), reads=[bkey(1, 1), "rmask"], writes=["innT"])

    def retention_tile(i, grp):
        c0 = i * 128
        ret_inner(c0)
        for h in range(4):
            P.op("pe", lambda e, h=h: e.matmul(p_o[:, h, :], lhsT=vb_tok[:, h * 128:(h + 1) * 128], rhs=innT[:, h, :], start=True, stop=False),
                 reads=["vb_tok", "innT"], writes=[bkey(0, 0)])
            P.op("pe", lambda e, h=h: e.matmul(p_o[:, h, :], lhsT=Sb[:, h, :], rhs=qdT[:, h, c0:c0 + 128], start=False, stop=True),
                 reads=["Sb", "qdT"], writes=[bkey(0, 0)])
        P.op("act", lambda e: e.copy(out=o32, in_=p_o), reads=[bkey(0, 0)], writes=["o32"])
        ret_norm(c0)

    def retention_sample():
        ret_inner(0)
        poh = [bank(0, 0)[:, 0:128], bank(0, 1)[:, 0:128], bank(1, 0)[:, 0:128], bank(3, 0)[:, 0:128]]
        pok = [bkey(0, 0), bkey(0, 1), bkey(1, 0), bkey(3, 0)]
        for h in range(4):
            P.op("pe", lambda e, h=h: e.matmul(poh[h], lhsT=vb_tok[:, h * 128:(h + 1) * 128], rhs=innT[:, h, :], start=True, stop=False),
                 reads=["vb_tok", "innT"], writes=[pok[h]])
        for b in range(16):
            s_ = b % 2
            P.dma("sp", lambda e, b=b, s_=s_: e.dma_start(out=S0[s_], in_=sret_in[b].rearrange("h d e -> d h e")), writes=[f"S0{s_}"], semkey=f"S0{s_}")
            P.op("pool", lambda e, s_=s_: e.tensor_copy(out=S0b[s_], in_=S0[s_]), reads=[f"S0{s_}"], writes=[f"S0b{s_}"])
            for h in range(4):
                P.op("pe", lambda e, h=h, b=b, s_=s_: e.matmul(poh[h][:, 8 * b:8 * b + 8], lhsT=S0b[s_][:, h, :], rhs=qdT[:, h, 8 * b:8 * b + 8],
                                                              start=False, stop=(b == 15)),
                     reads=[f"S0b{s_}", "qdT"], writes=[pok[h]])
            P.op("dve", lambda e, b=b, s_=s_: e.tensor_scalar(out=kdm[s_], in0=kd_tok, scalar1=seqmask[:, b:b + 1], scalar2=None, op0=ALU.mult),
                 reads=["kd_tok", "seqmask"], writes=[f"kdm{s_}"])
            for h in range(4):
                P.op("pe", lambda e, h=h, s_=s_: e.matmul(p_su[:, h, :], lhsT=kdm[s_][:, h * 128:(h + 1) * 128], rhs=vb_tok[:, h * 128:(h + 1) * 128], start=True, stop=True),
                     reads=[f"kdm{s_}", "vb_tok"], writes=[bkey(2, 1)])
            P.op("pool", lambda e, s_=s_: e.tensor_tensor(out=S0[s_], in0=S0[s_], in1=dc[:, 1, :].unsqueeze(2).to_broadcast([128, 4, 128]), op=ALU.mult),
                 reads=[f"S0{s_}", f"S0b{s_}", "dc"], writes=[f"S0{s_}"])
            P.op("dve", lambda e, s_=s_: e.tensor_tensor(out=S0[s_], in0=S0[s_], in1=p_su, op=ALU.add), reads=[f"S0{s_}", bkey(2, 1)], writes=[f"S0{s_}"])
            P.dma("sp", lambda e, b=b, s_=s_: e.dma_start(out=sret_out[b].rearrange("h d e -> d h e"), in_=S0[s_]), reads=[f"S0{s_}"], semkey=f"S0o{s_}", final=True)
        for h in range(4):
            P.op("act", lambda e, h=h: e.copy(out=o32[:, h, :], in_=poh[h]), reads=[pok[h]], writes=["o32"])
        ret_norm(0)

    PO = [bank(0, 0), bank(0, 1)]
    POK = [bkey(0, 0), bkey(0, 1)]

    def out_proj(blk_c0, ntok, sample, wfn, nk, rhs_fn, gate_row, rkeys):
        for m in range(KC):
            s = m % 2
            for k in range(nk):
                P.op("pe", lambda e, m=m, k=k, s=s: e.matmul(PO[s][:, :ntok], lhsT=wfn(k, m), rhs=rhs_fn(k), start=(k == 0), stop=(k == nk - 1)),
                     reads=rkeys, writes=[POK[s]])
            xs = xT[:, m, blk_c0:blk_c0 + ntok]
            xkey = ("xT", blk_c0 // 512)
            if not sample:
                P.op("dve", lambda e, m=m, s=s, xs=xs: e.scalar_tensor_tensor(out=xs, in0=PO[s][:, :ntok], scalar=modT[:, gate_row + m, 0:1], in1=xs,
                                                                             op0=ALU.mult, op1=ALU.add),
                     reads=[POK[s], "modT", xkey], writes=[xkey])
            else:
                gv = gtmp[:, :128].rearrange("p (b t) -> p b t", t=8)
                P.op("dve", lambda e, m=m, s=s, gv=gv: e.tensor_tensor(out=gv, in0=PO[s][:, :128].rearrange("p (b t) -> p b t", t=8),
                                                                      in1=modT[:, gate_row + m, 1:17].unsqueeze(2).to_broadcast([128, 16, 8]), op=ALU.mult),
                     reads=[POK[s], "modT"], writes=["gtmp"])
                P.op("dve", lambda e, xs=xs: e.tensor_tensor(out=xs, in0=xs, in1=gtmp[:, :128], op=ALU.add), reads=["gtmp", xkey], writes=[xkey])

    p_tr = bank(1, 1)[:, 256:384]
    p_v32 = bank(1, 1)[:, 384:512]

    def window_kv(tc0, kout, vout):
        pk2 = bank(1, 0)[:, 0:256].rearrange("p (a n) -> p a n", a=2)
        pst2 = bank(1, 0)[:, 256:512].rearrange("p (a n) -> p a n", a=2)
        for kv in range(2):
            for k in range(KC):
                P.op("pe", lambda e, kv=kv, k=k: e.matmul(pk2[:, kv, :], lhsT=wkd[:, k, kv, :], rhs=hT[:, k, tc0:tc0 + 128], start=(k == 0), stop=(k == KC - 1)),
                     reads=["wkd", "hT"], writes=[bkey(1, 0)])
        P.op("act", lambda e: e.activation(out=qsq[:, 0:256].rearrange("p (a n) -> p a n", a=2), in_=pk2, func=AF.Square), reads=[bkey(1, 0)], writes=["qsq"])
        for kv in range(2):
            P.op("pe", lambda e, kv=kv: e.matmul(pst2[:, kv, :], lhsT=bd, rhs=qsq[:, kv * 128:(kv + 1) * 128], start=True, stop=True), reads=["bd", "qsq"], writes=[bkey(1, 0)])
        P.op("act", lambda e: e.activation(out=qrs[:, 0:256].rearrange("p (a n) -> p a n", a=2), in_=pst2, func=AF.Sqrt, bias=epsb[:], scale=1.0),
             reads=[bkey(1, 0), "epsb"], writes=["qrs"])
        P.op("dve", lambda e: e.reciprocal(out=qrs[:, 0:256], in_=qrs[:, 0:256]), reads=["qrs"], writes=["qrs"])
        for kv in range(2):
            sl = slice(kv * 64, (kv + 1) * 64)
            P.op("dve", lambda e, kv=kv, sl=sl: e.scalar_tensor_tensor(out=knT[sl, :], in0=pk2[sl, kv, :], scalar=gk[sl, 0:1], in1=qrs[sl, kv * 128:(kv + 1) * 128],
                                                                       op0=ALU.mult, op1=ALU.mult),
                 reads=[bkey(1, 0), "gk", "qrs"], writes=["knT"])
        P.op("pe", lambda e: e.transpose(p_tr, knT, ident[:]), reads=["knT", "ident"], writes=[bkey(1, 1)])
        P.op("act", lambda e: e.copy(out=kv32[:, 0, :], in_=p_tr), reads=[bkey(1, 1)], writes=["kv32"])
        for k in range(KC):
            P.op("pe", lambda e, k=k: e.matmul(p_v32, lhsT=hT[:, k, tc0:tc0 + 128], rhs=win[:, k, 640:768], start=(k == 0), stop=(k == KC - 1)),
                 reads=["win", "hT"], writes=[bkey(1, 1)])
        P.op("dve", lambda e: e.tensor_copy(out=kv32[:, 1, :], in_=p_v32), reads=[bkey(1, 1)], writes=["kv32"])
        P.dma("sp", lambda e: e.dma_start(out=kout, in_=kv32[:, 0, :]), reads=["kv32"], semkey="kvo", final=True)
        P.dma("sp", lambda e: e.dma_start(out=vout, in_=kv32[:, 1, :]), reads=["kv32"], semkey="kvo", final=True)

    if KSTOP >= 2:
        for kv in range(2):
            proj_fm(lambda k, kv=kv: wkd[:, k, kv, :], 128,
                    lambda ps_, key, kv=kv: qknorm(ps_, key, 128, gk, "gk", kdT[:, kv, 0:128], "kdT"), src=hTh)
        tok_vd(0, 0, src=hTh)

    NB = TOK_P // BT
    for b in range(NB if KSTOP >= 3 else 0):
        ln_block(xT[:, :, b * BT:(b + 1) * BT], ("xT", (b * BT) // 512), BT, False, 0)
        project_block(BT, 0)
        if KSTOP < 4:
            continue
        for i in range(TPB):
            tok_vd(i * 128, i + 1)
        for i in range(TPB):
            attention_tile(i, 0, 2 if (b == 0 and i == 0) else 1)
        if KSTOP < 5:
            continue
        for i in range(TPB):
            tok_kv(i * 128, 0)
            retention_tile(i, 0)
            state_update()
            P.op("act", lambda e: e.copy(out=Sb, in_=S), reads=["S"], writes=["Sb"])
        if b == NB - 1:
            window_kv(BT - 128, pk_out, pv_out)
        if KSTOP < 6:
            continue
        out_proj(b * BT, BT, False, lambda k, m: wout[:, k, m * 128:(m + 1) * 128], KC, lambda k: oT[:, k, :BT], 16, ["wout", "oT"])
        P.op("pool", lambda e: e.tensor_copy(out=kdT[:, :, 0:128], in_=kdT[:, :, BT:BT + 128]), reads=["kdT"], writes=["kdT"])
        P.op("pool", lambda e: e.tensor_copy(out=vd_tok[:, 0, :], in_=vd_tok[:, TPB, :]), reads=["vd_tok"], writes=["vd_tok"])
    P.dma("sp", lambda e: e.dma_start(out=pret_out.rearrange("h d e -> d h e"), in_=S), reads=["S"], semkey="pret", final=True)

    if KSTOP >= 7:
        P.dma("sp", lambda e: e.dma_start(out=rmask, in_=rmask_d[:, 1]), writes=["rmask"], semkey="ld_rmask", group="ld_rmask")
        P.dma("sp", lambda e: e.dma_start(out=dq, in_=dq_d[:, 1]), writes=["dq"], semkey="ld_dq", group="ld_dq")
        ln_block(xT[:, :, TOK_P:TOK_P + 128], ("xT", 4), 128, True, 0)
        project_block(128, 1)
        tok_vd(0, 1)
        tok_kv(0, 1)
        window_kv(0, sk_out[:, 120:128, :], sv_out[:, 120:128, :])
        P.dma("sp", lambda e: e.dma_start(out=sk_out[:, 0:120, :], in_=cache_k[:, 8:128, :]), semkey="cko", final=True)
        P.dma("sp", lambda e: e.dma_start(out=sv_out[:, 0:120, :], in_=cache_v[:, 8:128, :]), semkey="cko", final=True)
        attention_sample()
        retention_sample()
        out_proj(TOK_P, 128, True, lambda k, m: wout[:, k, m * 128:(m + 1) * 128], KC, lambda k: oT[:, k, :128], 16, ["wout", "oT"])

    if KSTOP >= 8:
        ffn(0)

    P.barrier()
    cur[0] = mark1
    yst = [carve([128, D]) for _ in range(2)]
    for t in range(17):
        s = t % 2
        for k in range(KC):
            P.op("pe", lambda e, k=k, t=t, s=s: e.transpose(pT[s][:, k, :], xT[:, k, t * 128:(t + 1) * 128], ident[:]),
                 reads=[("xT", t // 4), "ident"], writes=[bkey(s, 0), bkey(s, 1)])
        P.op("act", lambda e, s=s: e.copy(out=yst[s], in_=Q[s][:, :]), reads=[bkey(s, 0), bkey(s, 1)], writes=[f"yst{s}"])
        P.dma("sp", lambda e, t=t, s=s: e.dma_start(out=x1_out[t * 128:(t + 1) * 128, :], in_=yst[s]), reads=[f"yst{s}"], semkey=f"yst{s}", final=True)

    stats = P.build()
    nc_allow.__exit__(None, None, None)
    es.close()
    return nc, stats


_CACHE = {}


def _tables(c):
    f32 = np.float32
    ident = np.eye(128, dtype=f32)
    bd = np.zeros((128, 128), f32)
    bd[:64, :64] = 1.0 / 64
    bd[64:, 64:] = 1.0 / 64
    j = np.arange(128)[:, None]
    i = np.arange(128)[None, :]
    NEG = -1e30
    dneg = np.full((128, 5, 128), NEG, f32)
    dneg[:, 0, :] = np.where(i >= j, -(i - j), NEG)
    dneg[:, 1, :] = np.where(i < j, -(i - j + 128), NEG)
    dneg[:, 2, :] = dneg[:, 1, :] if c > 0 else NEG
    same = (j // 8) == (i // 8)
    dneg[:, 3, :] = np.where(same & ((i % 8) >= (j % 8)), -((i % 8) - (j % 8)), NEG)
    dneg[:, 4, :] = np.where(j >= (i % 8) + 1, -(128 + (i % 8) - j), NEG)
    g = np.array(GAM, np.float64)
    rmask = np.zeros((128, 2, 4, 128), f32)
    dq = np.zeros((128, 2, 4, 128), f32)
    dk = np.zeros((128, 2, 4), f32)
    dc = np.zeros((128, 2, 4), f32)
    for h in range(4):
        rmask[:, 0, h, :] = np.where(i >= j, g[h] ** np.maximum(i - j, 0), 0.0)
        rmask[:, 1, h, :] = np.where(same & ((i % 8) >= (j % 8)), g[h] ** np.maximum((i % 8) - (j % 8), 0), 0.0)
        dq[:, 0, h, :] = g[h] ** (i + 1.0)
        dq[:, 1, h, :] = g[h] ** ((i % 8) + 1.0)
        dk[:, 0, h] = 128.0 ** -0.5 * g[h] ** (127.0 - j[:, 0])
        dk[:, 1, h] = 128.0 ** -0.5 * g[h] ** (7.0 - (j[:, 0] % 8))
        dc[:, 0, h] = g[h] ** 128.0
        dc[:, 1, h] = g[h] ** 8.0
    seqmask = (j // 8 == np.arange(16)[None, :]).astype(f32)
    wret = np.zeros((128, 8, 4), f32)
    for r in range(8):
        if r < c:
            for h in range(4):
                wret[:, r, h] = g[h] ** (2048.0 * (c - r - 1))
    return dict(ident=ident, bdones=bd, dneg=dneg, rmask=rmask, dq=dq, dk=dk, dc=dc, seqmask=seqmask, wret=wret)


def _get(stage):
    if stage not in _CACHE:
        _CACHE[stage] = build_program(stage)
    return _CACHE[stage]


def kernel(**inp):
    f32 = np.float32
    A = lambda k: np.ascontiguousarray(np.asarray(inp[k], f32))
    xp = A("x_prompt")[0]
    xs = A("x_sample")
    base = dict(norm_mix=A("norm_mix"), norm_ffn=A("norm_ffn"), even_w_in=A("even_w_in")[0])
    per_core = []
    for c in range(NCORES):
        x18 = np.zeros((18 * 128, D), f32)
        if c > 0:
            x18[0:128] = xp[c * TOK_P - 128:c * TOK_P]
        x18[128:128 + TOK_P] = xp[c * TOK_P:(c + 1) * TOK_P]
        x18[128 + TOK_P:] = xs[16 * c:16 * c + 16].reshape(128, D)
        cv = np.concatenate([A("c_prompt"), A("c_sample")[16 * c:16 * c + 16]], axis=0)
        t = _tables(c)
        m = dict(base)
        m.update(x=x18, cvec=np.ascontiguousarray(cv), ident=t["ident"], dk=t["dk"], dc=t["dc"])
        per_core.append((m, t))

    ncA, _ = _get("A")
    mapsA = []
    for m, _ in per_core:
        m = dict(m)
        m.update(ada_w=A("ada_w"), ada_b=A("ada_b"))
        mapsA.append(m)
    resA = run_bass_kernel_spmd(ncA, mapsA, core_ids=list(range(NCORES)))
    mods = [np.ascontiguousarray(resA.results[c]["modout"]) for c in range(NCORES)]
    lall = np.ascontiguousarray(np.stack([resA.results[c]["lret"] for c in range(NCORES)], axis=0))
    _CACHE["lall"] = lall

    ncB, statsB = _get("B")
    _CACHE["statsB"] = statsB
    in_maps = []
    for c in range(NCORES):
        m, t = per_core[c]
        m = dict(m)
        m.pop("cvec", None)
        m["modin"] = np.ascontiguousarray(mods[c][0])
        m.update(even_q_gain=A("even_q_gain"), even_k_gain=A("even_k_gain"), even_sinks=A("even_sinks"), even_ret_gain=A("even_ret_gain"),
                 even_w_out=A("even_w_out")[0], ffn_wg=A("ffn_wg"), ffn_wu=A("ffn_wu"), ffn_wd=A("ffn_wd"),
                 cache_k=np.ascontiguousarray(A("cache_win_k")[0, 16 * c:16 * c + 16].reshape(16, 128, 128)),
                 cache_v=np.ascontiguousarray(A("cache_win_v")[0, 16 * c:16 * c + 16].reshape(16, 128, 128)),
                 state_ret=np.ascontiguousarray(A("state_ret")[0, 16 * c:16 * c + 16]),
                 bdones=t["bdones"], dneg=t["dneg"], rmask=t["rmask"], dq=t["dq"], seqmask=t["seqmask"], wret=t["wret"], lall=lall)
        in_maps.append(m)
    resB = run_bass_kernel_spmd(ncB, in_maps, core_ids=list(range(NCORES)))
    R = resB.results
    _CACHE["R"] = R
    last = R[NCORES - 1]
    p_k = last["p_k"].reshape(1, 1, 128, 2, 64)
    p_v = last["p_v"].reshape(1, 1, 128, 2, 64)
    p_ret = last["p_ret"].reshape(1, 1, 4, 128, 128)
    s_k = np.concatenate([R[c]["s_k"] for c in range(NCORES)], axis=0).reshape(1, 128, 128, 2, 64)
    s_v = np.concatenate([R[c]["s_v"] for c in range(NCORES)], axis=0).reshape(1, 128, 128, 2, 64)
    s_ret = np.concatenate([R[c]["s_ret"] for c in range(NCORES)], axis=0)[None]

    tC = _tables_c()
    mapsC = []
    for c in range(NCORES):
        m0, t = per_core[c]
        x18 = np.zeros((18 * 128, D), f32)
        x18[128:] = R[c]["x1"]
        wsel = np.zeros((128, 8), f32)
        wsel[:, :c] = 1.0
        m = dict(base)
        m.update(x=x18, modin=np.ascontiguousarray(mods[c][1]), ident=t["ident"], dk=t["dk"], dc=t["dc"],
                 ffn_wg=A("ffn_wg"), ffn_wu=A("ffn_wu"), ffn_wd=A("ffn_wd"),
                 odd_A_re=A("odd_A_re")[0], odd_A_im=A("odd_A_im")[0], odd_log_dt=A("odd_log_dt"),
                 odd_B_re=A("odd_B_re")[0], odd_B_im=A("odd_B_im")[0], odd_C_re=A("odd_C_re")[0], odd_C_im=A("odd_C_im")[0],
                 odd_D=A("odd_D"), odd_glu_a=A("odd_glu_a")[0], odd_glu_b=A("odd_glu_b")[0],
                 s5_re=np.ascontiguousarray(A("state_s5_re")[0, 16 * c:16 * c + 16]), s5_im=np.ascontiguousarray(A("state_s5_im")[0, 16 * c:16 * c + 16]),
                 fall=np.zeros((8, 128, 64), f32), wsel=wsel, **tC)
        mapsC.append(m)
    ncC1, _ = _get("C1")
    resC1 = run_bass_kernel_spmd(ncC1, mapsC, core_ids=list(range(NCORES)))
    fall = np.ascontiguousarray(np.stack([resC1.results[c]["floc"] for c in range(NCORES)], axis=0))
    _CACHE["fall"] = fall
    for m in mapsC:
        m["fall"] = fall
    ncC2, statsC = _get("C2")
    _CACHE["statsC"] = statsC
    resC2 = run_bass_kernel_spmd(ncC2, mapsC, core_ids=list(range(NCORES)))
    RC = resC2.results
    _CACHE["RC"] = RC
    y_prompt = np.concatenate([RC[c]["y"][:TOK_P] for c in range(NCORES)], axis=0)[None]
    y_sample = np.concatenate([RC[c]["y"][TOK_P:].reshape(16, 8, D) for c in range(NCORES)], axis=0)
    lastC = RC[NCORES - 1]
    p_re = lastC["p_s5r"].reshape(1, 1, 64, 64)
    p_im = lastC["p_s5i"].reshape(1, 1, 64, 64)
    s_re = np.concatenate([RC[c]["s_s5r"] for c in range(NCORES)], axis=0)[None]
    s_im = np.concatenate([RC[c]["s_s5i"] for c in range(NCORES)], axis=0)[None]
    return (y_prompt.astype(f32), y_sample.astype(f32), p_k, p_v, p_ret, p_re, p_im, s_k, s_v, s_ret, s_re, s_im)


def _tables_c():
    f32 = np.float32
    t = np.arange(512)
    tal = np.broadcast_to((t // 64).astype(f32)[None, :], (128, 512)).copy()
    tbp = np.broadcast_to(((t % 64) + 1).astype(f32)[None, :], (128, 512)).copy()
    ts = np.arange(128)
    tbs = np.broadcast_to(((ts % 8) + 1).astype(f32)[None, :], (128, 128)).copy()
    amask = np.broadcast_to(((ts % 8) != 0).astype(f32)[None, :], (128, 128)).copy()
    maskg = (np.arange(128)[:, None] // 16 == np.arange(8)[None, :]).astype(f32)
    rott = np.zeros((128, 128), f32)
    for p in range(64):
        rott[64 + p, p] = -1.0
        rott[p, 64 + p] = 1.0
    return dict(tal=tal, tbp=tbp, tbs=tbs, amask=amask, maskg=maskg, rott=rott)
```

```python
import os
import numpy as np
from contextlib import ExitStack
import concourse.bass as bass
import concourse.mybir as mybir
from concourse.bass_utils import run_bass_kernel_spmd

F32 = mybir.dt.float32
BF16 = mybir.dt.bfloat16
I32 = mybir.dt.int32
ALU = mybir.AluOpType
AF = mybir.ActivationFunctionType

NCORES = 8
D = 1024
KC = 8
NPT = 16
TOK_P = NPT * 128
NTOK = TOK_P + 128
IN_W = 2816
DFF = 2816
FC = 22
BT = 128
TPB = BT // 128
EPS = 1e-6
ENGS = ("pe", "act", "dve", "pool", "sp")
GAM = [1.0 - 2.0 ** (-5.0 - h) for h in range(4)]
KSTOP = int(os.environ.get('KSTOP', '9'))
KSUB = int(os.environ.get('KSUB', '9'))


class Prog:
    def __init__(self, nc, same_engine_sync=("act", "dve", "pool")):
        self.nc = nc
        self.ins = []
        self.last_w = {}
        self.readers = {}
        self.same_sync = set(same_engine_sync)
        self.final_ids = []
        self.last_eng = {}
        self.last_dma = {}
        self.bar_deps = []
        self.bar_gen = 0
        self.eng_gen = {e: 0 for e in ENGS}

    def barrier(self):
        self.bar_deps = list(self.last_eng.values()) + list(self.last_dma.values())
        self.bar_gen += 1

    def _add(self, eng, fn, reads, writes, dma=False, semkey=None, group=None):
        iid = len(self.ins)
        deps = set()
        pk = tuple(k for k in reads if isinstance(k, str) and len(k) == 3 and k[0] == "Q" and k not in writes)
        writes = tuple(writes) + pk
        if self.eng_gen[eng] < self.bar_gen:
            deps.update(self.bar_deps)
            self.eng_gen[eng] = self.bar_gen
        for k in reads:
            w = self.last_w.get(k)
            if w is not None:
                deps.add(w)
        for k in writes:
            w = self.last_w.get(k)
            if w is not None:
                if group is not None and self.ins[w].get("group") == group:
                    deps.update(self.ins[w]["deps"])
                else:
                    deps.add(w)
            for r in self.readers.get(k, ()):
                deps.add(r)
        self.ins.append(dict(eng=eng, fn=fn, deps=sorted(deps), dma=dma, semkey=semkey, group=group))
        for k in reads:
            self.readers.setdefault(k, []).append(iid)
        for k in writes:
            self.last_w[k] = iid
            self.readers[k] = []
        self.last_eng[eng] = iid
        if dma:
            self.last_dma[semkey] = iid
        return iid

    def op(self, eng, fn, reads=(), writes=()):
        return self._add(eng, fn, tuple(reads), tuple(writes))

    def dma(self, eng, fn, reads=(), writes=(), semkey=None, group=None, final=False):
        assert semkey is not None
        iid = self._add(eng, fn, tuple(reads), tuple(writes), dma=True, semkey=semkey, group=group)
        if final:
            self.final_ids.append(iid)
        return iid

    def build(self):
        nc = self.nc
        ins = self.ins
        n = len(ins)
        needed = [False] * n
        for i, it in enumerate(ins):
            nd = []
            for d in it["deps"]:
                de = ins[d]
                if (not de["dma"]) and (not it["dma"]) and de["eng"] == it["eng"] and it["eng"] not in self.same_sync:
                    continue
                nd.append(d)
            it["deps"] = nd
            for d in nd:
                needed[d] = True
        for f in self.final_ids:
            needed[f] = True
        semkeys = []
        for it in ins:
            if it["dma"] and it["semkey"] not in semkeys:
                semkeys.append(it["semkey"])
        sem_objs = {}
        ctxs = []
        for e in ENGS:
            c = nc.semaphore(f"s_{e}")
            sem_objs[("eng", e)] = c.__enter__()
            ctxs.append(c)
        for j, k in enumerate(semkeys):
            c = nc.semaphore(f"d{j}")
            sem_objs[("dma", k)] = c.__enter__()
            ctxs.append(c)
        cnt = {}
        for i, it in enumerate(ins):
            if it["dma"]:
                key = ("dma", it["semkey"])
                cnt[key] = cnt.get(key, 0) + 16
                it["sig"] = (key, cnt[key])
            elif needed[i]:
                key = ("eng", it["eng"])
                cnt[key] = cnt.get(key, 0) + 1
                it["sig"] = (key, cnt[key])
            else:
                it["sig"] = None
        per = {e: [] for e in ENGS}
        for i, it in enumerate(ins):
            per[it["eng"]].append(i)
        final_waits = {}
        for f in self.final_ids:
            key, val = ins[f]["sig"]
            final_waits[key] = max(final_waits.get(key, 0), val)
        with nc.Block() as block:
            def make(e):
                def body(eng):
                    waited = {}
                    for i in per[e]:
                        it = ins[i]
                        req = {}
                        for d in it["deps"]:
                            key, val = ins[d]["sig"]
                            if waited.get(key, 0) >= val:
                                continue
                            req[key] = max(req.get(key, 0), val)
                        for key, val in req.items():
                            eng.wait_ge(sem_objs[key], val)
                            waited[key] = val
                        r = it["fn"](eng)
                        if it["sig"] is not None:
                            key, val = it["sig"]
                            r.then_inc(sem_objs[key], 16 if it["dma"] else 1)
                    if e == "sp":
                        for key, val in final_waits.items():
                            eng.wait_ge(sem_objs[key], val)
                return body
            block.tensor(make("pe"))
            block.scalar(make("act"))
            block.vector(make("dve"))
            block.gpsimd(make("pool"))
            block.sync(make("sp"))
        for c in reversed(ctxs):
            c.__exit__(None, None, None)
        return dict(n=n, per={e: len(per[e]) for e in ENGS}, sems=len(sem_objs), maxcnt=max(cnt.values()), cnt={k[1]: v for k, v in cnt.items() if k[0] == 'eng'})


def build_program(stage):
    nc = bass.Bass("TRN2", target_bir_lowering=False)
    es = ExitStack()

    def din(name, shape):
        return nc.dram_tensor(name, list(shape), F32, kind="ExternalInput").ap()

    def dout(name, shape):
        return nc.dram_tensor(name, list(shape), F32, kind="ExternalOutput").ap()

    def sb(name, shape, dt=F32):
        return es.enter_context(nc.sbuf_tensor("sb_" + name, list(shape), dt))

    P = Prog(nc)
    nc_allow = nc.allow_non_contiguous_dma(reason="small parameter vectors laid out feature-major")
    nc_allow.__enter__()

    x_in = din("x", [18 * 128, D])
    if stage == "A":
        cvec = din("cvec", [17, D])
        ada_w = din("ada_w", [2, D, 6 * D])
        ada_b = din("ada_b", [2, 6 * D])
        mod_out = dout("modout", [2, 128, 48 * 17])
    else:
        modin_d = din("modin", [128, 48 * 17])
    norm_mix = din("norm_mix", [2, D])
    norm_ffn = din("norm_ffn", [2, D])
    w_in = din("even_w_in", [D, IN_W])
    ident_d = din("ident", [128, 128])
    dk_d = din("dk", [128, 2, 4])
    dc_d = din("dc", [128, 2, 4])
    if stage == "A":
        lret_out = dout("lret", [4, 128, 128])
    if stage in ("C1", "C2"):
        wg_d = din("ffn_wg", [2, D, DFF])
        wu_d = din("ffn_wu", [2, D, DFF])
        wd_d = din("ffn_wd", [2, DFF, D])
        A_re_d = din("odd_A_re", [64, 64])
        A_im_d = din("odd_A_im", [64, 64])
        ldt_d = din("odd_log_dt", [1, 64])
        B_re_d = din("odd_B_re", [64, 64, 16])
        B_im_d = din("odd_B_im", [64, 64, 16])
        C_re_d = din("odd_C_re", [64, 16, 64])
        C_im_d = din("odd_C_im", [64, 16, 64])
        Dsk_d = din("odd_D", [1, D])
        glua_d = din("odd_glu_a", [D, D])
        glub_d = din("odd_glu_b", [D, D])
        s5r_in = din("s5_re", [16, 64, 64])
        s5i_in = din("s5_im", [16, 64, 64])
        fall_d = din("fall", [8, 128, 64])
        wsel_d = din("wsel", [128, 8])
        tal_d = din("tal", [128, 512])
        tbp_d = din("tbp", [128, 512])
        tbs_d = din("tbs", [128, 128])
        amask_d = din("amask", [128, 128])
        maskg_d = din("maskg", [128, 8])
        rott_d = din("rott", [128, 128])
        if stage == "C1":
            floc_out = dout("floc", [128, 64])
        else:
            y_out = dout("y", [17 * 128, D])
            ps5r_out = dout("p_s5r", [64, 64])
            ps5i_out = dout("p_s5i", [64, 64])
            ss5r_out = dout("s_s5r", [16, 64, 64])
            ss5i_out = dout("s_s5i", [16, 64, 64])
    if stage == "B":
        A_re_d = din("odd_A_re", [64, 64])
        A_im_d = din("odd_A_im", [64, 64])
        ldt_d = din("odd_log_dt", [1, 64])
        B_re_d = din("odd_B_re", [64, 64, 16])
        B_im_d = din("odd_B_im", [64, 64, 16])
        C_re_d = din("odd_C_re", [64, 16, 64])
        C_im_d = din("odd_C_im", [64, 16, 64])
        Dsk_d = din("odd_D", [1, D])
        tal_d = din("tal", [128, 512])
        tbp_d = din("tbp", [128, 512])
        tbs_d = din("tbs", [128, 128])
        amask_d = din("amask", [128, 128])
        maskg_d = din("maskg", [128, 8])
        rott_d = din("rott", [128, 128])
        modin1_d = din("modin1", [128, 48 * 17])
        floc_out = dout("floc", [128, 64])
        qgain = din("even_q_gain", [1, 64])
        kgain = din("even_k_gain", [1, 64])
        sinks_d = din("even_sinks", [1, 8])
        retg_d = din("even_ret_gain", [1, 512])
        w_out = din("even_w_out", [D, D])
        wg_d = din("ffn_wg", [2, D, DFF])
        wu_d = din("ffn_wu", [2, D, DFF])
        wd_d = din("ffn_wd", [2, DFF, D])
        cache_k = din("cache_k", [16, 128, 128])
        cache_v = din("cache_v", [16, 128, 128])
        sret_in = din("state_ret", [16, 4, 128, 128])
        bd_d = din("bdones", [128, 128])
        dneg_d = din("dneg", [128, 5, 128])
        rmask_d = din("rmask", [128, 2, 4, 128])
        dq_d = din("dq", [128, 2, 4, 128])
        seqmask_d = din("seqmask", [128, 16])
        wret_d = din("wret", [128, 8, 4])
        lall_d = din("lall", [8, 4, 128, 128])
        x1_out = dout("x1", [17 * 128, D])
        pk_out = dout("p_k", [128, 128])
        pv_out = dout("p_v", [128, 128])
        pret_out = dout("p_ret", [4, 128, 128])
        sk_out = dout("s_k", [16, 128, 128])
        sv_out = dout("s_v", [16, 128, 128])
        sret_out = dout("s_ret", [16, 4, 128, 128])

    Q = [es.enter_context(nc.psum_tensor(f"ps_Q{i}", [128, 1024], F32)) for i in range(4)]

    def bank(i, h):
        return Q[i][:, h * 512:(h + 1) * 512]

    def bkey(i, h):
        return f"Q{i}{'ab'[h]}"

    ARENA_W = 33 * 1024
    arena = sb("arena", [128, ARENA_W])
    cur = [0]

    def carve(shape, dt=F32):
        n = int(np.prod(shape[1:]))
        words = n if dt in (F32, I32) else (n + 1) // 2
        words = (words + 7) // 8 * 8
        off = cur[0]
        assert off + words <= ARENA_W, ("arena overflow", off, words)
        cur[0] = off + words
        v = arena[:, off:off + words]
        if dt != F32:
            v = v.bitcast(dt)
        v = v[:, 0:n]
        if len(shape) == 3:
            v = v.rearrange("p (a b) -> p a b", a=shape[1])
        elif len(shape) == 4:
            v = v.rearrange("p (a b c) -> p a b c", a=shape[1], b=shape[2])
        elif len(shape) == 5:
            v = v.rearrange("p (a b c d) -> p a b c d", a=shape[1], b=shape[2], c=shape[3])
        return v

    ident = sb("ident", [128, 128])
    ones_m = sb("ones_m", [128, 128], BF16)
    ones_b = sb("ones_b", [128, 128], BF16)
    ones_g = sb("ones_g", [128, 128], BF16)
    epsb = sb("epsb", [128, 1])
    xT = sb("xT", [128, KC, NTOK])
    cT = sb("cT", [128, KC, 17], BF16)
    adab = sb("adab", [128, 2, 48])
    modT = sb("modT", [128, 48, 17])
    normg = sb("normg", [128, 2, 2, KC])
    G = sb("G", [128, KC, 17])
    dk = sb("dk", [128, 2, 4])
    dc = sb("dc", [128, 2, 4])
    P.dma("sp", lambda e: e.dma_start(out=ident[:], in_=ident_d), writes=["ident"], semkey="ld_ident", group="ld_ident")
    P.dma("sp", lambda e: e.dma_start(out=dk[:], in_=dk_d), writes=["dk"], semkey="ld_dk", group="ld_dk")
    P.dma("sp", lambda e: e.dma_start(out=dc[:], in_=dc_d), writes=["dc"], semkey="ld_dc", group="ld_dc")
    if stage == "A":
        P.dma("sp", lambda e: e.dma_start(out=adab[:], in_=ada_b.rearrange("l (m p) -> p l m", p=128)), writes=["adab"], semkey="ld_adab", group="ld_adab")
    P.dma("sp", lambda e: e.dma_start(out=normg[:, 0], in_=norm_mix.rearrange("l (k p) -> p l k", p=128)), writes=["normg"], semkey="ld_normg", group="ld_normg")
    P.dma("sp", lambda e: e.dma_start(out=normg[:, 1], in_=norm_ffn.rearrange("l (k p) -> p l k", p=128)), writes=["normg"], semkey="ld_normg", group="ld_normg")
    P.op("dve", lambda e: e.memset(ones_m[:], 1.0 / 1024.0), writes=["ones_m"])
    P.op("dve", lambda e: e.memset(ones_b[:], 1.0), writes=["ones_b"])
    P.op("dve", lambda e: e.memset(ones_g[:], 1.0 / 128.0), writes=["ones_g"])
    P.op("dve", lambda e: e.memset(epsb[:], EPS), writes=["epsb"])

    mark0 = cur[0]
    xst = [carve([128, D]) for _ in range(2)]
    c_sb = carve([128, D])
    adaw = [carve([128, KC, 768], BF16) for _ in range(2)]
    pT = [Q[i][:, :].rearrange("p (k n) -> p k n", k=KC) for i in range(2)]

    def load_tile(t, dst_ap, dst_key):
        s = t % 2
        P.dma("sp", lambda e: e.dma_start(out=xst[s], in_=x_in[t * 128:(t + 1) * 128, :]), writes=[f"xst{s}"], semkey=f"xst{s}")
        for k in range(KC):
            P.op("pe", lambda e, k=k: e.transpose(pT[s][:, k, :], xst[s][:, k * 128:(k + 1) * 128], ident[:]),
                 reads=[f"xst{s}", "ident"], writes=[bkey(s, 0), bkey(s, 1)])
        P.op("act", lambda e: e.copy(out=dst_ap, in_=pT[s]), reads=[bkey(s, 0), bkey(s, 1)], writes=[dst_key])

    for t in range(1, 18):
        c0 = (t - 1) * 128
        load_tile(t, xT[:, :, c0:c0 + 128], ("xT", (t - 1) // 4))

    if stage == "A":
        P.dma("sp", lambda e: e.dma_start(out=c_sb[0:17, :], in_=cvec), writes=["c_sb"], semkey="c_sb")
        P.op("act", lambda e: e.activation(out=c_sb[0:17, :], in_=c_sb[0:17, :], func=AF.Silu), reads=["c_sb"], writes=["c_sb"])
        for k in range(KC):
            P.op("pe", lambda e, k=k: e.transpose(pT[0][:, k, 0:17], c_sb[0:17, k * 128:(k + 1) * 128], ident[0:17, 0:17]),
                 reads=["c_sb", "ident"], writes=[bkey(0, 0), bkey(0, 1)])
        P.op("dve", lambda e: e.tensor_copy(out=cT[:], in_=pT[0][:, :, 0:17]), reads=[bkey(0, 0), bkey(0, 1)], writes=["cT"])

    pm = [bank(2, i)[:, 0:408].rearrange("p (m c) -> p m c", c=17) for i in range(2)]

    def modulation(layer):
        for cb in range(8):
            s = cb % 2
            P.dma("pool", lambda e, cb=cb, s=s: e.dma_start(out=adaw[s], in_=ada_w[layer, :, cb * 768:(cb + 1) * 768].rearrange("(k p) n -> p k n", p=128)),
                  writes=[f"adaw{s}"], semkey=f"adaw{s}")
            for mm in range(6):
                m = cb * 6 + mm
                for k in range(KC):
                    P.op("pe", lambda e, k=k, s=s, m=m, mm=mm: e.matmul(pm[m // 24][:, m % 24, :], lhsT=adaw[s][:, k, mm * 128:(mm + 1) * 128],
                                                                 rhs=cT[:, k, :], start=(k == 0), stop=(k == KC - 1)),
                         reads=[f"adaw{s}", "cT"], writes=[bkey(2, m // 24)])
        for h in range(2):
            P.op("dve", lambda e, h=h: e.tensor_tensor(out=modT[:, h * 24:(h + 1) * 24, :], in0=pm[h],
                                                       in1=adab[:, layer, h * 24:(h + 1) * 24].unsqueeze(2).to_broadcast([128, 24, 17]),
                                                       op=ALU.add),
                 reads=[bkey(2, h), "adab"], writes=["modT"])

    def make_G(layer, which):
        r0 = 8 if which == 0 else 32
        P.op("dve", lambda e: e.tensor_scalar(out=G[:], in0=modT[:, r0:r0 + 8, :], scalar1=1.0, scalar2=None, op0=ALU.add),
             reads=["modT"], writes=["G"])
        P.op("dve", lambda e: e.tensor_tensor(out=G[:], in0=G[:], in1=normg[:, which, layer, :].unsqueeze(2).to_broadcast([128, KC, 17]), op=ALU.mult),
             reads=["G", "normg"], writes=["G"])

    LAYER = 1 if stage in ("C1", "C2") else 0
    if stage == "A":
        for lay in (1, 0):
            modulation(lay)
            P.dma("sp", lambda e, lay=lay: e.dma_start(out=mod_out[lay], in_=modT[:].rearrange("p m c -> p (m c)")), reads=["modT"], semkey="modo", final=True)
    else:
        P.dma("sp", lambda e: e.dma_start(out=modT[:].rearrange("p m c -> p (m c)"), in_=modin_d), writes=["modT"], semkey="ld_modT")
    make_G(LAYER, 0)
    P.barrier()
    cur[0] = mark0

    sq = [carve([128, BT], BF16) for _ in range(2)]
    rstd = carve([128, BT])
    tn = [carve([128, BT]) for _ in range(2)]
    mark_ln = cur[0]
    hT = carve([128, KC, BT], BF16)
    p_ss = bank(3, 0)

    def ln_block(src_ap, src_key, ntok, sample, shift_row, out_ap=None, out_key="hT"):
        out_ap = hT if out_ap is None else out_ap
        for k in range(KC):
            s = k % 2
            P.op("act", lambda e, k=k, s=s: e.activation(out=sq[s][:, :ntok], in_=src_ap[:, k, :], func=AF.Square),
                 reads=[src_key], writes=[f"sq{s}"])
            P.op("pe", lambda e, k=k, s=s: e.matmul(p_ss[:, :ntok], lhsT=ones_m[:], rhs=sq[s][:, :ntok], start=(k == 0), stop=(k == KC - 1)),
                 reads=[f"sq{s}", "ones_m"], writes=[bkey(3, 0)])
        P.op("act", lambda e: e.activation(out=rstd[:, :ntok], in_=p_ss[:, :ntok], func=AF.Sqrt, bias=epsb[:], scale=1.0),
             reads=[bkey(3, 0), "epsb"], writes=["rstd"])
        P.op("dve", lambda e: e.reciprocal(out=rstd[:, :ntok], in_=rstd[:, :ntok]), reads=["rstd"], writes=["rstd"])
        for k in range(KC):
            s = k % 2
            P.op("dve", lambda e, k=k, s=s: e.tensor_tensor(out=tn[s][:, :ntok], in0=src_ap[:, k, :], in1=rstd[:, :ntok], op=ALU.mult),
                 reads=[src_key, "rstd"], writes=[f"tn{s}"])
            if not sample:
                P.op("act", lambda e, k=k, s=s: e.activation(out=out_ap[:, k, :ntok], in_=tn[s][:, :ntok], func=AF.Identity,
                                                             scale=G[:, k, 0:1], bias=modT[:, shift_row + k, 0:1]),
                     reads=[f"tn{s}", "G", "modT"], writes=[out_key])
            else:
                tv = tn[s][:, :128].rearrange("p (b t) -> p b t", t=8)
                P.op("dve", lambda e, k=k, tv=tv: e.tensor_tensor(out=tv, in0=tv, in1=G[:, k, 1:17].unsqueeze(2).to_broadcast([128, 16, 8]), op=ALU.mult),
                     reads=[f"tn{s}", "G"], writes=[f"tn{s}"])
                P.op("dve", lambda e, k=k, tv=tv: e.tensor_tensor(out=out_ap[:, k, :128].rearrange("p (b t) -> p b t", t=8), in0=tv,
                                                                  in1=modT[:, shift_row + k, 1:17].unsqueeze(2).to_broadcast([128, 16, 8]), op=ALU.add),
                     reads=[f"tn{s}", "modT"], writes=[out_key])

    win = None
    if stage in ("A", "B"):
        win = carve([128, KC, IN_W], BF16)
        for k in range(KC):
            P.dma("pool", lambda e, k=k: e.dma_start(out=win[:, k, :], in_=w_in[k * 128:(k + 1) * 128, :]), writes=["win"], semkey="win", group="win")

    S = carve([128, 4, 128])
    Sb = carve([128, 4, 128], BF16)
    kd_tok = carve([128, 512], BF16)
    vb_tok = carve([128, 512], BF16)
    p_kb = bank(3, 1)
    p_vb = bank(2, 0)
    p_su = bank(2, 1).rearrange("p (h e) -> p h e", h=4)

    def tok_kv(tc0, grp):
        for k in range(KC):
            P.op("pe", lambda e, k=k: e.matmul(p_kb, lhsT=hT[:, k, tc0:tc0 + 128], rhs=win[:, k, 1280:1792], start=(k == 0), stop=(k == KC - 1)),
                 reads=["hT", "win"], writes=[bkey(3, 1)])
        for k in range(KC):
            P.op("pe", lambda e, k=k: e.matmul(p_vb, lhsT=hT[:, k, tc0:tc0 + 128], rhs=win[:, k, 1792:2304], start=(k == 0), stop=(k == KC - 1)),
                 reads=["hT", "win"], writes=[bkey(2, 0)])
        P.op("dve", lambda e: e.tensor_tensor(out=kd_tok.rearrange("p (h d) -> p h d", h=4), in0=p_kb.rearrange("p (h d) -> p h d", h=4),
                                              in1=dk[:, grp, :].unsqueeze(2).to_broadcast([128, 4, 128]), op=ALU.mult),
             reads=[bkey(3, 1), "dk"], writes=["kd_tok"])
        P.op("act", lambda e: e.copy(out=vb_tok, in_=p_vb), reads=[bkey(2, 0)], writes=["vb_tok"])

    def state_update():
        for h in range(4):
            P.op("pe", lambda e, h=h: e.matmul(p_su[:, h, :], lhsT=kd_tok[:, h * 128:(h + 1) * 128], rhs=vb_tok[:, h * 128:(h + 1) * 128], start=True, stop=True),
                 reads=["kd_tok", "vb_tok"], writes=[bkey(2, 1)])
        P.op("dve", lambda e: e.tensor_tensor(out=S, in0=S, in1=dc[:, 0, :].unsqueeze(2).to_broadcast([128, 4, 128]), op=ALU.mult),
             reads=["S", "dc"], writes=["S"])
        P.op("dve", lambda e: e.tensor_tensor(out=S, in0=S, in1=p_su, op=ALU.add), reads=["S", bkey(2, 1)], writes=["S"])

    def ffn(layer):
        GC = 2
        NG = FC // GC
        P.barrier()
        cur[0] = mark_ln
        make_G(layer, 1)
        hT_all = carve([128, KC, NTOK], BF16)
        wslot = [(carve([128, KC, GC * 128], BF16), carve([128, KC, GC * 128], BF16), carve([128, GC, D], BF16)) for _ in range(3)]
        hid = [carve([128, GC, 512], BF16) for _ in range(2)]
        sgf = [carve([128, 512]) for _ in range(2)]
        gt2 = carve([128, 128])
        print("arena words used (ffn):", cur[0], "of", ARENA_W)
        for t in range(NPT):
            ln_block(xT[:, :, t * 128:(t + 1) * 128], ("xT", t // 4), 128, False, 24, out_ap=hT_all[:, :, t * 128:(t + 1) * 128], out_key="hT_all")
        ln_block(xT[:, :, TOK_P:TOK_P + 128], ("xT", 4), 128, True, 24, out_ap=hT_all[:, :, TOK_P:TOK_P + 128], out_key="hT_all")
        UP = [(bank(0, 0), bkey(0, 0), bank(0, 1), bkey(0, 1)), (bank(1, 0), bkey(1, 0), bank(1, 1), bkey(1, 1))]
        DN = [(bank(2, 0), bkey(2, 0)), (bank(2, 1), bkey(2, 1)), (bank(3, 0), bkey(3, 0)), (bank(3, 1), bkey(3, 1))]
        ui = 0
        di = 0
        blocks = [(b * 512, 512, False) for b in range(4)] + [(TOK_P, 128, True)]
        for g in range(NG):
            ws = g % 3
            wgs, wus, wds = wslot[ws]
            c0h = g * GC * 128
            P.dma("pool", lambda e, wgs=wgs, c0h=c0h: e.dma_start(out=wgs, in_=wg_d[layer, :, c0h:c0h + GC * 128].rearrange("(k p) n -> p k n", p=128)),
                  writes=[f"wg{ws}"], semkey=f"wg{ws}")
            P.dma("pool", lambda e, wus=wus, c0h=c0h: e.dma_start(out=wus, in_=wu_d[layer, :, c0h:c0h + GC * 128].rearrange("(k p) n -> p k n", p=128)),
                  writes=[f"wu{ws}"], semkey=f"wu{ws}")
            P.dma("pool", lambda e, wds=wds, c0h=c0h: e.dma_start(out=wds, in_=wd_d[layer, c0h:c0h + GC * 128, :].rearrange("(c p) n -> p c n", p=128)),
                  writes=[f"wd{ws}"], semkey=f"wd{ws}")
            for (t0, nt, smp) in blocks:
                hs = ui % 2
                for c in range(GC):
                    gps, gkey, ups, ukey = UP[ui % 2]
                    ui += 1
                    for k in range(KC):
                        P.op("pe", lambda e, k=k, c=c, gps=gps, wgs=wgs, nt=nt, t0=t0: e.matmul(gps[:, :nt], lhsT=wgs[:, k, c * 128:(c + 1) * 128], rhs=hT_all[:, k, t0:t0 + nt],
                                                                                  start=(k == 0), stop=(k == KC - 1)),
                             reads=[f"wg{ws}", "hT_all"], writes=[gkey])
                    for k in range(KC):
                        P.op("pe", lambda e, k=k, c=c, ups=ups, wus=wus, nt=nt, t0=t0: e.matmul(ups[:, :nt], lhsT=wus[:, k, c * 128:(c + 1) * 128], rhs=hT_all[:, k, t0:t0 + nt],
                                                                                  start=(k == 0), stop=(k == KC - 1)),
                             reads=[f"wu{ws}", "hT_all"], writes=[ukey])
                    sgs = sgf[c % 2]
                    P.op("act", lambda e, gps=gps, sgs=sgs, nt=nt: e.activation(out=sgs[:, :nt], in_=gps[:, :nt], func=AF.Silu), reads=[gkey], writes=[f"sgf{c % 2}"])
                    P.op("dve", lambda e, ups=ups, sgs=sgs, hs=hs, c=c, nt=nt: e.tensor_tensor(out=hid[hs][:, c, :nt], in0=ups[:, :nt], in1=sgs[:, :nt], op=ALU.mult),
                         reads=[ukey, f"sgf{c % 2}"], writes=[f"hid{hs}"])
                for m in range(KC):
                    dps, dkey = DN[di % 4]
                    di += 1
                    for c in range(GC):
                        P.op("pe", lambda e, m=m, c=c, dps=dps, wds=wds, hs=hs, nt=nt: e.matmul(dps[:, :nt], lhsT=wds[:, c, m * 128:(m + 1) * 128], rhs=hid[hs][:, c, :nt],
                                                                                         start=(c == 0), stop=(c == GC - 1)),
                             reads=[f"wd{ws}", f"hid{hs}"], writes=[dkey])
                    xs = xT[:, m, t0:t0 + nt]
                    xkey = ("xT", t0 // 512)
                    if not smp:
                        P.op("dve", lambda e, m=m, dps=dps, xs=xs, nt=nt: e.scalar_tensor_tensor(out=xs, in0=dps[:, :nt], scalar=modT[:, 40 + m, 0:1], in1=xs,
                                                                                         op0=ALU.mult, op1=ALU.add),
                             reads=[dkey, "modT", xkey], writes=[xkey])
                    else:
                        gv = gt2.rearrange("p (b t) -> p b t", t=8)
                        P.op("dve", lambda e, m=m, dps=dps, gv=gv: e.tensor_tensor(out=gv, in0=dps[:, :128].rearrange("p (b t) -> p b t", t=8),
                                                                                  in1=modT[:, 40 + m, 1:17].unsqueeze(2).to_broadcast([128, 16, 8]), op=ALU.mult),
                             reads=[dkey, "modT"], writes=["gt2"])
                        P.op("dve", lambda e, xs=xs: e.tensor_tensor(out=xs, in0=xs, in1=gt2, op=ALU.add), reads=["gt2", xkey], writes=[xkey])
        return hT_all

    if stage == "A":
        P.op("dve", lambda e: e.memset(S, 0.0), writes=["S"])
        for b in range(TOK_P // BT):
            ln_block(xT[:, :, b * BT:(b + 1) * BT], ("xT", (b * BT) // 512), BT, False, 0)
            for i in range(TPB):
                tok_kv(i * 128, 0)
                state_update()
        P.dma("sp", lambda e: e.dma_start(out=lret_out.rearrange("h d e -> d h e"), in_=S), reads=["S"], semkey="lret", final=True)
        stats = P.build()
        nc_allow.__exit__(None, None, None)
        es.close()
        return nc, stats

    def s5_stage(full):
        TWO_PI = 2.0 * np.pi
        cur[0] = mark_ln
        hT_all = carve([128, KC, NTOK], BF16)
        for t in range(NPT):
            ln_block(xT[:, :, t * 128:(t + 1) * 128], ("xT", t // 4), 128, False, 0, out_ap=hT_all[:, :, t * 128:(t + 1) * 128], out_key="hT_all")
        ln_block(xT[:, :, TOK_P:TOK_P + 128], ("xT", 4), 128, True, 0, out_ap=hT_all[:, :, TOK_P:TOK_P + 128], out_key="hT_all")

        def t64():
            return carve([128, 64])
        AreT, AimT, dtT, ar, ai, rho, thr, ph64, ph512, sinT, cosT, tmpa, tmpb, lre, lim, fre, fim, rden64 = [t64() for _ in range(18)]
        PHB = carve([128, 64, 4])
        pib = carve([128, 1])
        Glast = carve([128, 64])
        Alast = carve([128, 64])
        Blast = carve([128, 64])
        tal = carve([128, 512])
        tbp = carve([128, 512])
        tbs = carve([128, 128])
        amask = carve([128, 128])
        maskg = carve([128, 8])
        rott = carve([128, 128])
        Dsk = carve([128, KC])
        Mc = carve([128, KC, 128], BF16)
        Mcsw = carve([128, KC, 128], BF16)
        CA = carve([128, 64, 128], BF16)
        CB = carve([128, 64, 128], BF16)
        mark_s5 = cur[0]
        Bn_re = carve([128, 64, 16])
        Bn_im = carve([128, 64, 16])
        tB1 = carve([128, 64, 16])
        tB2 = carve([128, 64, 16])
        Cn = carve([128, 16, 2, 64])
        Cn2 = carve([128, 16, 2, 64])
        CsA = carve([128, 64, 16])
        CsB = carve([128, 64, 16])
        print("arena words used (s5 prep):", cur[0], "of", ARENA_W)
        P.op("dve", lambda e: e.memset(pib, float(np.pi / 2)), writes=["pib"])
        for half in range(2):
            hs_ = slice(half * 64, half * 64 + 64)
            P.dma("sp", lambda e, hs_=hs_: e.dma_start(out=AreT[hs_, :], in_=A_re_d.rearrange("g p -> p g")), writes=["AreT"], semkey="ld_AreT", group="ld_AreT")
            P.dma("sp", lambda e, hs_=hs_: e.dma_start(out=AimT[hs_, :], in_=A_im_d.rearrange("g p -> p g")), writes=["AimT"], semkey="ld_AimT", group="ld_AimT")
        P.dma("sp", lambda e: e.dma_start(out=dtT, in_=ldt_d.partition_broadcast(128)), writes=["dtT"], semkey="ld_dtT")
        for nm, tl, dd in (("tal", tal, tal_d), ("tbp", tbp, tbp_d), ("tbs", tbs, tbs_d), ("amask", amask, amask_d), ("maskg", maskg, maskg_d), ("rott", rott, rott_d)):
            P.dma("sp", lambda e, tl=tl, dd=dd: e.dma_start(out=tl, in_=dd), writes=[nm], semkey="ld_" + nm)
        P.dma("sp", lambda e: e.dma_start(out=Dsk, in_=Dsk_d.rearrange("o (k p) -> p (o k)", p=128)), writes=["Dsk"], semkey="ld_Dsk")
        P.dma("sp", lambda e: e.dma_start(out=Bn_re[0:64], in_=B_re_d.rearrange("g p j -> p g j")), writes=["Bn_re"], semkey="ld_Bn_re")
        P.dma("sp", lambda e: e.dma_start(out=Bn_im[0:64], in_=B_im_d.rearrange("g p j -> p g j")), writes=["Bn_im"], semkey="ld_Bn_im")
        P.dma("sp", lambda e: e.dma_start(out=Cn[0:64, :, 0, :], in_=C_re_d), writes=["Cn"], semkey="ld_Cn", group="ld_Cn")
        P.dma("sp", lambda e: e.dma_start(out=Cn[0:64, :, 1, :], in_=C_im_d), writes=["Cn"], semkey="ld_Cn", group="ld_Cn")
        P.dma("sp", lambda e: e.dma_start(out=Cn2[0:64, :, 0, :], in_=C_im_d), writes=["Cn2"], semkey="ld_Cn2", group="ld_Cn2")
        P.dma("sp", lambda e: e.dma_start(out=Cn2[0:64, :, 1, :], in_=C_re_d), writes=["Cn2"], semkey="ld_Cn2", group="ld_Cn2")

        def V(fn, reads, writes, eng="dve"):
            P.op(eng, fn, reads=reads, writes=writes)

        V(lambda e: e.activation(out=dtT, in_=dtT, func=AF.Exp), ["dtT"], ["dtT"], "act")
        V(lambda e: e.tensor_tensor(out=ar, in0=AreT, in1=dtT, op=ALU.mult), ["AreT", "dtT"], ["ar"])
        V(lambda e: e.tensor_tensor(out=ai, in0=AimT, in1=dtT, op=ALU.mult), ["AimT", "dtT"], ["ai"])
        V(lambda e: e.activation(out=rho, in_=ar, func=AF.Exp), ["ar"], ["rho"], "act")
        ti64 = carve([128, 64], I32)
        aiT = t64()

        def fracr(out, okey, inp, ikey):
            V(lambda e: e.tensor_copy(out=ti64, in_=inp), [ikey], ["ti64"])
            V(lambda e: e.tensor_copy(out=tmpb, in_=ti64), ["ti64"], ["tmpb"])
            V(lambda e: e.tensor_tensor(out=out, in0=inp, in1=tmpb, op=ALU.subtract), [ikey, "tmpb"], [okey])

        V(lambda e: e.tensor_scalar(out=aiT, in0=ai, scalar1=float(1.0 / TWO_PI), scalar2=None, op0=ALU.mult), ["ai"], ["aiT"])
        fracr(thr, "thr", aiT, "aiT")
        V(lambda e: e.tensor_scalar(out=tmpa, in0=aiT, scalar1=64.0, scalar2=None, op0=ALU.mult), ["aiT"], ["tmpa"])
        fracr(ph64, "ph64", tmpa, "tmpa")
        V(lambda e: e.tensor_scalar(out=tmpa, in0=aiT, scalar1=512.0, scalar2=None, op0=ALU.mult), ["aiT"], ["tmpa"])
        fracr(ph512, "ph512", tmpa, "tmpa")
        for blk in range(4):
            V(lambda e, blk=blk: e.tensor_scalar(out=tmpa, in0=ph512, scalar1=float(blk), scalar2=None, op0=ALU.mult), ["ph512"], ["tmpa"])
            fracr(PHB[:, :, blk], "PHB", tmpa, "tmpa")

        def sincos(ang, akey, s_out, skey, c_out, ckey, tmp, tkey):
            V(lambda e: e.activation(out=s_out, in_=ang, func=AF.Sin, scale=TWO_PI), [akey], [skey], "act")
            V(lambda e: e.activation(out=tmp, in_=ang, func=AF.Abs), [akey], [tkey], "act")
            V(lambda e: e.activation(out=c_out, in_=tmp, func=AF.Sin, scale=-TWO_PI, bias=pib[:]), [tkey, "pib"], [ckey], "act")

        sincos(thr, "thr", sinT, "sinT", cosT, "cosT", tmpa, "tmpa")
        V(lambda e: e.tensor_tensor(out=lre, in0=rho, in1=cosT, op=ALU.mult), ["rho", "cosT"], ["lre"])
        V(lambda e: e.tensor_tensor(out=lim, in0=rho, in1=sinT, op=ALU.mult), ["rho", "sinT"], ["lim"])
        V(lambda e: e.tensor_scalar(out=tmpa, in0=lre, scalar1=-1.0, scalar2=None, op0=ALU.add), ["lre"], ["tmpa"])
        V(lambda e: e.tensor_tensor(out=rden64, in0=AreT, in1=AreT, op=ALU.mult), ["AreT"], ["rden64"])
        V(lambda e: e.tensor_tensor(out=tmpb, in0=AimT, in1=AimT, op=ALU.mult), ["AimT"], ["tmpb"])
        V(lambda e: e.tensor_tensor(out=rden64, in0=rden64, in1=tmpb, op=ALU.add), ["rden64", "tmpb"], ["rden64"])
        V(lambda e: e.reciprocal(out=rden64, in_=rden64), ["rden64"], ["rden64"])
        V(lambda e: e.tensor_tensor(out=fre, in0=tmpa, in1=AreT, op=ALU.mult), ["tmpa", "AreT"], ["fre"])
        V(lambda e: e.tensor_tensor(out=tmpb, in0=lim, in1=AimT, op=ALU.mult), ["lim", "AimT"], ["tmpb"])
        V(lambda e: e.tensor_tensor(out=fre, in0=fre, in1=tmpb, op=ALU.add), ["fre", "tmpb"], ["fre"])
        V(lambda e: e.tensor_tensor(out=fre, in0=fre, in1=rden64, op=ALU.mult), ["fre", "rden64"], ["fre"])
        V(lambda e: e.tensor_tensor(out=fim, in0=lim, in1=AreT, op=ALU.mult), ["lim", "AreT"], ["fim"])
        V(lambda e: e.tensor_tensor(out=tmpb, in0=tmpa, in1=AimT, op=ALU.mult), ["tmpa", "AimT"], ["tmpb"])
        V(lambda e: e.tensor_tensor(out=fim, in0=fim, in1=tmpb, op=ALU.subtract), ["fim", "tmpb"], ["fim"])
        V(lambda e: e.tensor_tensor(out=fim, in0=fim, in1=rden64, op=ALU.mult), ["fim", "rden64"], ["fim"])
        h64 = slice(0, 64)
        frb = fre[h64, :].unsqueeze(2).to_broadcast([64, 64, 16])
        fib = fim[h64, :].unsqueeze(2).to_broadcast([64, 64, 16])
        V(lambda e: e.tensor_tensor(out=tB1[h64], in0=Bn_re[h64], in1=frb, op=ALU.mult), ["Bn_re", "fre"], ["tB1"])
        V(lambda e: e.tensor_tensor(out=tB2[h64], in0=Bn_im[h64], in1=fib, op=ALU.mult), ["Bn_im", "fim"], ["tB2"])
        V(lambda e: e.tensor_tensor(out=tB1[h64], in0=tB1[h64], in1=tB2[h64], op=ALU.subtract), ["tB1", "tB2"], ["tB1"])
        V(lambda e: e.tensor_tensor(out=tB2[h64], in0=Bn_im[h64], in1=frb, op=ALU.mult), ["Bn_im", "fre"], ["tB2"])
        V(lambda e: e.tensor_tensor(out=Bn_im[h64], in0=Bn_re[h64], in1=fib, op=ALU.mult), ["Bn_re", "fim", "tB2"], ["Bn_im"])
        V(lambda e: e.tensor_tensor(out=tB2[h64], in0=tB2[h64], in1=Bn_im[h64], op=ALU.add), ["tB2", "Bn_im"], ["tB2"])
        ptr = bank(3, 0)
        for F in range(KC):
            P.op("pe", lambda e, F=F: e.transpose(ptr[:, 0:64], tB1[h64, 8 * F:8 * F + 8, :].rearrange("p a b -> p (a b)"), ident[0:64, 0:64]),
                 reads=["tB1", "ident"], writes=[bkey(3, 0)])
            P.op("pe", lambda e, F=F: e.transpose(ptr[:, 64:128], tB2[h64, 8 * F:8 * F + 8, :].rearrange("p a b -> p (a b)"), ident[0:64, 0:64]),
                 reads=["tB2", "ident"], writes=[bkey(3, 0)])
            V(lambda e, F=F: e.copy(out=Mc[:, F, :], in_=ptr[:, 0:128]), [bkey(3, 0)], ["Mc"], "act")
            V(lambda e, F=F: e.copy(out=Mcsw[:, F, 0:64], in_=ptr[:, 64:128]), [bkey(3, 0)], ["Mcsw"], "act")
            V(lambda e, F=F: e.mul(out=Mcsw[:, F, 64:128], in_=ptr[:, 0:64], mul=-1.0), [bkey(3, 0)], ["Mcsw"], "act")
        for (src, skey, dst, dkey) in ((Cn, "Cn", CsA, "CsA"), (Cn2, "Cn2", CsB, "CsB")):
            for ib in range(2):
                for ii in range(8):
                    i_ = ib * 8 + ii
                    P.op("pe", lambda e, src=src, i_=i_, ii=ii: e.transpose(ptr[:, ii * 64:(ii + 1) * 64], src[h64, i_, :, :].rearrange("p c q -> p (c q)"), ident[0:64, 0:64]),
                         reads=[skey, "ident"], writes=[bkey(3, 0)])
                V(lambda e, dst=dst, ib=ib: e.copy(out=dst[:, :, ib * 8:(ib + 1) * 8].rearrange("p g i -> p i g"), in_=ptr.rearrange("p (i g) -> p i g", i=8)),
                  [bkey(3, 0)], [dkey], "act")
        V(lambda e: e.tensor_scalar(out=CsA[64:128], in0=CsA[64:128], scalar1=-1.0, scalar2=None, op0=ALU.mult), ["CsA"], ["CsA"])
        V(lambda e: e.tensor_scalar(out=CsB, in0=CsB, scalar1=-1.0, scalar2=None, op0=ALU.mult), ["CsB"], ["CsB"])
        V(lambda e: e.memset(CA, 0.0), [], ["CA"], "pool")
        V(lambda e: e.memset(CB, 0.0), [], ["CB"], "pool")
        for gl in range(8):
            for (src, skey, dst, dkey) in ((CsA, "CsA", CA, "CA"), (CsB, "CsB", CB, "CB")):
                V(lambda e, src=src, dst=dst, gl=gl: e.tensor_copy(out=dst.rearrange("p (f g) c -> p f g c", g=8)[:, :, gl, 16 * gl:16 * gl + 16],
                                                                   in_=src.rearrange("p (f g) i -> p f g i", g=8)[:, :, gl, :]),
                  [skey], [dkey])

        hin = carve([128, 64]) if False else None
        P.op("dve", lambda e: e.memset(Glast, 0.0), writes=["Glast"])
        if full:
            ang = tmpa
            V(lambda e: e.tensor_scalar(out=ang, in0=aiT, scalar1=2048.0, scalar2=None, op0=ALU.mult), ["aiT"], ["tmpa"])
            fracr(fim, "fim", ang, "tmpa")
            sincos(fim, "fim", sinT, "sinT", cosT, "cosT", fre, "fre")
            V(lambda e: e.activation(out=lre, in_=ar, func=AF.Exp, scale=2048.0), ["ar"], ["lre"], "act")
            V(lambda e: e.tensor_tensor(out=lim, in0=lre, in1=sinT, op=ALU.mult), ["lre", "sinT"], ["lim"])
            V(lambda e: e.tensor_tensor(out=lre, in0=lre, in1=cosT, op=ALU.mult), ["lre", "cosT"], ["lre"])
            wsel = carve([128, 8])
            fr = [carve([128, 64]) for _ in range(2)]
            P.dma("sp", lambda e: e.dma_start(out=wsel, in_=wsel_d), writes=["wsel"], semkey="ld_wsel")
            prot = bank(3, 1)[:, 0:64]
            for r in range(7):
                s_ = r % 2
                P.dma("sp", lambda e, r=r, s_=s_: e.dma_start(out=fr[s_], in_=fall_d[r]), writes=[f"fr{s_}"], semkey=f"fr{s_}")
                P.op("pe", lambda e: e.matmul(prot, lhsT=rott, rhs=Glast, start=True, stop=True), reads=["rott", "Glast"], writes=[bkey(3, 1)])
                V(lambda e: e.tensor_tensor(out=tmpa, in0=lre, in1=Glast, op=ALU.mult), ["lre", "Glast"], ["tmpa"])
                V(lambda e: e.tensor_tensor(out=tmpb, in0=lim, in1=prot, op=ALU.mult), ["lim", bkey(3, 1)], ["tmpb"])
                V(lambda e: e.tensor_tensor(out=tmpa, in0=tmpa, in1=tmpb, op=ALU.add), ["tmpa", "tmpb"], ["tmpa"])
                V(lambda e, s_=s_: e.tensor_tensor(out=tmpa, in0=tmpa, in1=fr[s_], op=ALU.add), ["tmpa", f"fr{s_}"], ["tmpa"])
                V(lambda e: e.tensor_tensor(out=tmpa, in0=tmpa, in1=Glast, op=ALU.subtract), ["tmpa", "Glast"], ["tmpa"])
                V(lambda e, r=r: e.scalar_tensor_tensor(out=Glast, in0=tmpa, scalar=wsel[:, r:r + 1], in1=Glast, op0=ALU.mult, op1=ALU.add),
                  ["tmpa", "wsel", "Glast"], ["Glast"])
        P.barrier()
        cur[0] = mark_s5
        NW = 2
        um = [carve([128, 512], BF16) for _ in range(NW)]
        xang = [carve([128, 512]) for _ in range(NW)]
        sang = [carve([128, 512]) for _ in range(NW)]
        SINt = [carve([128, 512]) for _ in range(NW)]
        COSt = [carve([128, 512]) for _ in range(NW)]
        Wt = [carve([128, 512]) for _ in range(NW)]
        Ab = [carve([128, 512], BF16) for _ in range(NW)]
        Bb = [carve([128, 512], BF16) for _ in range(NW)]
        aseq = carve([128, 128])
        ysb = carve([128, 512])
        y2 = carve([128, 512])
        H0T = carve([128, 64, 16])
        Asl = carve([128, 64, 16])
        Bsl = carve([128, 64, 16])
        h0n = carve([128, 64, 128]) if False else None
        print("arena words used (s5 main):", cur[0], "of", ARENA_W)
        BU = [(bank(0, 0), bkey(0, 0), bank(0, 1), bkey(0, 1)), (bank(1, 0), bkey(1, 0), bank(1, 1), bkey(1, 1))]
        YP = [(bank(2, 0), bkey(2, 0)), (bank(2, 1), bkey(2, 1))]
        wi = [0]

        def s5_T(F, gl, t0, nt, blk, sample):
            g = 8 * F + gl
            w = wi[0] % NW
            wi[0] += 1
            bu, bukey, bus, buskey = BU[w % 2]
            V(lambda e: e.activation(out=um[w][:, :nt], in_=hT_all[:, F, t0:t0 + nt], func=AF.Copy, scale=maskg[:, gl:gl + 1]),
              ["hT_all", "maskg"], [f"um{w}"], "act")
            P.op("pe", lambda e: e.matmul(bu[:, :nt], lhsT=Mc[:, F, :], rhs=um[w][:, :nt], start=True, stop=True), reads=["Mc", f"um{w}"], writes=[bukey])
            P.op("pe", lambda e: e.matmul(bus[:, :nt], lhsT=Mcsw[:, F, :], rhs=um[w][:, :nt], start=True, stop=True), reads=["Mcsw", f"um{w}"], writes=[buskey])
            MAGIC = 12582912.0
            if not sample:
                V(lambda e: e.activation(out=xang[w], in_=tal, func=AF.Identity, scale=ph64[:, g:g + 1], bias=PHB[:, g, blk:blk + 1]),
                  ["tal", "ph64", "PHB"], [f"xang{w}"], "act")
                V(lambda e: e.scalar_tensor_tensor(out=xang[w], in0=tbp, scalar=thr[:, g:g + 1], in1=xang[w], op0=ALU.mult, op1=ALU.add),
                  ["tbp", "thr", f"xang{w}"], [f"xang{w}"])
            else:
                V(lambda e: e.activation(out=xang[w][:, :nt], in_=tbs, func=AF.Copy, scale=thr[:, g:g + 1]), ["tbs", "thr"], [f"xang{w}"], "act")
            V(lambda e: e.tensor_scalar(out=sang[w][:, :nt], in0=xang[w][:, :nt], scalar1=MAGIC, scalar2=MAGIC, op0=ALU.add, op1=ALU.subtract),
              [f"xang{w}"], [f"sang{w}"])
            V(lambda e: e.tensor_tensor(out=xang[w][:, :nt], in0=xang[w][:, :nt], in1=sang[w][:, :nt], op=ALU.subtract), [f"xang{w}", f"sang{w}"], [f"xang{w}"])
            V(lambda e: e.activation(out=SINt[w][:, :nt], in_=xang[w][:, :nt], func=AF.Sin, scale=TWO_PI), [f"xang{w}"], [f"SINt{w}"], "act")
            V(lambda e: e.activation(out=sang[w][:, :nt], in_=xang[w][:, :nt], func=AF.Abs), [f"xang{w}"], [f"sang{w}"], "act")
            V(lambda e: e.activation(out=COSt[w][:, :nt], in_=sang[w][:, :nt], func=AF.Sin, scale=-TWO_PI, bias=pib[:]), [f"sang{w}", "pib"], [f"COSt{w}"], "act")
            return dict(F=F, gl=gl, g=g, w=w, t0=t0, nt=nt, blk=blk, sample=sample, bu=bu, bukey=bukey, bus=bus, buskey=buskey)

        def s5_S(c, ypk, last_blk):
            F, gl, g, w, t0, nt, blk, sample = c['F'], c['gl'], c['g'], c['w'], c['t0'], c['nt'], c['blk'], c['sample']
            bu, bukey, bus, buskey = c['bu'], c['bukey'], c['bus'], c['buskey']
            yp, ykey = ypk
            V(lambda e: e.tensor_tensor(out=sang[w][:, :nt], in0=bu[:, :nt], in1=COSt[w][:, :nt], op=ALU.mult), [bukey, f"COSt{w}"], [f"sang{w}"])
            V(lambda e: e.tensor_tensor(out=Wt[w][:, :nt], in0=bus[:, :nt], in1=SINt[w][:, :nt], op=ALU.mult), [buskey, f"SINt{w}"], [f"Wt{w}"])
            V(lambda e: e.tensor_tensor(out=Wt[w][:, :nt], in0=Wt[w][:, :nt], in1=sang[w][:, :nt], op=ALU.add), [f"Wt{w}", f"sang{w}"], [f"Wt{w}"])
            if not sample:
                V(lambda e: e.tensor_tensor_scan(out=Wt[w], data0=rho[:, g:g + 1].to_broadcast([128, 512]), data1=Wt[w], initial=Glast[:, g:g + 1],
                                                 op0=ALU.mult, op1=ALU.add), [f"Wt{w}", "rho", "Glast"], [f"Wt{w}"])
                V(lambda e: e.copy(out=Glast[:, g:g + 1], in_=Wt[w][:, 511:512]), [f"Wt{w}"], ["Glast"], "act")
                if last_blk:
                    V(lambda e: e.tensor_tensor(out=Alast[:, g:g + 1], in0=Wt[w][:, 511:512], in1=COSt[w][:, 511:512], op=ALU.mult), [f"Wt{w}", f"COSt{w}"], ["Alast"])
                    V(lambda e: e.tensor_tensor(out=Blast[:, g:g + 1], in0=Wt[w][:, 511:512], in1=SINt[w][:, 511:512], op=ALU.mult), [f"Wt{w}", f"SINt{w}"], ["Blast"])
            else:
                wv = Wt[w][:, :128].rearrange("p (b t) -> p b t", t=8)
                V(lambda e: e.scalar_tensor_tensor(out=wv[:, :, 0], in0=H0T[:, g, :], scalar=rho[:, g:g + 1], in1=wv[:, :, 0], op0=ALU.mult, op1=ALU.add),
                  ["H0T", "rho", f"Wt{w}"], [f"Wt{w}"])
                V(lambda e: e.tensor_scalar(out=aseq, in0=amask, scalar1=rho[:, g:g + 1], scalar2=None, op0=ALU.mult), ["amask", "rho"], ["aseq"])
                V(lambda e: e.tensor_tensor_scan(out=Wt[w][:, :128], data0=aseq, data1=Wt[w][:, :128], initial=0.0, op0=ALU.mult, op1=ALU.add),
                  [f"Wt{w}", "aseq"], [f"Wt{w}"])
                cv = COSt[w][:, :128].rearrange("p (b t) -> p b t", t=8)
                sv = SINt[w][:, :128].rearrange("p (b t) -> p b t", t=8)
                V(lambda e: e.tensor_tensor(out=Asl[:, g, :], in0=wv[:, :, 7], in1=cv[:, :, 7], op=ALU.mult), [f"Wt{w}", f"COSt{w}"], ["Asl"])
                V(lambda e: e.tensor_tensor(out=Bsl[:, g, :], in0=wv[:, :, 7], in1=sv[:, :, 7], op=ALU.mult), [f"Wt{w}", f"SINt{w}"], ["Bsl"])
            if full:
                V(lambda e: e.tensor_tensor(out=Ab[w][:, :nt], in0=Wt[w][:, :nt], in1=COSt[w][:, :nt], op=ALU.mult), [f"Wt{w}", f"COSt{w}"], [f"Ab{w}"])
                V(lambda e: e.tensor_tensor(out=Bb[w][:, :nt], in0=Wt[w][:, :nt], in1=SINt[w][:, :nt], op=ALU.mult), [f"Wt{w}", f"SINt{w}"], [f"Bb{w}"], "pool")
                P.op("pe", lambda e: e.matmul(yp[:, :nt], lhsT=CA[:, g, :], rhs=Ab[w][:, :nt], start=(gl == 0), stop=False), reads=["CA", f"Ab{w}"], writes=[ykey])
                P.op("pe", lambda e: e.matmul(yp[:, :nt], lhsT=CB[:, g, :], rhs=Bb[w][:, :nt], start=False, stop=(gl == 7)), reads=["CB", f"Bb{w}"], writes=[ykey])

        def y_finish(F, t0, nt, ypk):
            yp, ykey = ypk
            V(lambda e: e.scalar_tensor_tensor(out=ysb[:, :nt], in0=hT_all[:, F, t0:t0 + nt], scalar=Dsk[:, F:F + 1], in1=yp[:, :nt], op0=ALU.mult, op1=ALU.add),
              ["hT_all", "Dsk", ykey], ["ysb"])
            V(lambda e: e.tensor_tensor(out=y2[:, :nt], in0=ysb[:, :nt], in1=ysb[:, :nt], op=ALU.mult), ["ysb"], ["y2"], "pool")
            V(lambda e: e.tensor_scalar(out=y2[:, :nt], in0=y2[:, :nt], scalar1=0.044715, scalar2=1.0, op0=ALU.mult, op1=ALU.add), ["y2"], ["y2"], "pool")
            V(lambda e: e.tensor_tensor(out=y2[:, :nt], in0=y2[:, :nt], in1=ysb[:, :nt], op=ALU.mult), ["y2", "ysb"], ["y2"], "pool")
            V(lambda e: e.activation(out=y2[:, :nt], in_=y2[:, :nt], func=AF.Tanh, scale=float(np.sqrt(2.0 / np.pi))), ["y2"], ["y2"], "act")
            V(lambda e: e.tensor_scalar(out=y2[:, :nt], in0=y2[:, :nt], scalar1=0.5, scalar2=0.5, op0=ALU.mult, op1=ALU.add), ["y2"], ["y2"])
            V(lambda e: e.tensor_tensor(out=hT_all[:, F, t0:t0 + nt], in0=y2[:, :nt], in1=ysb[:, :nt], op=ALU.mult), ["y2", "ysb"], ["hT_all"])

        if True:
            h0n = um
        s0n = carve([128, 64, 128]) if False else None
        hnat = carve([16, 64 * 128]) if False else None
        stg = xang[0]
        pth = bank(3, 0)
        for gq_ in range(16 if full else 0):
            P.dma("sp", lambda e, gq_=gq_: e.dma_start(out=stg[0:16, :].rearrange("p (g c) -> p g c", g=4)[:, :, 0:64], in_=s5r_in[:, 4 * gq_:4 * gq_ + 4, :]),
                  writes=["xang0"], semkey="stg", group=f"stg{gq_}")
            P.dma("sp", lambda e, gq_=gq_: e.dma_start(out=stg[0:16, :].rearrange("p (g c) -> p g c", g=4)[:, :, 64:128], in_=s5i_in[:, 4 * gq_:4 * gq_ + 4, :]),
                  writes=["xang0"], semkey="stg", group=f"stg{gq_}")
            for gg in range(4):
                P.op("pe", lambda e, gg=gg: e.transpose(pth[:, gg * 16:(gg + 1) * 16], stg[0:16, gg * 128:(gg + 1) * 128], ident[0:16, 0:16]),
                     reads=["xang0", "ident"], writes=[bkey(3, 0)])
            V(lambda e, gq_=gq_: e.copy(out=H0T[:, 4 * gq_:4 * gq_ + 4, :], in_=pth[:, 0:64].rearrange("p (g b) -> p g b", g=4)), [bkey(3, 0)], ["H0T"], "act")

        units = []
        yi = 0
        for F in range(KC):
            for blk in range(4):
                ypk = YP[yi % 2]
                yi += 1
                for gl in range(8):
                    units.append(dict(F=F, gl=gl, t0=blk * 512, nt=512, blk=blk, sample=False, ypk=ypk, last=(blk == 3), fin=(gl == 7)))
            if full:
                ypk = YP[yi % 2]
                yi += 1
                for gl in range(8):
                    units.append(dict(F=F, gl=gl, t0=TOK_P, nt=128, blk=0, sample=True, ypk=ypk, last=False, fin=(gl == 7)))
        ctx = s5_T(units[0]["F"], units[0]["gl"], units[0]["t0"], units[0]["nt"], units[0]["blk"], units[0]["sample"])
        for n, u in enumerate(units):
            nxt = None
            if n + 1 < len(units):
                v = units[n + 1]
                nxt = s5_T(v["F"], v["gl"], v["t0"], v["nt"], v["blk"], v["sample"])
            s5_S(ctx, u["ypk"], u["last"])
            if full and u["fin"]:
                y_finish(u["F"], u["t0"], u["nt"], u["ypk"])
            ctx = nxt

        pfin = bank(3, 1)[:, 0:64]
        P.op("pe", lambda e: e.matmul(pfin, lhsT=rott, rhs=Blast, start=True, stop=True), reads=["rott", "Blast"], writes=[bkey(3, 1)])
        V(lambda e: e.tensor_tensor(out=Alast, in0=Alast, in1=pfin, op=ALU.add), ["Alast", bkey(3, 1)], ["Alast"])
        if not full:
            P.dma("sp", lambda e: e.dma_start(out=floc_out, in_=Alast), reads=["Alast"], semkey="floc", final=True)
            return None
        ptp = bank(3, 0)[0:64, 0:128]
        P.op("pe", lambda e: e.transpose(ptp, Alast, ident[:]), reads=["Alast", "ident"], writes=[bkey(3, 0)])
        V(lambda e: e.copy(out=ysb[0:64, 0:128], in_=ptp), [bkey(3, 0)], ["ysb"], "act")
        P.dma("sp", lambda e: e.dma_start(out=ps5r_out, in_=ysb[0:64, 0:64]), reads=["ysb"], semkey="ps5", final=True)
        P.dma("sp", lambda e: e.dma_start(out=ps5i_out, in_=ysb[0:64, 64:128]), reads=["ysb"], semkey="ps5", final=True)
        for hf in range(2):
            pf2 = bank(3, 1)
            P.op("pe", lambda e, hf=hf: e.matmul(pf2, lhsT=rott, rhs=Bsl[:, 32 * hf:32 * hf + 32, :].rearrange("p g b -> p (g b)"), start=True, stop=True),
                 reads=["rott", "Bsl"], writes=[bkey(3, 1)])
            V(lambda e, hf=hf: e.tensor_tensor(out=Asl[:, 32 * hf:32 * hf + 32, :].rearrange("p g b -> p (g b)"), in0=Asl[:, 32 * hf:32 * hf + 32, :].rearrange("p g b -> p (g b)"),
                                               in1=pf2, op=ALU.add), ["Asl", bkey(3, 1)], ["Asl"])
        for gq_ in range(16):
            pto = bank(3, 0)[0:16, :]
            for gg in range(4):
                P.op("pe", lambda e, gq_=gq_, gg=gg: e.transpose(pto[:, gg * 128:(gg + 1) * 128], Asl[:, 4 * gq_ + gg, :], ident[:]),
                     reads=["Asl", "ident"], writes=[bkey(3, 0)])
            V(lambda e: e.copy(out=stg[0:16, :], in_=pto), [bkey(3, 0)], ["xang0"], "act")
            P.dma("sp", lambda e, gq_=gq_: e.dma_start(out=ss5r_out[:, 4 * gq_:4 * gq_ + 4, :], in_=stg[0:16, :].rearrange("p (g c) -> p g c", g=4)[:, :, 0:64]),
                  reads=["xang0"], semkey="ss5o", final=True)
            P.dma("sp", lambda e, gq_=gq_: e.dma_start(out=ss5i_out[:, 4 * gq_:4 * gq_ + 4, :], in_=stg[0:16, :].rearrange("p (g c) -> p g c", g=4)[:, :, 64:128]),
                  reads=["xang0"], semkey="ss5o", final=True)

        P.barrier()
        cur[0] = mark_ln + NTOK * KC // 2
        glua = carve([128, KC, D], BF16)
        glub = carve([128, KC, D], BF16)
        sgb = [carve([128, 512]) for _ in range(2)]
        prd = [carve([128, 512]) for _ in range(2)]
        gt3 = carve([128, 128])
        P.dma("pool", lambda e: e.dma_start(out=glua, in_=glua_d.rearrange("(k p) n -> p k n", p=128)), writes=["glua"], semkey="glua")
        P.dma("pool", lambda e: e.dma_start(out=glub, in_=glub_d.rearrange("(k p) n -> p k n", p=128)), writes=["glub"], semkey="glub")
        GA = [(bank(0, 0), bkey(0, 0), bank(0, 1), bkey(0, 1)), (bank(1, 0), bkey(1, 0), bank(1, 1), bkey(1, 1))]
        gi = 0
        for (t0, nt, smp) in [(b * 512, 512, False) for b in range(4)] + [(TOK_P, 128, True)]:
            for m in range(KC):
                pa, pak, pb, pbk = GA[gi % 2]
                w_ = gi % 2
                gi += 1
                for k in range(KC):
                    P.op("pe", lambda e, k=k, m=m, pa=pa, t0=t0, nt=nt: e.matmul(pa[:, :nt], lhsT=glua[:, k, m * 128:(m + 1) * 128], rhs=hT_all[:, k, t0:t0 + nt],
                                                                                  start=(k == 0), stop=(k == KC - 1)), reads=["glua", "hT_all"], writes=[pak])
                for k in range(KC):
                    P.op("pe", lambda e, k=k, m=m, pb=pb, t0=t0, nt=nt: e.matmul(pb[:, :nt], lhsT=glub[:, k, m * 128:(m + 1) * 128], rhs=hT_all[:, k, t0:t0 + nt],
                                                                                  start=(k == 0), stop=(k == KC - 1)), reads=["glub", "hT_all"], writes=[pbk])
                V(lambda e, pb=pb, w_=w_, nt=nt: e.activation(out=sgb[w_][:, :nt], in_=pb[:, :nt], func=AF.Sigmoid), [pbk], [f"sgb{w_}"], "act")
                V(lambda e, pa=pa, w_=w_, nt=nt: e.tensor_tensor(out=prd[w_][:, :nt], in0=pa[:, :nt], in1=sgb[w_][:, :nt], op=ALU.mult), [pak, f"sgb{w_}"], [f"prd{w_}"])
                xs = xT[:, m, t0:t0 + nt]
                xkey = ("xT", t0 // 512)
                if not smp:
                    V(lambda e, m=m, w_=w_, xs=xs, nt=nt: e.scalar_tensor_tensor(out=xs, in0=prd[w_][:, :nt], scalar=modT[:, 16 + m, 0:1], in1=xs, op0=ALU.mult, op1=ALU.add),
                      [f"prd{w_}", "modT", xkey], [xkey])
                else:
                    gv = gt3.rearrange("p (b t) -> p b t", t=8)
                    V(lambda e, m=m, w_=w_, gv=gv: e.tensor_tensor(out=gv, in0=prd[w_][:, :128].rearrange("p (b t) -> p b t", t=8),
                                                                  in1=modT[:, 16 + m, 1:17].unsqueeze(2).to_broadcast([128, 16, 8]), op=ALU.mult), [f"prd{w_}", "modT"], ["gt3"])
                    V(lambda e, xs=xs: e.tensor_tensor(out=xs, in0=xs, in1=gt3, op=ALU.add), ["gt3", xkey], [xkey])
        ffn(1)
        P.barrier()
        cur[0] = mark_ln
        yst = [carve([128, D]) for _ in range(2)]
        for t in range(17):
            s = t % 2
            for k in range(KC):
                P.op("pe", lambda e, k=k, t=t, s=s: e.transpose(pT[s][:, k, :], xT[:, k, t * 128:(t + 1) * 128], ident[:]),
                     reads=[("xT", t // 4), "ident"], writes=[bkey(s, 0), bkey(s, 1)])
            P.op("act", lambda e, s=s: e.copy(out=yst[s], in_=Q[s][:, :]), reads=[bkey(s, 0), bkey(s, 1)], writes=[f"yst{s}"])
            P.dma("sp", lambda e, t=t, s=s: e.dma_start(out=y_out[t * 128:(t + 1) * 128, :], in_=yst[s]), reads=[f"yst{s}"], semkey=f"yst{s}", final=True)
        stats = P.build()
        nc_allow.__exit__(None, None, None)
        es.close()
        return nc, stats

    if stage in ("C1", "C2"):
        r_ = s5_stage(stage == "C2")
        if r_ is not None:
            return r_
        stats = P.build()
        nc_allow.__exit__(None, None, None)
        es.close()
        return nc, stats

    hTh = carve([128, KC, 128], BF16)
    mark1 = cur[0]
    lst = [carve([128, 4, 128]) for _ in range(2)]
    wret = carve([128, 8, 4])
    xTh = carve([128, KC, 128])
    xst[0] = carve([128, D])
    load_tile(0, xTh, "xTh")
    P.dma("sp", lambda e: e.dma_start(out=wret, in_=wret_d), writes=["wret"], semkey="wret")
    P.op("dve", lambda e: e.memset(S, 0.0), writes=["S"])
    for r in range(8):
        s = r % 2
        P.dma("sp", lambda e, r=r, s=s: e.dma_start(out=lst[s], in_=lall_d[r].rearrange("h d e -> d h e")), writes=[f"lst{s}"], semkey=f"lst{s}")
        P.op("dve", lambda e, r=r, s=s: e.tensor_tensor(out=lst[s], in0=lst[s], in1=wret[:, r, :].unsqueeze(2).to_broadcast([128, 4, 128]), op=ALU.mult),
             reads=[f"lst{s}", "wret"], writes=[f"lst{s}"])
        P.op("dve", lambda e, s=s: e.tensor_tensor(out=S, in0=S, in1=lst[s], op=ALU.add), reads=["S", f"lst{s}"], writes=["S"])
    P.op("act", lambda e: e.copy(out=Sb, in_=S), reads=["S"], writes=["Sb"])
    ln_block(xTh, "xTh", 128, False, 0)
    P.op("pool", lambda e: e.tensor_copy(out=hTh, in_=hT[:, :, 0:128]), reads=["hT"], writes=["hTh"])
    P.barrier()
    cur[0] = mark1

    wkd = carve([128, KC, 2, 128], BF16)
    wvd = carve([128, KC, 2, 128], BF16)
    for kv in range(2):
        for half in range(2):
            P.op("pool", lambda e, kv=kv, half=half: e.tensor_copy(out=wkd[:, :, kv, half * 64:(half + 1) * 64], in_=win[:, :, 512 + 64 * kv:576 + 64 * kv]),
                 reads=["win"], writes=["wkd"])
            P.op("pool", lambda e, kv=kv, half=half: e.tensor_copy(out=wvd[:, :, kv, half * 64:(half + 1) * 64], in_=win[:, :, 640 + 64 * kv:704 + 64 * kv]),
                 reads=["win"], writes=["wvd"])
    wout = carve([128, KC, D], BF16)
    P.dma("pool", lambda e: e.dma_start(out=wout, in_=w_out.rearrange("(k p) n -> p k n", p=128)), writes=["wout"], semkey="wout")
    gq = carve([128, 1])
    gk = carve([128, 1])
    bd_f = carve([128, 128])
    bd = carve([128, 128], BF16)
    dneg = carve([128, 5, 128])
    rmask = carve([128, 4, 128])
    dq = carve([128, 4, 128])
    seqmask = carve([128, 16])
    esink = carve([128, 8])
    retg = carve([128, 4])
    for half in range(2):
        P.dma("sp", lambda e, half=half: e.dma_start(out=gq[half * 64:(half + 1) * 64, :], in_=qgain.rearrange("o d -> d o")), writes=["gq"], semkey="ld_gq", group="ld_gq")
        P.dma("sp", lambda e, half=half: e.dma_start(out=gk[half * 64:(half + 1) * 64, :], in_=kgain.rearrange("o d -> d o")), writes=["gk"], semkey="ld_gk", group="ld_gk")
    P.dma("sp", lambda e: e.dma_start(out=bd_f, in_=bd_d), writes=["bd_f"], semkey="ld_bd_f", group="ld_bd_f")
    P.dma("sp", lambda e: e.dma_start(out=dneg, in_=dneg_d), writes=["dneg"], semkey="ld_dneg", group="ld_dneg")
    P.dma("sp", lambda e: e.dma_start(out=rmask, in_=rmask_d[:, 0]), writes=["rmask"], semkey="ld_rmask", group="ld_rmask")
    P.dma("sp", lambda e: e.dma_start(out=dq, in_=dq_d[:, 0]), writes=["dq"], semkey="ld_dq", group="ld_dq")
    P.dma("sp", lambda e: e.dma_start(out=seqmask, in_=seqmask_d), writes=["seqmask"], semkey="ld_seqmask", group="ld_seqmask")
    P.dma("sp", lambda e: e.dma_start(out=esink, in_=sinks_d.partition_broadcast(128)), writes=["esink"], semkey="ld_esink", group="ld_esink")
    P.dma("sp", lambda e: e.dma_start(out=retg, in_=retg_d.rearrange("o (h e) -> e (o h)", h=4)), writes=["retg"], semkey="ld_retg", group="ld_retg")
    P.op("dve", lambda e: e.tensor_copy(out=bd, in_=bd_f), reads=["bd_f"], writes=["bd"])
    P.op("act", lambda e: e.activation(out=esink, in_=esink, func=AF.Exp), reads=["esink"], writes=["esink"])
    P.op("dve", lambda e: e.tensor_scalar(out=gq, in0=gq, scalar1=0.125, scalar2=None, op0=ALU.mult), reads=["gq"], writes=["gq"])

    qnT = carve([128, 4, BT], BF16)
    kdT = carve([128, 2, 128 + BT], BF16)
    qbT = carve([128, 4, BT], BF16)
    qdT = carve([128, 4, BT], BF16)
    kbT = carve([128, 4, BT], BF16)
    sgT = carve([128, 4, BT], BF16)
    vd_tok = carve([128, 1 + TPB, 256], BF16)
    oT = carve([128, KC, BT], BF16)
    qsq = carve([128, max(BT, 256)], BF16)
    qrs = carve([128, max(BT, 256)])
    sc_sb = carve([128, 2, 2, 2, 128])
    pTt = carve([128, 2, 2, 2, 128], BF16)
    rden = carve([128, 2, 2, 128])
    innT = carve([128, 4, 128], BF16)
    o32 = carve([128, 4, 128])
    obf = carve([128, 4, 128], BF16)
    osq = carve([128, 4, 128], BF16)
    t1 = carve([128, 4, 128])
    t2 = o32
    kv32 = carve([128, 2, 128])
    knT = carve([128, 128])
    gtmp = carve([128, 128])
    cacheT = carve([128, 16, 128], BF16)
    vcache = carve([128, 16, 128], BF16)
    kst = carve([128, 4, 128])
    S0 = [carve([128, 4, 128]) for _ in range(2)]
    S0b = [carve([128, 4, 128], BF16) for _ in range(2)]
    kdm = [carve([128, 512], BF16) for _ in range(2)]
    mark2 = cur[0]
    print("arena words used (mixer):", cur[0], "of", ARENA_W)

    PJ = [bank(0, 0), bank(0, 1)]
    PJK = [bkey(0, 0), bkey(0, 1)]
    p_st = bank(1, 0)
    pj_i = [0]

    def proj_fm(lhs_fn, ntok, evac, src=None):
        s = pj_i[0] % 2
        pj_i[0] += 1
        src = hT if src is None else src
        for k in range(KC):
            P.op("pe", lambda e, k=k, s=s: e.matmul(PJ[s][:, :ntok], lhsT=lhs_fn(k), rhs=src[:, k, :ntok], start=(k == 0), stop=(k == KC - 1)),
                 reads=["hT", "hTh", "win", "wkd"], writes=[PJK[s]])
        evac(PJ[s][:, :ntok], PJK[s])

    def qknorm(psum, pkey, ntok, gain, gkey, out_ap, out_key):
        P.op("act", lambda e: e.activation(out=qsq[:, :ntok], in_=psum, func=AF.Square), reads=[pkey], writes=["qsq"])
        P.op("pe", lambda e: e.matmul(p_st[:, :ntok], lhsT=bd, rhs=qsq[:, :ntok], start=True, stop=True), reads=["bd", "qsq"], writes=[bkey(1, 0)])
        P.op("act", lambda e: e.activation(out=qrs[:, :ntok], in_=p_st[:, :ntok], func=AF.Sqrt, bias=epsb[:], scale=1.0), reads=[bkey(1, 0), "epsb"], writes=["qrs"])
        P.op("dve", lambda e: e.reciprocal(out=qrs[:, :ntok], in_=qrs[:, :ntok]), reads=["qrs"], writes=["qrs"])
        P.op("dve", lambda e: e.scalar_tensor_tensor(out=out_ap, in0=psum, scalar=gain[:, 0:1], in1=qrs[:, :ntok], op0=ALU.mult, op1=ALU.mult),
             reads=[pkey, "qrs", gkey], writes=[out_key])

    def project_block(ntok, grp):
        for j in range(4):
            proj_fm(lambda k, j=j: win[:, k, j * 128:(j + 1) * 128], ntok,
                    lambda ps_, key, j=j: qknorm(ps_, key, ntok, gq, "gq", qnT[:, j, :ntok], "qnT"))
        if KSUB < 2:
            return
        for kv in range(2):
            proj_fm(lambda k, kv=kv: wkd[:, k, kv, :], ntok,
                    lambda ps_, key, kv=kv: qknorm(ps_, key, ntok, gk, "gk", kdT[:, kv, 128:128 + ntok], "kdT"))
        if KSUB < 3:
            return
        for h in range(4):
            def ev_q(ps_, key, h=h):
                P.op("act", lambda e: e.copy(out=qbT[:, h, :ntok], in_=ps_), reads=[key], writes=["qbT"])
                P.op("dve", lambda e: e.tensor_tensor(out=qdT[:, h, :ntok].rearrange("p (t i) -> p t i", i=128), in0=ps_.rearrange("p (t i) -> p t i", i=128),
                                                      in1=dq[:, h, :].unsqueeze(1).to_broadcast([128, ntok // 128, 128]), op=ALU.mult),
                     reads=[key, "dq"], writes=["qdT"])
            proj_fm(lambda k, h=h: win[:, k, 768 + h * 128:768 + (h + 1) * 128], ntok, ev_q)
        if KSUB < 4:
            return
        for h in range(4):
            proj_fm(lambda k, h=h: win[:, k, 1280 + h * 128:1280 + (h + 1) * 128], ntok,
                    lambda ps_, key, h=h: P.op("act", lambda e: e.mul(out=kbT[:, h, :ntok], in_=ps_, mul=128.0 ** -0.5), reads=[key], writes=["kbT"]))
        if KSUB < 5:
            return
        for h in range(4):
            proj_fm(lambda k, h=h: win[:, k, 2304 + h * 128:2304 + (h + 1) * 128], ntok,
                    lambda ps_, key, h=h: P.op("act", lambda e: e.activation(out=sgT[:, h, :ntok], in_=ps_, func=AF.Silu), reads=[key], writes=["sgT"]))

    p_vd = bank(1, 1)[:, 0:256]

    def tok_vd(tc0, slot, src=None):
        src = hT if src is None else src
        for k in range(KC):
            P.op("pe", lambda e, k=k: e.matmul(p_vd, lhsT=src[:, k, tc0:tc0 + 128], rhs=wvd[:, k, :, :].rearrange("p a b -> p (a b)"), start=(k == 0), stop=(k == KC - 1)),
                 reads=["hT", "hTh", "wvd"], writes=[bkey(1, 1)])
        P.op("act", lambda e: e.copy(out=vd_tok[:, slot, :], in_=p_vd), reads=[bkey(1, 1)], writes=["vd_tok"])

    SLOPE = [2.0 ** (-(h + 1)) for h in range(8)]
    p_sc = Q[3][:, :].rearrange("p (a c t q) -> p a c t q", a=2, c=2, t=2)
    p_num = bank(2, 0).rearrange("p (a c q) -> p a c q", a=2, c=2)
    p_den = bank(2, 1).rearrange("p (a c q) -> p a c q", a=2, c=2)
    SCK = [bkey(3, 0), bkey(3, 1)]

    def attn_softmax(kv, dn_own, dn_prev):
        for half in range(2):
            for c in range(2):
                h0 = 4 * kv + 2 * c + half
                P.op("dve", lambda e, half=half, c=c, h0=h0: e.scalar_tensor_tensor(out=sc_sb[:, half, c, 0, :], in0=dneg[:, dn_prev, :], scalar=SLOPE[h0],
                                                                                  in1=p_sc[:, half, c, 0, :], op0=ALU.mult, op1=ALU.add),
                     reads=["dneg"] + SCK, writes=["sc_sb"])
                P.op("dve", lambda e, half=half, c=c, h0=h0: e.scalar_tensor_tensor(out=sc_sb[:, half, c, 1, :], in0=dneg[:, dn_own, :], scalar=SLOPE[h0],
                                                                                  in1=p_sc[:, half, c, 1, :], op0=ALU.mult, op1=ALU.add),
                     reads=["dneg"] + SCK, writes=["sc_sb"])
        P.op("act", lambda e: e.activation(out=pTt, in_=sc_sb, func=AF.Exp), reads=["sc_sb"], writes=["pTt"])

    def attn_finish(kv, c0):
        for part in range(2):
            P.op("pe", lambda e, part=part: e.matmul(p_den, lhsT=ones_b[:], rhs=pTt[:, :, :, part, :], start=(part == 0), stop=(part == 1)),
                 reads=["ones_b", "pTt"], writes=[bkey(2, 1)])
        es_v = esink[:, 4 * kv:4 * kv + 4].rearrange("p (c a) -> p a c", a=2)
        P.op("dve", lambda e, es_v=es_v: e.tensor_tensor(out=rden, in0=p_den, in1=es_v.unsqueeze(3).to_broadcast([128, 2, 2, 128]), op=ALU.add),
             reads=[bkey(2, 1), "esink"], writes=["rden"])
        P.op("dve", lambda e: e.reciprocal(out=rden, in_=rden), reads=["rden"], writes=["rden"])
        for half in range(2):
            sl = slice(half * 64, half * 64 + 64)
            P.op("dve", lambda e, half=half, sl=sl, kv=kv: e.tensor_tensor(out=oT[sl, 2 * kv:2 * kv + 2, c0:c0 + 128], in0=p_num[sl, half, :, :],
                                                                         in1=rden[sl, half, :, :], op=ALU.mult),
                 reads=[bkey(2, 0), "rden"], writes=["oT"])

    def attention_tile(i, dn_own, dn_prev):
        c0 = i * 128
        for kv in range(2):
            for half in range(2):
                sl = slice(half * 64, half * 64 + 64)
                P.op("pe", lambda e, kv=kv, half=half, sl=sl: e.matmul(p_sc[:, half, :, 1, :], lhsT=kdT[sl, kv, 128 + c0:256 + c0],
                                                                       rhs=qnT[sl, 2 * kv:2 * kv + 2, c0:c0 + 128], start=True, stop=True),
                     reads=["kdT", "qnT"], writes=SCK)
                P.op("pe", lambda e, kv=kv, half=half, sl=sl: e.matmul(p_sc[:, half, :, 0, :], lhsT=kdT[sl, kv, c0:128 + c0],
                                                                       rhs=qnT[sl, 2 * kv:2 * kv + 2, c0:c0 + 128], start=True, stop=True),
                     reads=["kdT", "qnT"], writes=SCK)
            attn_softmax(kv, dn_own, dn_prev)
            parts = [(0, vd_tok[:, i, kv * 128:(kv + 1) * 128]), (1, vd_tok[:, i + 1, kv * 128:(kv + 1) * 128])]
            for n_, (part, lh) in enumerate(parts):
                P.op("pe", lambda e, part=part, lh=lh, n_=n_: e.matmul(p_num, lhsT=lh, rhs=pTt[:, :, :, part, :], start=(n_ == 0), stop=(n_ == 1)),
                     reads=["vd_tok", "pTt"], writes=[bkey(2, 0)])
            attn_finish(kv, c0)

    def attention_sample():
        p_ct = bank(1, 1)[:, 0:128]
        for kv in range(2):
            for half in range(2):
                P.dma("pool", lambda e, kv=kv, half=half: e.dma_start(out=vcache[:, :, half * 64:(half + 1) * 64],
                                                                      in_=cache_v[:, :, kv * 64:(kv + 1) * 64].rearrange("b w d -> w b d")),
                      writes=["vcache"], semkey="vcache", group=f"vc{kv}")
            for g4 in range(4):
                for half in range(2):
                    P.dma("sp", lambda e, kv=kv, half=half, g4=g4: e.dma_start(out=kst[:, :, half * 64:(half + 1) * 64],
                                                                              in_=cache_k[4 * g4:4 * g4 + 4, :, kv * 64:(kv + 1) * 64].rearrange("b w d -> w b d")),
                          writes=["kst"], semkey="kst", group=f"kst{kv}_{g4}")
                for bb in range(4):
                    b = 4 * g4 + bb
                    P.op("pe", lambda e, bb=bb: e.transpose(p_ct, kst[:, bb, :], ident[:]), reads=["kst", "ident"], writes=[bkey(1, 1)])
                    P.op("act", lambda e, b=b: e.copy(out=cacheT[:, b, :], in_=p_ct), reads=[bkey(1, 1)], writes=["cacheT"])
            for half in range(2):
                sl = slice(half * 64, half * 64 + 64)
                P.op("pe", lambda e, kv=kv, half=half, sl=sl: e.matmul(p_sc[:, half, :, 1, :], lhsT=kdT[sl, kv, 128:256],
                                                                       rhs=qnT[sl, 2 * kv:2 * kv + 2, 0:128], start=True, stop=True),
                     reads=["kdT", "qnT"], writes=SCK)
                for b in range(16):
                    P.op("pe", lambda e, kv=kv, half=half, sl=sl, b=b: e.matmul(p_sc[:, half, :, 0, 8 * b:8 * b + 8], lhsT=cacheT[sl, b, :],
                                                                                 rhs=qnT[sl, 2 * kv:2 * kv + 2, 8 * b:8 * b + 8], start=True, stop=True),
                         reads=["cacheT", "qnT"], writes=SCK)
            attn_softmax(kv, 3, 4)
            P.op("pe", lambda e, kv=kv: e.matmul(p_num, lhsT=vd_tok[:, 1, kv * 128:(kv + 1) * 128], rhs=pTt[:, :, :, 1, :], start=True, stop=False),
                 reads=["vd_tok", "pTt"], writes=[bkey(2, 0)])
            for b in range(16):
                P.op("pe", lambda e, b=b: e.matmul(p_num[:, :, :, 8 * b:8 * b + 8], lhsT=vcache[:, b, :], rhs=pTt[:, :, :, 0, 8 * b:8 * b + 8],
                                                   start=False, stop=(b == 15)),
                     reads=["vcache", "pTt"], writes=[bkey(2, 0)])
            attn_finish(kv, 0)

    p_in = bank(1, 1).rearrange("p (h i) -> p h i", h=4)
    p_o = bank(0, 0).rearrange("p (h i) -> p h i", h=4)
    p_mu = bank(0, 1).rearrange("p (h i) -> p h i", h=4)
    p_e2 = bank(1, 0).rearrange("p (h i) -> p h i", h=4)

    def ret_norm(c0):
        P.op("dve", lambda e: e.tensor_copy(out=obf, in_=o32), reads=["o32"], writes=["obf"])
        P.op("act", lambda e: e.activation(out=osq, in_=o32, func=AF.Square), reads=["o32"], writes=["osq"])
        for h in range(4):
            P.op("pe", lambda e, h=h: e.matmul(p_mu[:, h, :], lhsT=ones_g[:], rhs=obf[:, h, :], start=True, stop=True), reads=["ones_g", "obf"], writes=[bkey(0, 1)])
        for h in range(4):
            P.op("pe", lambda e, h=h: e.matmul(p_e2[:, h, :], lhsT=ones_g[:], rhs=osq[:, h, :], start=True, stop=True), reads=["ones_g", "osq"], writes=[bkey(1, 0)])
        P.op("act", lambda e: e.activation(out=t1, in_=p_mu, func=AF.Square), reads=[bkey(0, 1)], writes=["t1"])
        P.op("dve", lambda e: e.tensor_tensor(out=t1, in0=p_e2, in1=t1, op=ALU.subtract), reads=[bkey(1, 0), "t1"], writes=["t1"])
        P.op("dve", lambda e: e.tensor_scalar(out=t1, in0=t1, scalar1=0.0, scalar2=None, op0=ALU.max), reads=["t1"], writes=["t1"])
        P.op("act", lambda e: e.activation(out=t1, in_=t1, func=AF.Sqrt, bias=epsb[:], scale=1.0), reads=["t1", "epsb"], writes=["t1"])
        P.op("dve", lambda e: e.reciprocal(out=t1, in_=t1), reads=["t1"], writes=["t1"])
        P.op("dve", lambda e: e.tensor_tensor(out=o32, in0=o32, in1=p_mu, op=ALU.subtract), reads=["o32", bkey(0, 1)], writes=["o32"])
        P.op("dve", lambda e: e.tensor_tensor(out=o32, in0=o32, in1=t1, op=ALU.mult), reads=["o32", "t1"], writes=["o32"])
        P.op("dve", lambda e: e.tensor_tensor(out=o32, in0=o32, in1=retg.unsqueeze(2).to_broadcast([128, 4, 128]), op=ALU.mult), reads=["o32", "retg"], writes=["o32"])
        P.op("dve", lambda e: e.tensor_tensor(out=oT[:, 4:8, c0:c0 + 128], in0=o32, in1=sgT[:, :, c0:c0 + 128], op=ALU.mult), reads=["o32", "sgT"], writes=["oT"])

    def ret_inner(c0):
        for h in range(4):
            P.op("pe", lambda e, h=h: e.matmul(p_in[:, h, :], lhsT=kbT[:, h, c0:c0 + 128], rhs=qbT[:, h, c0:c0 + 128], start=True, stop=True),
                 reads=["kbT", "qbT"], writes=[bkey(1, 1)])
        P.op("dve", lambda e: e.tensor_tensor(out=innT, in0=p_in, in1=rmask, op=ALU.mult), reads=[bkey(1, 1), "rmask"], writes=["innT"])

    def retention_tile(i, grp):
        c0 = i * 128
        ret_inner(c0)
        for h in range(4):
            P.op("pe", lambda e, h=h: e.matmul(p_o[:, h, :], lhsT=vb_tok[:, h * 128:(h + 1) * 128], rhs=innT[:, h, :], start=True, stop=False),
                 reads=["vb_tok", "innT"], writes=[bkey(0, 0)])
            P.op("pe", lambda e, h=h: e.matmul(p_o[:, h, :], lhsT=Sb[:, h, :], rhs=qdT[:, h, c0:c0 + 128], start=False, stop=True),
                 reads=["Sb", "qdT"], writes=[bkey(0, 0)])
        P.op("act", lambda e: e.copy(out=o32, in_=p_o), reads=[bkey(0, 0)], writes=["o32"])
        ret_norm(c0)

    def retention_sample():
        ret_inner(0)
        poh = [bank(0, 0)[:, 0:128], bank(0, 1)[:, 0:128], bank(1, 0)[:, 0:128], bank(3, 0)[:, 0:128]]
        pok = [bkey(0, 0), bkey(0, 1), bkey(1, 0), bkey(3, 0)]
        for h in range(4):
            P.op("pe", lambda e, h=h: e.matmul(poh[h], lhsT=vb_tok[:, h * 128:(h + 1) * 128], rhs=innT[:, h, :], start=True, stop=False),
                 reads=["vb_tok", "innT"], writes=[pok[h]])
        for b in range(16):
            s_ = b % 2
            P.dma("sp", lambda e, b=b, s_=s_: e.dma_start(out=S0[s_], in_=sret_in[b].rearrange("h d e -> d h e")), writes=[f"S0{s_}"], semkey=f"S0{s_}")
            P.op("pool", lambda e, s_=s_: e.tensor_copy(out=S0b[s_], in_=S0[s_]), reads=[f"S0{s_}"], writes=[f"S0b{s_}"])
            for h in range(4):
                P.op("pe", lambda e, h=h, b=b, s_=s_: e.matmul(poh[h][:, 8 * b:8 * b + 8], lhsT=S0b[s_][:, h, :], rhs=qdT[:, h, 8 * b:8 * b + 8],
                                                              start=False, stop=(b == 15)),
                     reads=[f"S0b{s_}", "qdT"], writes=[pok[h]])
            P.op("dve", lambda e, b=b, s_=s_: e.tensor_scalar(out=kdm[s_], in0=kd_tok, scalar1=seqmask[:, b:b + 1], scalar2=None, op0=ALU.mult),
                 reads=["kd_tok", "seqmask"], writes=[f"kdm{s_}"])
            for h in range(4):
                P.op("pe", lambda e, h=h, s_=s_: e.matmul(p_su[:, h, :], lhsT=kdm[s_][:, h * 128:(h + 1) * 128], rhs=vb_tok[:, h * 128:(h + 1) * 128], start=True, stop=True),
                     reads=[f"kdm{s_}", "vb_tok"], writes=[bkey(2, 1)])
            P.op("pool", lambda e, s_=s_: e.tensor_tensor(out=S0[s_], in0=S0[s_], in1=dc[:, 1, :].unsqueeze(2).to_broadcast([128, 4, 128]), op=ALU.mult),
                 reads=[f"S0{s_}", f"S0b{s_}", "dc"], writes=[f"S0{s_}"])
            P.op("dve", lambda e, s_=s_: e.tensor_tensor(out=S0[s_], in0=S0[s_], in1=p_su, op=ALU.add), reads=[f"S0{s_}", bkey(2, 1)], writes=[f"S0{s_}"])
            P.dma("sp", lambda e, b=b, s_=s_: e.dma_start(out=sret_out[b].rearrange("h d e -> d h e"), in_=S0[s_]), reads=[f"S0{s_}"], semkey=f"S0o{s_}", final=True)
        for h in range(4):
            P.op("act", lambda e, h=h: e.copy(out=o32[:, h, :], in_=poh[h]), reads=[pok[h]], writes=["o32"])
        ret_norm(0)

    PO = [bank(0, 0), bank(0, 1)]
    POK = [bkey(0, 0), bkey(0, 1)]

    def out_proj(blk_c0, ntok, sample, wfn, nk, rhs_fn, gate_row, rkeys):
        for m in range(KC):
            s = m % 2
            for k in range(nk):
                P.op("pe", lambda e, m=m, k=k, s=s: e.matmul(PO[s][:, :ntok], lhsT=wfn(k, m), rhs=rhs_fn(k), start=(k == 0), stop=(k == nk - 1)),
                     reads=rkeys, writes=[POK[s]])
            xs = xT[:, m, blk_c0:blk_c0 + ntok]
            xkey = ("xT", blk_c0 // 512)
            if not sample:
                P.op("dve", lambda e, m=m, s=s, xs=xs: e.scalar_tensor_tensor(out=xs, in0=PO[s][:, :ntok], scalar=modT[:, gate_row + m, 0:1], in1=xs,
                                                                             op0=ALU.mult, op1=ALU.add),
                     reads=[POK[s], "modT", xkey], writes=[xkey])
            else:
                gv = gtmp[:, :128].rearrange("p (b t) -> p b t", t=8)
                P.op("dve", lambda e, m=m, s=s, gv=gv: e.tensor_tensor(out=gv, in0=PO[s][:, :128].rearrange("p (b t) -> p b t", t=8),
                                                                      in1=modT[:, gate_row + m, 1:17].unsqueeze(2).to_broadcast([128, 16, 8]), op=ALU.mult),
                     reads=[POK[s], "modT"], writes=["gtmp"])
                P.op("dve", lambda e, xs=xs: e.tensor_tensor(out=xs, in0=xs, in1=gtmp[:, :128], op=ALU.add), reads=["gtmp", xkey], writes=[xkey])

    p_tr = bank(1, 1)[:, 256:384]
    p_v32 = bank(1, 1)[:, 384:512]

    def window_kv(tc0, kout, vout):
        pk2 = bank(1, 0)[:, 0:256].rearrange("p (a n) -> p a n", a=2)
        pst2 = bank(1, 0)[:, 256:512].rearrange("p (a n) -> p a n", a=2)
        for kv in range(2):
            for k in range(KC):
                P.op("pe", lambda e, kv=kv, k=k: e.matmul(pk2[:, kv, :], lhsT=wkd[:, k, kv, :], rhs=hT[:, k, tc0:tc0 + 128], start=(k == 0), stop=(k == KC - 1)),
                     reads=["wkd", "hT"], writes=[bkey(1, 0)])
        P.op("act", lambda e: e.activation(out=qsq[:, 0:256].rearrange("p (a n) -> p a n", a=2), in_=pk2, func=AF.Square), reads=[bkey(1, 0)], writes=["qsq"])
        for kv in range(2):
            P.op("pe", lambda e, kv=kv: e.matmul(pst2[:, kv, :], lhsT=bd, rhs=qsq[:, kv * 128:(kv + 1) * 128], start=True, stop=True), reads=["bd", "qsq"], writes=[bkey(1, 0)])
        P.op("act", lambda e: e.activation(out=qrs[:, 0:256].rearrange("p (a n) -> p a n", a=2), in_=pst2, func=AF.Sqrt, bias=epsb[:], scale=1.0),
             reads=[bkey(1, 0), "epsb"], writes=["qrs"])
        P.op("dve", lambda e: e.reciprocal(out=qrs[:, 0:256], in_=qrs[:, 0:256]), reads=["qrs"], writes=["qrs"])
        for kv in range(2):
            sl = slice(kv * 64, (kv + 1) * 64)
            P.op("dve", lambda e, kv=kv, sl=sl: e.scalar_tensor_tensor(out=knT[sl, :], in0=pk2[sl, kv, :], scalar=gk[sl, 0:1], in1=qrs[sl, kv * 128:(kv + 1) * 128],
                                                                       op0=ALU.mult, op1=ALU.mult),
                 reads=[bkey(1, 0), "gk", "qrs"], writes=["knT"])
        P.op("pe", lambda e: e.transpose(p_tr, knT, ident[:]), reads=["knT", "ident"], writes=[bkey(1, 1)])
        P.op("act", lambda e: e.copy(out=kv32[:, 0, :], in_=p_tr), reads=[bkey(1, 1)], writes=["kv32"])
        for k in range(KC):
            P.op("pe", lambda e, k=k: e.matmul(p_v32, lhsT=hT[:, k, tc0:tc0 + 128], rhs=win[:, k, 640:768], start=(k == 0), stop=(k == KC - 1)),
                 reads=["win", "hT"], writes=[bkey(1, 1)])
        P.op("dve", lambda e: e.tensor_copy(out=kv32[:, 1, :], in_=p_v32), reads=[bkey(1, 1)], writes=["kv32"])
        P.dma("sp", lambda e: e.dma_start(out=kout, in_=kv32[:, 0, :]), reads=["kv32"], semkey="kvo", final=True)
        P.dma("sp", lambda e: e.dma_start(out=vout, in_=kv32[:, 1, :]), reads=["kv32"], semkey="kvo", final=True)

    if KSTOP >= 2:
        for kv in range(2):
            proj_fm(lambda k, kv=kv: wkd[:, k, kv, :], 128,
                    lambda ps_, key, kv=kv: qknorm(ps_, key, 128, gk, "gk", kdT[:, kv, 0:128], "kdT"), src=hTh)
        tok_vd(0, 0, src=hTh)

    NB = TOK_P // BT
    for b in range(NB if KSTOP >= 3 else 0):
        ln_block(xT[:, :, b * BT:(b + 1) * BT], ("xT", (b * BT) // 512), BT, False, 0)
        project_block(BT, 0)
        if KSTOP < 4:
            continue
        for i in range(TPB):
            tok_vd(i * 128, i + 1)
        for i in range(TPB):
            attention_tile(i, 0, 2 if (b == 0 and i == 0) else 1)
        if KSTOP < 5:
            continue
        for i in range(TPB):
            tok_kv(i * 128, 0)
            retention_tile(i, 0)
            state_update()
            P.op("act", lambda e: e.copy(out=Sb, in_=S), reads=["S"], writes=["Sb"])
        if b == NB - 1:
            window_kv(BT - 128, pk_out, pv_out)
        if KSTOP < 6:
            continue
        out_proj(b * BT, BT, False, lambda k, m: wout[:, k, m * 128:(m + 1) * 128], KC, lambda k: oT[:, k, :BT], 16, ["wout", "oT"])
        P.op("pool", lambda e: e.tensor_copy(out=kdT[:, :, 0:128], in_=kdT[:, :, BT:BT + 128]), reads=["kdT"], writes=["kdT"])
        P.op("pool", lambda e: e.tensor_copy(out=vd_tok[:, 0, :], in_=vd_tok[:, TPB, :]), reads=["vd_tok"], writes=["vd_tok"])
    P.dma("sp", lambda e: e.dma_start(out=pret_out.rearrange("h d e -> d h e"), in_=S), reads=["S"], semkey="pret", final=True)

    if KSTOP >= 7:
        P.dma("sp", lambda e: e.dma_start(out=rmask, in_=rmask_d[:, 1]), writes=["rmask"], semkey="ld_rmask", group="ld_rmask")
        P.dma("sp", lambda e: e.dma_start(out=dq, in_=dq_d[:, 1]), writes=["dq"], semkey="ld_dq", group="ld_dq")
        ln_block(xT[:, :, TOK_P:TOK_P + 128], ("xT", 4), 128, True, 0)
        project_block(128, 1)
        tok_vd(0, 1)
        tok_kv(0, 1)
        window_kv(0, sk_out[:, 120:128, :], sv_out[:, 120:128, :])
        P.dma("sp", lambda e: e.dma_start(out=sk_out[:, 0:120, :], in_=cache_k[:, 8:128, :]), semkey="cko", final=True)
        P.dma("sp", lambda e: e.dma_start(out=sv_out[:, 0:120, :], in_=cache_v[:, 8:128, :]), semkey="cko", final=True)
        attention_sample()
        retention_sample()
        out_proj(TOK_P, 128, True, lambda k, m: wout[:, k, m * 128:(m + 1) * 128], KC, lambda k: oT[:, k, :128], 16, ["wout", "oT"])

    if KSTOP >= 8:
        ffn(0)

    P.barrier()
    cur[0] = mark1
    yst = [carve([128, D]) for _ in range(2)]
    for t in range(17):
        s = t % 2
        for k in range(KC):
            P.op("pe", lambda e, k=k, t=t, s=s: e.transpose(pT[s][:, k, :], xT[:, k, t * 128:(t + 1) * 128], ident[:]),
                 reads=[("xT", t // 4), "ident"], writes=[bkey(s, 0), bkey(s, 1)])
        P.op("act", lambda e, s=s: e.copy(out=yst[s], in_=Q[s][:, :]), reads=[bkey(s, 0), bkey(s, 1)], writes=[f"yst{s}"])
        P.dma("sp", lambda e, t=t, s=s: e.dma_start(out=x1_out[t * 128:(t + 1) * 128, :], in_=yst[s]), reads=[f"yst{s}"], semkey=f"yst{s}", final=True)

    P.barrier()
    P.dma("sp", lambda e: e.dma_start(out=modT[:].rearrange("p m c -> p (m c)"), in_=modin1_d), writes=["modT"], semkey="ld_modT1")
    make_G(1, 0)
    s5_stage(False)

    stats = P.build()
    nc_allow.__exit__(None, None, None)
    es.close()
    return nc, stats


_CACHE = {}


def _tables(c):
    f32 = np.float32
    ident = np.eye(128, dtype=f32)
    bd = np.zeros((128, 128), f32)
    bd[:64, :64] = 1.0 / 64
    bd[64:, 64:] = 1.0 / 64
    j = np.arange(128)[:, None]
    i = np.arange(128)[None, :]
    NEG = -1e30
    dneg = np.full((128, 5, 128), NEG, f32)
    dneg[:, 0, :] = np.where(i >= j, -(i - j), NEG)
    dneg[:, 1, :] = np.where(i < j, -(i - j + 128), NEG)
    dneg[:, 2, :] = dneg[:, 1, :] if c > 0 else NEG
    same = (j // 8) == (i // 8)
    dneg[:, 3, :] = np.where(same & ((i % 8) >= (j % 8)), -((i % 8) - (j % 8)), NEG)
    dneg[:, 4, :] = np.where(j >= (i % 8) + 1, -(128 + (i % 8) - j), NEG)
    g = np.array(GAM, np.float64)
    rmask = np.zeros((128, 2, 4, 128), f32)
    dq = np.zeros((128, 2, 4, 128), f32)
    dk = np.zeros((128, 2, 4), f32)
    dc = np.zeros((128, 2, 4), f32)
    for h in range(4):
        rmask[:, 0, h, :] = np.where(i >= j, g[h] ** np.maximum(i - j, 0), 0.0)
        rmask[:, 1, h, :] = np.where(same & ((i % 8) >= (j % 8)), g[h] ** np.maximum((i % 8) - (j % 8), 0), 0.0)
        dq[:, 0, h, :] = g[h] ** (i + 1.0)
        dq[:, 1, h, :] = g[h] ** ((i % 8) + 1.0)
        dk[:, 0, h] = 128.0 ** -0.5 * g[h] ** (127.0 - j[:, 0])
        dk[:, 1, h] = 128.0 ** -0.5 * g[h] ** (7.0 - (j[:, 0] % 8))
        dc[:, 0, h] = g[h] ** 128.0
        dc[:, 1, h] = g[h] ** 8.0
    seqmask = (j // 8 == np.arange(16)[None, :]).astype(f32)
    wret = np.zeros((128, 8, 4), f32)
    for r in range(8):
        if r < c:
            for h in range(4):
                wret[:, r, h] = g[h] ** (2048.0 * (c - r - 1))
    return dict(ident=ident, bdones=bd, dneg=dneg, rmask=rmask, dq=dq, dk=dk, dc=dc, seqmask=seqmask, wret=wret)


def _get(stage):
    if stage not in _CACHE:
        _CACHE[stage] = build_program(stage)
    return _CACHE[stage]


def kernel(**inp):
    f32 = np.float32
    A = lambda k: np.ascontiguousarray(np.asarray(inp[k], f32))
    xp = A("x_prompt")[0]
    xs = A("x_sample")
    base = dict(norm_mix=A("norm_mix"), norm_ffn=A("norm_ffn"), even_w_in=A("even_w_in")[0])
    per_core = []
    for c in range(NCORES):
        x18 = np.zeros((18 * 128, D), f32)
        if c > 0:
            x18[0:128] = xp[c * TOK_P - 128:c * TOK_P]
        x18[128:128 + TOK_P] = xp[c * TOK_P:(c + 1) * TOK_P]
        x18[128 + TOK_P:] = xs[16 * c:16 * c + 16].reshape(128, D)
        cv = np.concatenate([A("c_prompt"), A("c_sample")[16 * c:16 * c + 16]], axis=0)
        t = _tables(c)
        m = dict(base)
        m.update(x=x18, cvec=np.ascontiguousarray(cv), ident=t["ident"], dk=t["dk"], dc=t["dc"])
        per_core.append((m, t))

    ncA, _ = _get("A")
    mapsA = []
    for m, _ in per_core:
        m = dict(m)
        m.update(ada_w=A("ada_w"), ada_b=A("ada_b"))
        mapsA.append(m)
    resA = run_bass_kernel_spmd(ncA, mapsA, core_ids=list(range(NCORES)))
    mods = [np.ascontiguousarray(resA.results[c]["modout"]) for c in range(NCORES)]
    lall = np.ascontiguousarray(np.stack([resA.results[c]["lret"] for c in range(NCORES)], axis=0))
    _CACHE["lall"] = lall

    ncB, statsB = _get("B")
    _CACHE["statsB"] = statsB
    in_maps = []
    for c in range(NCORES):
        m, t = per_core[c]
        m = dict(m)
        m.pop("cvec", None)
        m["modin"] = np.ascontiguousarray(mods[c][0])
        m["modin1"] = np.ascontiguousarray(mods[c][1])
        m.update(odd_A_re=A("odd_A_re")[0], odd_A_im=A("odd_A_im")[0], odd_log_dt=A("odd_log_dt"), odd_B_re=A("odd_B_re")[0], odd_B_im=A("odd_B_im")[0],
                 odd_C_re=A("odd_C_re")[0], odd_C_im=A("odd_C_im")[0], odd_D=A("odd_D"), **_tables_c())
        m.update(even_q_gain=A("even_q_gain"), even_k_gain=A("even_k_gain"), even_sinks=A("even_sinks"), even_ret_gain=A("even_ret_gain"),
                 even_w_out=A("even_w_out")[0], ffn_wg=A("ffn_wg"), ffn_wu=A("ffn_wu"), ffn_wd=A("ffn_wd"),
                 cache_k=np.ascontiguousarray(A("cache_win_k")[0, 16 * c:16 * c + 16].reshape(16, 128, 128)),
                 cache_v=np.ascontiguousarray(A("cache_win_v")[0, 16 * c:16 * c + 16].reshape(16, 128, 128)),
                 state_ret=np.ascontiguousarray(A("state_ret")[0, 16 * c:16 * c + 16]),
                 bdones=t["bdones"], dneg=t["dneg"], rmask=t["rmask"], dq=t["dq"], seqmask=t["seqmask"], wret=t["wret"], lall=lall)
        in_maps.append(m)
    resB = run_bass_kernel_spmd(ncB, in_maps, core_ids=list(range(NCORES)))
    R = resB.results
    _CACHE["R"] = R
    last = R[NCORES - 1]
    p_k = last["p_k"].reshape(1, 1, 128, 2, 64)
    p_v = last["p_v"].reshape(1, 1, 128, 2, 64)
    p_ret = last["p_ret"].reshape(1, 1, 4, 128, 128)
    s_k = np.concatenate([R[c]["s_k"] for c in range(NCORES)], axis=0).reshape(1, 128, 128, 2, 64)
    s_v = np.concatenate([R[c]["s_v"] for c in range(NCORES)], axis=0).reshape(1, 128, 128, 2, 64)
    s_ret = np.concatenate([R[c]["s_ret"] for c in range(NCORES)], axis=0)[None]

    tC = _tables_c()
    mapsC = []
    for c in range(NCORES):
        m0, t = per_core[c]
        x18 = np.zeros((18 * 128, D), f32)
        x18[128:] = R[c]["x1"]
        wsel = np.zeros((128, 8), f32)
        wsel[:, :c] = 1.0
        m = dict(base)
        m.update(x=x18, modin=np.ascontiguousarray(mods[c][1]), ident=t["ident"], dk=t["dk"], dc=t["dc"],
                 ffn_wg=A("ffn_wg"), ffn_wu=A("ffn_wu"), ffn_wd=A("ffn_wd"),
                 odd_A_re=A("odd_A_re")[0], odd_A_im=A("odd_A_im")[0], odd_log_dt=A("odd_log_dt"),
                 odd_B_re=A("odd_B_re")[0], odd_B_im=A("odd_B_im")[0], odd_C_re=A("odd_C_re")[0], odd_C_im=A("odd_C_im")[0],
                 odd_D=A("odd_D"), odd_glu_a=A("odd_glu_a")[0], odd_glu_b=A("odd_glu_b")[0],
                 s5_re=np.ascontiguousarray(A("state_s5_re")[0, 16 * c:16 * c + 16]), s5_im=np.ascontiguousarray(A("state_s5_im")[0, 16 * c:16 * c + 16]),
                 fall=np.zeros((8, 128, 64), f32), wsel=wsel, **tC)
        mapsC.append(m)
    fall = np.ascontiguousarray(np.stack([R[c]["floc"] for c in range(NCORES)], axis=0))
    _CACHE["fall"] = fall
    for m in mapsC:
        m["fall"] = fall
    ncC2, statsC = _get("C2")
    _CACHE["statsC"] = statsC
    resC2 = run_bass_kernel_spmd(ncC2, mapsC, core_ids=list(range(NCORES)))
    RC = resC2.results
    _CACHE["RC"] = RC
    y_prompt = np.concatenate([RC[c]["y"][:TOK_P] for c in range(NCORES)], axis=0)[None]
    y_sample = np.concatenate([RC[c]["y"][TOK_P:].reshape(16, 8, D) for c in range(NCORES)], axis=0)
    lastC = RC[NCORES - 1]
    p_re = lastC["p_s5r"].reshape(1, 1, 64, 64)
    p_im = lastC["p_s5i"].reshape(1, 1, 64, 64)
    s_re = np.concatenate([RC[c]["s_s5r"] for c in range(NCORES)], axis=0)[None]
    s_im = np.concatenate([RC[c]["s_s5i"] for c in range(NCORES)], axis=0)[None]
    return (y_prompt.astype(f32), y_sample.astype(f32), p_k, p_v, p_ret, p_re, p_im, s_k, s_v, s_ret, s_re, s_im)


def _tables_c():
    f32 = np.float32
    t = np.arange(512)
    tal = np.broadcast_to((t // 64).astype(f32)[None, :], (128, 512)).copy()
    tbp = np.broadcast_to(((t % 64) + 1).astype(f32)[None, :], (128, 512)).copy()
    ts = np.arange(128)
    tbs = np.broadcast_to(((ts % 8) + 1).astype(f32)[None, :], (128, 128)).copy()
    amask = np.broadcast_to(((ts % 8) != 0).astype(f32)[None, :], (128, 128)).copy()
    maskg = (np.arange(128)[:, None] // 16 == np.arange(8)[None, :]).astype(f32)
    rott = np.zeros((128, 128), f32)
    for p in range(64):
        rott[64 + p, p] = -1.0
        rott[p, 64 + p] = 1.0
    return dict(tal=tal, tbp=tbp, tbs=tbs, amask=amask, maskg=maskg, rott=rott)
```

```python
import os
import numpy as np
from contextlib import ExitStack
import concourse.bass as bass
import concourse.mybir as mybir
from concourse.bass_utils import run_bass_kernel_spmd

F32 = mybir.dt.float32
BF16 = mybir.dt.bfloat16
I32 = mybir.dt.int32
ALU = mybir.AluOpType
AF = mybir.ActivationFunctionType

NCORES = 8
D = 1024
KC = 8
NPT = 16
TOK_P = NPT * 128
NTOK = TOK_P + 128
IN_W = 2816
DFF = 2816
FC = 22
BT = 128
TPB = BT // 128
EPS = 1e-6
ENGS = ("pe", "act", "dve", "pool", "sp")
GAM = [1.0 - 2.0 ** (-5.0 - h) for h in range(4)]
KSTOP = int(os.environ.get('KSTOP', '9'))
KSUB = int(os.environ.get('KSUB', '9'))


class Prog:
    def __init__(self, nc, same_engine_sync=("act", "dve", "pool")):
        self.nc = nc
        self.ins = []
        self.last_w = {}
        self.readers = {}
        self.same_sync = set(same_engine_sync)
        self.final_ids = []
        self.last_eng = {}
        self.last_dma = {}
        self.bar_deps = []
        self.bar_gen = 0
        self.eng_gen = {e: 0 for e in ENGS}

    def barrier(self):
        self.bar_deps = list(self.last_eng.values()) + list(self.last_dma.values())
        self.bar_gen += 1

    def _add(self, eng, fn, reads, writes, dma=False, semkey=None, group=None):
        iid = len(self.ins)
        deps = set()
        pk = tuple(k for k in reads if isinstance(k, str) and len(k) == 3 and k[0] == "Q" and k not in writes)
        writes = tuple(writes) + pk
        if self.eng_gen[eng] < self.bar_gen:
            deps.update(self.bar_deps)
            self.eng_gen[eng] = self.bar_gen
        for k in reads:
            w = self.last_w.get(k)
            if w is not None:
                deps.add(w)
        for k in writes:
            w = self.last_w.get(k)
            if w is not None:
                if group is not None and self.ins[w].get("group") == group:
                    deps.update(self.ins[w]["deps"])
                else:
                    deps.add(w)
            for r in self.readers.get(k, ()):
                deps.add(r)
        self.ins.append(dict(eng=eng, fn=fn, deps=sorted(deps), dma=dma, semkey=semkey, group=group))
        for k in reads:
            self.readers.setdefault(k, []).append(iid)
        for k in writes:
            self.last_w[k] = iid
            self.readers[k] = []
        self.last_eng[eng] = iid
        if dma:
            self.last_dma[semkey] = iid
        return iid

    capture = None

    def op(self, eng, fn, reads=(), writes=()):
        if self.capture is not None:
            self.capture.append((eng, fn, tuple(reads), tuple(writes)))
            return None
        return self._add(eng, fn, tuple(reads), tuple(writes))

    def dma(self, eng, fn, reads=(), writes=(), semkey=None, group=None, final=False):
        assert semkey is not None
        iid = self._add(eng, fn, tuple(reads), tuple(writes), dma=True, semkey=semkey, group=group)
        if final:
            self.final_ids.append(iid)
        return iid

    def build(self):
        nc = self.nc
        ins = self.ins
        n = len(ins)
        needed = [False] * n
        for i, it in enumerate(ins):
            nd = []
            for d in it["deps"]:
                de = ins[d]
                if (not de["dma"]) and (not it["dma"]) and de["eng"] == it["eng"] and it["eng"] not in self.same_sync:
                    continue
                nd.append(d)
            it["deps"] = nd
            for d in nd:
                needed[d] = True
        for f in self.final_ids:
            needed[f] = True
        semkeys = []
        for it in ins:
            if it["dma"] and it["semkey"] not in semkeys:
                semkeys.append(it["semkey"])
        sem_objs = {}
        ctxs = []
        for e in ENGS:
            c = nc.semaphore(f"s_{e}")
            sem_objs[("eng", e)] = c.__enter__()
            ctxs.append(c)
        for j, k in enumerate(semkeys):
            c = nc.semaphore(f"d{j}")
            sem_objs[("dma", k)] = c.__enter__()
            ctxs.append(c)
        cnt = {}
        for i, it in enumerate(ins):
            if it["dma"]:
                key = ("dma", it["semkey"])
                cnt[key] = cnt.get(key, 0) + 16
                it["sig"] = (key, cnt[key])
            elif needed[i]:
                key = ("eng", it["eng"])
                cnt[key] = cnt.get(key, 0) + 1
                it["sig"] = (key, cnt[key])
            else:
                it["sig"] = None
        per = {e: [] for e in ENGS}
        for i, it in enumerate(ins):
            per[it["eng"]].append(i)
        final_waits = {}
        for f in self.final_ids:
            key, val = ins[f]["sig"]
            final_waits[key] = max(final_waits.get(key, 0), val)
        with nc.Block() as block:
            def make(e):
                def body(eng):
                    waited = {}
                    for i in per[e]:
                        it = ins[i]
                        req = {}
                        for d in it["deps"]:
                            key, val = ins[d]["sig"]
                            if waited.get(key, 0) >= val:
                                continue
                            req[key] = max(req.get(key, 0), val)
                        for key, val in req.items():
                            eng.wait_ge(sem_objs[key], val)
                            waited[key] = val
                        r = it["fn"](eng)
                        if it["sig"] is not None:
                            key, val = it["sig"]
                            r.then_inc(sem_objs[key], 16 if it["dma"] else 1)
                    if e == "sp":
                        for key, val in final_waits.items():
                            eng.wait_ge(sem_objs[key], val)
                return body
            block.tensor(make("pe"))
            block.scalar(make("act"))
            block.vector(make("dve"))
            block.gpsimd(make("pool"))
            block.sync(make("sp"))
        for c in reversed(ctxs):
            c.__exit__(None, None, None)
        return dict(n=n, per={e: len(per[e]) for e in ENGS}, sems=len(sem_objs), maxcnt=max(cnt.values()), cnt={k[1]: v for k, v in cnt.items() if k[0] == 'eng'})


def build_program(stage):
    nc = bass.Bass("TRN2", target_bir_lowering=False)
    es = ExitStack()

    def din(name, shape):
        return nc.dram_tensor(name, list(shape), F32, kind="ExternalInput").ap()

    def dout(name, shape):
        return nc.dram_tensor(name, list(shape), F32, kind="ExternalOutput").ap()

    def sb(name, shape, dt=F32):
        return es.enter_context(nc.sbuf_tensor("sb_" + name, list(shape), dt))

    P = Prog(nc)
    nc_allow = nc.allow_non_contiguous_dma(reason="small parameter vectors laid out feature-major")
    nc_allow.__enter__()

    x_in = din("x", [18 * 128, D])
    if stage == "A":
        cvec = din("cvec", [17, D])
        ada_w = din("ada_w", [2, D, 6 * D])
        ada_b = din("ada_b", [2, 6 * D])
        mod_out = dout("modout", [2, 128, 48 * 17])
    else:
        modin_d = din("modin", [128, 48 * 17])
    norm_mix = din("norm_mix", [2, D])
    norm_ffn = din("norm_ffn", [2, D])
    w_in = din("even_w_in", [D, IN_W])
    ident_d = din("ident", [128, 128])
    dk_d = din("dk", [128, 2, 4])
    dc_d = din("dc", [128, 2, 4])
    if stage == "A":
        lret_out = dout("lret", [4, 128, 128])
    if stage in ("C1", "C2"):
        wg_d = din("ffn_wg", [2, D, DFF])
        wu_d = din("ffn_wu", [2, D, DFF])
        wd_d = din("ffn_wd", [2, DFF, D])
        A_re_d = din("odd_A_re", [64, 64])
        A_im_d = din("odd_A_im", [64, 64])
        ldt_d = din("odd_log_dt", [1, 64])
        B_re_d = din("odd_B_re", [64, 64, 16])
        B_im_d = din("odd_B_im", [64, 64, 16])
        C_re_d = din("odd_C_re", [64, 16, 64])
        C_im_d = din("odd_C_im", [64, 16, 64])
        Dsk_d = din("odd_D", [1, D])
        glua_d = din("odd_glu_a", [D, D])
        glub_d = din("odd_glu_b", [D, D])
        s5r_in = din("s5_re", [16, 64, 64])
        s5i_in = din("s5_im", [16, 64, 64])
        fall_d = din("fall", [8, 128, 64])
        wsel_d = din("wsel", [128, 8])
        tal_d = din("tal", [128, 512])
        tbp_d = din("tbp", [128, 512])
        tbs_d = din("tbs", [128, 128])
        amask_d = din("amask", [128, 128])
        maskg_d = din("maskg", [128, 8])
        rott_d = din("rott", [128, 128])
        if stage == "C1":
            floc_out = dout("floc", [128, 64])
        else:
            y_out = dout("y", [17 * 128, D])
            ps5r_out = dout("p_s5r", [64, 64])
            ps5i_out = dout("p_s5i", [64, 64])
            ss5r_out = dout("s_s5r", [16, 64, 64])
            ss5i_out = dout("s_s5i", [16, 64, 64])
    if stage == "B":
        A_re_d = din("odd_A_re", [64, 64])
        A_im_d = din("odd_A_im", [64, 64])
        ldt_d = din("odd_log_dt", [1, 64])
        B_re_d = din("odd_B_re", [64, 64, 16])
        B_im_d = din("odd_B_im", [64, 64, 16])
        C_re_d = din("odd_C_re", [64, 16, 64])
        C_im_d = din("odd_C_im", [64, 16, 64])
        Dsk_d = din("odd_D", [1, D])
        tal_d = din("tal", [128, 512])
        tbp_d = din("tbp", [128, 512])
        tbs_d = din("tbs", [128, 128])
        amask_d = din("amask", [128, 128])
        maskg_d = din("maskg", [128, 8])
        rott_d = din("rott", [128, 128])
        modin1_d = din("modin1", [128, 48 * 17])
        floc_out = dout("floc", [128, 64])
        qgain = din("even_q_gain", [1, 64])
        kgain = din("even_k_gain", [1, 64])
        sinks_d = din("even_sinks", [1, 8])
        retg_d = din("even_ret_gain", [1, 512])
        w_out = din("even_w_out", [D, D])
        wg_d = din("ffn_wg", [2, D, DFF])
        wu_d = din("ffn_wu", [2, D, DFF])
        wd_d = din("ffn_wd", [2, DFF, D])
        cache_k = din("cache_k", [16, 128, 128])
        cache_v = din("cache_v", [16, 128, 128])
        sret_in = din("state_ret", [16, 4, 128, 128])
        bd_d = din("bdones", [128, 128])
        dneg_d = din("dneg", [128, 5, 128])
        rmask_d = din("rmask", [128, 2, 4, 128])
        dq_d = din("dq", [128, 2, 4, 128])
        seqmask_d = din("seqmask", [128, 16])
        wret_d = din("wret", [128, 8, 4])
        lall_d = din("lall", [8, 4, 128, 128])
        x1_out = dout("x1", [17 * 128, D])
        pk_out = dout("p_k", [128, 128])
        pv_out = dout("p_v", [128, 128])
        pret_out = dout("p_ret", [4, 128, 128])
        sk_out = dout("s_k", [16, 128, 128])
        sv_out = dout("s_v", [16, 128, 128])
        sret_out = dout("s_ret", [16, 4, 128, 128])

    Q = [es.enter_context(nc.psum_tensor(f"ps_Q{i}", [128, 1024], F32)) for i in range(4)]

    def bank(i, h):
        return Q[i][:, h * 512:(h + 1) * 512]

    def bkey(i, h):
        return f"Q{i}{'ab'[h]}"

    ARENA_W = 33 * 1024
    arena = sb("arena", [128, ARENA_W])
    cur = [0]

    def carve(shape, dt=F32):
        n = int(np.prod(shape[1:]))
        words = n if dt in (F32, I32) else (n + 1) // 2
        words = (words + 7) // 8 * 8
        off = cur[0]
        assert off + words <= ARENA_W, ("arena overflow", off, words)
        cur[0] = off + words
        v = arena[:, off:off + words]
        if dt != F32:
            v = v.bitcast(dt)
        v = v[:, 0:n]
        if len(shape) == 3:
            v = v.rearrange("p (a b) -> p a b", a=shape[1])
        elif len(shape) == 4:
            v = v.rearrange("p (a b c) -> p a b c", a=shape[1], b=shape[2])
        elif len(shape) == 5:
            v = v.rearrange("p (a b c d) -> p a b c d", a=shape[1], b=shape[2], c=shape[3])
        return v

    ident = sb("ident", [128, 128])
    ones_m = sb("ones_m", [128, 128], BF16)
    ones_b = sb("ones_b", [128, 128], BF16)
    ones_g = sb("ones_g", [128, 128], BF16)
    epsb = sb("epsb", [128, 1])
    xT = sb("xT", [128, KC, NTOK])
    cT = sb("cT", [128, KC, 17], BF16)
    adab = sb("adab", [128, 2, 48])
    modT = sb("modT", [128, 48, 17])
    normg = sb("normg", [128, 2, 2, KC])
    G = sb("G", [128, KC, 17])
    dk = sb("dk", [128, 2, 4])
    dc = sb("dc", [128, 2, 4])
    P.dma("sp", lambda e: e.dma_start(out=ident[:], in_=ident_d), writes=["ident"], semkey="ld_ident", group="ld_ident")
    P.dma("sp", lambda e: e.dma_start(out=dk[:], in_=dk_d), writes=["dk"], semkey="ld_dk", group="ld_dk")
    P.dma("sp", lambda e: e.dma_start(out=dc[:], in_=dc_d), writes=["dc"], semkey="ld_dc", group="ld_dc")
    if stage == "A":
        P.dma("sp", lambda e: e.dma_start(out=adab[:], in_=ada_b.rearrange("l (m p) -> p l m", p=128)), writes=["adab"], semkey="ld_adab", group="ld_adab")
    P.dma("sp", lambda e: e.dma_start(out=normg[:, 0], in_=norm_mix.rearrange("l (k p) -> p l k", p=128)), writes=["normg"], semkey="ld_normg", group="ld_normg")
    P.dma("sp", lambda e: e.dma_start(out=normg[:, 1], in_=norm_ffn.rearrange("l (k p) -> p l k", p=128)), writes=["normg"], semkey="ld_normg", group="ld_normg")
    P.op("dve", lambda e: e.memset(ones_m[:], 1.0 / 1024.0), writes=["ones_m"])
    P.op("dve", lambda e: e.memset(ones_b[:], 1.0), writes=["ones_b"])
    P.op("dve", lambda e: e.memset(ones_g[:], 1.0 / 128.0), writes=["ones_g"])
    P.op("dve", lambda e: e.memset(epsb[:], EPS), writes=["epsb"])

    mark0 = cur[0]
    xst = [carve([128, D]) for _ in range(2)]
    c_sb = carve([128, D])
    adaw = [carve([128, KC, 768], BF16) for _ in range(2)]
    pT = [Q[i][:, :].rearrange("p (k n) -> p k n", k=KC) for i in range(2)]

    def load_tile(t, dst_ap, dst_key):
        s = t % 2
        P.dma("sp", lambda e: e.dma_start(out=xst[s], in_=x_in[t * 128:(t + 1) * 128, :]), writes=[f"xst{s}"], semkey=f"xst{s}")
        for k in range(KC):
            P.op("pe", lambda e, k=k: e.transpose(pT[s][:, k, :], xst[s][:, k * 128:(k + 1) * 128], ident[:]),
                 reads=[f"xst{s}", "ident"], writes=[bkey(s, 0), bkey(s, 1)])
        P.op("act", lambda e: e.copy(out=dst_ap, in_=pT[s]), reads=[bkey(s, 0), bkey(s, 1)], writes=[dst_key])

    for t in range(1, 18):
        c0 = (t - 1) * 128
        load_tile(t, xT[:, :, c0:c0 + 128], ("xT", (t - 1) // 4))

    if stage == "A":
        P.dma("sp", lambda e: e.dma_start(out=c_sb[0:17, :], in_=cvec), writes=["c_sb"], semkey="c_sb")
        P.op("act", lambda e: e.activation(out=c_sb[0:17, :], in_=c_sb[0:17, :], func=AF.Silu), reads=["c_sb"], writes=["c_sb"])
        for k in range(KC):
            P.op("pe", lambda e, k=k: e.transpose(pT[0][:, k, 0:17], c_sb[0:17, k * 128:(k + 1) * 128], ident[0:17, 0:17]),
                 reads=["c_sb", "ident"], writes=[bkey(0, 0), bkey(0, 1)])
        P.op("dve", lambda e: e.tensor_copy(out=cT[:], in_=pT[0][:, :, 0:17]), reads=[bkey(0, 0), bkey(0, 1)], writes=["cT"])

    pm = [bank(2, i)[:, 0:408].rearrange("p (m c) -> p m c", c=17) for i in range(2)]

    def modulation(layer):
        for cb in range(8):
            s = cb % 2
            P.dma("pool", lambda e, cb=cb, s=s: e.dma_start(out=adaw[s], in_=ada_w[layer, :, cb * 768:(cb + 1) * 768].rearrange("(k p) n -> p k n", p=128)),
                  writes=[f"adaw{s}"], semkey=f"adaw{s}")
            for mm in range(6):
                m = cb * 6 + mm
                for k in range(KC):
                    P.op("pe", lambda e, k=k, s=s, m=m, mm=mm: e.matmul(pm[m // 24][:, m % 24, :], lhsT=adaw[s][:, k, mm * 128:(mm + 1) * 128],
                                                                 rhs=cT[:, k, :], start=(k == 0), stop=(k == KC - 1)),
                         reads=[f"adaw{s}", "cT"], writes=[bkey(2, m // 24)])
        for h in range(2):
            P.op("dve", lambda e, h=h: e.tensor_tensor(out=modT[:, h * 24:(h + 1) * 24, :], in0=pm[h],
                                                       in1=adab[:, layer, h * 24:(h + 1) * 24].unsqueeze(2).to_broadcast([128, 24, 17]),
                                                       op=ALU.add),
                 reads=[bkey(2, h), "adab"], writes=["modT"])

    def make_G(layer, which):
        r0 = 8 if which == 0 else 32
        P.op("dve", lambda e: e.tensor_scalar(out=G[:], in0=modT[:, r0:r0 + 8, :], scalar1=1.0, scalar2=None, op0=ALU.add),
             reads=["modT"], writes=["G"])
        P.op("dve", lambda e: e.tensor_tensor(out=G[:], in0=G[:], in1=normg[:, which, layer, :].unsqueeze(2).to_broadcast([128, KC, 17]), op=ALU.mult),
             reads=["G", "normg"], writes=["G"])

    LAYER = 1 if stage in ("C1", "C2") else 0
    if stage == "A":
        for lay in (1, 0):
            modulation(lay)
            P.dma("sp", lambda e, lay=lay: e.dma_start(out=mod_out[lay], in_=modT[:].rearrange("p m c -> p (m c)")), reads=["modT"], semkey="modo", final=True)
    else:
        P.dma("sp", lambda e: e.dma_start(out=modT[:].rearrange("p m c -> p (m c)"), in_=modin_d), writes=["modT"], semkey="ld_modT")
    make_G(LAYER, 0)
    P.barrier()
    cur[0] = mark0

    sq = [carve([128, BT], BF16) for _ in range(2)]
    rstd = carve([128, BT])
    tn = [carve([128, BT]) for _ in range(2)]
    mark_ln = cur[0]
    hT = carve([128, KC, BT], BF16)
    p_ss = bank(3, 0)

    def ln_block(src_ap, src_key, ntok, sample, shift_row, out_ap=None, out_key="hT"):
        out_ap = hT if out_ap is None else out_ap
        for k in range(KC):
            s = k % 2
            P.op("act", lambda e, k=k, s=s: e.activation(out=sq[s][:, :ntok], in_=src_ap[:, k, :], func=AF.Square),
                 reads=[src_key], writes=[f"sq{s}"])
            P.op("pe", lambda e, k=k, s=s: e.matmul(p_ss[:, :ntok], lhsT=ones_m[:], rhs=sq[s][:, :ntok], start=(k == 0), stop=(k == KC - 1)),
                 reads=[f"sq{s}", "ones_m"], writes=[bkey(3, 0)])
        P.op("act", lambda e: e.activation(out=rstd[:, :ntok], in_=p_ss[:, :ntok], func=AF.Sqrt, bias=epsb[:], scale=1.0),
             reads=[bkey(3, 0), "epsb"], writes=["rstd"])
        P.op("dve", lambda e: e.reciprocal(out=rstd[:, :ntok], in_=rstd[:, :ntok]), reads=["rstd"], writes=["rstd"])
        for k in range(KC):
            s = k % 2
            P.op("dve", lambda e, k=k, s=s: e.tensor_tensor(out=tn[s][:, :ntok], in0=src_ap[:, k, :], in1=rstd[:, :ntok], op=ALU.mult),
                 reads=[src_key, "rstd"], writes=[f"tn{s}"])
            if not sample:
                P.op("act", lambda e, k=k, s=s: e.activation(out=out_ap[:, k, :ntok], in_=tn[s][:, :ntok], func=AF.Identity,
                                                             scale=G[:, k, 0:1], bias=modT[:, shift_row + k, 0:1]),
                     reads=[f"tn{s}", "G", "modT"], writes=[out_key])
            else:
                tv = tn[s][:, :128].rearrange("p (b t) -> p b t", t=8)
                P.op("dve", lambda e, k=k, tv=tv: e.tensor_tensor(out=tv, in0=tv, in1=G[:, k, 1:17].unsqueeze(2).to_broadcast([128, 16, 8]), op=ALU.mult),
                     reads=[f"tn{s}", "G"], writes=[f"tn{s}"])
                P.op("dve", lambda e, k=k, tv=tv: e.tensor_tensor(out=out_ap[:, k, :128].rearrange("p (b t) -> p b t", t=8), in0=tv,
                                                                  in1=modT[:, shift_row + k, 1:17].unsqueeze(2).to_broadcast([128, 16, 8]), op=ALU.add),
                     reads=[f"tn{s}", "modT"], writes=[out_key])

    win = None
    if stage in ("A", "B"):
        win = carve([128, KC, IN_W], BF16)
        for k in range(KC):
            P.dma("pool", lambda e, k=k: e.dma_start(out=win[:, k, :], in_=w_in[k * 128:(k + 1) * 128, :]), writes=["win"], semkey="win", group="win")

    S = carve([128, 4, 128])
    Sb = carve([128, 4, 128], BF16)
    kd_tok = carve([128, 512], BF16)
    vb_tok = carve([128, 512], BF16)
    p_kb = bank(3, 1)
    p_vb = bank(2, 0)
    p_su = bank(2, 1).rearrange("p (h e) -> p h e", h=4)

    def tok_kv(tc0, grp):
        for k in range(KC):
            P.op("pe", lambda e, k=k: e.matmul(p_kb, lhsT=hT[:, k, tc0:tc0 + 128], rhs=win[:, k, 1280:1792], start=(k == 0), stop=(k == KC - 1)),
                 reads=["hT", "win"], writes=[bkey(3, 1)])
        for k in range(KC):
            P.op("pe", lambda e, k=k: e.matmul(p_vb, lhsT=hT[:, k, tc0:tc0 + 128], rhs=win[:, k, 1792:2304], start=(k == 0), stop=(k == KC - 1)),
                 reads=["hT", "win"], writes=[bkey(2, 0)])
        P.op("dve", lambda e: e.tensor_tensor(out=kd_tok.rearrange("p (h d) -> p h d", h=4), in0=p_kb.rearrange("p (h d) -> p h d", h=4),
                                              in1=dk[:, grp, :].unsqueeze(2).to_broadcast([128, 4, 128]), op=ALU.mult),
             reads=[bkey(3, 1), "dk"], writes=["kd_tok"])
        P.op("act", lambda e: e.copy(out=vb_tok, in_=p_vb), reads=[bkey(2, 0)], writes=["vb_tok"])

    def state_update():
        for h in range(4):
            P.op("pe", lambda e, h=h: e.matmul(p_su[:, h, :], lhsT=kd_tok[:, h * 128:(h + 1) * 128], rhs=vb_tok[:, h * 128:(h + 1) * 128], start=True, stop=True),
                 reads=["kd_tok", "vb_tok"], writes=[bkey(2, 1)])
        P.op("dve", lambda e: e.tensor_tensor(out=S, in0=S, in1=dc[:, 0, :].unsqueeze(2).to_broadcast([128, 4, 128]), op=ALU.mult),
             reads=["S", "dc"], writes=["S"])
        P.op("dve", lambda e: e.tensor_tensor(out=S, in0=S, in1=p_su, op=ALU.add), reads=["S", bkey(2, 1)], writes=["S"])

    def ffn(layer):
        GC = 2
        NG = FC // GC
        P.barrier()
        cur[0] = mark_ln
        make_G(layer, 1)
        hT_all = carve([128, KC, NTOK], BF16)
        wslot = [(carve([128, KC, GC * 128], BF16), carve([128, KC, GC * 128], BF16), carve([128, GC, D], BF16)) for _ in range(3)]
        hid = [carve([128, GC, 512], BF16) for _ in range(2)]
        sgf = [carve([128, 512]) for _ in range(2)]
        gt2 = carve([128, 128])
        print("arena words used (ffn):", cur[0], "of", ARENA_W)
        for t in range(NPT):
            ln_block(xT[:, :, t * 128:(t + 1) * 128], ("xT", t // 4), 128, False, 24, out_ap=hT_all[:, :, t * 128:(t + 1) * 128], out_key="hT_all")
        ln_block(xT[:, :, TOK_P:TOK_P + 128], ("xT", 4), 128, True, 24, out_ap=hT_all[:, :, TOK_P:TOK_P + 128], out_key="hT_all")
        UP = [(bank(0, 0), bkey(0, 0), bank(0, 1), bkey(0, 1)), (bank(1, 0), bkey(1, 0), bank(1, 1), bkey(1, 1))]
        DN = [(bank(2, 0), bkey(2, 0)), (bank(2, 1), bkey(2, 1)), (bank(3, 0), bkey(3, 0)), (bank(3, 1), bkey(3, 1))]
        ui = 0
        di = 0
        blocks = [(b * 512, 512, False) for b in range(4)] + [(TOK_P, 128, True)]
        for g in range(NG):
            ws = g % 3
            wgs, wus, wds = wslot[ws]
            c0h = g * GC * 128
            P.dma("pool", lambda e, wgs=wgs, c0h=c0h: e.dma_start(out=wgs, in_=wg_d[layer, :, c0h:c0h + GC * 128].rearrange("(k p) n -> p k n", p=128)),
                  writes=[f"wg{ws}"], semkey=f"wg{ws}")
            P.dma("pool", lambda e, wus=wus, c0h=c0h: e.dma_start(out=wus, in_=wu_d[layer, :, c0h:c0h + GC * 128].rearrange("(k p) n -> p k n", p=128)),
                  writes=[f"wu{ws}"], semkey=f"wu{ws}")
            P.dma("pool", lambda e, wds=wds, c0h=c0h: e.dma_start(out=wds, in_=wd_d[layer, c0h:c0h + GC * 128, :].rearrange("(c p) n -> p c n", p=128)),
                  writes=[f"wd{ws}"], semkey=f"wd{ws}")
            for (t0, nt, smp) in blocks:
                hs = ui % 2
                for c in range(GC):
                    gps, gkey, ups, ukey = UP[ui % 2]
                    ui += 1
                    for k in range(KC):
                        P.op("pe", lambda e, k=k, c=c, gps=gps, wgs=wgs, nt=nt, t0=t0: e.matmul(gps[:, :nt], lhsT=wgs[:, k, c * 128:(c + 1) * 128], rhs=hT_all[:, k, t0:t0 + nt],
                                                                                  start=(k == 0), stop=(k == KC - 1)),
                             reads=[f"wg{ws}", "hT_all"], writes=[gkey])
                    for k in range(KC):
                        P.op("pe", lambda e, k=k, c=c, ups=ups, wus=wus, nt=nt, t0=t0: e.matmul(ups[:, :nt], lhsT=wus[:, k, c * 128:(c + 1) * 128], rhs=hT_all[:, k, t0:t0 + nt],
                                                                                  start=(k == 0), stop=(k == KC - 1)),
                             reads=[f"wu{ws}", "hT_all"], writes=[ukey])
                    sgs = sgf[c % 2]
                    P.op("act", lambda e, gps=gps, sgs=sgs, nt=nt: e.activation(out=sgs[:, :nt], in_=gps[:, :nt], func=AF.Silu), reads=[gkey], writes=[f"sgf{c % 2}"])
                    P.op("dve", lambda e, ups=ups, sgs=sgs, hs=hs, c=c, nt=nt: e.tensor_tensor(out=hid[hs][:, c, :nt], in0=ups[:, :nt], in1=sgs[:, :nt], op=ALU.mult),
                         reads=[ukey, f"sgf{c % 2}"], writes=[f"hid{hs}"])
                for m in range(KC):
                    dps, dkey = DN[di % 4]
                    di += 1
                    for c in range(GC):
                        P.op("pe", lambda e, m=m, c=c, dps=dps, wds=wds, hs=hs, nt=nt: e.matmul(dps[:, :nt], lhsT=wds[:, c, m * 128:(m + 1) * 128], rhs=hid[hs][:, c, :nt],
                                                                                         start=(c == 0), stop=(c == GC - 1)),
                             reads=[f"wd{ws}", f"hid{hs}"], writes=[dkey])
                    xs = xT[:, m, t0:t0 + nt]
                    xkey = ("xT", t0 // 512)
                    if not smp:
                        P.op("dve", lambda e, m=m, dps=dps, xs=xs, nt=nt: e.scalar_tensor_tensor(out=xs, in0=dps[:, :nt], scalar=modT[:, 40 + m, 0:1], in1=xs,
                                                                                         op0=ALU.mult, op1=ALU.add),
                             reads=[dkey, "modT", xkey], writes=[xkey])
                    else:
                        gv = gt2.rearrange("p (b t) -> p b t", t=8)
                        P.op("dve", lambda e, m=m, dps=dps, gv=gv: e.tensor_tensor(out=gv, in0=dps[:, :128].rearrange("p (b t) -> p b t", t=8),
                                                                                  in1=modT[:, 40 + m, 1:17].unsqueeze(2).to_broadcast([128, 16, 8]), op=ALU.mult),
                             reads=[dkey, "modT"], writes=["gt2"])
                        P.op("dve", lambda e, xs=xs: e.tensor_tensor(out=xs, in0=xs, in1=gt2, op=ALU.add), reads=["gt2", xkey], writes=[xkey])
        return hT_all

    if stage == "A":
        P.op("dve", lambda e: e.memset(S, 0.0), writes=["S"])
        for b in range(TOK_P // BT):
            ln_block(xT[:, :, b * BT:(b + 1) * BT], ("xT", (b * BT) // 512), BT, False, 0)
            for i in range(TPB):
                tok_kv(i * 128, 0)
                state_update()
        P.dma("sp", lambda e: e.dma_start(out=lret_out.rearrange("h d e -> d h e"), in_=S), reads=["S"], semkey="lret", final=True)
        stats = P.build()
        nc_allow.__exit__(None, None, None)
        es.close()
        return nc, stats

    def s5_stage(full):
        TWO_PI = 2.0 * np.pi
        cur[0] = mark_ln
        hT_all = carve([128, KC, NTOK], BF16)
        for t in range(NPT):
            ln_block(xT[:, :, t * 128:(t + 1) * 128], ("xT", t // 4), 128, False, 0, out_ap=hT_all[:, :, t * 128:(t + 1) * 128], out_key="hT_all")
        ln_block(xT[:, :, TOK_P:TOK_P + 128], ("xT", 4), 128, True, 0, out_ap=hT_all[:, :, TOK_P:TOK_P + 128], out_key="hT_all")

        def t64():
            return carve([128, 64])
        AreT, AimT, dtT, ar, ai, rho, thr, ph64, ph512, sinT, cosT, tmpa, tmpb, lre, lim, fre, fim, rden64 = [t64() for _ in range(18)]
        PHB = carve([128, 64, 4])
        pib = carve([128, 1])
        Glast = carve([128, 64])
        Alast = carve([128, 64])
        Blast = carve([128, 64])
        tal = carve([128, 512])
        tbp = carve([128, 512])
        tbs = carve([128, 128])
        amask = carve([128, 128])
        maskg = carve([128, 8])
        rott = carve([128, 128])
        Dsk = carve([128, KC])
        Mc = carve([128, KC, 128], BF16)
        Mcsw = carve([128, KC, 128], BF16)
        CA = carve([128, 64, 128], BF16)
        CB = carve([128, 64, 128], BF16)
        mark_s5 = cur[0]
        Bn_re = carve([128, 64, 16])
        Bn_im = carve([128, 64, 16])
        tB1 = carve([128, 64, 16])
        tB2 = carve([128, 64, 16])
        Cn = carve([128, 16, 2, 64])
        Cn2 = carve([128, 16, 2, 64])
        CsA = carve([128, 64, 16])
        CsB = carve([128, 64, 16])
        print("arena words used (s5 prep):", cur[0], "of", ARENA_W)
        P.op("dve", lambda e: e.memset(pib, float(np.pi / 2)), writes=["pib"])
        for half in range(2):
            hs_ = slice(half * 64, half * 64 + 64)
            P.dma("sp", lambda e, hs_=hs_: e.dma_start(out=AreT[hs_, :], in_=A_re_d.rearrange("g p -> p g")), writes=["AreT"], semkey="ld_AreT", group="ld_AreT")
            P.dma("sp", lambda e, hs_=hs_: e.dma_start(out=AimT[hs_, :], in_=A_im_d.rearrange("g p -> p g")), writes=["AimT"], semkey="ld_AimT", group="ld_AimT")
        P.dma("sp", lambda e: e.dma_start(out=dtT, in_=ldt_d.partition_broadcast(128)), writes=["dtT"], semkey="ld_dtT")
        for nm, tl, dd in (("tal", tal, tal_d), ("tbp", tbp, tbp_d), ("tbs", tbs, tbs_d), ("amask", amask, amask_d), ("maskg", maskg, maskg_d), ("rott", rott, rott_d)):
            P.dma("sp", lambda e, tl=tl, dd=dd: e.dma_start(out=tl, in_=dd), writes=[nm], semkey="ld_" + nm)
        P.dma("sp", lambda e: e.dma_start(out=Dsk, in_=Dsk_d.rearrange("o (k p) -> p (o k)", p=128)), writes=["Dsk"], semkey="ld_Dsk")
        P.dma("sp", lambda e: e.dma_start(out=Bn_re[0:64], in_=B_re_d.rearrange("g p j -> p g j")), writes=["Bn_re"], semkey="ld_Bn_re")
        P.dma("sp", lambda e: e.dma_start(out=Bn_im[0:64], in_=B_im_d.rearrange("g p j -> p g j")), writes=["Bn_im"], semkey="ld_Bn_im")
        P.dma("sp", lambda e: e.dma_start(out=Cn[0:64, :, 0, :], in_=C_re_d), writes=["Cn"], semkey="ld_Cn", group="ld_Cn")
        P.dma("sp", lambda e: e.dma_start(out=Cn[0:64, :, 1, :], in_=C_im_d), writes=["Cn"], semkey="ld_Cn", group="ld_Cn")
        P.dma("sp", lambda e: e.dma_start(out=Cn2[0:64, :, 0, :], in_=C_im_d), writes=["Cn2"], semkey="ld_Cn2", group="ld_Cn2")
        P.dma("sp", lambda e: e.dma_start(out=Cn2[0:64, :, 1, :], in_=C_re_d), writes=["Cn2"], semkey="ld_Cn2", group="ld_Cn2")

        def V(fn, reads, writes, eng="dve"):
            P.op(eng, fn, reads=reads, writes=writes)

        V(lambda e: e.activation(out=dtT, in_=dtT, func=AF.Exp), ["dtT"], ["dtT"], "act")
        V(lambda e: e.tensor_tensor(out=ar, in0=AreT, in1=dtT, op=ALU.mult), ["AreT", "dtT"], ["ar"])
        V(lambda e: e.tensor_tensor(out=ai, in0=AimT, in1=dtT, op=ALU.mult), ["AimT", "dtT"], ["ai"])
        V(lambda e: e.activation(out=rho, in_=ar, func=AF.Exp), ["ar"], ["rho"], "act")
        ti64 = carve([128, 64], I32)
        aiT = t64()

        def fracr(out, okey, inp, ikey):
            V(lambda e: e.tensor_copy(out=ti64, in_=inp), [ikey], ["ti64"])
            V(lambda e: e.tensor_copy(out=tmpb, in_=ti64), ["ti64"], ["tmpb"])
            V(lambda e: e.tensor_tensor(out=out, in0=inp, in1=tmpb, op=ALU.subtract), [ikey, "tmpb"], [okey])

        V(lambda e: e.tensor_scalar(out=aiT, in0=ai, scalar1=float(1.0 / TWO_PI), scalar2=None, op0=ALU.mult), ["ai"], ["aiT"])
        fracr(thr, "thr", aiT, "aiT")
        V(lambda e: e.tensor_scalar(out=tmpa, in0=aiT, scalar1=64.0, scalar2=None, op0=ALU.mult), ["aiT"], ["tmpa"])
        fracr(ph64, "ph64", tmpa, "tmpa")
        V(lambda e: e.tensor_scalar(out=tmpa, in0=aiT, scalar1=512.0, scalar2=None, op0=ALU.mult), ["aiT"], ["tmpa"])
        fracr(ph512, "ph512", tmpa, "tmpa")
        for blk in range(4):
            V(lambda e, blk=blk: e.tensor_scalar(out=tmpa, in0=ph512, scalar1=float(blk), scalar2=None, op0=ALU.mult), ["ph512"], ["tmpa"])
            fracr(PHB[:, :, blk], "PHB", tmpa, "tmpa")

        def sincos(ang, akey, s_out, skey, c_out, ckey, tmp, tkey):
            V(lambda e: e.activation(out=s_out, in_=ang, func=AF.Sin, scale=TWO_PI), [akey], [skey], "act")
            V(lambda e: e.activation(out=tmp, in_=ang, func=AF.Abs), [akey], [tkey], "act")
            V(lambda e: e.activation(out=c_out, in_=tmp, func=AF.Sin, scale=-TWO_PI, bias=pib[:]), [tkey, "pib"], [ckey], "act")

        sincos(thr, "thr", sinT, "sinT", cosT, "cosT", tmpa, "tmpa")
        V(lambda e: e.tensor_tensor(out=lre, in0=rho, in1=cosT, op=ALU.mult), ["rho", "cosT"], ["lre"])
        V(lambda e: e.tensor_tensor(out=lim, in0=rho, in1=sinT, op=ALU.mult), ["rho", "sinT"], ["lim"])
        V(lambda e: e.tensor_scalar(out=tmpa, in0=lre, scalar1=-1.0, scalar2=None, op0=ALU.add), ["lre"], ["tmpa"])
        V(lambda e: e.tensor_tensor(out=rden64, in0=AreT, in1=AreT, op=ALU.mult), ["AreT"], ["rden64"])
        V(lambda e: e.tensor_tensor(out=tmpb, in0=AimT, in1=AimT, op=ALU.mult), ["AimT"], ["tmpb"])
        V(lambda e: e.tensor_tensor(out=rden64, in0=rden64, in1=tmpb, op=ALU.add), ["rden64", "tmpb"], ["rden64"])
        V(lambda e: e.reciprocal(out=rden64, in_=rden64), ["rden64"], ["rden64"])
        V(lambda e: e.tensor_tensor(out=fre, in0=tmpa, in1=AreT, op=ALU.mult), ["tmpa", "AreT"], ["fre"])
        V(lambda e: e.tensor_tensor(out=tmpb, in0=lim, in1=AimT, op=ALU.mult), ["lim", "AimT"], ["tmpb"])
        V(lambda e: e.tensor_tensor(out=fre, in0=fre, in1=tmpb, op=ALU.add), ["fre", "tmpb"], ["fre"])
        V(lambda e: e.tensor_tensor(out=fre, in0=fre, in1=rden64, op=ALU.mult), ["fre", "rden64"], ["fre"])
        V(lambda e: e.tensor_tensor(out=fim, in0=lim, in1=AreT, op=ALU.mult), ["lim", "AreT"], ["fim"])
        V(lambda e: e.tensor_tensor(out=tmpb, in0=tmpa, in1=AimT, op=ALU.mult), ["tmpa", "AimT"], ["tmpb"])
        V(lambda e: e.tensor_tensor(out=fim, in0=fim, in1=tmpb, op=ALU.subtract), ["fim", "tmpb"], ["fim"])
        V(lambda e: e.tensor_tensor(out=fim, in0=fim, in1=rden64, op=ALU.mult), ["fim", "rden64"], ["fim"])
        h64 = slice(0, 64)
        frb = fre[h64, :].unsqueeze(2).to_broadcast([64, 64, 16])
        fib = fim[h64, :].unsqueeze(2).to_broadcast([64, 64, 16])
        V(lambda e: e.tensor_tensor(out=tB1[h64], in0=Bn_re[h64], in1=frb, op=ALU.mult), ["Bn_re", "fre"], ["tB1"])
        V(lambda e: e.tensor_tensor(out=tB2[h64], in0=Bn_im[h64], in1=fib, op=ALU.mult), ["Bn_im", "fim"], ["tB2"])
        V(lambda e: e.tensor_tensor(out=tB1[h64], in0=tB1[h64], in1=tB2[h64], op=ALU.subtract), ["tB1", "tB2"], ["tB1"])
        V(lambda e: e.tensor_tensor(out=tB2[h64], in0=Bn_im[h64], in1=frb, op=ALU.mult), ["Bn_im", "fre"], ["tB2"])
        V(lambda e: e.tensor_tensor(out=Bn_im[h64], in0=Bn_re[h64], in1=fib, op=ALU.mult), ["Bn_re", "fim", "tB2"], ["Bn_im"])
        V(lambda e: e.tensor_tensor(out=tB2[h64], in0=tB2[h64], in1=Bn_im[h64], op=ALU.add), ["tB2", "Bn_im"], ["tB2"])
        ptr = bank(3, 0)
        for F in range(KC):
            P.op("pe", lambda e, F=F: e.transpose(ptr[:, 0:64], tB1[h64, 8 * F:8 * F + 8, :].rearrange("p a b -> p (a b)"), ident[0:64, 0:64]),
                 reads=["tB1", "ident"], writes=[bkey(3, 0)])
            P.op("pe", lambda e, F=F: e.transpose(ptr[:, 64:128], tB2[h64, 8 * F:8 * F + 8, :].rearrange("p a b -> p (a b)"), ident[0:64, 0:64]),
                 reads=["tB2", "ident"], writes=[bkey(3, 0)])
            V(lambda e, F=F: e.copy(out=Mc[:, F, :], in_=ptr[:, 0:128]), [bkey(3, 0)], ["Mc"], "act")
            V(lambda e, F=F: e.copy(out=Mcsw[:, F, 0:64], in_=ptr[:, 64:128]), [bkey(3, 0)], ["Mcsw"], "act")
            V(lambda e, F=F: e.mul(out=Mcsw[:, F, 64:128], in_=ptr[:, 0:64], mul=-1.0), [bkey(3, 0)], ["Mcsw"], "act")
        for (src, skey, dst, dkey) in ((Cn, "Cn", CsA, "CsA"), (Cn2, "Cn2", CsB, "CsB")):
            for ib in range(2):
                for ii in range(8):
                    i_ = ib * 8 + ii
                    P.op("pe", lambda e, src=src, i_=i_, ii=ii: e.transpose(ptr[:, ii * 64:(ii + 1) * 64], src[h64, i_, :, :].rearrange("p c q -> p (c q)"), ident[0:64, 0:64]),
                         reads=[skey, "ident"], writes=[bkey(3, 0)])
                V(lambda e, dst=dst, ib=ib: e.copy(out=dst[:, :, ib * 8:(ib + 1) * 8].rearrange("p g i -> p i g"), in_=ptr.rearrange("p (i g) -> p i g", i=8)),
                  [bkey(3, 0)], [dkey], "act")
        V(lambda e: e.tensor_scalar(out=CsA[64:128], in0=CsA[64:128], scalar1=-1.0, scalar2=None, op0=ALU.mult), ["CsA"], ["CsA"])
        V(lambda e: e.tensor_scalar(out=CsB, in0=CsB, scalar1=-1.0, scalar2=None, op0=ALU.mult), ["CsB"], ["CsB"])
        V(lambda e: e.memset(CA, 0.0), [], ["CA"], "pool")
        V(lambda e: e.memset(CB, 0.0), [], ["CB"], "pool")
        for gl in range(8):
            for (src, skey, dst, dkey) in ((CsA, "CsA", CA, "CA"), (CsB, "CsB", CB, "CB")):
                V(lambda e, src=src, dst=dst, gl=gl: e.tensor_copy(out=dst.rearrange("p (f g) c -> p f g c", g=8)[:, :, gl, 16 * gl:16 * gl + 16],
                                                                   in_=src.rearrange("p (f g) i -> p f g i", g=8)[:, :, gl, :]),
                  [skey], [dkey])

        hin = carve([128, 64]) if False else None
        P.op("dve", lambda e: e.memset(Glast, 0.0), writes=["Glast"])
        if full:
            ang = tmpa
            V(lambda e: e.tensor_scalar(out=ang, in0=aiT, scalar1=2048.0, scalar2=None, op0=ALU.mult), ["aiT"], ["tmpa"])
            fracr(fim, "fim", ang, "tmpa")
            sincos(fim, "fim", sinT, "sinT", cosT, "cosT", fre, "fre")
            V(lambda e: e.activation(out=lre, in_=ar, func=AF.Exp, scale=2048.0), ["ar"], ["lre"], "act")
            V(lambda e: e.tensor_tensor(out=lim, in0=lre, in1=sinT, op=ALU.mult), ["lre", "sinT"], ["lim"])
            V(lambda e: e.tensor_tensor(out=lre, in0=lre, in1=cosT, op=ALU.mult), ["lre", "cosT"], ["lre"])
            wsel = carve([128, 8])
            fr = [carve([128, 64]) for _ in range(2)]
            P.dma("sp", lambda e: e.dma_start(out=wsel, in_=wsel_d), writes=["wsel"], semkey="ld_wsel")
            prot = bank(3, 1)[:, 0:64]
            for r in range(7):
                s_ = r % 2
                P.dma("sp", lambda e, r=r, s_=s_: e.dma_start(out=fr[s_], in_=fall_d[r]), writes=[f"fr{s_}"], semkey=f"fr{s_}")
                P.op("pe", lambda e: e.matmul(prot, lhsT=rott, rhs=Glast, start=True, stop=True), reads=["rott", "Glast"], writes=[bkey(3, 1)])
                V(lambda e: e.tensor_tensor(out=tmpa, in0=lre, in1=Glast, op=ALU.mult), ["lre", "Glast"], ["tmpa"])
                V(lambda e: e.tensor_tensor(out=tmpb, in0=lim, in1=prot, op=ALU.mult), ["lim", bkey(3, 1)], ["tmpb"])
                V(lambda e: e.tensor_tensor(out=tmpa, in0=tmpa, in1=tmpb, op=ALU.add), ["tmpa", "tmpb"], ["tmpa"])
                V(lambda e, s_=s_: e.tensor_tensor(out=tmpa, in0=tmpa, in1=fr[s_], op=ALU.add), ["tmpa", f"fr{s_}"], ["tmpa"])
                V(lambda e: e.tensor_tensor(out=tmpa, in0=tmpa, in1=Glast, op=ALU.subtract), ["tmpa", "Glast"], ["tmpa"])
                V(lambda e, r=r: e.scalar_tensor_tensor(out=Glast, in0=tmpa, scalar=wsel[:, r:r + 1], in1=Glast, op0=ALU.mult, op1=ALU.add),
                  ["tmpa", "wsel", "Glast"], ["Glast"])
        P.barrier()
        cur[0] = mark_s5
        NW = 2
        um = [carve([128, 512], BF16) for _ in range(NW)]
        xang = [carve([128, 512]) for _ in range(NW)]
        sang = [carve([128, 512]) for _ in range(NW)]
        SINt = [carve([128, 512]) for _ in range(NW)]
        COSt = [carve([128, 512]) for _ in range(NW)]
        Wt = [carve([128, 512]) for _ in range(NW)]
        Ab = [carve([128, 512], BF16) for _ in range(NW)]
        Bb = [carve([128, 512], BF16) for _ in range(NW)]
        aseq = carve([128, 128])
        ysb = carve([128, 512])
        y2 = carve([128, 512])
        H0T = carve([128, 64, 16])
        Asl = carve([128, 64, 16])
        Bsl = carve([128, 64, 16])
        h0n = carve([128, 64, 128]) if False else None
        print("arena words used (s5 main):", cur[0], "of", ARENA_W)
        BU = [(bank(0, 0), bkey(0, 0), bank(0, 1), bkey(0, 1)), (bank(1, 0), bkey(1, 0), bank(1, 1), bkey(1, 1))]
        YP = [(bank(2, 0), bkey(2, 0)), (bank(2, 1), bkey(2, 1))]
        wi = [0]

        def s5_T(F, gl, t0, nt, blk, sample):
            g = 8 * F + gl
            w = wi[0] % NW
            wi[0] += 1
            bu, bukey, bus, buskey = BU[w % 2]
            V(lambda e: e.activation(out=um[w][:, :nt], in_=hT_all[:, F, t0:t0 + nt], func=AF.Copy, scale=maskg[:, gl:gl + 1]),
              ["hT_all", "maskg"], [f"um{w}"], "act")
            P.op("pe", lambda e: e.matmul(bu[:, :nt], lhsT=Mc[:, F, :], rhs=um[w][:, :nt], start=True, stop=True), reads=["Mc", f"um{w}"], writes=[bukey])
            P.op("pe", lambda e: e.matmul(bus[:, :nt], lhsT=Mcsw[:, F, :], rhs=um[w][:, :nt], start=True, stop=True), reads=["Mcsw", f"um{w}"], writes=[buskey])
            MAGIC = 12582912.0
            if not sample:
                V(lambda e: e.activation(out=xang[w], in_=tal, func=AF.Identity, scale=ph64[:, g:g + 1], bias=PHB[:, g, blk:blk + 1]),
                  ["tal", "ph64", "PHB"], [f"xang{w}"], "act")
                V(lambda e: e.scalar_tensor_tensor(out=xang[w], in0=tbp, scalar=thr[:, g:g + 1], in1=xang[w], op0=ALU.mult, op1=ALU.add),
                  ["tbp", "thr", f"xang{w}"], [f"xang{w}"])
            else:
                V(lambda e: e.activation(out=xang[w][:, :nt], in_=tbs, func=AF.Copy, scale=thr[:, g:g + 1]), ["tbs", "thr"], [f"xang{w}"], "act")
            V(lambda e: e.tensor_scalar(out=sang[w][:, :nt], in0=xang[w][:, :nt], scalar1=MAGIC, scalar2=MAGIC, op0=ALU.add, op1=ALU.subtract),
              [f"xang{w}"], [f"sang{w}"])
            V(lambda e: e.tensor_tensor(out=xang[w][:, :nt], in0=xang[w][:, :nt], in1=sang[w][:, :nt], op=ALU.subtract), [f"xang{w}", f"sang{w}"], [f"xang{w}"])
            V(lambda e: e.activation(out=SINt[w][:, :nt], in_=xang[w][:, :nt], func=AF.Sin, scale=TWO_PI), [f"xang{w}"], [f"SINt{w}"], "act")
            V(lambda e: e.activation(out=sang[w][:, :nt], in_=xang[w][:, :nt], func=AF.Abs), [f"xang{w}"], [f"sang{w}"], "act")
            V(lambda e: e.activation(out=COSt[w][:, :nt], in_=sang[w][:, :nt], func=AF.Sin, scale=-TWO_PI, bias=pib[:]), [f"sang{w}", "pib"], [f"COSt{w}"], "act")
            return dict(F=F, gl=gl, g=g, w=w, t0=t0, nt=nt, blk=blk, sample=sample, bu=bu, bukey=bukey, bus=bus, buskey=buskey)

        def s5_S(c, ypk, last_blk):
            F, gl, g, w, t0, nt, blk, sample = c['F'], c['gl'], c['g'], c['w'], c['t0'], c['nt'], c['blk'], c['sample']
            bu, bukey, bus, buskey = c['bu'], c['bukey'], c['bus'], c['buskey']
            yp, ykey = ypk
            V(lambda e: e.tensor_tensor(out=sang[w][:, :nt], in0=bu[:, :nt], in1=COSt[w][:, :nt], op=ALU.mult), [bukey, f"COSt{w}"], [f"sang{w}"])
            V(lambda e: e.tensor_tensor(out=Wt[w][:, :nt], in0=bus[:, :nt], in1=SINt[w][:, :nt], op=ALU.mult), [buskey, f"SINt{w}"], [f"Wt{w}"])
            V(lambda e: e.tensor_tensor(out=Wt[w][:, :nt], in0=Wt[w][:, :nt], in1=sang[w][:, :nt], op=ALU.add), [f"Wt{w}", f"sang{w}"], [f"Wt{w}"])
            if not sample:
                V(lambda e: e.tensor_tensor_scan(out=Wt[w], data0=rho[:, g:g + 1].to_broadcast([128, 512]), data1=Wt[w], initial=Glast[:, g:g + 1],
                                                 op0=ALU.mult, op1=ALU.add), [f"Wt{w}", "rho", "Glast"], [f"Wt{w}"])
                V(lambda e: e.copy(out=Glast[:, g:g + 1], in_=Wt[w][:, 511:512]), [f"Wt{w}"], ["Glast"], "act")
                if last_blk:
                    V(lambda e: e.tensor_tensor(out=Alast[:, g:g + 1], in0=Wt[w][:, 511:512], in1=COSt[w][:, 511:512], op=ALU.mult), [f"Wt{w}", f"COSt{w}"], ["Alast"])
                    V(lambda e: e.tensor_tensor(out=Blast[:, g:g + 1], in0=Wt[w][:, 511:512], in1=SINt[w][:, 511:512], op=ALU.mult), [f"Wt{w}", f"SINt{w}"], ["Blast"])
            else:
                wv = Wt[w][:, :128].rearrange("p (b t) -> p b t", t=8)
                V(lambda e: e.scalar_tensor_tensor(out=wv[:, :, 0], in0=H0T[:, g, :], scalar=rho[:, g:g + 1], in1=wv[:, :, 0], op0=ALU.mult, op1=ALU.add),
                  ["H0T", "rho", f"Wt{w}"], [f"Wt{w}"])
                V(lambda e: e.tensor_scalar(out=aseq, in0=amask, scalar1=rho[:, g:g + 1], scalar2=None, op0=ALU.mult), ["amask", "rho"], ["aseq"])
                V(lambda e: e.tensor_tensor_scan(out=Wt[w][:, :128], data0=aseq, data1=Wt[w][:, :128], initial=0.0, op0=ALU.mult, op1=ALU.add),
                  [f"Wt{w}", "aseq"], [f"Wt{w}"])
                cv = COSt[w][:, :128].rearrange("p (b t) -> p b t", t=8)
                sv = SINt[w][:, :128].rearrange("p (b t) -> p b t", t=8)
                V(lambda e: e.tensor_tensor(out=Asl[:, g, :], in0=wv[:, :, 7], in1=cv[:, :, 7], op=ALU.mult), [f"Wt{w}", f"COSt{w}"], ["Asl"])
                V(lambda e: e.tensor_tensor(out=Bsl[:, g, :], in0=wv[:, :, 7], in1=sv[:, :, 7], op=ALU.mult), [f"Wt{w}", f"SINt{w}"], ["Bsl"])
            if full:
                V(lambda e: e.tensor_tensor(out=Ab[w][:, :nt], in0=Wt[w][:, :nt], in1=COSt[w][:, :nt], op=ALU.mult), [f"Wt{w}", f"COSt{w}"], [f"Ab{w}"])
                V(lambda e: e.tensor_tensor(out=Bb[w][:, :nt], in0=Wt[w][:, :nt], in1=SINt[w][:, :nt], op=ALU.mult), [f"Wt{w}", f"SINt{w}"], [f"Bb{w}"], "pool")
                P.op("pe", lambda e: e.matmul(yp[:, :nt], lhsT=CA[:, g, :], rhs=Ab[w][:, :nt], start=(gl == 0), stop=False), reads=["CA", f"Ab{w}"], writes=[ykey])
                P.op("pe", lambda e: e.matmul(yp[:, :nt], lhsT=CB[:, g, :], rhs=Bb[w][:, :nt], start=False, stop=(gl == 7)), reads=["CB", f"Bb{w}"], writes=[ykey])

        def y_finish(F, t0, nt, ypk):
            yp, ykey = ypk
            V(lambda e: e.scalar_tensor_tensor(out=ysb[:, :nt], in0=hT_all[:, F, t0:t0 + nt], scalar=Dsk[:, F:F + 1], in1=yp[:, :nt], op0=ALU.mult, op1=ALU.add),
              ["hT_all", "Dsk", ykey], ["ysb"])
            V(lambda e: e.tensor_tensor(out=y2[:, :nt], in0=ysb[:, :nt], in1=ysb[:, :nt], op=ALU.mult), ["ysb"], ["y2"], "pool")
            V(lambda e: e.tensor_scalar(out=y2[:, :nt], in0=y2[:, :nt], scalar1=0.044715, scalar2=1.0, op0=ALU.mult, op1=ALU.add), ["y2"], ["y2"], "pool")
            V(lambda e: e.tensor_tensor(out=y2[:, :nt], in0=y2[:, :nt], in1=ysb[:, :nt], op=ALU.mult), ["y2", "ysb"], ["y2"], "pool")
            V(lambda e: e.activation(out=y2[:, :nt], in_=y2[:, :nt], func=AF.Tanh, scale=float(np.sqrt(2.0 / np.pi))), ["y2"], ["y2"], "act")
            V(lambda e: e.tensor_scalar(out=y2[:, :nt], in0=y2[:, :nt], scalar1=0.5, scalar2=0.5, op0=ALU.mult, op1=ALU.add), ["y2"], ["y2"])
            V(lambda e: e.tensor_tensor(out=hT_all[:, F, t0:t0 + nt], in0=y2[:, :nt], in1=ysb[:, :nt], op=ALU.mult), ["y2", "ysb"], ["hT_all"])

        if True:
            h0n = um
        s0n = carve([128, 64, 128]) if False else None
        hnat = carve([16, 64 * 128]) if False else None
        stg = xang[0]
        pth = bank(3, 0)
        for gq_ in range(16 if full else 0):
            P.dma("sp", lambda e, gq_=gq_: e.dma_start(out=stg[0:16, :].rearrange("p (g c) -> p g c", g=4)[:, :, 0:64], in_=s5r_in[:, 4 * gq_:4 * gq_ + 4, :]),
                  writes=["xang0"], semkey="stg", group=f"stg{gq_}")
            P.dma("sp", lambda e, gq_=gq_: e.dma_start(out=stg[0:16, :].rearrange("p (g c) -> p g c", g=4)[:, :, 64:128], in_=s5i_in[:, 4 * gq_:4 * gq_ + 4, :]),
                  writes=["xang0"], semkey="stg", group=f"stg{gq_}")
            for gg in range(4):
                P.op("pe", lambda e, gg=gg: e.transpose(pth[:, gg * 16:(gg + 1) * 16], stg[0:16, gg * 128:(gg + 1) * 128], ident[0:16, 0:16]),
                     reads=["xang0", "ident"], writes=[bkey(3, 0)])
            V(lambda e, gq_=gq_: e.copy(out=H0T[:, 4 * gq_:4 * gq_ + 4, :], in_=pth[:, 0:64].rearrange("p (g b) -> p g b", g=4)), [bkey(3, 0)], ["H0T"], "act")

        units = []
        yi = 0
        for F in range(KC):
            for blk in range(4):
                ypk = YP[yi % 2]
                yi += 1
                for gl in range(8):
                    units.append(dict(F=F, gl=gl, t0=blk * 512, nt=512, blk=blk, sample=False, ypk=ypk, last=(blk == 3), fin=(gl == 7)))
            if full:
                ypk = YP[yi % 2]
                yi += 1
                for gl in range(8):
                    units.append(dict(F=F, gl=gl, t0=TOK_P, nt=128, blk=0, sample=True, ypk=ypk, last=False, fin=(gl == 7)))
        def cap(fn_, *a_):
            P.capture = []
            r_ = fn_(*a_)
            ops_ = P.capture
            P.capture = None
            return ops_, r_

        def emit(ops_):
            for o_ in ops_:
                P.op(*o_)

        u0 = units[0]
        opsT, ctx = cap(s5_T, u0["F"], u0["gl"], u0["t0"], u0["nt"], u0["blk"], u0["sample"])
        emit(opsT)
        for n, u in enumerate(units):
            opsS, _ = cap(s5_S, ctx, u["ypk"], u["last"])
            if n + 1 < len(units):
                v = units[n + 1]
                opsT, nxt = cap(s5_T, v["F"], v["gl"], v["t0"], v["nt"], v["blk"], v["sample"])
            else:
                opsT, nxt = [], None
            pre = opsT[:4]
            rest = opsT[4:]
            k_ = 0
            while k_ < len(rest) and rest[k_][0] == "dve":
                k_ += 1
            t_dve, t_tail = rest[:k_], rest[k_:]
            seq = list(pre)
            for j_ in range(3):
                seq.append(opsS[j_])
                if j_ < len(t_dve):
                    seq.append(t_dve[j_])
            seq += t_dve[3:] + t_tail + opsS[3:]
            assert len(seq) == len(opsT) + len(opsS)
            emit(seq)
            if full and u["fin"]:
                y_finish(u["F"], u["t0"], u["nt"], u["ypk"])
            ctx = nxt

        pfin = bank(3, 1)[:, 0:64]
        P.op("pe", lambda e: e.matmul(pfin, lhsT=rott, rhs=Blast, start=True, stop=True), reads=["rott", "Blast"], writes=[bkey(3, 1)])
        V(lambda e: e.tensor_tensor(out=Alast, in0=Alast, in1=pfin, op=ALU.add), ["Alast", bkey(3, 1)], ["Alast"])
        if not full:
            P.dma("sp", lambda e: e.dma_start(out=floc_out, in_=Alast), reads=["Alast"], semkey="floc", final=True)
            return None
        ptp = bank(3, 0)[0:64, 0:128]
        P.op("pe", lambda e: e.transpose(ptp, Alast, ident[:]), reads=["Alast", "ident"], writes=[bkey(3, 0)])
        V(lambda e: e.copy(out=ysb[0:64, 0:128], in_=ptp), [bkey(3, 0)], ["ysb"], "act")
        P.dma("sp", lambda e: e.dma_start(out=ps5r_out, in_=ysb[0:64, 0:64]), reads=["ysb"], semkey="ps5", final=True)
        P.dma("sp", lambda e: e.dma_start(out=ps5i_out, in_=ysb[0:64, 64:128]), reads=["ysb"], semkey="ps5", final=True)
        for hf in range(2):
            pf2 = bank(3, 1)
            P.op("pe", lambda e, hf=hf: e.matmul(pf2, lhsT=rott, rhs=Bsl[:, 32 * hf:32 * hf + 32, :].rearrange("p g b -> p (g b)"), start=True, stop=True),
                 reads=["rott", "Bsl"], writes=[bkey(3, 1)])
            V(lambda e, hf=hf: e.tensor_tensor(out=Asl[:, 32 * hf:32 * hf + 32, :].rearrange("p g b -> p (g b)"), in0=Asl[:, 32 * hf:32 * hf + 32, :].rearrange("p g b -> p (g b)"),
                                               in1=pf2, op=ALU.add), ["Asl", bkey(3, 1)], ["Asl"])
        for gq_ in range(16):
            pto = bank(3, 0)[0:16, :]
            for gg in range(4):
                P.op("pe", lambda e, gq_=gq_, gg=gg: e.transpose(pto[:, gg * 128:(gg + 1) * 128], Asl[:, 4 * gq_ + gg, :], ident[:]),
                     reads=["Asl", "ident"], writes=[bkey(3, 0)])
            V(lambda e: e.copy(out=stg[0:16, :], in_=pto), [bkey(3, 0)], ["xang0"], "act")
            P.dma("sp", lambda e, gq_=gq_: e.dma_start(out=ss5r_out[:, 4 * gq_:4 * gq_ + 4, :], in_=stg[0:16, :].rearrange("p (g c) -> p g c", g=4)[:, :, 0:64]),
                  reads=["xang0"], semkey="ss5o", final=True)
            P.dma("sp", lambda e, gq_=gq_: e.dma_start(out=ss5i_out[:, 4 * gq_:4 * gq_ + 4, :], in_=stg[0:16, :].rearrange("p (g c) -> p g c", g=4)[:, :, 64:128]),
                  reads=["xang0"], semkey="ss5o", final=True)

        P.barrier()
        cur[0] = mark_ln + NTOK * KC // 2
        glua = carve([128, KC, D], BF16)
        glub = carve([128, KC, D], BF16)
        sgb = [carve([128, 512]) for _ in range(2)]
        prd = [carve([128, 512]) for _ in range(2)]
        gt3 = carve([128, 128])
        P.dma("pool", lambda e: e.dma_start(out=glua, in_=glua_d.rearrange("(k p) n -> p k n", p=128)), writes=["glua"], semkey="glua")
        P.dma("pool", lambda e: e.dma_start(out=glub, in_=glub_d.rearrange("(k p) n -> p k n", p=128)), writes=["glub"], semkey="glub")
        GA = [(bank(0, 0), bkey(0, 0), bank(0, 1), bkey(0, 1)), (bank(1, 0), bkey(1, 0), bank(1, 1), bkey(1, 1))]
        gi = 0
        for (t0, nt, smp) in [(b * 512, 512, False) for b in range(4)] + [(TOK_P, 128, True)]:
            for m in range(KC):
                pa, pak, pb, pbk = GA[gi % 2]
                w_ = gi % 2
                gi += 1
                for k in range(KC):
                    P.op("pe", lambda e, k=k, m=m, pa=pa, t0=t0, nt=nt: e.matmul(pa[:, :nt], lhsT=glua[:, k, m * 128:(m + 1) * 128], rhs=hT_all[:, k, t0:t0 + nt],
                                                                                  start=(k == 0), stop=(k == KC - 1)), reads=["glua", "hT_all"], writes=[pak])
                for k in range(KC):
                    P.op("pe", lambda e, k=k, m=m, pb=pb, t0=t0, nt=nt: e.matmul(pb[:, :nt], lhsT=glub[:, k, m * 128:(m + 1) * 128], rhs=hT_all[:, k, t0:t0 + nt],
                                                                                  start=(k == 0), stop=(k == KC - 1)), reads=["glub", "hT_all"], writes=[pbk])
                V(lambda e, pb=pb, w_=w_, nt=nt: e.activation(out=sgb[w_][:, :nt], in_=pb[:, :nt], func=AF.Sigmoid), [pbk], [f"sgb{w_}"], "act")
                V(lambda e, pa=pa, w_=w_, nt=nt: e.tensor_tensor(out=prd[w_][:, :nt], in0=pa[:, :nt], in1=sgb[w_][:, :nt], op=ALU.mult), [pak, f"sgb{w_}"], [f"prd{w_}"])
                xs = xT[:, m, t0:t0 + nt]
                xkey = ("xT", t0 // 512)
                if not smp:
                    V(lambda e, m=m, w_=w_, xs=xs, nt=nt: e.scalar_tensor_tensor(out=xs, in0=prd[w_][:, :nt], scalar=modT[:, 16 + m, 0:1], in1=xs, op0=ALU.mult, op1=ALU.add),
                      [f"prd{w_}", "modT", xkey], [xkey])
                else:
                    gv = gt3.rearrange("p (b t) -> p b t", t=8)
                    V(lambda e, m=m, w_=w_, gv=gv: e.tensor_tensor(out=gv, in0=prd[w_][:, :128].rearrange("p (b t) -> p b t", t=8),
                                                                  in1=modT[:, 16 + m, 1:17].unsqueeze(2).to_broadcast([128, 16, 8]), op=ALU.mult), [f"prd{w_}", "modT"], ["gt3"])
                    V(lambda e, xs=xs: e.tensor_tensor(out=xs, in0=xs, in1=gt3, op=ALU.add), ["gt3", xkey], [xkey])
        ffn(1)
        P.barrier()
        cur[0] = mark_ln
        yst = [carve([128, D]) for _ in range(2)]
        for t in range(17):
            s = t % 2
            for k in range(KC):
                P.op("pe", lambda e, k=k, t=t, s=s: e.transpose(pT[s][:, k, :], xT[:, k, t * 128:(t + 1) * 128], ident[:]),
                     reads=[("xT", t // 4), "ident"], writes=[bkey(s, 0), bkey(s, 1)])
            P.op("act", lambda e, s=s: e.copy(out=yst[s], in_=Q[s][:, :]), reads=[bkey(s, 0), bkey(s, 1)], writes=[f"yst{s}"])
            P.dma("sp", lambda e, t=t, s=s: e.dma_start(out=y_out[t * 128:(t + 1) * 128, :], in_=yst[s]), reads=[f"yst{s}"], semkey=f"yst{s}", final=True)
        stats = P.build()
        nc_allow.__exit__(None, None, None)
        es.close()
        return nc, stats

    if stage in ("C1", "C2"):
        r_ = s5_stage(stage == "C2")
        if r_ is not None:
            return r_
        stats = P.build()
        nc_allow.__exit__(None, None, None)
        es.close()
        return nc, stats

    hTh = carve([128, KC, 128], BF16)
    mark1 = cur[0]
    lst = [carve([128, 4, 128]) for _ in range(2)]
    wret = carve([128, 8, 4])
    xTh = carve([128, KC, 128])
    xst[0] = carve([128, D])
    load_tile(0, xTh, "xTh")
    P.dma("sp", lambda e: e.dma_start(out=wret, in_=wret_d), writes=["wret"], semkey="wret")
    P.op("dve", lambda e: e.memset(S, 0.0), writes=["S"])
    for r in range(8):
        s = r % 2
        P.dma("sp", lambda e, r=r, s=s: e.dma_start(out=lst[s], in_=lall_d[r].rearrange("h d e -> d h e")), writes=[f"lst{s}"], semkey=f"lst{s}")
        P.op("dve", lambda e, r=r, s=s: e.tensor_tensor(out=lst[s], in0=lst[s], in1=wret[:, r, :].unsqueeze(2).to_broadcast([128, 4, 128]), op=ALU.mult),
             reads=[f"lst{s}", "wret"], writes=[f"lst{s}"])
        P.op("dve", lambda e, s=s: e.tensor_tensor(out=S, in0=S, in1=lst[s], op=ALU.add), reads=["S", f"lst{s}"], writes=["S"])
    P.op("act", lambda e: e.copy(out=Sb, in_=S), reads=["S"], writes=["Sb"])
    ln_block(xTh, "xTh", 128, False, 0)
    P.op("pool", lambda e: e.tensor_copy(out=hTh, in_=hT[:, :, 0:128]), reads=["hT"], writes=["hTh"])
    P.barrier()
    cur[0] = mark1

    wkd = carve([128, KC, 2, 128], BF16)
    wvd = carve([128, KC, 2, 128], BF16)
    for kv in range(2):
        for half in range(2):
            P.op("pool", lambda e, kv=kv, half=half: e.tensor_copy(out=wkd[:, :, kv, half * 64:(half + 1) * 64], in_=win[:, :, 512 + 64 * kv:576 + 64 * kv]),
                 reads=["win"], writes=["wkd"])
            P.op("pool", lambda e, kv=kv, half=half: e.tensor_copy(out=wvd[:, :, kv, half * 64:(half + 1) * 64], in_=win[:, :, 640 + 64 * kv:704 + 64 * kv]),
                 reads=["win"], writes=["wvd"])
    wout = carve([128, KC, D], BF16)
    P.dma("pool", lambda e: e.dma_start(out=wout, in_=w_out.rearrange("(k p) n -> p k n", p=128)), writes=["wout"], semkey="wout")
    gq = carve([128, 1])
    gk = carve([128, 1])
    bd_f = carve([128, 128])
    bd = carve([128, 128], BF16)
    dneg = carve([128, 5, 128])
    rmask = carve([128, 4, 128])
    dq = carve([128, 4, 128])
    seqmask = carve([128, 16])
    esink = carve([128, 8])
    retg = carve([128, 4])
    for half in range(2):
        P.dma("sp", lambda e, half=half: e.dma_start(out=gq[half * 64:(half + 1) * 64, :], in_=qgain.rearrange("o d -> d o")), writes=["gq"], semkey="ld_gq", group="ld_gq")
        P.dma("sp", lambda e, half=half: e.dma_start(out=gk[half * 64:(half + 1) * 64, :], in_=kgain.rearrange("o d -> d o")), writes=["gk"], semkey="ld_gk", group="ld_gk")
    P.dma("sp", lambda e: e.dma_start(out=bd_f, in_=bd_d), writes=["bd_f"], semkey="ld_bd_f", group="ld_bd_f")
    P.dma("sp", lambda e: e.dma_start(out=dneg, in_=dneg_d), writes=["dneg"], semkey="ld_dneg", group="ld_dneg")
    P.dma("sp", lambda e: e.dma_start(out=rmask, in_=rmask_d[:, 0]), writes=["rmask"], semkey="ld_rmask", group="ld_rmask")
    P.dma("sp", lambda e: e.dma_start(out=dq, in_=dq_d[:, 0]), writes=["dq"], semkey="ld_dq", group="ld_dq")
    P.dma("sp", lambda e: e.dma_start(out=seqmask, in_=seqmask_d), writes=["seqmask"], semkey="ld_seqmask", group="ld_seqmask")
    P.dma("sp", lambda e: e.dma_start(out=esink, in_=sinks_d.partition_broadcast(128)), writes=["esink"], semkey="ld_esink", group="ld_esink")
    P.dma("sp", lambda e: e.dma_start(out=retg, in_=retg_d.rearrange("o (h e) -> e (o h)", h=4)), writes=["retg"], semkey="ld_retg", group="ld_retg")
    P.op("dve", lambda e: e.tensor_copy(out=bd, in_=bd_f), reads=["bd_f"], writes=["bd"])
    P.op("act", lambda e: e.activation(out=esink, in_=esink, func=AF.Exp), reads=["esink"], writes=["esink"])
    P.op("dve", lambda e: e.tensor_scalar(out=gq, in0=gq, scalar1=0.125, scalar2=None, op0=ALU.mult), reads=["gq"], writes=["gq"])

    qnT = carve([128, 4, BT], BF16)
    kdT = carve([128, 2, 128 + BT], BF16)
    qbT = carve([128, 4, BT], BF16)
    qdT = carve([128, 4, BT], BF16)
    kbT = carve([128, 4, BT], BF16)
    sgT = carve([128, 4, BT], BF16)
    vd_tok = carve([128, 1 + TPB, 256], BF16)
    oT = carve([128, KC, BT], BF16)
    qsq = carve([128, max(BT, 256)], BF16)
    qrs = carve([128, max(BT, 256)])
    sc_sb = carve([128, 2, 2, 2, 128])
    pTt = carve([128, 2, 2, 2, 128], BF16)
    rden = carve([128, 2, 2, 128])
    innT = carve([128, 4, 128], BF16)
    o32 = carve([128, 4, 128])
    obf = carve([128, 4, 128], BF16)
    osq = carve([128, 4, 128], BF16)
    t1 = carve([128, 4, 128])
    t2 = o32
    kv32 = carve([128, 2, 128])
    knT = carve([128, 128])
    gtmp = carve([128, 128])
    cacheT = carve([128, 16, 128], BF16)
    vcache = carve([128, 16, 128], BF16)
    kst = carve([128, 4, 128])
    S0 = [carve([128, 4, 128]) for _ in range(2)]
    S0b = [carve([128, 4, 128], BF16) for _ in range(2)]
    kdm = [carve([128, 512], BF16) for _ in range(2)]
    mark2 = cur[0]
    print("arena words used (mixer):", cur[0], "of", ARENA_W)

    PJ = [bank(0, 0), bank(0, 1)]
    PJK = [bkey(0, 0), bkey(0, 1)]
    p_st = bank(1, 0)
    pj_i = [0]

    def proj_fm(lhs_fn, ntok, evac, src=None):
        s = pj_i[0] % 2
        pj_i[0] += 1
        src = hT if src is None else src
        for k in range(KC):
            P.op("pe", lambda e, k=k, s=s: e.matmul(PJ[s][:, :ntok], lhsT=lhs_fn(k), rhs=src[:, k, :ntok], start=(k == 0), stop=(k == KC - 1)),
                 reads=["hT", "hTh", "win", "wkd"], writes=[PJK[s]])
        evac(PJ[s][:, :ntok], PJK[s])

    def qknorm(psum, pkey, ntok, gain, gkey, out_ap, out_key):
        P.op("act", lambda e: e.activation(out=qsq[:, :ntok], in_=psum, func=AF.Square), reads=[pkey], writes=["qsq"])
        P.op("pe", lambda e: e.matmul(p_st[:, :ntok], lhsT=bd, rhs=qsq[:, :ntok], start=True, stop=True), reads=["bd", "qsq"], writes=[bkey(1, 0)])
        P.op("act", lambda e: e.activation(out=qrs[:, :ntok], in_=p_st[:, :ntok], func=AF.Sqrt, bias=epsb[:], scale=1.0), reads=[bkey(1, 0), "epsb"], writes=["qrs"])
        P.op("dve", lambda e: e.reciprocal(out=qrs[:, :ntok], in_=qrs[:, :ntok]), reads=["qrs"], writes=["qrs"])
        P.op("dve", lambda e: e.scalar_tensor_tensor(out=out_ap, in0=psum, scalar=gain[:, 0:1], in1=qrs[:, :ntok], op0=ALU.mult, op1=ALU.mult),
             reads=[pkey, "qrs", gkey], writes=[out_key])

    def project_block(ntok, grp):
        for j in range(4):
            proj_fm(lambda k, j=j: win[:, k, j * 128:(j + 1) * 128], ntok,
                    lambda ps_, key, j=j: qknorm(ps_, key, ntok, gq, "gq", qnT[:, j, :ntok], "qnT"))
        if KSUB < 2:
            return
        for kv in range(2):
            proj_fm(lambda k, kv=kv: wkd[:, k, kv, :], ntok,
                    lambda ps_, key, kv=kv: qknorm(ps_, key, ntok, gk, "gk", kdT[:, kv, 128:128 + ntok], "kdT"))
        if KSUB < 3:
            return
        for h in range(4):
            def ev_q(ps_, key, h=h):
                P.op("act", lambda e: e.copy(out=qbT[:, h, :ntok], in_=ps_), reads=[key], writes=["qbT"])
                P.op("dve", lambda e: e.tensor_tensor(out=qdT[:, h, :ntok].rearrange("p (t i) -> p t i", i=128), in0=ps_.rearrange("p (t i) -> p t i", i=128),
                                                      in1=dq[:, h, :].unsqueeze(1).to_broadcast([128, ntok // 128, 128]), op=ALU.mult),
                     reads=[key, "dq"], writes=["qdT"])
            proj_fm(lambda k, h=h: win[:, k, 768 + h * 128:768 + (h + 1) * 128], ntok, ev_q)
        if KSUB < 4:
            return
        for h in range(4):
            proj_fm(lambda k, h=h: win[:, k, 1280 + h * 128:1280 + (h + 1) * 128], ntok,
                    lambda ps_, key, h=h: P.op("act", lambda e: e.mul(out=kbT[:, h, :ntok], in_=ps_, mul=128.0 ** -0.5), reads=[key], writes=["kbT"]))
        if KSUB < 5:
            return
        for h in range(4):
            proj_fm(lambda k, h=h: win[:, k, 2304 + h * 128:2304 + (h + 1) * 128], ntok,
                    lambda ps_, key, h=h: P.op("act", lambda e: e.activation(out=sgT[:, h, :ntok], in_=ps_, func=AF.Silu), reads=[key], writes=["sgT"]))

    p_vd = bank(1, 1)[:, 0:256]

    def tok_vd(tc0, slot, src=None):
        src = hT if src is None else src
        for k in range(KC):
            P.op("pe", lambda e, k=k: e.matmul(p_vd, lhsT=src[:, k, tc0:tc0 + 128], rhs=wvd[:, k, :, :].rearrange("p a b -> p (a b)"), start=(k == 0), stop=(k == KC - 1)),
                 reads=["hT", "hTh", "wvd"], writes=[bkey(1, 1)])
        P.op("act", lambda e: e.copy(out=vd_tok[:, slot, :], in_=p_vd), reads=[bkey(1, 1)], writes=["vd_tok"])

    SLOPE = [2.0 ** (-(h + 1)) for h in range(8)]
    p_sc = Q[3][:, :].rearrange("p (a c t q) -> p a c t q", a=2, c=2, t=2)
    p_num = bank(2, 0).rearrange("p (a c q) -> p a c q", a=2, c=2)
    p_den = bank(2, 1).rearrange("p (a c q) -> p a c q", a=2, c=2)
    SCK = [bkey(3, 0), bkey(3, 1)]

    def attn_softmax(kv, dn_own, dn_prev):
        for half in range(2):
            for c in range(2):
                h0 = 4 * kv + 2 * c + half
                P.op("dve", lambda e, half=half, c=c, h0=h0: e.scalar_tensor_tensor(out=sc_sb[:, half, c, 0, :], in0=dneg[:, dn_prev, :], scalar=SLOPE[h0],
                                                                                  in1=p_sc[:, half, c, 0, :], op0=ALU.mult, op1=ALU.add),
                     reads=["dneg"] + SCK, writes=["sc_sb"])
                P.op("dve", lambda e, half=half, c=c, h0=h0: e.scalar_tensor_tensor(out=sc_sb[:, half, c, 1, :], in0=dneg[:, dn_own, :], scalar=SLOPE[h0],
                                                                                  in1=p_sc[:, half, c, 1, :], op0=ALU.mult, op1=ALU.add),
                     reads=["dneg"] + SCK, writes=["sc_sb"])
        P.op("act", lambda e: e.activation(out=pTt, in_=sc_sb, func=AF.Exp), reads=["sc_sb"], writes=["pTt"])

    def attn_finish(kv, c0):
        for part in range(2):
            P.op("pe", lambda e, part=part: e.matmul(p_den, lhsT=ones_b[:], rhs=pTt[:, :, :, part, :], start=(part == 0), stop=(part == 1)),
                 reads=["ones_b", "pTt"], writes=[bkey(2, 1)])
        es_v = esink[:, 4 * kv:4 * kv + 4].rearrange("p (c a) -> p a c", a=2)
        P.op("dve", lambda e, es_v=es_v: e.tensor_tensor(out=rden, in0=p_den, in1=es_v.unsqueeze(3).to_broadcast([128, 2, 2, 128]), op=ALU.add),
             reads=[bkey(2, 1), "esink"], writes=["rden"])
        P.op("dve", lambda e: e.reciprocal(out=rden, in_=rden), reads=["rden"], writes=["rden"])
        for half in range(2):
            sl = slice(half * 64, half * 64 + 64)
            P.op("dve", lambda e, half=half, sl=sl, kv=kv: e.tensor_tensor(out=oT[sl, 2 * kv:2 * kv + 2, c0:c0 + 128], in0=p_num[sl, half, :, :],
                                                                         in1=rden[sl, half, :, :], op=ALU.mult),
                 reads=[bkey(2, 0), "rden"], writes=["oT"])

    def attention_tile(i, dn_own, dn_prev):
        c0 = i * 128
        for kv in range(2):
            for half in range(2):
                sl = slice(half * 64, half * 64 + 64)
                P.op("pe", lambda e, kv=kv, half=half, sl=sl: e.matmul(p_sc[:, half, :, 1, :], lhsT=kdT[sl, kv, 128 + c0:256 + c0],
                                                                       rhs=qnT[sl, 2 * kv:2 * kv + 2, c0:c0 + 128], start=True, stop=True),
                     reads=["kdT", "qnT"], writes=SCK)
                P.op("pe", lambda e, kv=kv, half=half, sl=sl: e.matmul(p_sc[:, half, :, 0, :], lhsT=kdT[sl, kv, c0:128 + c0],
                                                                       rhs=qnT[sl, 2 * kv:2 * kv + 2, c0:c0 + 128], start=True, stop=True),
                     reads=["kdT", "qnT"], writes=SCK)
            attn_softmax(kv, dn_own, dn_prev)
            parts = [(0, vd_tok[:, i, kv * 128:(kv + 1) * 128]), (1, vd_tok[:, i + 1, kv * 128:(kv + 1) * 128])]
            for n_, (part, lh) in enumerate(parts):
                P.op("pe", lambda e, part=part, lh=lh, n_=n_: e.matmul(p_num, lhsT=lh, rhs=pTt[:, :, :, part, :], start=(n_ == 0), stop=(n_ == 1)),
                     reads=["vd_tok", "pTt"], writes=[bkey(2, 0)])
            attn_finish(kv, c0)

    def attention_sample():
        p_ct = bank(1, 1)[:, 0:128]
        for kv in range(2):
            for half in range(2):
                P.dma("pool", lambda e, kv=kv, half=half: e.dma_start(out=vcache[:, :, half * 64:(half + 1) * 64],
                                                                      in_=cache_v[:, :, kv * 64:(kv + 1) * 64].rearrange("b w d -> w b d")),
                      writes=["vcache"], semkey="vcache", group=f"vc{kv}")
            for g4 in range(4):
                for half in range(2):
                    P.dma("sp", lambda e, kv=kv, half=half, g4=g4: e.dma_start(out=kst[:, :, half * 64:(half + 1) * 64],
                                                                              in_=cache_k[4 * g4:4 * g4 + 4, :, kv * 64:(kv + 1) * 64].rearrange("b w d -> w b d")),
                          writes=["kst"], semkey="kst", group=f"kst{kv}_{g4}")
                for bb in range(4):
                    b = 4 * g4 + bb
                    P.op("pe", lambda e, bb=bb: e.transpose(p_ct, kst[:, bb, :], ident[:]), reads=["kst", "ident"], writes=[bkey(1, 1)])
                    P.op("act", lambda e, b=b: e.copy(out=cacheT[:, b, :], in_=p_ct), reads=[bkey(1, 1)], writes=["cacheT"])
            for half in range(2):
                sl = slice(half * 64, half * 64 + 64)
                P.op("pe", lambda e, kv=kv, half=half, sl=sl: e.matmul(p_sc[:, half, :, 1, :], lhsT=kdT[sl, kv, 128:256],
                                                                       rhs=qnT[sl, 2 * kv:2 * kv + 2, 0:128], start=True, stop=True),
                     reads=["kdT", "qnT"], writes=SCK)
                for b in range(16):
                    P.op("pe", lambda e, kv=kv, half=half, sl=sl, b=b: e.matmul(p_sc[:, half, :, 0, 8 * b:8 * b + 8], lhsT=cacheT[sl, b, :],
                                                                                 rhs=qnT[sl, 2 * kv:2 * kv + 2, 8 * b:8 * b + 8], start=True, stop=True),
                         reads=["cacheT", "qnT"], writes=SCK)
            attn_softmax(kv, 3, 4)
            P.op("pe", lambda e, kv=kv: e.matmul(p_num, lhsT=vd_tok[:, 1, kv * 128:(kv + 1) * 128], rhs=pTt[:, :, :, 1, :], start=True, stop=False),
                 reads=["vd_tok", "pTt"], writes=[bkey(2, 0)])
            for b in range(16):
                P.op("pe", lambda e, b=b: e.matmul(p_num[:, :, :, 8 * b:8 * b + 8], lhsT=vcache[:, b, :], rhs=pTt[:, :, :, 0, 8 * b:8 * b + 8],
                                                   start=False, stop=(b == 15)),
                     reads=["vcache", "pTt"], writes=[bkey(2, 0)])
            attn_finish(kv, 0)

    p_in = bank(1, 1).rearrange("p (h i) -> p h i", h=4)
    p_o = bank(0, 0).rearrange("p (h i) -> p h i", h=4)
    p_mu = bank(0, 1).rearrange("p (h i) -> p h i", h=4)
    p_e2 = bank(1, 0).rearrange("p (h i) -> p h i", h=4)

    def ret_norm(c0):
        P.op("dve", lambda e: e.tensor_copy(out=obf, in_=o32), reads=["o32"], writes=["obf"])
        P.op("act", lambda e: e.activation(out=osq, in_=o32, func=AF.Square), reads=["o32"], writes=["osq"])
        for h in range(4):
            P.op("pe", lambda e, h=h: e.matmul(p_mu[:, h, :], lhsT=ones_g[:], rhs=obf[:, h, :], start=True, stop=True), reads=["ones_g", "obf"], writes=[bkey(0, 1)])
        for h in range(4):
            P.op("pe", lambda e, h=h: e.matmul(p_e2[:, h, :], lhsT=ones_g[:], rhs=osq[:, h, :], start=True, stop=True), reads=["ones_g", "osq"], writes=[bkey(1, 0)])
        P.op("act", lambda e: e.activation(out=t1, in_=p_mu, func=AF.Square), reads=[bkey(0, 1)], writes=["t1"])
        P.op("dve", lambda e: e.tensor_tensor(out=t1, in0=p_e2, in1=t1, op=ALU.subtract), reads=[bkey(1, 0), "t1"], writes=["t1"])
        P.op("dve", lambda e: e.tensor_scalar(out=t1, in0=t1, scalar1=0.0, scalar2=None, op0=ALU.max), reads=["t1"], writes=["t1"])
        P.op("act", lambda e: e.activation(out=t1, in_=t1, func=AF.Sqrt, bias=epsb[:], scale=1.0), reads=["t1", "epsb"], writes=["t1"])
        P.op("dve", lambda e: e.reciprocal(out=t1, in_=t1), reads=["t1"], writes=["t1"])
        P.op("dve", lambda e: e.tensor_tensor(out=o32, in0=o32, in1=p_mu, op=ALU.subtract), reads=["o32", bkey(0, 1)], writes=["o32"])
        P.op("dve", lambda e: e.tensor_tensor(out=o32, in0=o32, in1=t1, op=ALU.mult), reads=["o32", "t1"], writes=["o32"])
        P.op("dve", lambda e: e.tensor_tensor(out=o32, in0=o32, in1=retg.unsqueeze(2).to_broadcast([128, 4, 128]), op=ALU.mult), reads=["o32", "retg"], writes=["o32"])
        P.op("dve", lambda e: e.tensor_tensor(out=oT[:, 4:8, c0:c0 + 128], in0=o32, in1=sgT[:, :, c0:c0 + 128], op=ALU.mult), reads=["o32", "sgT"], writes=["oT"])

    def ret_inner(c0):
        for h in range(4):
            P.op("pe", lambda e, h=h: e.matmul(p_in[:, h, :], lhsT=kbT[:, h, c0:c0 + 128], rhs=qbT[:, h, c0:c0 + 128], start=True, stop=True),
                 reads=["kbT", "qbT"], writes=[bkey(1, 1)])
        P.op("dve", lambda e: e.tensor_tensor(out=innT, in0=p_in, in1=rmask, op=ALU.mult), reads=[bkey(1, 1), "rmask"], writes=["innT"])

    def retention_tile(i, grp):
        c0 = i * 128
        ret_inner(c0)
        for h in range(4):
            P.op("pe", lambda e, h=h: e.matmul(p_o[:, h, :], lhsT=vb_tok[:, h * 128:(h + 1) * 128], rhs=innT[:, h, :], start=True, stop=False),
                 reads=["vb_tok", "innT"], writes=[bkey(0, 0)])
            P.op("pe", lambda e, h=h: e.matmul(p_o[:, h, :], lhsT=Sb[:, h, :], rhs=qdT[:, h, c0:c0 + 128], start=False, stop=True),
                 reads=["Sb", "qdT"], writes=[bkey(0, 0)])
        P.op("act", lambda e: e.copy(out=o32, in_=p_o), reads=[bkey(0, 0)], writes=["o32"])
        ret_norm(c0)

    def retention_sample():
        ret_inner(0)
        poh = [bank(0, 0)[:, 0:128], bank(0, 1)[:, 0:128], bank(1, 0)[:, 0:128], bank(3, 0)[:, 0:128]]
        pok = [bkey(0, 0), bkey(0, 1), bkey(1, 0), bkey(3, 0)]
        for h in range(4):
            P.op("pe", lambda e, h=h: e.matmul(poh[h], lhsT=vb_tok[:, h * 128:(h + 1) * 128], rhs=innT[:, h, :], start=True, stop=False),
                 reads=["vb_tok", "innT"], writes=[pok[h]])
        for b in range(16):
            s_ = b % 2
            P.dma("sp", lambda e, b=b, s_=s_: e.dma_start(out=S0[s_], in_=sret_in[b].rearrange("h d e -> d h e")), writes=[f"S0{s_}"], semkey=f"S0{s_}")
            P.op("pool", lambda e, s_=s_: e.tensor_copy(out=S0b[s_], in_=S0[s_]), reads=[f"S0{s_}"], writes=[f"S0b{s_}"])
            for h in range(4):
                P.op("pe", lambda e, h=h, b=b, s_=s_: e.matmul(poh[h][:, 8 * b:8 * b + 8], lhsT=S0b[s_][:, h, :], rhs=qdT[:, h, 8 * b:8 * b + 8],
                                                              start=False, stop=(b == 15)),
                     reads=[f"S0b{s_}", "qdT"], writes=[pok[h]])
            P.op("dve", lambda e, b=b, s_=s_: e.tensor_scalar(out=kdm[s_], in0=kd_tok, scalar1=seqmask[:, b:b + 1], scalar2=None, op0=ALU.mult),
                 reads=["kd_tok", "seqmask"], writes=[f"kdm{s_}"])
            for h in range(4):
                P.op("pe", lambda e, h=h, s_=s_: e.matmul(p_su[:, h, :], lhsT=kdm[s_][:, h * 128:(h + 1) * 128], rhs=vb_tok[:, h * 128:(h + 1) * 128], start=True, stop=True),
                     reads=[f"kdm{s_}", "vb_tok"], writes=[bkey(2, 1)])
            P.op("pool", lambda e, s_=s_: e.tensor_tensor(out=S0[s_], in0=S0[s_], in1=dc[:, 1, :].unsqueeze(2).to_broadcast([128, 4, 128]), op=ALU.mult),
                 reads=[f"S0{s_}", f"S0b{s_}", "dc"], writes=[f"S0{s_}"])
            P.op("dve", lambda e, s_=s_: e.tensor_tensor(out=S0[s_], in0=S0[s_], in1=p_su, op=ALU.add), reads=[f"S0{s_}", bkey(2, 1)], writes=[f"S0{s_}"])
            P.dma("sp", lambda e, b=b, s_=s_: e.dma_start(out=sret_out[b].rearrange("h d e -> d h e"), in_=S0[s_]), reads=[f"S0{s_}"], semkey=f"S0o{s_}", final=True)
        for h in range(4):
            P.op("act", lambda e, h=h: e.copy(out=o32[:, h, :], in_=poh[h]), reads=[pok[h]], writes=["o32"])
        ret_norm(0)

    PO = [bank(0, 0), bank(0, 1)]
    POK = [bkey(0, 0), bkey(0, 1)]

    def out_proj(blk_c0, ntok, sample, wfn, nk, rhs_fn, gate_row, rkeys):
        for m in range(KC):
            s = m % 2
            for k in range(nk):
                P.op("pe", lambda e, m=m, k=k, s=s: e.matmul(PO[s][:, :ntok], lhsT=wfn(k, m), rhs=rhs_fn(k), start=(k == 0), stop=(k == nk - 1)),
                     reads=rkeys, writes=[POK[s]])
            xs = xT[:, m, blk_c0:blk_c0 + ntok]
            xkey = ("xT", blk_c0 // 512)
            if not sample:
                P.op("dve", lambda e, m=m, s=s, xs=xs: e.scalar_tensor_tensor(out=xs, in0=PO[s][:, :ntok], scalar=modT[:, gate_row + m, 0:1], in1=xs,
                                                                             op0=ALU.mult, op1=ALU.add),
                     reads=[POK[s], "modT", xkey], writes=[xkey])
            else:
                gv = gtmp[:, :128].rearrange("p (b t) -> p b t", t=8)
                P.op("dve", lambda e, m=m, s=s, gv=gv: e.tensor_tensor(out=gv, in0=PO[s][:, :128].rearrange("p (b t) -> p b t", t=8),
                                                                      in1=modT[:, gate_row + m, 1:17].unsqueeze(2).to_broadcast([128, 16, 8]), op=ALU.mult),
                     reads=[POK[s], "modT"], writes=["gtmp"])
                P.op("dve", lambda e, xs=xs: e.tensor_tensor(out=xs, in0=xs, in1=gtmp[:, :128], op=ALU.add), reads=["gtmp", xkey], writes=[xkey])

    p_tr = bank(1, 1)[:, 256:384]
    p_v32 = bank(1, 1)[:, 384:512]

    def window_kv(tc0, kout, vout):
        pk2 = bank(1, 0)[:, 0:256].rearrange("p (a n) -> p a n", a=2)
        pst2 = bank(1, 0)[:, 256:512].rearrange("p (a n) -> p a n", a=2)
        for kv in range(2):
            for k in range(KC):
                P.op("pe", lambda e, kv=kv, k=k: e.matmul(pk2[:, kv, :], lhsT=wkd[:, k, kv, :], rhs=hT[:, k, tc0:tc0 + 128], start=(k == 0), stop=(k == KC - 1)),
                     reads=["wkd", "hT"], writes=[bkey(1, 0)])
        P.op("act", lambda e: e.activation(out=qsq[:, 0:256].rearrange("p (a n) -> p a n", a=2), in_=pk2, func=AF.Square), reads=[bkey(1, 0)], writes=["qsq"])
        for kv in range(2):
            P.op("pe", lambda e, kv=kv: e.matmul(pst2[:, kv, :], lhsT=bd, rhs=qsq[:, kv * 128:(kv + 1) * 128], start=True, stop=True), reads=["bd", "qsq"], writes=[bkey(1, 0)])
        P.op("act", lambda e: e.activation(out=qrs[:, 0:256].rearrange("p (a n) -> p a n", a=2), in_=pst2, func=AF.Sqrt, bias=epsb[:], scale=1.0),
             reads=[bkey(1, 0), "epsb"], writes=["qrs"])
        P.op("dve", lambda e: e.reciprocal(out=qrs[:, 0:256], in_=qrs[:, 0:256]), reads=["qrs"], writes=["qrs"])
        for kv in range(2):
            sl = slice(kv * 64, (kv + 1) * 64)
            P.op("dve", lambda e, kv=kv, sl=sl: e.scalar_tensor_tensor(out=knT[sl, :], in0=pk2[sl, kv, :], scalar=gk[sl, 0:1], in1=qrs[sl, kv * 128:(kv + 1) * 128],
                                                                       op0=ALU.mult, op1=ALU.mult),
                 reads=[bkey(1, 0), "gk", "qrs"], writes=["knT"])
        P.op("pe", lambda e: e.transpose(p_tr, knT, ident[:]), reads=["knT", "ident"], writes=[bkey(1, 1)])
        P.op("act", lambda e: e.copy(out=kv32[:, 0, :], in_=p_tr), reads=[bkey(1, 1)], writes=["kv32"])
        for k in range(KC):
            P.op("pe", lambda e, k=k: e.matmul(p_v32, lhsT=hT[:, k, tc0:tc0 + 128], rhs=win[:, k, 640:768], start=(k == 0), stop=(k == KC - 1)),
                 reads=["win", "hT"], writes=[bkey(1, 1)])
        P.op("dve", lambda e: e.tensor_copy(out=kv32[:, 1, :], in_=p_v32), reads=[bkey(1, 1)], writes=["kv32"])
        P.dma("sp", lambda e: e.dma_start(out=kout, in_=kv32[:, 0, :]), reads=["kv32"], semkey="kvo", final=True)
        P.dma("sp", lambda e: e.dma_start(out=vout, in_=kv32[:, 1, :]), reads=["kv32"], semkey="kvo", final=True)

    if KSTOP >= 2:
        for kv in range(2):
            proj_fm(lambda k, kv=kv: wkd[:, k, kv, :], 128,
                    lambda ps_, key, kv=kv: qknorm(ps_, key, 128, gk, "gk", kdT[:, kv, 0:128], "kdT"), src=hTh)
        tok_vd(0, 0, src=hTh)

    NB = TOK_P // BT
    for b in range(NB if KSTOP >= 3 else 0):
        ln_block(xT[:, :, b * BT:(b + 1) * BT], ("xT", (b * BT) // 512), BT, False, 0)
        project_block(BT, 0)
        if KSTOP < 4:
            continue
        for i in range(TPB):
            tok_vd(i * 128, i + 1)
        for i in range(TPB):
            attention_tile(i, 0, 2 if (b == 0 and i == 0) else 1)
        if KSTOP < 5:
            continue
        for i in range(TPB):
            tok_kv(i * 128, 0)
            retention_tile(i, 0)
            state_update()
            P.op("act", lambda e: e.copy(out=Sb, in_=S), reads=["S"], writes=["Sb"])
        if b == NB - 1:
            window_kv(BT - 128, pk_out, pv_out)
        if KSTOP < 6:
            continue
        out_proj(b * BT, BT, False, lambda k, m: wout[:, k, m * 128:(m + 1) * 128], KC, lambda k: oT[:, k, :BT], 16, ["wout", "oT"])
        P.op("pool", lambda e: e.tensor_copy(out=kdT[:, :, 0:128], in_=kdT[:, :, BT:BT + 128]), reads=["kdT"], writes=["kdT"])
        P.op("pool", lambda e: e.tensor_copy(out=vd_tok[:, 0, :], in_=vd_tok[:, TPB, :]), reads=["vd_tok"], writes=["vd_tok"])
    P.dma("sp", lambda e: e.dma_start(out=pret_out.rearrange("h d e -> d h e"), in_=S), reads=["S"], semkey="pret", final=True)

    if KSTOP >= 7:
        P.dma("sp", lambda e: e.dma_start(out=rmask, in_=rmask_d[:, 1]), writes=["rmask"], semkey="ld_rmask", group="ld_rmask")
        P.dma("sp", lambda e: e.dma_start(out=dq, in_=dq_d[:, 1]), writes=["dq"], semkey="ld_dq", group="ld_dq")
        ln_block(xT[:, :, TOK_P:TOK_P + 128], ("xT", 4), 128, True, 0)
        project_block(128, 1)
        tok_vd(0, 1)
        tok_kv(0, 1)
        window_kv(0, sk_out[:, 120:128, :], sv_out[:, 120:128, :])
        P.dma("sp", lambda e: e.dma_start(out=sk_out[:, 0:120, :], in_=cache_k[:, 8:128, :]), semkey="cko", final=True)
        P.dma("sp", lambda e: e.dma_start(out=sv_out[:, 0:120, :], in_=cache_v[:, 8:128, :]), semkey="cko", final=True)
        attention_sample()
        retention_sample()
        out_proj(TOK_P, 128, True, lambda k, m: wout[:, k, m * 128:(m + 1) * 128], KC, lambda k: oT[:, k, :128], 16, ["wout", "oT"])

    if KSTOP >= 8:
        ffn(0)

    P.barrier()
    cur[0] = mark1
    yst = [carve([128, D]) for _ in range(2)]
    for t in range(17):
        s = t % 2
        for k in range(KC):
            P.op("pe", lambda e, k=k, t=t, s=s: e.transpose(pT[s][:, k, :], xT[:, k, t * 128:(t + 1) * 128], ident[:]),
                 reads=[("xT", t // 4), "ident"], writes=[bkey(s, 0), bkey(s, 1)])
        P.op("act", lambda e, s=s: e.copy(out=yst[s], in_=Q[s][:, :]), reads=[bkey(s, 0), bkey(s, 1)], writes=[f"yst{s}"])
        P.dma("sp", lambda e, t=t, s=s: e.dma_start(out=x1_out[t * 128:(t + 1) * 128, :], in_=yst[s]), reads=[f"yst{s}"], semkey=f"yst{s}", final=True)

    P.barrier()
    P.dma("sp", lambda e: e.dma_start(out=modT[:].rearrange("p m c -> p (m c)"), in_=modin1_d), writes=["modT"], semkey="ld_modT1")
    make_G(1, 0)
    s5_stage(False)

    stats = P.build()
    nc_allow.__exit__(None, None, None)
    es.close()
    return nc, stats


_CACHE = {}


def _tables(c):
    f32 = np.float32
    ident = np.eye(128, dtype=f32)
    bd = np.zeros((128, 128), f32)
    bd[:64, :64] = 1.0 / 64
    bd[64:, 64:] = 1.0 / 64
    j = np.arange(128)[:, None]
    i = np.arange(128)[None, :]
    NEG = -1e30
    dneg = np.full((128, 5, 128), NEG, f32)
    dneg[:, 0, :] = np.where(i >= j, -(i - j), NEG)
    dneg[:, 1, :] = np.where(i < j, -(i - j + 128), NEG)
    dneg[:, 2, :] = dneg[:, 1, :] if c > 0 else NEG
    same = (j // 8) == (i // 8)
    dneg[:, 3, :] = np.where(same & ((i % 8) >= (j % 8)), -((i % 8) - (j % 8)), NEG)
    dneg[:, 4, :] = np.where(j >= (i % 8) + 1, -(128 + (i % 8) - j), NEG)
    g = np.array(GAM, np.float64)
    rmask = np.zeros((128, 2, 4, 128), f32)
    dq = np.zeros((128, 2, 4, 128), f32)
    dk = np.zeros((128, 2, 4), f32)
    dc = np.zeros((128, 2, 4), f32)
    for h in range(4):
        rmask[:, 0, h, :] = np.where(i >= j, g[h] ** np.maximum(i - j, 0), 0.0)
        rmask[:, 1, h, :] = np.where(same & ((i % 8) >= (j % 8)), g[h] ** np.maximum((i % 8) - (j % 8), 0), 0.0)
        dq[:, 0, h, :] = g[h] ** (i + 1.0)
        dq[:, 1, h, :] = g[h] ** ((i % 8) + 1.0)
        dk[:, 0, h] = 128.0 ** -0.5 * g[h] ** (127.0 - j[:, 0])
        dk[:, 1, h] = 128.0 ** -0.5 * g[h] ** (7.0 - (j[:, 0] % 8))
        dc[:, 0, h] = g[h] ** 128.0
        dc[:, 1, h] = g[h] ** 8.0
    seqmask = (j // 8 == np.arange(16)[None, :]).astype(f32)
    wret = np.zeros((128, 8, 4), f32)
    for r in range(8):
        if r < c:
            for h in range(4):
                wret[:, r, h] = g[h] ** (2048.0 * (c - r - 1))
    return dict(ident=ident, bdones=bd, dneg=dneg, rmask=rmask, dq=dq, dk=dk, dc=dc, seqmask=seqmask, wret=wret)


def _get(stage):
    if stage not in _CACHE:
        _CACHE[stage] = build_program(stage)
    return _CACHE[stage]


def kernel(**inp):
    f32 = np.float32
    A = lambda k: np.ascontiguousarray(np.asarray(inp[k], f32))
    xp = A("x_prompt")[0]
    xs = A("x_sample")
    base = dict(norm_mix=A("norm_mix"), norm_ffn=A("norm_ffn"), even_w_in=A("even_w_in")[0])
    per_core = []
    for c in range(NCORES):
        x18 = np.zeros((18 * 128, D), f32)
        if c > 0:
            x18[0:128] = xp[c * TOK_P - 128:c * TOK_P]
        x18[128:128 + TOK_P] = xp[c * TOK_P:(c + 1) * TOK_P]
        x18[128 + TOK_P:] = xs[16 * c:16 * c + 16].reshape(128, D)
        cv = np.concatenate([A("c_prompt"), A("c_sample")[16 * c:16 * c + 16]], axis=0)
        t = _tables(c)
        m = dict(base)
        m.update(x=x18, cvec=np.ascontiguousarray(cv), ident=t["ident"], dk=t["dk"], dc=t["dc"])
        per_core.append((m, t))

    ncA, _ = _get("A")
    mapsA = []
    for m, _ in per_core:
        m = dict(m)
        m.update(ada_w=A("ada_w"), ada_b=A("ada_b"))
        mapsA.append(m)
    resA = run_bass_kernel_spmd(ncA, mapsA, core_ids=list(range(NCORES)))
    mods = [np.ascontiguousarray(resA.results[c]["modout"]) for c in range(NCORES)]
    lall = np.ascontiguousarray(np.stack([resA.results[c]["lret"] for c in range(NCORES)], axis=0))
    _CACHE["lall"] = lall

    ncB, statsB = _get("B")
    _CACHE["statsB"] = statsB
    in_maps = []
    for c in range(NCORES):
        m, t = per_core[c]
        m = dict(m)
        m.pop("cvec", None)
        m["modin"] = np.ascontiguousarray(mods[c][0])
        m["modin1"] = np.ascontiguousarray(mods[c][1])
        m.update(odd_A_re=A("odd_A_re")[0], odd_A_im=A("odd_A_im")[0], odd_log_dt=A("odd_log_dt"), odd_B_re=A("odd_B_re")[0], odd_B_im=A("odd_B_im")[0],
                 odd_C_re=A("odd_C_re")[0], odd_C_im=A("odd_C_im")[0], odd_D=A("odd_D"), **_tables_c())
        m.update(even_q_gain=A("even_q_gain"), even_k_gain=A("even_k_gain"), even_sinks=A("even_sinks"), even_ret_gain=A("even_ret_gain"),
                 even_w_out=A("even_w_out")[0], ffn_wg=A("ffn_wg"), ffn_wu=A("ffn_wu"), ffn_wd=A("ffn_wd"),
                 cache_k=np.ascontiguousarray(A("cache_win_k")[0, 16 * c:16 * c + 16].reshape(16, 128, 128)),
                 cache_v=np.ascontiguousarray(A("cache_win_v")[0, 16 * c:16 * c + 16].reshape(16, 128, 128)),
                 state_ret=np.ascontiguousarray(A("state_ret")[0, 16 * c:16 * c + 16]),
                 bdones=t["bdones"], dneg=t["dneg"], rmask=t["rmask"], dq=t["dq"], seqmask=t["seqmask"], wret=t["wret"], lall=lall)
        in_maps.append(m)
    resB = run_bass_kernel_spmd(ncB, in_maps, core_ids=list(range(NCORES)))
    R = resB.results
    _CACHE["R"] = R
    last = R[NCORES - 1]
    p_k = last["p_k"].reshape(1, 1, 128, 2, 64)
    p_v = last["p_v"].reshape(1, 1, 128, 2, 64)
    p_ret = last["p_ret"].reshape(1, 1, 4, 128, 128)
    s_k = np.concatenate([R[c]["s_k"] for c in range(NCORES)], axis=0).reshape(1, 128, 128, 2, 64)
    s_v = np.concatenate([R[c]["s_v"] for c in range(NCORES)], axis=0).reshape(1, 128, 128, 2, 64)
    s_ret = np.concatenate([R[c]["s_ret"] for c in range(NCORES)], axis=0)[None]

    tC = _tables_c()
    mapsC = []
    for c in range(NCORES):
        m0, t = per_core[c]
        x18 = np.zeros((18 * 128, D), f32)
        x18[128:] = R[c]["x1"]
        wsel = np.zeros((128, 8), f32)
        wsel[:, :c] = 1.0
        m = dict(base)
        m.update(x=x18, modin=np.ascontiguousarray(mods[c][1]), ident=t["ident"], dk=t["dk"], dc=t["dc"],
                 ffn_wg=A("ffn_wg"), ffn_wu=A("ffn_wu"), ffn_wd=A("ffn_wd"),
                 odd_A_re=A("odd_A_re")[0], odd_A_im=A("odd_A_im")[0], odd_log_dt=A("odd_log_dt"),
                 odd_B_re=A("odd_B_re")[0], odd_B_im=A("odd_B_im")[0], odd_C_re=A("odd_C_re")[0], odd_C_im=A("odd_C_im")[0],
                 odd_D=A("odd_D"), odd_glu_a=A("odd_glu_a")[0], odd_glu_b=A("odd_glu_b")[0],
                 s5_re=np.ascontiguousarray(A("state_s5_re")[0, 16 * c:16 * c + 16]), s5_im=np.ascontiguousarray(A("state_s5_im")[0, 16 * c:16 * c + 16]),
                 fall=np.zeros((8, 128, 64), f32), wsel=wsel, **tC)
        mapsC.append(m)
    fall = np.ascontiguousarray(np.stack([R[c]["floc"] for c in range(NCORES)], axis=0))
    _CACHE["fall"] = fall
    for m in mapsC:
        m["fall"] = fall
    ncC2, statsC = _get("C2")
    _CACHE["statsC"] = statsC
    resC2 = run_bass_kernel_spmd(ncC2, mapsC, core_ids=list(range(NCORES)))
    RC = resC2.results
    _CACHE["RC"] = RC
    y_prompt = np.concatenate([RC[c]["y"][:TOK_P] for c in range(NCORES)], axis=0)[None]
    y_sample = np.concatenate([RC[c]["y"][TOK_P:].reshape(16, 8, D) for c in range(NCORES)], axis=0)
    lastC = RC[NCORES - 1]
    p_re = lastC["p_s5r"].reshape(1, 1, 64, 64)
    p_im = lastC["p_s5i"].reshape(1, 1, 64, 64)
    s_re = np.concatenate([RC[c]["s_s5r"] for c in range(NCORES)], axis=0)[None]
    s_im = np.concatenate([RC[c]["s_s5i"] for c in range(NCORES)], axis=0)[None]
    return (y_prompt.astype(f32), y_sample.astype(f32), p_k, p_v, p_ret, p_re, p_im, s_k, s_v, s_ret, s_re, s_im)


def _tables_c():
    f32 = np.float32
    t = np.arange(512)
    tal = np.broadcast_to((t // 64).astype(f32)[None, :], (128, 512)).copy()
    tbp = np.broadcast_to(((t % 64) + 1).astype(f32)[None, :], (128, 512)).copy()
    ts = np.arange(128)
    tbs = np.broadcast_to(((ts % 8) + 1).astype(f32)[None, :], (128, 128)).copy()
    amask = np.broadcast_to(((ts % 8) != 0).astype(f32)[None, :], (128, 128)).copy()
    maskg = (np.arange(128)[:, None] // 16 == np.arange(8)[None, :]).astype(f32)
    rott = np.zeros((128, 128), f32)
    for p in range(64):
        rott[64 + p, p] = -1.0
        rott[p, 64 + p] = 1.0
    return dict(tal=tal, tbp=tbp, tbs=tbs, amask=amask, maskg=maskg, rott=rott)
```

```python
import os
import numpy as np
from contextlib import ExitStack
import concourse.bass as bass
import concourse.mybir as mybir
from concourse.bass_utils import run_bass_kernel_spmd

F32 = mybir.dt.float32
BF16 = mybir.dt.bfloat16
I32 = mybir.dt.int32
ALU = mybir.AluOpType
AF = mybir.ActivationFunctionType

NCORES = 8
D = 1024
KC = 8
NPT = 16
TOK_P = NPT * 128
NTOK = TOK_P + 128
IN_W = 2816
DFF = 2816
FC = 22
BT = 128
TPB = BT // 128
EPS = 1e-6
ENGS = ("pe", "act", "dve", "pool", "sp")
GAM = [1.0 - 2.0 ** (-5.0 - h) for h in range(4)]
KSTOP = int(os.environ.get('KSTOP', '9'))
KSUB = int(os.environ.get('KSUB', '9'))


class Prog:
    def __init__(self, nc, same_engine_sync=("act", "dve", "pool")):
        self.nc = nc
        self.ins = []
        self.last_w = {}
        self.readers = {}
        self.same_sync = set(same_engine_sync)
        self.final_ids = []
        self.last_eng = {}
        self.last_dma = {}
        self.bar_deps = []
        self.bar_gen = 0
        self.eng_gen = {e: 0 for e in ENGS}

    def barrier(self):
        self.bar_deps = list(self.last_eng.values()) + list(self.last_dma.values())
        self.bar_gen += 1

    def _add(self, eng, fn, reads, writes, dma=False, semkey=None, group=None):
        iid = len(self.ins)
        deps = set()
        pk = tuple(k for k in reads if isinstance(k, str) and len(k) == 3 and k[0] == "Q" and k not in writes)
        writes = tuple(writes) + pk
        if self.eng_gen[eng] < self.bar_gen:
            deps.update(self.bar_deps)
            self.eng_gen[eng] = self.bar_gen
        for k in reads:
            w = self.last_w.get(k)
            if w is not None:
                deps.add(w)
        for k in writes:
            w = self.last_w.get(k)
            if w is not None:
                if group is not None and self.ins[w].get("group") == group:
                    deps.update(self.ins[w]["deps"])
                else:
                    deps.add(w)
            for r in self.readers.get(k, ()):
                deps.add(r)
        self.ins.append(dict(eng=eng, fn=fn, deps=sorted(deps), dma=dma, semkey=semkey, group=group))
        for k in reads:
            self.readers.setdefault(k, []).append(iid)
        for k in writes:
            self.last_w[k] = iid
            self.readers[k] = []
        self.last_eng[eng] = iid
        if dma:
            self.last_dma[semkey] = iid
        return iid

    capture = None

    def op(self, eng, fn, reads=(), writes=()):
        if self.capture is not None:
            self.capture.append((eng, fn, tuple(reads), tuple(writes)))
            return None
        return self._add(eng, fn, tuple(reads), tuple(writes))

    def dma(self, eng, fn, reads=(), writes=(), semkey=None, group=None, final=False):
        assert semkey is not None
        iid = self._add(eng, fn, tuple(reads), tuple(writes), dma=True, semkey=semkey, group=group)
        if final:
            self.final_ids.append(iid)
        return iid

    def build(self):
        nc = self.nc
        ins = self.ins
        n = len(ins)
        needed = [False] * n
        for i, it in enumerate(ins):
            nd = []
            for d in it["deps"]:
                de = ins[d]
                if (not de["dma"]) and (not it["dma"]) and de["eng"] == it["eng"] and it["eng"] not in self.same_sync:
                    continue
                nd.append(d)
            it["deps"] = nd
            for d in nd:
                needed[d] = True
        for f in self.final_ids:
            needed[f] = True
        semkeys = []
        for it in ins:
            if it["dma"] and it["semkey"] not in semkeys:
                semkeys.append(it["semkey"])
        sem_objs = {}
        ctxs = []
        for e in ENGS:
            c = nc.semaphore(f"s_{e}")
            sem_objs[("eng", e)] = c.__enter__()
            ctxs.append(c)
        for j, k in enumerate(semkeys):
            c = nc.semaphore(f"d{j}")
            sem_objs[("dma", k)] = c.__enter__()
            ctxs.append(c)
        cnt = {}
        for i, it in enumerate(ins):
            if it["dma"]:
                key = ("dma", it["semkey"])
                cnt[key] = cnt.get(key, 0) + 16
                it["sig"] = (key, cnt[key])
            elif needed[i]:
                key = ("eng", it["eng"])
                cnt[key] = cnt.get(key, 0) + 1
                it["sig"] = (key, cnt[key])
            else:
                it["sig"] = None
        per = {e: [] for e in ENGS}
        for i, it in enumerate(ins):
            per[it["eng"]].append(i)
        final_waits = {}
        for f in self.final_ids:
            key, val = ins[f]["sig"]
            final_waits[key] = max(final_waits.get(key, 0), val)
        with nc.Block() as block:
            def make(e):
                def body(eng):
                    waited = {}
                    for i in per[e]:
                        it = ins[i]
                        req = {}
                        for d in it["deps"]:
                            key, val = ins[d]["sig"]
                            if waited.get(key, 0) >= val:
                                continue
                            req[key] = max(req.get(key, 0), val)
                        for key, val in req.items():
                            eng.wait_ge(sem_objs[key], val)
                            waited[key] = val
                        r = it["fn"](eng)
                        if it["sig"] is not None:
                            key, val = it["sig"]
                            r.then_inc(sem_objs[key], 16 if it["dma"] else 1)
                    if e == "sp":
                        for key, val in final_waits.items():
                            eng.wait_ge(sem_objs[key], val)
                return body
            block.tensor(make("pe"))
            block.scalar(make("act"))
            block.vector(make("dve"))
            block.gpsimd(make("pool"))
            block.sync(make("sp"))
        for c in reversed(ctxs):
            c.__exit__(None, None, None)
        return dict(n=n, per={e: len(per[e]) for e in ENGS}, sems=len(sem_objs), maxcnt=max(cnt.values()), cnt={k[1]: v for k, v in cnt.items() if k[0] == 'eng'})


def build_program(stage):
    nc = bass.Bass("TRN2", target_bir_lowering=False)
    es = ExitStack()

    def din(name, shape):
        return nc.dram_tensor(name, list(shape), F32, kind="ExternalInput").ap()

    def dout(name, shape):
        return nc.dram_tensor(name, list(shape), F32, kind="ExternalOutput").ap()

    def sb(name, shape, dt=F32):
        return es.enter_context(nc.sbuf_tensor("sb_" + name, list(shape), dt))

    P = Prog(nc)
    nc_allow = nc.allow_non_contiguous_dma(reason="small parameter vectors laid out feature-major")
    nc_allow.__enter__()

    x_in = din("x", [18 * 128, D])
    if stage == "A":
        cvec = din("cvec", [17, D])
        ada_w = din("ada_w", [2, D, 6 * D])
        ada_b = din("ada_b", [2, 6 * D])
        mod_out = dout("modout", [2, 128, 48 * 17])
    else:
        modin_d = din("modin", [128, 48 * 17])
    norm_mix = din("norm_mix", [2, D])
    norm_ffn = din("norm_ffn", [2, D])
    w_in = din("even_w_in", [D, IN_W])
    ident_d = din("ident", [128, 128])
    dk_d = din("dk", [128, 2, 4])
    dc_d = din("dc", [128, 2, 4])
    if stage == "A":
        lret_out = dout("lret", [4, 128, 128])
    if stage in ("C1", "C2"):
        wg_d = din("ffn_wg", [2, D, DFF])
        wu_d = din("ffn_wu", [2, D, DFF])
        wd_d = din("ffn_wd", [2, DFF, D])
        A_re_d = din("odd_A_re", [64, 64])
        A_im_d = din("odd_A_im", [64, 64])
        ldt_d = din("odd_log_dt", [1, 64])
        B_re_d = din("odd_B_re", [64, 64, 16])
        B_im_d = din("odd_B_im", [64, 64, 16])
        C_re_d = din("odd_C_re", [64, 16, 64])
        C_im_d = din("odd_C_im", [64, 16, 64])
        Dsk_d = din("odd_D", [1, D])
        glua_d = din("odd_glu_a", [D, D])
        glub_d = din("odd_glu_b", [D, D])
        s5r_in = din("s5_re", [16, 64, 64])
        s5i_in = din("s5_im", [16, 64, 64])
        fall_d = din("fall", [8, 128, 64])
        wsel_d = din("wsel", [128, 8])
        tal_d = din("tal", [128, 512])
        tbp_d = din("tbp", [128, 512])
        tbs_d = din("tbs", [128, 128])
        amask_d = din("amask", [128, 128])
        maskg_d = din("maskg", [128, 8])
        rott_d = din("rott", [128, 128])
        if stage == "C1":
            floc_out = dout("floc", [128, 64])
        else:
            y_out = dout("y", [17 * 128, D])
            ps5r_out = dout("p_s5r", [64, 64])
            ps5i_out = dout("p_s5i", [64, 64])
            ss5r_out = dout("s_s5r", [16, 64, 64])
            ss5i_out = dout("s_s5i", [16, 64, 64])
    if stage == "B":
        A_re_d = din("odd_A_re", [64, 64])
        A_im_d = din("odd_A_im", [64, 64])
        ldt_d = din("odd_log_dt", [1, 64])
        B_re_d = din("odd_B_re", [64, 64, 16])
        B_im_d = din("odd_B_im", [64, 64, 16])
        C_re_d = din("odd_C_re", [64, 16, 64])
        C_im_d = din("odd_C_im", [64, 16, 64])
        Dsk_d = din("odd_D", [1, D])
        tal_d = din("tal", [128, 512])
        tbp_d = din("tbp", [128, 512])
        tbs_d = din("tbs", [128, 128])
        amask_d = din("amask", [128, 128])
        maskg_d = din("maskg", [128, 8])
        rott_d = din("rott", [128, 128])
        modin1_d = din("modin1", [128, 48 * 17])
        floc_out = dout("floc", [128, 64])
        qgain = din("even_q_gain", [1, 64])
        kgain = din("even_k_gain", [1, 64])
        sinks_d = din("even_sinks", [1, 8])
        retg_d = din("even_ret_gain", [1, 512])
        w_out = din("even_w_out", [D, D])
        wg_d = din("ffn_wg", [2, D, DFF])
        wu_d = din("ffn_wu", [2, D, DFF])
        wd_d = din("ffn_wd", [2, DFF, D])
        cache_k = din("cache_k", [16, 128, 128])
        cache_v = din("cache_v", [16, 128, 128])
        sret_in = din("state_ret", [16, 4, 128, 128])
        bd_d = din("bdones", [128, 128])
        dneg_d = din("dneg", [128, 5, 128])
        rmask_d = din("rmask", [128, 2, 4, 128])
        dq_d = din("dq", [128, 2, 4, 128])
        seqmask_d = din("seqmask", [128, 16])
        wret_d = din("wret", [128, 8, 4])
        lall_d = din("lall", [8, 4, 128, 128])
        x1_out = dout("x1", [17 * 128, D])
        pk_out = dout("p_k", [128, 128])
        pv_out = dout("p_v", [128, 128])
        pret_out = dout("p_ret", [4, 128, 128])
        sk_out = dout("s_k", [16, 128, 128])
        sv_out = dout("s_v", [16, 128, 128])
        sret_out = dout("s_ret", [16, 4, 128, 128])

    Q = [es.enter_context(nc.psum_tensor(f"ps_Q{i}", [128, 1024], F32)) for i in range(4)]

    def bank(i, h):
        return Q[i][:, h * 512:(h + 1) * 512]

    def bkey(i, h):
        return f"Q{i}{'ab'[h]}"

    ARENA_W = 33 * 1024
    arena = sb("arena", [128, ARENA_W])
    cur = [0]

    def carve(shape, dt=F32):
        n = int(np.prod(shape[1:]))
        words = n if dt in (F32, I32) else (n + 1) // 2
        words = (words + 7) // 8 * 8
        off = cur[0]
        assert off + words <= ARENA_W, ("arena overflow", off, words)
        cur[0] = off + words
        v = arena[:, off:off + words]
        if dt != F32:
            v = v.bitcast(dt)
        v = v[:, 0:n]
        if len(shape) == 3:
            v = v.rearrange("p (a b) -> p a b", a=shape[1])
        elif len(shape) == 4:
            v = v.rearrange("p (a b c) -> p a b c", a=shape[1], b=shape[2])
        elif len(shape) == 5:
            v = v.rearrange("p (a b c d) -> p a b c d", a=shape[1], b=shape[2], c=shape[3])
        return v

    ident = sb("ident", [128, 128])
    ones_m = sb("ones_m", [128, 128], BF16)
    ones_b = sb("ones_b", [128, 128], BF16)
    ones_g = sb("ones_g", [128, 128], BF16)
    epsb = sb("epsb", [128, 1])
    xT = sb("xT", [128, KC, NTOK])
    cT = sb("cT", [128, KC, 17], BF16)
    adab = sb("adab", [128, 2, 48])
    modT = sb("modT", [128, 48, 17])
    normg = sb("normg", [128, 2, 2, KC])
    G = sb("G", [128, KC, 17])
    dk = sb("dk", [128, 2, 4])
    dc = sb("dc", [128, 2, 4])
    P.dma("sp", lambda e: e.dma_start(out=ident[:], in_=ident_d), writes=["ident"], semkey="ld_ident", group="ld_ident")
    P.dma("sp", lambda e: e.dma_start(out=dk[:], in_=dk_d), writes=["dk"], semkey="ld_dk", group="ld_dk")
    P.dma("sp", lambda e: e.dma_start(out=dc[:], in_=dc_d), writes=["dc"], semkey="ld_dc", group="ld_dc")
    if stage == "A":
        P.dma("sp", lambda e: e.dma_start(out=adab[:], in_=ada_b.rearrange("l (m p) -> p l m", p=128)), writes=["adab"], semkey="ld_adab", group="ld_adab")
    P.dma("sp", lambda e: e.dma_start(out=normg[:, 0], in_=norm_mix.rearrange("l (k p) -> p l k", p=128)), writes=["normg"], semkey="ld_normg", group="ld_normg")
    P.dma("sp", lambda e: e.dma_start(out=normg[:, 1], in_=norm_ffn.rearrange("l (k p) -> p l k", p=128)), writes=["normg"], semkey="ld_normg", group="ld_normg")
    P.op("dve", lambda e: e.memset(ones_m[:], 1.0 / 1024.0), writes=["ones_m"])
    P.op("dve", lambda e: e.memset(ones_b[:], 1.0), writes=["ones_b"])
    P.op("dve", lambda e: e.memset(ones_g[:], 1.0 / 128.0), writes=["ones_g"])
    P.op("dve", lambda e: e.memset(epsb[:], EPS), writes=["epsb"])

    mark0 = cur[0]
    xst = [carve([128, D]) for _ in range(2)]
    c_sb = carve([128, D])
    adaw = [carve([128, KC, 768], BF16) for _ in range(2)]
    pT = [Q[i][:, :].rearrange("p (k n) -> p k n", k=KC) for i in range(2)]

    def load_tile(t, dst_ap, dst_key):
        s = t % 2
        P.dma("sp", lambda e: e.dma_start(out=xst[s], in_=x_in[t * 128:(t + 1) * 128, :]), writes=[f"xst{s}"], semkey=f"xst{s}")
        for k in range(KC):
            P.op("pe", lambda e, k=k: e.transpose(pT[s][:, k, :], xst[s][:, k * 128:(k + 1) * 128], ident[:]),
                 reads=[f"xst{s}", "ident"], writes=[bkey(s, 0), bkey(s, 1)])
        P.op("act", lambda e: e.copy(out=dst_ap, in_=pT[s]), reads=[bkey(s, 0), bkey(s, 1)], writes=[dst_key])

    for t in range(1, 18):
        c0 = (t - 1) * 128
        load_tile(t, xT[:, :, c0:c0 + 128], ("xT", (t - 1) // 4))

    if stage == "A":
        P.dma("sp", lambda e: e.dma_start(out=c_sb[0:17, :], in_=cvec), writes=["c_sb"], semkey="c_sb")
        P.op("act", lambda e: e.activation(out=c_sb[0:17, :], in_=c_sb[0:17, :], func=AF.Silu), reads=["c_sb"], writes=["c_sb"])
        for k in range(KC):
            P.op("pe", lambda e, k=k: e.transpose(pT[0][:, k, 0:17], c_sb[0:17, k * 128:(k + 1) * 128], ident[0:17, 0:17]),
                 reads=["c_sb", "ident"], writes=[bkey(0, 0), bkey(0, 1)])
        P.op("dve", lambda e: e.tensor_copy(out=cT[:], in_=pT[0][:, :, 0:17]), reads=[bkey(0, 0), bkey(0, 1)], writes=["cT"])

    pm = [bank(2, i)[:, 0:408].rearrange("p (m c) -> p m c", c=17) for i in range(2)]

    def modulation(layer):
        for cb in range(8):
            s = cb % 2
            P.dma("pool", lambda e, cb=cb, s=s: e.dma_start(out=adaw[s], in_=ada_w[layer, :, cb * 768:(cb + 1) * 768].rearrange("(k p) n -> p k n", p=128)),
                  writes=[f"adaw{s}"], semkey=f"adaw{s}")
            for mm in range(6):
                m = cb * 6 + mm
                for k in range(KC):
                    P.op("pe", lambda e, k=k, s=s, m=m, mm=mm: e.matmul(pm[m // 24][:, m % 24, :], lhsT=adaw[s][:, k, mm * 128:(mm + 1) * 128],
                                                                 rhs=cT[:, k, :], start=(k == 0), stop=(k == KC - 1)),
                         reads=[f"adaw{s}", "cT"], writes=[bkey(2, m // 24)])
        for h in range(2):
            P.op("dve", lambda e, h=h: e.tensor_tensor(out=modT[:, h * 24:(h + 1) * 24, :], in0=pm[h],
                                                       in1=adab[:, layer, h * 24:(h + 1) * 24].unsqueeze(2).to_broadcast([128, 24, 17]),
                                                       op=ALU.add),
                 reads=[bkey(2, h), "adab"], writes=["modT"])

    def make_G(layer, which):
        r0 = 8 if which == 0 else 32
        P.op("dve", lambda e: e.tensor_scalar(out=G[:], in0=modT[:, r0:r0 + 8, :], scalar1=1.0, scalar2=None, op0=ALU.add),
             reads=["modT"], writes=["G"])
        P.op("dve", lambda e: e.tensor_tensor(out=G[:], in0=G[:], in1=normg[:, which, layer, :].unsqueeze(2).to_broadcast([128, KC, 17]), op=ALU.mult),
             reads=["G", "normg"], writes=["G"])

    LAYER = 1 if stage in ("C1", "C2") else 0
    if stage == "A":
        for lay in (1, 0):
            modulation(lay)
            P.dma("sp", lambda e, lay=lay: e.dma_start(out=mod_out[lay], in_=modT[:].rearrange("p m c -> p (m c)")), reads=["modT"], semkey="modo", final=True)
    else:
        P.dma("sp", lambda e: e.dma_start(out=modT[:].rearrange("p m c -> p (m c)"), in_=modin_d), writes=["modT"], semkey="ld_modT")
    make_G(LAYER, 0)
    P.barrier()
    cur[0] = mark0

    sq = [carve([128, BT], BF16) for _ in range(2)]
    rstd = carve([128, BT])
    tn = [carve([128, BT]) for _ in range(2)]
    mark_ln = cur[0]
    hT = carve([128, KC, BT], BF16)
    p_ss = bank(3, 0)

    def ln_block(src_ap, src_key, ntok, sample, shift_row, out_ap=None, out_key="hT"):
        out_ap = hT if out_ap is None else out_ap
        for k in range(KC):
            s = k % 2
            P.op("act", lambda e, k=k, s=s: e.activation(out=sq[s][:, :ntok], in_=src_ap[:, k, :], func=AF.Square),
                 reads=[src_key], writes=[f"sq{s}"])
            P.op("pe", lambda e, k=k, s=s: e.matmul(p_ss[:, :ntok], lhsT=ones_m[:], rhs=sq[s][:, :ntok], start=(k == 0), stop=(k == KC - 1)),
                 reads=[f"sq{s}", "ones_m"], writes=[bkey(3, 0)])
        P.op("act", lambda e: e.activation(out=rstd[:, :ntok], in_=p_ss[:, :ntok], func=AF.Sqrt, bias=epsb[:], scale=1.0),
             reads=[bkey(3, 0), "epsb"], writes=["rstd"])
        P.op("dve", lambda e: e.reciprocal(out=rstd[:, :ntok], in_=rstd[:, :ntok]), reads=["rstd"], writes=["rstd"])
        for k in range(KC):
            s = k % 2
            P.op("dve", lambda e, k=k, s=s: e.tensor_tensor(out=tn[s][:, :ntok], in0=src_ap[:, k, :], in1=rstd[:, :ntok], op=ALU.mult),
                 reads=[src_key, "rstd"], writes=[f"tn{s}"])
            if not sample:
                P.op("act", lambda e, k=k, s=s: e.activation(out=out_ap[:, k, :ntok], in_=tn[s][:, :ntok], func=AF.Identity,
                                                             scale=G[:, k, 0:1], bias=modT[:, shift_row + k, 0:1]),
                     reads=[f"tn{s}", "G", "modT"], writes=[out_key])
            else:
                tv = tn[s][:, :128].rearrange("p (b t) -> p b t", t=8)
                P.op("dve", lambda e, k=k, tv=tv: e.tensor_tensor(out=tv, in0=tv, in1=G[:, k, 1:17].unsqueeze(2).to_broadcast([128, 16, 8]), op=ALU.mult),
                     reads=[f"tn{s}", "G"], writes=[f"tn{s}"])
                P.op("dve", lambda e, k=k, tv=tv: e.tensor_tensor(out=out_ap[:, k, :128].rearrange("p (b t) -> p b t", t=8), in0=tv,
                                                                  in1=modT[:, shift_row + k, 1:17].unsqueeze(2).to_broadcast([128, 16, 8]), op=ALU.add),
                     reads=[f"tn{s}", "modT"], writes=[out_key])

    win = None
    if stage in ("A", "B"):
        win = carve([128, KC, IN_W], BF16)
        for k in range(KC):
            P.dma("pool", lambda e, k=k: e.dma_start(out=win[:, k, :], in_=w_in[k * 128:(k + 1) * 128, :]), writes=["win"], semkey="win", group="win")

    S = carve([128, 4, 128])
    Sb = carve([128, 4, 128], BF16)
    kd_tok = carve([128, 512], BF16)
    vb_tok = carve([128, 512], BF16)
    p_kb = bank(3, 1)
    p_vb = bank(2, 0)
    p_su = bank(2, 1).rearrange("p (h e) -> p h e", h=4)

    def tok_kv(tc0, grp):
        for k in range(KC):
            P.op("pe", lambda e, k=k: e.matmul(p_kb, lhsT=hT[:, k, tc0:tc0 + 128], rhs=win[:, k, 1280:1792], start=(k == 0), stop=(k == KC - 1)),
                 reads=["hT", "win"], writes=[bkey(3, 1)])
        for k in range(KC):
            P.op("pe", lambda e, k=k: e.matmul(p_vb, lhsT=hT[:, k, tc0:tc0 + 128], rhs=win[:, k, 1792:2304], start=(k == 0), stop=(k == KC - 1)),
                 reads=["hT", "win"], writes=[bkey(2, 0)])
        P.op("dve", lambda e: e.tensor_tensor(out=kd_tok.rearrange("p (h d) -> p h d", h=4), in0=p_kb.rearrange("p (h d) -> p h d", h=4),
                                              in1=dk[:, grp, :].unsqueeze(2).to_broadcast([128, 4, 128]), op=ALU.mult),
             reads=[bkey(3, 1), "dk"], writes=["kd_tok"])
        P.op("act", lambda e: e.copy(out=vb_tok, in_=p_vb), reads=[bkey(2, 0)], writes=["vb_tok"])

    def state_update():
        for h in range(4):
            P.op("pe", lambda e, h=h: e.matmul(p_su[:, h, :], lhsT=kd_tok[:, h * 128:(h + 1) * 128], rhs=vb_tok[:, h * 128:(h + 1) * 128], start=True, stop=True),
                 reads=["kd_tok", "vb_tok"], writes=[bkey(2, 1)])
        P.op("dve", lambda e: e.tensor_tensor(out=S, in0=S, in1=dc[:, 0, :].unsqueeze(2).to_broadcast([128, 4, 128]), op=ALU.mult),
             reads=["S", "dc"], writes=["S"])
        P.op("dve", lambda e: e.tensor_tensor(out=S, in0=S, in1=p_su, op=ALU.add), reads=["S", bkey(2, 1)], writes=["S"])

    def ffn(layer):
        GC = 2
        NG = FC // GC
        P.barrier()
        cur[0] = mark_ln
        make_G(layer, 1)
        hT_all = carve([128, KC, NTOK], BF16)
        wslot = [(carve([128, KC, GC * 128], BF16), carve([128, KC, GC * 128], BF16), carve([128, GC, D], BF16)) for _ in range(3)]
        hid = [carve([128, GC, 512], BF16) for _ in range(2)]
        sgf = [carve([128, 512]) for _ in range(2)]
        gt2 = carve([128, 128])
        print("arena words used (ffn):", cur[0], "of", ARENA_W)
        for t in range(NPT):
            ln_block(xT[:, :, t * 128:(t + 1) * 128], ("xT", t // 4), 128, False, 24, out_ap=hT_all[:, :, t * 128:(t + 1) * 128], out_key="hT_all")
        ln_block(xT[:, :, TOK_P:TOK_P + 128], ("xT", 4), 128, True, 24, out_ap=hT_all[:, :, TOK_P:TOK_P + 128], out_key="hT_all")
        UP = [(bank(0, 0), bkey(0, 0), bank(0, 1), bkey(0, 1)), (bank(1, 0), bkey(1, 0), bank(1, 1), bkey(1, 1))]
        DN = [(bank(2, 0), bkey(2, 0)), (bank(2, 1), bkey(2, 1)), (bank(3, 0), bkey(3, 0)), (bank(3, 1), bkey(3, 1))]
        ui = 0
        di = 0
        blocks = [(b * 512, 512, False) for b in range(4)] + [(TOK_P, 128, True)]
        for g in range(NG):
            ws = g % 3
            wgs, wus, wds = wslot[ws]
            c0h = g * GC * 128
            P.dma("pool", lambda e, wgs=wgs, c0h=c0h: e.dma_start(out=wgs, in_=wg_d[layer, :, c0h:c0h + GC * 128].rearrange("(k p) n -> p k n", p=128)),
                  writes=[f"wg{ws}"], semkey=f"wg{ws}")
            P.dma("pool", lambda e, wus=wus, c0h=c0h: e.dma_start(out=wus, in_=wu_d[layer, :, c0h:c0h + GC * 128].rearrange("(k p) n -> p k n", p=128)),
                  writes=[f"wu{ws}"], semkey=f"wu{ws}")
            P.dma("pool", lambda e, wds=wds, c0h=c0h: e.dma_start(out=wds, in_=wd_d[layer, c0h:c0h + GC * 128, :].rearrange("(c p) n -> p c n", p=128)),
                  writes=[f"wd{ws}"], semkey=f"wd{ws}")
            for (t0, nt, smp) in blocks:
                hs = ui % 2
                for c in range(GC):
                    gps, gkey, ups, ukey = UP[ui % 2]
                    ui += 1
                    for k in range(KC):
                        P.op("pe", lambda e, k=k, c=c, gps=gps, wgs=wgs, nt=nt, t0=t0: e.matmul(gps[:, :nt], lhsT=wgs[:, k, c * 128:(c + 1) * 128], rhs=hT_all[:, k, t0:t0 + nt],
                                                                                  start=(k == 0), stop=(k == KC - 1)),
                             reads=[f"wg{ws}", "hT_all"], writes=[gkey])
                    for k in range(KC):
                        P.op("pe", lambda e, k=k, c=c, ups=ups, wus=wus, nt=nt, t0=t0: e.matmul(ups[:, :nt], lhsT=wus[:, k, c * 128:(c + 1) * 128], rhs=hT_all[:, k, t0:t0 + nt],
                                                                                  start=(k == 0), stop=(k == KC - 1)),
                             reads=[f"wu{ws}", "hT_all"], writes=[ukey])
                    sgs = sgf[c % 2]
                    P.op("act", lambda e, gps=gps, sgs=sgs, nt=nt: e.activation(out=sgs[:, :nt], in_=gps[:, :nt], func=AF.Silu), reads=[gkey], writes=[f"sgf{c % 2}"])
                    P.op("dve", lambda e, ups=ups, sgs=sgs, hs=hs, c=c, nt=nt: e.tensor_tensor(out=hid[hs][:, c, :nt], in0=ups[:, :nt], in1=sgs[:, :nt], op=ALU.mult),
                         reads=[ukey, f"sgf{c % 2}"], writes=[f"hid{hs}"])
                for m in range(KC):
                    dps, dkey = DN[di % 4]
                    di += 1
                    for c in range(GC):
                        P.op("pe", lambda e, m=m, c=c, dps=dps, wds=wds, hs=hs, nt=nt: e.matmul(dps[:, :nt], lhsT=wds[:, c, m * 128:(m + 1) * 128], rhs=hid[hs][:, c, :nt],
                                                                                         start=(c == 0), stop=(c == GC - 1)),
                             reads=[f"wd{ws}", f"hid{hs}"], writes=[dkey])
                    xs = xT[:, m, t0:t0 + nt]
                    xkey = ("xT", t0 // 512)
                    if not smp:
                        P.op("dve", lambda e, m=m, dps=dps, xs=xs, nt=nt: e.scalar_tensor_tensor(out=xs, in0=dps[:, :nt], scalar=modT[:, 40 + m, 0:1], in1=xs,
                                                                                         op0=ALU.mult, op1=ALU.add),
                             reads=[dkey, "modT", xkey], writes=[xkey])
                    else:
                        gv = gt2.rearrange("p (b t) -> p b t", t=8)
                        P.op("dve", lambda e, m=m, dps=dps, gv=gv: e.tensor_tensor(out=gv, in0=dps[:, :128].rearrange("p (b t) -> p b t", t=8),
                                                                                  in1=modT[:, 40 + m, 1:17].unsqueeze(2).to_broadcast([128, 16, 8]), op=ALU.mult),
                             reads=[dkey, "modT"], writes=["gt2"])
                        P.op("dve", lambda e, xs=xs: e.tensor_tensor(out=xs, in0=xs, in1=gt2, op=ALU.add), reads=["gt2", xkey], writes=[xkey])
        return hT_all

    if stage == "A":
        P.op("dve", lambda e: e.memset(S, 0.0), writes=["S"])
        for b in range(TOK_P // BT):
            ln_block(xT[:, :, b * BT:(b + 1) * BT], ("xT", (b * BT) // 512), BT, False, 0)
            for i in range(TPB):
                tok_kv(i * 128, 0)
                state_update()
        P.dma("sp", lambda e: e.dma_start(out=lret_out.rearrange("h d e -> d h e"), in_=S), reads=["S"], semkey="lret", final=True)
        stats = P.build()
        nc_allow.__exit__(None, None, None)
        es.close()
        return nc, stats

    def s5_stage(full):
        TWO_PI = 2.0 * np.pi
        cur[0] = mark_ln
        hT_all = carve([128, KC, NTOK], BF16)
        for t in range(NPT):
            ln_block(xT[:, :, t * 128:(t + 1) * 128], ("xT", t // 4), 128, False, 0, out_ap=hT_all[:, :, t * 128:(t + 1) * 128], out_key="hT_all")
        ln_block(xT[:, :, TOK_P:TOK_P + 128], ("xT", 4), 128, True, 0, out_ap=hT_all[:, :, TOK_P:TOK_P + 128], out_key="hT_all")

        def t64():
            return carve([128, 64])
        AreT, AimT, dtT, ar, ai, rho, thr, ph64, ph512, sinT, cosT, tmpa, tmpb, lre, lim, fre, fim, rden64 = [t64() for _ in range(18)]
        PHB = carve([128, 64, 4])
        pib = carve([128, 1])
        Glast = carve([128, 64])
        Alast = carve([128, 64])
        Blast = carve([128, 64])
        tal = carve([128, 512])
        tbp = carve([128, 512])
        tbs = carve([128, 128])
        amask = carve([128, 128])
        maskg = carve([128, 8])
        rott = carve([128, 128])
        Dsk = carve([128, KC])
        Mc = carve([128, KC, 128], BF16)
        Mcsw = carve([128, KC, 128], BF16)
        CA = carve([128, 64, 128], BF16)
        CB = carve([128, 64, 128], BF16)
        mark_s5 = cur[0]
        Bn_re = carve([128, 64, 16])
        Bn_im = carve([128, 64, 16])
        tB1 = carve([128, 64, 16])
        tB2 = carve([128, 64, 16])
        Cn = carve([128, 16, 2, 64])
        Cn2 = carve([128, 16, 2, 64])
        CsA = carve([128, 64, 16])
        CsB = carve([128, 64, 16])
        print("arena words used (s5 prep):", cur[0], "of", ARENA_W)
        P.op("dve", lambda e: e.memset(pib, float(np.pi / 2)), writes=["pib"])
        for half in range(2):
            hs_ = slice(half * 64, half * 64 + 64)
            P.dma("sp", lambda e, hs_=hs_: e.dma_start(out=AreT[hs_, :], in_=A_re_d.rearrange("g p -> p g")), writes=["AreT"], semkey="ld_AreT", group="ld_AreT")
            P.dma("sp", lambda e, hs_=hs_: e.dma_start(out=AimT[hs_, :], in_=A_im_d.rearrange("g p -> p g")), writes=["AimT"], semkey="ld_AimT", group="ld_AimT")
        P.dma("sp", lambda e: e.dma_start(out=dtT, in_=ldt_d.partition_broadcast(128)), writes=["dtT"], semkey="ld_dtT")
        for nm, tl, dd in (("tal", tal, tal_d), ("tbp", tbp, tbp_d), ("tbs", tbs, tbs_d), ("amask", amask, amask_d), ("maskg", maskg, maskg_d), ("rott", rott, rott_d)):
            P.dma("sp", lambda e, tl=tl, dd=dd: e.dma_start(out=tl, in_=dd), writes=[nm], semkey="ld_" + nm)
        P.dma("sp", lambda e: e.dma_start(out=Dsk, in_=Dsk_d.rearrange("o (k p) -> p (o k)", p=128)), writes=["Dsk"], semkey="ld_Dsk")
        P.dma("sp", lambda e: e.dma_start(out=Bn_re[0:64], in_=B_re_d.rearrange("g p j -> p g j")), writes=["Bn_re"], semkey="ld_Bn_re")
        P.dma("sp", lambda e: e.dma_start(out=Bn_im[0:64], in_=B_im_d.rearrange("g p j -> p g j")), writes=["Bn_im"], semkey="ld_Bn_im")
        P.dma("sp", lambda e: e.dma_start(out=Cn[0:64, :, 0, :], in_=C_re_d), writes=["Cn"], semkey="ld_Cn", group="ld_Cn")
        P.dma("sp", lambda e: e.dma_start(out=Cn[0:64, :, 1, :], in_=C_im_d), writes=["Cn"], semkey="ld_Cn", group="ld_Cn")
        P.dma("sp", lambda e: e.dma_start(out=Cn2[0:64, :, 0, :], in_=C_im_d), writes=["Cn2"], semkey="ld_Cn2", group="ld_Cn2")
        P.dma("sp", lambda e: e.dma_start(out=Cn2[0:64, :, 1, :], in_=C_re_d), writes=["Cn2"], semkey="ld_Cn2", group="ld_Cn2")

        def V(fn, reads, writes, eng="dve"):
            P.op(eng, fn, reads=reads, writes=writes)

        V(lambda e: e.activation(out=dtT, in_=dtT, func=AF.Exp), ["dtT"], ["dtT"], "act")
        V(lambda e: e.tensor_tensor(out=ar, in0=AreT, in1=dtT, op=ALU.mult), ["AreT", "dtT"], ["ar"])
        V(lambda e: e.tensor_tensor(out=ai, in0=AimT, in1=dtT, op=ALU.mult), ["AimT", "dtT"], ["ai"])
        V(lambda e: e.activation(out=rho, in_=ar, func=AF.Exp), ["ar"], ["rho"], "act")
        ti64 = carve([128, 64], I32)
        aiT = t64()

        def fracr(out, okey, inp, ikey):
            V(lambda e: e.tensor_copy(out=ti64, in_=inp), [ikey], ["ti64"])
            V(lambda e: e.tensor_copy(out=tmpb, in_=ti64), ["ti64"], ["tmpb"])
            V(lambda e: e.tensor_tensor(out=out, in0=inp, in1=tmpb, op=ALU.subtract), [ikey, "tmpb"], [okey])

        V(lambda e: e.tensor_scalar(out=aiT, in0=ai, scalar1=float(1.0 / TWO_PI), scalar2=None, op0=ALU.mult), ["ai"], ["aiT"])
        fracr(thr, "thr", aiT, "aiT")
        V(lambda e: e.tensor_scalar(out=tmpa, in0=aiT, scalar1=64.0, scalar2=None, op0=ALU.mult), ["aiT"], ["tmpa"])
        fracr(ph64, "ph64", tmpa, "tmpa")
        V(lambda e: e.tensor_scalar(out=tmpa, in0=aiT, scalar1=512.0, scalar2=None, op0=ALU.mult), ["aiT"], ["tmpa"])
        fracr(ph512, "ph512", tmpa, "tmpa")
        for blk in range(4):
            V(lambda e, blk=blk: e.tensor_scalar(out=tmpa, in0=ph512, scalar1=float(blk), scalar2=None, op0=ALU.mult), ["ph512"], ["tmpa"])
            fracr(PHB[:, :, blk], "PHB", tmpa, "tmpa")

        def sincos(ang, akey, s_out, skey, c_out, ckey, tmp, tkey):
            V(lambda e: e.activation(out=s_out, in_=ang, func=AF.Sin, scale=TWO_PI), [akey], [skey], "act")
            V(lambda e: e.activation(out=tmp, in_=ang, func=AF.Abs), [akey], [tkey], "act")
            V(lambda e: e.activation(out=c_out, in_=tmp, func=AF.Sin, scale=-TWO_PI, bias=pib[:]), [tkey, "pib"], [ckey], "act")

        sincos(thr, "thr", sinT, "sinT", cosT, "cosT", tmpa, "tmpa")
        V(lambda e: e.tensor_tensor(out=lre, in0=rho, in1=cosT, op=ALU.mult), ["rho", "cosT"], ["lre"])
        V(lambda e: e.tensor_tensor(out=lim, in0=rho, in1=sinT, op=ALU.mult), ["rho", "sinT"], ["lim"])
        V(lambda e: e.tensor_scalar(out=tmpa, in0=lre, scalar1=-1.0, scalar2=None, op0=ALU.add), ["lre"], ["tmpa"])
        V(lambda e: e.tensor_tensor(out=rden64, in0=AreT, in1=AreT, op=ALU.mult), ["AreT"], ["rden64"])
        V(lambda e: e.tensor_tensor(out=tmpb, in0=AimT, in1=AimT, op=ALU.mult), ["AimT"], ["tmpb"])
        V(lambda e: e.tensor_tensor(out=rden64, in0=rden64, in1=tmpb, op=ALU.add), ["rden64", "tmpb"], ["rden64"])
        V(lambda e: e.reciprocal(out=rden64, in_=rden64), ["rden64"], ["rden64"])
        V(lambda e: e.tensor_tensor(out=fre, in0=tmpa, in1=AreT, op=ALU.mult), ["tmpa", "AreT"], ["fre"])
        V(lambda e: e.tensor_tensor(out=tmpb, in0=lim, in1=AimT, op=ALU.mult), ["lim", "AimT"], ["tmpb"])
        V(lambda e: e.tensor_tensor(out=fre, in0=fre, in1=tmpb, op=ALU.add), ["fre", "tmpb"], ["fre"])
        V(lambda e: e.tensor_tensor(out=fre, in0=fre, in1=rden64, op=ALU.mult), ["fre", "rden64"], ["fre"])
        V(lambda e: e.tensor_tensor(out=fim, in0=lim, in1=AreT, op=ALU.mult), ["lim", "AreT"], ["fim"])
        V(lambda e: e.tensor_tensor(out=tmpb, in0=tmpa, in1=AimT, op=ALU.mult), ["tmpa", "AimT"], ["tmpb"])
        V(lambda e: e.tensor_tensor(out=fim, in0=fim, in1=tmpb, op=ALU.subtract), ["fim", "tmpb"], ["fim"])
        V(lambda e: e.tensor_tensor(out=fim, in0=fim, in1=rden64, op=ALU.mult), ["fim", "rden64"], ["fim"])
        h64 = slice(0, 64)
        frb = fre[h64, :].unsqueeze(2).to_broadcast([64, 64, 16])
        fib = fim[h64, :].unsqueeze(2).to_broadcast([64, 64, 16])
        V(lambda e: e.tensor_tensor(out=tB1[h64], in0=Bn_re[h64], in1=frb, op=ALU.mult), ["Bn_re", "fre"], ["tB1"])
        V(lambda e: e.tensor_tensor(out=tB2[h64], in0=Bn_im[h64], in1=fib, op=ALU.mult), ["Bn_im", "fim"], ["tB2"])
        V(lambda e: e.tensor_tensor(out=tB1[h64], in0=tB1[h64], in1=tB2[h64], op=ALU.subtract), ["tB1", "tB2"], ["tB1"])
        V(lambda e: e.tensor_tensor(out=tB2[h64], in0=Bn_im[h64], in1=frb, op=ALU.mult), ["Bn_im", "fre"], ["tB2"])
        V(lambda e: e.tensor_tensor(out=Bn_im[h64], in0=Bn_re[h64], in1=fib, op=ALU.mult), ["Bn_re", "fim", "tB2"], ["Bn_im"])
        V(lambda e: e.tensor_tensor(out=tB2[h64], in0=tB2[h64], in1=Bn_im[h64], op=ALU.add), ["tB2", "Bn_im"], ["tB2"])
        ptr = bank(3, 0)
        for F in range(KC):
            P.op("pe", lambda e, F=F: e.transpose(ptr[:, 0:64], tB1[h64, 8 * F:8 * F + 8, :].rearrange("p a b -> p (a b)"), ident[0:64, 0:64]),
                 reads=["tB1", "ident"], writes=[bkey(3, 0)])
            P.op("pe", lambda e, F=F: e.transpose(ptr[:, 64:128], tB2[h64, 8 * F:8 * F + 8, :].rearrange("p a b -> p (a b)"), ident[0:64, 0:64]),
                 reads=["tB2", "ident"], writes=[bkey(3, 0)])
            V(lambda e, F=F: e.copy(out=Mc[:, F, :], in_=ptr[:, 0:128]), [bkey(3, 0)], ["Mc"], "act")
            V(lambda e, F=F: e.copy(out=Mcsw[:, F, 0:64], in_=ptr[:, 64:128]), [bkey(3, 0)], ["Mcsw"], "act")
            V(lambda e, F=F: e.mul(out=Mcsw[:, F, 64:128], in_=ptr[:, 0:64], mul=-1.0), [bkey(3, 0)], ["Mcsw"], "act")
        for (src, skey, dst, dkey) in ((Cn, "Cn", CsA, "CsA"), (Cn2, "Cn2", CsB, "CsB")):
            for ib in range(2):
                for ii in range(8):
                    i_ = ib * 8 + ii
                    P.op("pe", lambda e, src=src, i_=i_, ii=ii: e.transpose(ptr[:, ii * 64:(ii + 1) * 64], src[h64, i_, :, :].rearrange("p c q -> p (c q)"), ident[0:64, 0:64]),
                         reads=[skey, "ident"], writes=[bkey(3, 0)])
                V(lambda e, dst=dst, ib=ib: e.copy(out=dst[:, :, ib * 8:(ib + 1) * 8].rearrange("p g i -> p i g"), in_=ptr.rearrange("p (i g) -> p i g", i=8)),
                  [bkey(3, 0)], [dkey], "act")
        V(lambda e: e.tensor_scalar(out=CsA[64:128], in0=CsA[64:128], scalar1=-1.0, scalar2=None, op0=ALU.mult), ["CsA"], ["CsA"])
        V(lambda e: e.tensor_scalar(out=CsB, in0=CsB, scalar1=-1.0, scalar2=None, op0=ALU.mult), ["CsB"], ["CsB"])
        V(lambda e: e.memset(CA, 0.0), [], ["CA"], "pool")
        V(lambda e: e.memset(CB, 0.0), [], ["CB"], "pool")
        for gl in range(8):
            for (src, skey, dst, dkey) in ((CsA, "CsA", CA, "CA"), (CsB, "CsB", CB, "CB")):
                V(lambda e, src=src, dst=dst, gl=gl: e.tensor_copy(out=dst.rearrange("p (f g) c -> p f g c", g=8)[:, :, gl, 16 * gl:16 * gl + 16],
                                                                   in_=src.rearrange("p (f g) i -> p f g i", g=8)[:, :, gl, :]),
                  [skey], [dkey])

        hin = carve([128, 64]) if False else None
        P.op("dve", lambda e: e.memset(Glast, 0.0), writes=["Glast"])
        if full:
            ang = tmpa
            V(lambda e: e.tensor_scalar(out=ang, in0=aiT, scalar1=2048.0, scalar2=None, op0=ALU.mult), ["aiT"], ["tmpa"])
            fracr(fim, "fim", ang, "tmpa")
            sincos(fim, "fim", sinT, "sinT", cosT, "cosT", fre, "fre")
            V(lambda e: e.activation(out=lre, in_=ar, func=AF.Exp, scale=2048.0), ["ar"], ["lre"], "act")
            V(lambda e: e.tensor_tensor(out=lim, in0=lre, in1=sinT, op=ALU.mult), ["lre", "sinT"], ["lim"])
            V(lambda e: e.tensor_tensor(out=lre, in0=lre, in1=cosT, op=ALU.mult), ["lre", "cosT"], ["lre"])
            wsel = carve([128, 8])
            fr = [carve([128, 64]) for _ in range(2)]
            P.dma("sp", lambda e: e.dma_start(out=wsel, in_=wsel_d), writes=["wsel"], semkey="ld_wsel")
            prot = bank(3, 1)[:, 0:64]
            for r in range(7):
                s_ = r % 2
                P.dma("sp", lambda e, r=r, s_=s_: e.dma_start(out=fr[s_], in_=fall_d[r]), writes=[f"fr{s_}"], semkey=f"fr{s_}")
                P.op("pe", lambda e: e.matmul(prot, lhsT=rott, rhs=Glast, start=True, stop=True), reads=["rott", "Glast"], writes=[bkey(3, 1)])
                V(lambda e: e.tensor_tensor(out=tmpa, in0=lre, in1=Glast, op=ALU.mult), ["lre", "Glast"], ["tmpa"])
                V(lambda e: e.tensor_tensor(out=tmpb, in0=lim, in1=prot, op=ALU.mult), ["lim", bkey(3, 1)], ["tmpb"])
                V(lambda e: e.tensor_tensor(out=tmpa, in0=tmpa, in1=tmpb, op=ALU.add), ["tmpa", "tmpb"], ["tmpa"])
                V(lambda e, s_=s_: e.tensor_tensor(out=tmpa, in0=tmpa, in1=fr[s_], op=ALU.add), ["tmpa", f"fr{s_}"], ["tmpa"])
                V(lambda e: e.tensor_tensor(out=tmpa, in0=tmpa, in1=Glast, op=ALU.subtract), ["tmpa", "Glast"], ["tmpa"])
                V(lambda e, r=r: e.scalar_tensor_tensor(out=Glast, in0=tmpa, scalar=wsel[:, r:r + 1], in1=Glast, op0=ALU.mult, op1=ALU.add),
                  ["tmpa", "wsel", "Glast"], ["Glast"])
        P.barrier()
        cur[0] = mark_s5
        NW = 2
        um = [carve([128, 512], BF16) for _ in range(NW)]
        xang = [carve([128, 512]) for _ in range(NW)]
        sang = [carve([128, 512]) for _ in range(NW)]
        SINt = [carve([128, 512]) for _ in range(NW)]
        COSt = [carve([128, 512]) for _ in range(NW)]
        Wt = [carve([128, 512]) for _ in range(NW)]
        Ab = [carve([128, 512], BF16) for _ in range(NW)]
        Bb = [carve([128, 512], BF16) for _ in range(NW)]
        aseq = carve([128, 128])
        ysb = carve([128, 512])
        y2 = carve([128, 512])
        H0T = carve([128, 64, 16])
        Asl = carve([128, 64, 16])
        Bsl = carve([128, 64, 16])
        h0n = carve([128, 64, 128]) if False else None
        print("arena words used (s5 main):", cur[0], "of", ARENA_W)
        BU = [(bank(0, 0), bkey(0, 0), bank(0, 1), bkey(0, 1)), (bank(1, 0), bkey(1, 0), bank(1, 1), bkey(1, 1))]
        YP = [(bank(2, 0), bkey(2, 0)), (bank(2, 1), bkey(2, 1))]
        wi = [0]

        def s5_T(F, gl, t0, nt, blk, sample):
            g = 8 * F + gl
            w = wi[0] % NW
            wi[0] += 1
            bu, bukey, bus, buskey = BU[w % 2]
            V(lambda e: e.activation(out=um[w][:, :nt], in_=hT_all[:, F, t0:t0 + nt], func=AF.Copy, scale=maskg[:, gl:gl + 1]),
              [("hT_all", F, t0), "maskg"], [f"um{w}"], "act")
            P.op("pe", lambda e: e.matmul(bu[:, :nt], lhsT=Mc[:, F, :], rhs=um[w][:, :nt], start=True, stop=True), reads=["Mc", f"um{w}"], writes=[bukey])
            P.op("pe", lambda e: e.matmul(bus[:, :nt], lhsT=Mcsw[:, F, :], rhs=um[w][:, :nt], start=True, stop=True), reads=["Mcsw", f"um{w}"], writes=[buskey])
            MAGIC = 12582912.0
            if not sample:
                V(lambda e: e.activation(out=xang[w], in_=tal, func=AF.Identity, scale=ph64[:, g:g + 1], bias=PHB[:, g, blk:blk + 1]),
                  ["tal", "ph64", "PHB"], [f"xang{w}"], "act")
                V(lambda e: e.scalar_tensor_tensor(out=xang[w], in0=tbp, scalar=thr[:, g:g + 1], in1=xang[w], op0=ALU.mult, op1=ALU.add),
                  ["tbp", "thr", f"xang{w}"], [f"xang{w}"])
            else:
                V(lambda e: e.activation(out=xang[w][:, :nt], in_=tbs, func=AF.Copy, scale=thr[:, g:g + 1]), ["tbs", "thr"], [f"xang{w}"], "act")
            V(lambda e: e.tensor_scalar(out=sang[w][:, :nt], in0=xang[w][:, :nt], scalar1=MAGIC, scalar2=MAGIC, op0=ALU.add, op1=ALU.subtract),
              [f"xang{w}"], [f"sang{w}"])
            V(lambda e: e.tensor_tensor(out=xang[w][:, :nt], in0=xang[w][:, :nt], in1=sang[w][:, :nt], op=ALU.subtract), [f"xang{w}", f"sang{w}"], [f"xang{w}"])
            V(lambda e: e.activation(out=SINt[w][:, :nt], in_=xang[w][:, :nt], func=AF.Sin, scale=TWO_PI), [f"xang{w}"], [f"SINt{w}"], "act")
            V(lambda e: e.activation(out=sang[w][:, :nt], in_=xang[w][:, :nt], func=AF.Abs), [f"xang{w}"], [f"sang{w}"], "act")
            V(lambda e: e.activation(out=COSt[w][:, :nt], in_=sang[w][:, :nt], func=AF.Sin, scale=-TWO_PI, bias=pib[:]), [f"sang{w}", "pib"], [f"COSt{w}"], "act")
            return dict(F=F, gl=gl, g=g, w=w, t0=t0, nt=nt, blk=blk, sample=sample, bu=bu, bukey=bukey, bus=bus, buskey=buskey)

        def s5_S(c, ypk, last_blk):
            F, gl, g, w, t0, nt, blk, sample = c['F'], c['gl'], c['g'], c['w'], c['t0'], c['nt'], c['blk'], c['sample']
            bu, bukey, bus, buskey = c['bu'], c['bukey'], c['bus'], c['buskey']
            yp, ykey = ypk
            V(lambda e: e.tensor_tensor(out=sang[w][:, :nt], in0=bu[:, :nt], in1=COSt[w][:, :nt], op=ALU.mult), [bukey, f"COSt{w}"], [f"sang{w}"])
            V(lambda e: e.tensor_tensor(out=Wt[w][:, :nt], in0=bus[:, :nt], in1=SINt[w][:, :nt], op=ALU.mult), [buskey, f"SINt{w}"], [f"Wt{w}"])
            V(lambda e: e.tensor_tensor(out=Wt[w][:, :nt], in0=Wt[w][:, :nt], in1=sang[w][:, :nt], op=ALU.add), [f"Wt{w}", f"sang{w}"], [f"Wt{w}"])
            if not sample:
                V(lambda e: e.tensor_tensor_scan(out=Wt[w], data0=rho[:, g:g + 1].to_broadcast([128, 512]), data1=Wt[w], initial=Glast[:, g:g + 1],
                                                 op0=ALU.mult, op1=ALU.add), [f"Wt{w}", "rho", "Glast"], [f"Wt{w}"])
                V(lambda e: e.copy(out=Glast[:, g:g + 1], in_=Wt[w][:, 511:512]), [f"Wt{w}"], ["Glast"], "act")
                if last_blk:
                    V(lambda e: e.tensor_tensor(out=Alast[:, g:g + 1], in0=Wt[w][:, 511:512], in1=COSt[w][:, 511:512], op=ALU.mult), [f"Wt{w}", f"COSt{w}"], ["Alast"])
                    V(lambda e: e.tensor_tensor(out=Blast[:, g:g + 1], in0=Wt[w][:, 511:512], in1=SINt[w][:, 511:512], op=ALU.mult), [f"Wt{w}", f"SINt{w}"], ["Blast"])
            else:
                wv = Wt[w][:, :128].rearrange("p (b t) -> p b t", t=8)
                V(lambda e: e.scalar_tensor_tensor(out=wv[:, :, 0], in0=H0T[:, g, :], scalar=rho[:, g:g + 1], in1=wv[:, :, 0], op0=ALU.mult, op1=ALU.add),
                  ["H0T", "rho", f"Wt{w}"], [f"Wt{w}"])
                V(lambda e: e.tensor_scalar(out=aseq, in0=amask, scalar1=rho[:, g:g + 1], scalar2=None, op0=ALU.mult), ["amask", "rho"], ["aseq"])
                V(lambda e: e.tensor_tensor_scan(out=Wt[w][:, :128], data0=aseq, data1=Wt[w][:, :128], initial=0.0, op0=ALU.mult, op1=ALU.add),
                  [f"Wt{w}", "aseq"], [f"Wt{w}"])
                cv = COSt[w][:, :128].rearrange("p (b t) -> p b t", t=8)
                sv = SINt[w][:, :128].rearrange("p (b t) -> p b t", t=8)
                V(lambda e: e.tensor_tensor(out=Asl[:, g, :], in0=wv[:, :, 7], in1=cv[:, :, 7], op=ALU.mult), [f"Wt{w}", f"COSt{w}"], ["Asl"])
                V(lambda e: e.tensor_tensor(out=Bsl[:, g, :], in0=wv[:, :, 7], in1=sv[:, :, 7], op=ALU.mult), [f"Wt{w}", f"SINt{w}"], ["Bsl"])
            if full:
                V(lambda e: e.tensor_tensor(out=Ab[w][:, :nt], in0=Wt[w][:, :nt], in1=COSt[w][:, :nt], op=ALU.mult), [f"Wt{w}", f"COSt{w}"], [f"Ab{w}"])
                V(lambda e: e.tensor_tensor(out=Bb[w][:, :nt], in0=Wt[w][:, :nt], in1=SINt[w][:, :nt], op=ALU.mult), [f"Wt{w}", f"SINt{w}"], [f"Bb{w}"], "pool")
                P.op("pe", lambda e: e.matmul(yp[:, :nt], lhsT=CA[:, g, :], rhs=Ab[w][:, :nt], start=(gl == 0), stop=False), reads=["CA", f"Ab{w}"], writes=[ykey])
                P.op("pe", lambda e: e.matmul(yp[:, :nt], lhsT=CB[:, g, :], rhs=Bb[w][:, :nt], start=False, stop=(gl == 7)), reads=["CB", f"Bb{w}"], writes=[ykey])

        def y_finish(F, t0, nt, ypk):
            yp, ykey = ypk
            V(lambda e: e.scalar_tensor_tensor(out=ysb[:, :nt], in0=hT_all[:, F, t0:t0 + nt], scalar=Dsk[:, F:F + 1], in1=yp[:, :nt], op0=ALU.mult, op1=ALU.add),
              [("hT_all", F, t0), "Dsk", ykey], ["ysb"])
            V(lambda e: e.tensor_tensor(out=y2[:, :nt], in0=ysb[:, :nt], in1=ysb[:, :nt], op=ALU.mult), ["ysb"], ["y2"], "pool")
            V(lambda e: e.tensor_scalar(out=y2[:, :nt], in0=y2[:, :nt], scalar1=0.044715, scalar2=1.0, op0=ALU.mult, op1=ALU.add), ["y2"], ["y2"], "pool")
            V(lambda e: e.tensor_tensor(out=y2[:, :nt], in0=y2[:, :nt], in1=ysb[:, :nt], op=ALU.mult), ["y2", "ysb"], ["y2"], "pool")
            V(lambda e: e.activation(out=y2[:, :nt], in_=y2[:, :nt], func=AF.Tanh, scale=float(np.sqrt(2.0 / np.pi))), ["y2"], ["y2"], "act")
            V(lambda e: e.tensor_scalar(out=y2[:, :nt], in0=y2[:, :nt], scalar1=0.5, scalar2=0.5, op0=ALU.mult, op1=ALU.add), ["y2"], ["y2"])
            V(lambda e: e.tensor_tensor(out=hT_all[:, F, t0:t0 + nt], in0=y2[:, :nt], in1=ysb[:, :nt], op=ALU.mult), ["y2", "ysb"], [("hT_all", F, t0)])

        if True:
            h0n = um
        s0n = carve([128, 64, 128]) if False else None
        hnat = carve([16, 64 * 128]) if False else None
        stg = xang[0]
        pth = bank(3, 0)
        for gq_ in range(16 if full else 0):
            P.dma("sp", lambda e, gq_=gq_: e.dma_start(out=stg[0:16, :].rearrange("p (g c) -> p g c", g=4)[:, :, 0:64], in_=s5r_in[:, 4 * gq_:4 * gq_ + 4, :]),
                  writes=["xang0"], semkey="stg", group=f"stg{gq_}")
            P.dma("sp", lambda e, gq_=gq_: e.dma_start(out=stg[0:16, :].rearrange("p (g c) -> p g c", g=4)[:, :, 64:128], in_=s5i_in[:, 4 * gq_:4 * gq_ + 4, :]),
                  writes=["xang0"], semkey="stg", group=f"stg{gq_}")
            for gg in range(4):
                P.op("pe", lambda e, gg=gg: e.transpose(pth[:, gg * 16:(gg + 1) * 16], stg[0:16, gg * 128:(gg + 1) * 128], ident[0:16, 0:16]),
                     reads=["xang0", "ident"], writes=[bkey(3, 0)])
            V(lambda e, gq_=gq_: e.copy(out=H0T[:, 4 * gq_:4 * gq_ + 4, :], in_=pth[:, 0:64].rearrange("p (g b) -> p g b", g=4)), [bkey(3, 0)], ["H0T"], "act")

        units = []
        yi = 0
        for F in range(KC):
            for blk in range(4):
                ypk = YP[yi % 2]
                yi += 1
                for gl in range(8):
                    units.append(dict(F=F, gl=gl, t0=blk * 512, nt=512, blk=blk, sample=False, ypk=ypk, last=(blk == 3), fin=(gl == 7)))
            if full:
                ypk = YP[yi % 2]
                yi += 1
                for gl in range(8):
                    units.append(dict(F=F, gl=gl, t0=TOK_P, nt=128, blk=0, sample=True, ypk=ypk, last=False, fin=(gl == 7)))
        def cap(fn_, *a_):
            P.capture = []
            r_ = fn_(*a_)
            ops_ = P.capture
            P.capture = None
            return ops_, r_

        def emit(ops_):
            for o_ in ops_:
                P.op(*o_)

        u0 = units[0]
        opsT, ctx = cap(s5_T, u0["F"], u0["gl"], u0["t0"], u0["nt"], u0["blk"], u0["sample"])
        emit(opsT)
        for n, u in enumerate(units):
            opsS, _ = cap(s5_S, ctx, u["ypk"], u["last"])
            if n + 1 < len(units):
                v = units[n + 1]
                opsT, nxt = cap(s5_T, v["F"], v["gl"], v["t0"], v["nt"], v["blk"], v["sample"])
            else:
                opsT, nxt = [], None
            pre = opsT[:4]
            rest = opsT[4:]
            k_ = 0
            while k_ < len(rest) and rest[k_][0] == "dve":
                k_ += 1
            t_dve, t_tail = rest[:k_], rest[k_:]
            seq = list(pre)
            for j_ in range(3):
                seq.append(opsS[j_])
                if j_ < len(t_dve):
                    seq.append(t_dve[j_])
            seq += t_dve[3:] + t_tail + opsS[3:]
            assert len(seq) == len(opsT) + len(opsS)
            emit(seq)
            if full and u["fin"]:
                y_finish(u["F"], u["t0"], u["nt"], u["ypk"])
            ctx = nxt

        pfin = bank(3, 1)[:, 0:64]
        P.op("pe", lambda e: e.matmul(pfin, lhsT=rott, rhs=Blast, start=True, stop=True), reads=["rott", "Blast"], writes=[bkey(3, 1)])
        V(lambda e: e.tensor_tensor(out=Alast, in0=Alast, in1=pfin, op=ALU.add), ["Alast", bkey(3, 1)], ["Alast"])
        if not full:
            P.dma("sp", lambda e: e.dma_start(out=floc_out, in_=Alast), reads=["Alast"], semkey="floc", final=True)
            return None
        ptp = bank(3, 0)[0:64, 0:128]
        P.op("pe", lambda e: e.transpose(ptp, Alast, ident[:]), reads=["Alast", "ident"], writes=[bkey(3, 0)])
        V(lambda e: e.copy(out=ysb[0:64, 0:128], in_=ptp), [bkey(3, 0)], ["ysb"], "act")
        P.dma("sp", lambda e: e.dma_start(out=ps5r_out, in_=ysb[0:64, 0:64]), reads=["ysb"], semkey="ps5", final=True)
        P.dma("sp", lambda e: e.dma_start(out=ps5i_out, in_=ysb[0:64, 64:128]), reads=["ysb"], semkey="ps5", final=True)
        for hf in range(2):
            pf2 = bank(3, 1)
            P.op("pe", lambda e, hf=hf: e.matmul(pf2, lhsT=rott, rhs=Bsl[:, 32 * hf:32 * hf + 32, :].rearrange("p g b -> p (g b)"), start=True, stop=True),
                 reads=["rott", "Bsl"], writes=[bkey(3, 1)])
            V(lambda e, hf=hf: e.tensor_tensor(out=Asl[:, 32 * hf:32 * hf + 32, :].rearrange("p g b -> p (g b)"), in0=Asl[:, 32 * hf:32 * hf + 32, :].rearrange("p g b -> p (g b)"),
                                               in1=pf2, op=ALU.add), ["Asl", bkey(3, 1)], ["Asl"])
        for gq_ in range(16):
            pto = bank(3, 0)[0:16, :]
            for gg in range(4):
                P.op("pe", lambda e, gq_=gq_, gg=gg: e.transpose(pto[:, gg * 128:(gg + 1) * 128], Asl[:, 4 * gq_ + gg, :], ident[:]),
                     reads=["Asl", "ident"], writes=[bkey(3, 0)])
            V(lambda e: e.copy(out=stg[0:16, :], in_=pto), [bkey(3, 0)], ["xang0"], "act")
            P.dma("sp", lambda e, gq_=gq_: e.dma_start(out=ss5r_out[:, 4 * gq_:4 * gq_ + 4, :], in_=stg[0:16, :].rearrange("p (g c) -> p g c", g=4)[:, :, 0:64]),
                  reads=["xang0"], semkey="ss5o", final=True)
            P.dma("sp", lambda e, gq_=gq_: e.dma_start(out=ss5i_out[:, 4 * gq_:4 * gq_ + 4, :], in_=stg[0:16, :].rearrange("p (g c) -> p g c", g=4)[:, :, 64:128]),
                  reads=["xang0"], semkey="ss5o", final=True)

        P.barrier()
        cur[0] = mark_ln + NTOK * KC // 2
        glua = carve([128, KC, D], BF16)
        glub = carve([128, KC, D], BF16)
        sgb = [carve([128, 512]) for _ in range(2)]
        prd = [carve([128, 512]) for _ in range(2)]
        gt3 = carve([128, 128])
        P.dma("pool", lambda e: e.dma_start(out=glua, in_=glua_d.rearrange("(k p) n -> p k n", p=128)), writes=["glua"], semkey="glua")
        P.dma("pool", lambda e: e.dma_start(out=glub, in_=glub_d.rearrange("(k p) n -> p k n", p=128)), writes=["glub"], semkey="glub")
        GA = [(bank(0, 0), bkey(0, 0), bank(0, 1), bkey(0, 1)), (bank(1, 0), bkey(1, 0), bank(1, 1), bkey(1, 1))]
        gi = 0
        for (t0, nt, smp) in [(b * 512, 512, False) for b in range(4)] + [(TOK_P, 128, True)]:
            for m in range(KC):
                pa, pak, pb, pbk = GA[gi % 2]
                w_ = gi % 2
                gi += 1
                for k in range(KC):
                    P.op("pe", lambda e, k=k, m=m, pa=pa, t0=t0, nt=nt: e.matmul(pa[:, :nt], lhsT=glua[:, k, m * 128:(m + 1) * 128], rhs=hT_all[:, k, t0:t0 + nt],
                                                                                  start=(k == 0), stop=(k == KC - 1)), reads=["glua", "hT_all"], writes=[pak])
                for k in range(KC):
                    P.op("pe", lambda e, k=k, m=m, pb=pb, t0=t0, nt=nt: e.matmul(pb[:, :nt], lhsT=glub[:, k, m * 128:(m + 1) * 128], rhs=hT_all[:, k, t0:t0 + nt],
                                                                                  start=(k == 0), stop=(k == KC - 1)), reads=["glub", "hT_all"], writes=[pbk])
                V(lambda e, pb=pb, w_=w_, nt=nt: e.activation(out=sgb[w_][:, :nt], in_=pb[:, :nt], func=AF.Sigmoid), [pbk], [f"sgb{w_}"], "act")
                V(lambda e, pa=pa, w_=w_, nt=nt: e.tensor_tensor(out=prd[w_][:, :nt], in0=pa[:, :nt], in1=sgb[w_][:, :nt], op=ALU.mult), [pak, f"sgb{w_}"], [f"prd{w_}"])
                xs = xT[:, m, t0:t0 + nt]
                xkey = ("xT", t0 // 512)
                if not smp:
                    V(lambda e, m=m, w_=w_, xs=xs, nt=nt: e.scalar_tensor_tensor(out=xs, in0=prd[w_][:, :nt], scalar=modT[:, 16 + m, 0:1], in1=xs, op0=ALU.mult, op1=ALU.add),
                      [f"prd{w_}", "modT", xkey], [xkey])
                else:
                    gv = gt3.rearrange("p (b t) -> p b t", t=8)
                    V(lambda e, m=m, w_=w_, gv=gv: e.tensor_tensor(out=gv, in0=prd[w_][:, :128].rearrange("p (b t) -> p b t", t=8),
                                                                  in1=modT[:, 16 + m, 1:17].unsqueeze(2).to_broadcast([128, 16, 8]), op=ALU.mult), [f"prd{w_}", "modT"], ["gt3"])
                    V(lambda e, xs=xs: e.tensor_tensor(out=xs, in0=xs, in1=gt3, op=ALU.add), ["gt3", xkey], [xkey])
        ffn(1)
        P.barrier()
        cur[0] = mark_ln
        yst = [carve([128, D]) for _ in range(2)]
        for t in range(17):
            s = t % 2
            for k in range(KC):
                P.op("pe", lambda e, k=k, t=t, s=s: e.transpose(pT[s][:, k, :], xT[:, k, t * 128:(t + 1) * 128], ident[:]),
                     reads=[("xT", t // 4), "ident"], writes=[bkey(s, 0), bkey(s, 1)])
            P.op("act", lambda e, s=s: e.copy(out=yst[s], in_=Q[s][:, :]), reads=[bkey(s, 0), bkey(s, 1)], writes=[f"yst{s}"])
            P.dma("sp", lambda e, t=t, s=s: e.dma_start(out=y_out[t * 128:(t + 1) * 128, :], in_=yst[s]), reads=[f"yst{s}"], semkey=f"yst{s}", final=True)
        stats = P.build()
        nc_allow.__exit__(None, None, None)
        es.close()
        return nc, stats

    if stage in ("C1", "C2"):
        r_ = s5_stage(stage == "C2")
        if r_ is not None:
            return r_
        stats = P.build()
        nc_allow.__exit__(None, None, None)
        es.close()
        return nc, stats

    hTh = carve([128, KC, 128], BF16)
    mark1 = cur[0]
    lst = [carve([128, 4, 128]) for _ in range(2)]
    wret = carve([128, 8, 4])
    xTh = carve([128, KC, 128])
    xst[0] = carve([128, D])
    load_tile(0, xTh, "xTh")
    P.dma("sp", lambda e: e.dma_start(out=wret, in_=wret_d), writes=["wret"], semkey="wret")
    P.op("dve", lambda e: e.memset(S, 0.0), writes=["S"])
    for r in range(8):
        s = r % 2
        P.dma("sp", lambda e, r=r, s=s: e.dma_start(out=lst[s], in_=lall_d[r].rearrange("h d e -> d h e")), writes=[f"lst{s}"], semkey=f"lst{s}")
        P.op("dve", lambda e, r=r, s=s: e.tensor_tensor(out=lst[s], in0=lst[s], in1=wret[:, r, :].unsqueeze(2).to_broadcast([128, 4, 128]), op=ALU.mult),
             reads=[f"lst{s}", "wret"], writes=[f"lst{s}"])
        P.op("dve", lambda e, s=s: e.tensor_tensor(out=S, in0=S, in1=lst[s], op=ALU.add), reads=["S", f"lst{s}"], writes=["S"])
    P.op("act", lambda e: e.copy(out=Sb, in_=S), reads=["S"], writes=["Sb"])
    ln_block(xTh, "xTh", 128, False, 0)
    P.op("pool", lambda e: e.tensor_copy(out=hTh, in_=hT[:, :, 0:128]), reads=["hT"], writes=["hTh"])
    P.barrier()
    cur[0] = mark1

    wkd = carve([128, KC, 2, 128], BF16)
    wvd = carve([128, KC, 2, 128], BF16)
    for kv in range(2):
        for half in range(2):
            P.op("pool", lambda e, kv=kv, half=half: e.tensor_copy(out=wkd[:, :, kv, half * 64:(half + 1) * 64], in_=win[:, :, 512 + 64 * kv:576 + 64 * kv]),
                 reads=["win"], writes=["wkd"])
            P.op("pool", lambda e, kv=kv, half=half: e.tensor_copy(out=wvd[:, :, kv, half * 64:(half + 1) * 64], in_=win[:, :, 640 + 64 * kv:704 + 64 * kv]),
                 reads=["win"], writes=["wvd"])
    wout = carve([128, KC, D], BF16)
    P.dma("pool", lambda e: e.dma_start(out=wout, in_=w_out.rearrange("(k p) n -> p k n", p=128)), writes=["wout"], semkey="wout")
    gq = carve([128, 1])
    gk = carve([128, 1])
    bd_f = carve([128, 128])
    bd = carve([128, 128], BF16)
    dneg = carve([128, 5, 128])
    rmask = carve([128, 4, 128])
    dq = carve([128, 4, 128])
    seqmask = carve([128, 16])
    esink = carve([128, 8])
    retg = carve([128, 4])
    for half in range(2):
        P.dma("sp", lambda e, half=half: e.dma_start(out=gq[half * 64:(half + 1) * 64, :], in_=qgain.rearrange("o d -> d o")), writes=["gq"], semkey="ld_gq", group="ld_gq")
        P.dma("sp", lambda e, half=half: e.dma_start(out=gk[half * 64:(half + 1) * 64, :], in_=kgain.rearrange("o d -> d o")), writes=["gk"], semkey="ld_gk", group="ld_gk")
    P.dma("sp", lambda e: e.dma_start(out=bd_f, in_=bd_d), writes=["bd_f"], semkey="ld_bd_f", group="ld_bd_f")
    P.dma("sp", lambda e: e.dma_start(out=dneg, in_=dneg_d), writes=["dneg"], semkey="ld_dneg", group="ld_dneg")
    P.dma("sp", lambda e: e.dma_start(out=rmask, in_=rmask_d[:, 0]), writes=["rmask"], semkey="ld_rmask", group="ld_rmask")
    P.dma("sp", lambda e: e.dma_start(out=dq, in_=dq_d[:, 0]), writes=["dq"], semkey="ld_dq", group="ld_dq")
    P.dma("sp", lambda e: e.dma_start(out=seqmask, in_=seqmask_d), writes=["seqmask"], semkey="ld_seqmask", group="ld_seqmask")
    P.dma("sp", lambda e: e.dma_start(out=esink, in_=sinks_d.partition_broadcast(128)), writes=["esink"], semkey="ld_esink", group="ld_esink")
    P.dma("sp", lambda e: e.dma_start(out=retg, in_=retg_d.rearrange("o (h e) -> e (o h)", h=4)), writes=["retg"], semkey="ld_retg", group="ld_retg")
    P.op("dve", lambda e: e.tensor_copy(out=bd, in_=bd_f), reads=["bd_f"], writes=["bd"])
    P.op("act", lambda e: e.activation(out=esink, in_=esink, func=AF.Exp), reads=["esink"], writes=["esink"])
    P.op("dve", lambda e: e.tensor_scalar(out=gq, in0=gq, scalar1=0.125, scalar2=None, op0=ALU.mult), reads=["gq"], writes=["gq"])

    qnT = carve([128, 4, BT], BF16)
    kdT = carve([128, 2, 128 + BT], BF16)
    qbT = carve([128, 4, BT], BF16)
    qdT = carve([128, 4, BT], BF16)
    kbT = carve([128, 4, BT], BF16)
    sgT = carve([128, 4, BT], BF16)
    vd_tok = carve([128, 1 + TPB, 256], BF16)
    oT = carve([128, KC, BT], BF16)
    qsq = carve([128, max(BT, 256)], BF16)
    qrs = carve([128, max(BT, 256)])
    sc_sb = carve([128, 2, 2, 2, 128])
    pTt = carve([128, 2, 2, 2, 128], BF16)
    rden = carve([128, 2, 2, 128])
    innT = carve([128, 4, 128], BF16)
    o32 = carve([128, 4, 128])
    obf = carve([128, 4, 128], BF16)
    osq = carve([128, 4, 128], BF16)
    t1 = carve([128, 4, 128])
    t2 = o32
    kv32 = carve([128, 2, 128])
    knT = carve([128, 128])
    gtmp = carve([128, 128])
    cacheT = carve([128, 16, 128], BF16)
    vcache = carve([128, 16, 128], BF16)
    kst = carve([128, 4, 128])
    S0 = [carve([128, 4, 128]) for _ in range(2)]
    S0b = [carve([128, 4, 128], BF16) for _ in range(2)]
    kdm = [carve([128, 512], BF16) for _ in range(2)]
    mark2 = cur[0]
    print("arena words used (mixer):", cur[0], "of", ARENA_W)

    PJ = [bank(0, 0), bank(0, 1)]
    PJK = [bkey(0, 0), bkey(0, 1)]
    p_st = bank(1, 0)
    pj_i = [0]

    def proj_fm(lhs_fn, ntok, evac, src=None):
        s = pj_i[0] % 2
        pj_i[0] += 1
        src = hT if src is None else src
        for k in range(KC):
            P.op("pe", lambda e, k=k, s=s: e.matmul(PJ[s][:, :ntok], lhsT=lhs_fn(k), rhs=src[:, k, :ntok], start=(k == 0), stop=(k == KC - 1)),
                 reads=["hT", "hTh", "win", "wkd"], writes=[PJK[s]])
        evac(PJ[s][:, :ntok], PJK[s])

    def qknorm(psum, pkey, ntok, gain, gkey, out_ap, out_key):
        P.op("act", lambda e: e.activation(out=qsq[:, :ntok], in_=psum, func=AF.Square), reads=[pkey], writes=["qsq"])
        P.op("pe", lambda e: e.matmul(p_st[:, :ntok], lhsT=bd, rhs=qsq[:, :ntok], start=True, stop=True), reads=["bd", "qsq"], writes=[bkey(1, 0)])
        P.op("act", lambda e: e.activation(out=qrs[:, :ntok], in_=p_st[:, :ntok], func=AF.Sqrt, bias=epsb[:], scale=1.0), reads=[bkey(1, 0), "epsb"], writes=["qrs"])
        P.op("dve", lambda e: e.reciprocal(out=qrs[:, :ntok], in_=qrs[:, :ntok]), reads=["qrs"], writes=["qrs"])
        P.op("dve", lambda e: e.scalar_tensor_tensor(out=out_ap, in0=psum, scalar=gain[:, 0:1], in1=qrs[:, :ntok], op0=ALU.mult, op1=ALU.mult),
             reads=[pkey, "qrs", gkey], writes=[out_key])

    def project_block(ntok, grp):
        for j in range(4):
            proj_fm(lambda k, j=j: win[:, k, j * 128:(j + 1) * 128], ntok,
                    lambda ps_, key, j=j: qknorm(ps_, key, ntok, gq, "gq", qnT[:, j, :ntok], "qnT"))
        if KSUB < 2:
            return
        for kv in range(2):
            proj_fm(lambda k, kv=kv: wkd[:, k, kv, :], ntok,
                    lambda ps_, key, kv=kv: qknorm(ps_, key, ntok, gk, "gk", kdT[:, kv, 128:128 + ntok], "kdT"))
        if KSUB < 3:
            return
        for h in range(4):
            def ev_q(ps_, key, h=h):
                P.op("act", lambda e: e.copy(out=qbT[:, h, :ntok], in_=ps_), reads=[key], writes=["qbT"])
                P.op("dve", lambda e: e.tensor_tensor(out=qdT[:, h, :ntok].rearrange("p (t i) -> p t i", i=128), in0=ps_.rearrange("p (t i) -> p t i", i=128),
                                                      in1=dq[:, h, :].unsqueeze(1).to_broadcast([128, ntok // 128, 128]), op=ALU.mult),
                     reads=[key, "dq"], writes=["qdT"])
            proj_fm(lambda k, h=h: win[:, k, 768 + h * 128:768 + (h + 1) * 128], ntok, ev_q)
        if KSUB < 4:
            return
        for h in range(4):
            proj_fm(lambda k, h=h: win[:, k, 1280 + h * 128:1280 + (h + 1) * 128], ntok,
                    lambda ps_, key, h=h: P.op("act", lambda e: e.mul(out=kbT[:, h, :ntok], in_=ps_, mul=128.0 ** -0.5), reads=[key], writes=["kbT"]))
        if KSUB < 5:
            return
        for h in range(4):
            proj_fm(lambda k, h=h: win[:, k, 2304 + h * 128:2304 + (h + 1) * 128], ntok,
                    lambda ps_, key, h=h: P.op("act", lambda e: e.activation(out=sgT[:, h, :ntok], in_=ps_, func=AF.Silu), reads=[key], writes=["sgT"]))

    p_vd = bank(1, 1)[:, 0:256]

    def tok_vd(tc0, slot, src=None):
        src = hT if src is None else src
        for k in range(KC):
            P.op("pe", lambda e, k=k: e.matmul(p_vd, lhsT=src[:, k, tc0:tc0 + 128], rhs=wvd[:, k, :, :].rearrange("p a b -> p (a b)"), start=(k == 0), stop=(k == KC - 1)),
                 reads=["hT", "hTh", "wvd"], writes=[bkey(1, 1)])
        P.op("act", lambda e: e.copy(out=vd_tok[:, slot, :], in_=p_vd), reads=[bkey(1, 1)], writes=["vd_tok"])

    SLOPE = [2.0 ** (-(h + 1)) for h in range(8)]
    p_sc = Q[3][:, :].rearrange("p (a c t q) -> p a c t q", a=2, c=2, t=2)
    p_num = bank(2, 0).rearrange("p (a c q) -> p a c q", a=2, c=2)
    p_den = bank(2, 1).rearrange("p (a c q) -> p a c q", a=2, c=2)
    SCK = [bkey(3, 0), bkey(3, 1)]

    def attn_softmax(kv, dn_own, dn_prev):
        for half in range(2):
            for c in range(2):
                h0 = 4 * kv + 2 * c + half
                P.op("dve", lambda e, half=half, c=c, h0=h0: e.scalar_tensor_tensor(out=sc_sb[:, half, c, 0, :], in0=dneg[:, dn_prev, :], scalar=SLOPE[h0],
                                                                                  in1=p_sc[:, half, c, 0, :], op0=ALU.mult, op1=ALU.add),
                     reads=["dneg"] + SCK, writes=["sc_sb"])
                P.op("dve", lambda e, half=half, c=c, h0=h0: e.scalar_tensor_tensor(out=sc_sb[:, half, c, 1, :], in0=dneg[:, dn_own, :], scalar=SLOPE[h0],
                                                                                  in1=p_sc[:, half, c, 1, :], op0=ALU.mult, op1=ALU.add),
                     reads=["dneg"] + SCK, writes=["sc_sb"])
        P.op("act", lambda e: e.activation(out=pTt, in_=sc_sb, func=AF.Exp), reads=["sc_sb"], writes=["pTt"])

    def attn_finish(kv, c0):
        for part in range(2):
            P.op("pe", lambda e, part=part: e.matmul(p_den, lhsT=ones_b[:], rhs=pTt[:, :, :, part, :], start=(part == 0), stop=(part == 1)),
                 reads=["ones_b", "pTt"], writes=[bkey(2, 1)])
        es_v = esink[:, 4 * kv:4 * kv + 4].rearrange("p (c a) -> p a c", a=2)
        P.op("dve", lambda e, es_v=es_v: e.tensor_tensor(out=rden, in0=p_den, in1=es_v.unsqueeze(3).to_broadcast([128, 2, 2, 128]), op=ALU.add),
             reads=[bkey(2, 1), "esink"], writes=["rden"])
        P.op("dve", lambda e: e.reciprocal(out=rden, in_=rden), reads=["rden"], writes=["rden"])
        for half in range(2):
            sl = slice(half * 64, half * 64 + 64)
            P.op("dve", lambda e, half=half, sl=sl, kv=kv: e.tensor_tensor(out=oT[sl, 2 * kv:2 * kv + 2, c0:c0 + 128], in0=p_num[sl, half, :, :],
                                                                         in1=rden[sl, half, :, :], op=ALU.mult),
                 reads=[bkey(2, 0), "rden"], writes=["oT"])

    def attention_tile(i, dn_own, dn_prev):
        c0 = i * 128
        for kv in range(2):
            for half in range(2):
                sl = slice(half * 64, half * 64 + 64)
                P.op("pe", lambda e, kv=kv, half=half, sl=sl: e.matmul(p_sc[:, half, :, 1, :], lhsT=kdT[sl, kv, 128 + c0:256 + c0],
                                                                       rhs=qnT[sl, 2 * kv:2 * kv + 2, c0:c0 + 128], start=True, stop=True),
                     reads=["kdT", "qnT"], writes=SCK)
                P.op("pe", lambda e, kv=kv, half=half, sl=sl: e.matmul(p_sc[:, half, :, 0, :], lhsT=kdT[sl, kv, c0:128 + c0],
                                                                       rhs=qnT[sl, 2 * kv:2 * kv + 2, c0:c0 + 128], start=True, stop=True),
                     reads=["kdT", "qnT"], writes=SCK)
            attn_softmax(kv, dn_own, dn_prev)
            parts = [(0, vd_tok[:, i, kv * 128:(kv + 1) * 128]), (1, vd_tok[:, i + 1, kv * 128:(kv + 1) * 128])]
            for n_, (part, lh) in enumerate(parts):
                P.op("pe", lambda e, part=part, lh=lh, n_=n_: e.matmul(p_num, lhsT=lh, rhs=pTt[:, :, :, part, :], start=(n_ == 0), stop=(n_ == 1)),
                     reads=["vd_tok", "pTt"], writes=[bkey(2, 0)])
            attn_finish(kv, c0)

    def attention_sample():
        p_ct = bank(1, 1)[:, 0:128]
        for kv in range(2):
            for half in range(2):
                P.dma("pool", lambda e, kv=kv, half=half: e.dma_start(out=vcache[:, :, half * 64:(half + 1) * 64],
                                                                      in_=cache_v[:, :, kv * 64:(kv + 1) * 64].rearrange("b w d -> w b d")),
                      writes=["vcache"], semkey="vcache", group=f"vc{kv}")
            for g4 in range(4):
                for half in range(2):
                    P.dma("sp", lambda e, kv=kv, half=half, g4=g4: e.dma_start(out=kst[:, :, half * 64:(half + 1) * 64],
                                                                              in_=cache_k[4 * g4:4 * g4 + 4, :, kv * 64:(kv + 1) * 64].rearrange("b w d -> w b d")),
                          writes=["kst"], semkey="kst", group=f"kst{kv}_{g4}")
                for bb in range(4):
                    b = 4 * g4 + bb
                    P.op("pe", lambda e, bb=bb: e.transpose(p_ct, kst[:, bb, :], ident[:]), reads=["kst", "ident"], writes=[bkey(1, 1)])
                    P.op("act", lambda e, b=b: e.copy(out=cacheT[:, b, :], in_=p_ct), reads=[bkey(1, 1)], writes=["cacheT"])
            for half in range(2):
                sl = slice(half * 64, half * 64 + 64)
                P.op("pe", lambda e, kv=kv, half=half, sl=sl: e.matmul(p_sc[:, half, :, 1, :], lhsT=kdT[sl, kv, 128:256],
                                                                       rhs=qnT[sl, 2 * kv:2 * kv + 2, 0:128], start=True, stop=True),
                     reads=["kdT", "qnT"], writes=SCK)
                for b in range(16):
                    P.op("pe", lambda e, kv=kv, half=half, sl=sl, b=b: e.matmul(p_sc[:, half, :, 0, 8 * b:8 * b + 8], lhsT=cacheT[sl, b, :],
                                                                                 rhs=qnT[sl, 2 * kv:2 * kv + 2, 8 * b:8 * b + 8], start=True, stop=True),
                         reads=["cacheT", "qnT"], writes=SCK)
            attn_softmax(kv, 3, 4)
            P.op("pe", lambda e, kv=kv: e.matmul(p_num, lhsT=vd_tok[:, 1, kv * 128:(kv + 1) * 128], rhs=pTt[:, :, :, 1, :], start=True, stop=False),
                 reads=["vd_tok", "pTt"], writes=[bkey(2, 0)])
            for b in range(16):
                P.op("pe", lambda e, b=b: e.matmul(p_num[:, :, :, 8 * b:8 * b + 8], lhsT=vcache[:, b, :], rhs=pTt[:, :, :, 0, 8 * b:8 * b + 8],
                                                   start=False, stop=(b == 15)),
                     reads=["vcache", "pTt"], writes=[bkey(2, 0)])
            attn_finish(kv, 0)

    p_in = bank(1, 1).rearrange("p (h i) -> p h i", h=4)
    p_o = bank(0, 0).rearrange("p (h i) -> p h i", h=4)
    p_mu = bank(0, 1).rearrange("p (h i) -> p h i", h=4)
    p_e2 = bank(1, 0).rearrange("p (h i) -> p h i", h=4)

    def ret_norm(c0):
        P.op("dve", lambda e: e.tensor_copy(out=obf, in_=o32), reads=["o32"], writes=["obf"])
        P.op("act", lambda e: e.activation(out=osq, in_=o32, func=AF.Square), reads=["o32"], writes=["osq"])
        for h in range(4):
            P.op("pe", lambda e, h=h: e.matmul(p_mu[:, h, :], lhsT=ones_g[:], rhs=obf[:, h, :], start=True, stop=True), reads=["ones_g", "obf"], writes=[bkey(0, 1)])
        for h in range(4):
            P.op("pe", lambda e, h=h: e.matmul(p_e2[:, h, :], lhsT=ones_g[:], rhs=osq[:, h, :], start=True, stop=True), reads=["ones_g", "osq"], writes=[bkey(1, 0)])
        P.op("act", lambda e: e.activation(out=t1, in_=p_mu, func=AF.Square), reads=[bkey(0, 1)], writes=["t1"])
        P.op("dve", lambda e: e.tensor_tensor(out=t1, in0=p_e2, in1=t1, op=ALU.subtract), reads=[bkey(1, 0), "t1"], writes=["t1"])
        P.op("dve", lambda e: e.tensor_scalar(out=t1, in0=t1, scalar1=0.0, scalar2=None, op0=ALU.max), reads=["t1"], writes=["t1"])
        P.op("act", lambda e: e.activation(out=t1, in_=t1, func=AF.Sqrt, bias=epsb[:], scale=1.0), reads=["t1", "epsb"], writes=["t1"])
        P.op("dve", lambda e: e.reciprocal(out=t1, in_=t1), reads=["t1"], writes=["t1"])
        P.op("dve", lambda e: e.tensor_tensor(out=o32, in0=o32, in1=p_mu, op=ALU.subtract), reads=["o32", bkey(0, 1)], writes=["o32"])
        P.op("dve", lambda e: e.tensor_tensor(out=o32, in0=o32, in1=t1, op=ALU.mult), reads=["o32", "t1"], writes=["o32"])
        P.op("dve", lambda e: e.tensor_tensor(out=o32, in0=o32, in1=retg.unsqueeze(2).to_broadcast([128, 4, 128]), op=ALU.mult), reads=["o32", "retg"], writes=["o32"])
        P.op("dve", lambda e: e.tensor_tensor(out=oT[:, 4:8, c0:c0 + 128], in0=o32, in1=sgT[:, :, c0:c0 + 128], op=ALU.mult), reads=["o32", "sgT"], writes=["oT"])

    def ret_inner(c0):
        for h in range(4):
            P.op("pe", lambda e, h=h: e.matmul(p_in[:, h, :], lhsT=kbT[:, h, c0:c0 + 128], rhs=qbT[:, h, c0:c0 + 128], start=True, stop=True),
                 reads=["kbT", "qbT"], writes=[bkey(1, 1)])
        P.op("dve", lambda e: e.tensor_tensor(out=innT, in0=p_in, in1=rmask, op=ALU.mult), reads=[bkey(1, 1), "rmask"], writes=["innT"])

    def retention_tile(i, grp):
        c0 = i * 128
        ret_inner(c0)
        for h in range(4):
            P.op("pe", lambda e, h=h: e.matmul(p_o[:, h, :], lhsT=vb_tok[:, h * 128:(h + 1) * 128], rhs=innT[:, h, :], start=True, stop=False),
                 reads=["vb_tok", "innT"], writes=[bkey(0, 0)])
            P.op("pe", lambda e, h=h: e.matmul(p_o[:, h, :], lhsT=Sb[:, h, :], rhs=qdT[:, h, c0:c0 + 128], start=False, stop=True),
                 reads=["Sb", "qdT"], writes=[bkey(0, 0)])
        P.op("act", lambda e: e.copy(out=o32, in_=p_o), reads=[bkey(0, 0)], writes=["o32"])
        ret_norm(c0)

    def retention_sample():
        ret_inner(0)
        poh = [bank(0, 0)[:, 0:128], bank(0, 1)[:, 0:128], bank(1, 0)[:, 0:128], bank(3, 0)[:, 0:128]]
        pok = [bkey(0, 0), bkey(0, 1), bkey(1, 0), bkey(3, 0)]
        for h in range(4):
            P.op("pe", lambda e, h=h: e.matmul(poh[h], lhsT=vb_tok[:, h * 128:(h + 1) * 128], rhs=innT[:, h, :], start=True, stop=False),
                 reads=["vb_tok", "innT"], writes=[pok[h]])
        for b in range(16):
            s_ = b % 2
            P.dma("sp", lambda e, b=b, s_=s_: e.dma_start(out=S0[s_], in_=sret_in[b].rearrange("h d e -> d h e")), writes=[f"S0{s_}"], semkey=f"S0{s_}")
            P.op("pool", lambda e, s_=s_: e.tensor_copy(out=S0b[s_], in_=S0[s_]), reads=[f"S0{s_}"], writes=[f"S0b{s_}"])
            for h in range(4):
                P.op("pe", lambda e, h=h, b=b, s_=s_: e.matmul(poh[h][:, 8 * b:8 * b + 8], lhsT=S0b[s_][:, h, :], rhs=qdT[:, h, 8 * b:8 * b + 8],
                                                              start=False, stop=(b == 15)),
                     reads=[f"S0b{s_}", "qdT"], writes=[pok[h]])
            P.op("dve", lambda e, b=b, s_=s_: e.tensor_scalar(out=kdm[s_], in0=kd_tok, scalar1=seqmask[:, b:b + 1], scalar2=None, op0=ALU.mult),
                 reads=["kd_tok", "seqmask"], writes=[f"kdm{s_}"])
            for h in range(4):
                P.op("pe", lambda e, h=h, s_=s_: e.matmul(p_su[:, h, :], lhsT=kdm[s_][:, h * 128:(h + 1) * 128], rhs=vb_tok[:, h * 128:(h + 1) * 128], start=True, stop=True),
                     reads=[f"kdm{s_}", "vb_tok"], writes=[bkey(2, 1)])
            P.op("pool", lambda e, s_=s_: e.tensor_tensor(out=S0[s_], in0=S0[s_], in1=dc[:, 1, :].unsqueeze(2).to_broadcast([128, 4, 128]), op=ALU.mult),
                 reads=[f"S0{s_}", f"S0b{s_}", "dc"], writes=[f"S0{s_}"])
            P.op("dve", lambda e, s_=s_: e.tensor_tensor(out=S0[s_], in0=S0[s_], in1=p_su, op=ALU.add), reads=[f"S0{s_}", bkey(2, 1)], writes=[f"S0{s_}"])
            P.dma("sp", lambda e, b=b, s_=s_: e.dma_start(out=sret_out[b].rearrange("h d e -> d h e"), in_=S0[s_]), reads=[f"S0{s_}"], semkey=f"S0o{s_}", final=True)
        for h in range(4):
            P.op("act", lambda e, h=h: e.copy(out=o32[:, h, :], in_=poh[h]), reads=[pok[h]], writes=["o32"])
        ret_norm(0)

    PO = [bank(0, 0), bank(0, 1)]
    POK = [bkey(0, 0), bkey(0, 1)]

    def out_proj(blk_c0, ntok, sample, wfn, nk, rhs_fn, gate_row, rkeys):
        for m in range(KC):
            s = m % 2
            for k in range(nk):
                P.op("pe", lambda e, m=m, k=k, s=s: e.matmul(PO[s][:, :ntok], lhsT=wfn(k, m), rhs=rhs_fn(k), start=(k == 0), stop=(k == nk - 1)),
                     reads=rkeys, writes=[POK[s]])
            xs = xT[:, m, blk_c0:blk_c0 + ntok]
            xkey = ("xT", blk_c0 // 512)
            if not sample:
                P.op("dve", lambda e, m=m, s=s, xs=xs: e.scalar_tensor_tensor(out=xs, in0=PO[s][:, :ntok], scalar=modT[:, gate_row + m, 0:1], in1=xs,
                                                                             op0=ALU.mult, op1=ALU.add),
                     reads=[POK[s], "modT", xkey], writes=[xkey])
            else:
                gv = gtmp[:, :128].rearrange("p (b t) -> p b t", t=8)
                P.op("dve", lambda e, m=m, s=s, gv=gv: e.tensor_tensor(out=gv, in0=PO[s][:, :128].rearrange("p (b t) -> p b t", t=8),
                                                                      in1=modT[:, gate_row + m, 1:17].unsqueeze(2).to_broadcast([128, 16, 8]), op=ALU.mult),
                     reads=[POK[s], "modT"], writes=["gtmp"])
                P.op("dve", lambda e, xs=xs: e.tensor_tensor(out=xs, in0=xs, in1=gtmp[:, :128], op=ALU.add), reads=["gtmp", xkey], writes=[xkey])

    p_tr = bank(1, 1)[:, 256:384]
    p_v32 = bank(1, 1)[:, 384:512]

    def window_kv(tc0, kout, vout):
        pk2 = bank(1, 0)[:, 0:256].rearrange("p (a n) -> p a n", a=2)
        pst2 = bank(1, 0)[:, 256:512].rearrange("p (a n) -> p a n", a=2)
        for kv in range(2):
            for k in range(KC):
                P.op("pe", lambda e, kv=kv, k=k: e.matmul(pk2[:, kv, :], lhsT=wkd[:, k, kv, :], rhs=hT[:, k, tc0:tc0 + 128], start=(k == 0), stop=(k == KC - 1)),
                     reads=["wkd", "hT"], writes=[bkey(1, 0)])
        P.op("act", lambda e: e.activation(out=qsq[:, 0:256].rearrange("p (a n) -> p a n", a=2), in_=pk2, func=AF.Square), reads=[bkey(1, 0)], writes=["qsq"])
        for kv in range(2):
            P.op("pe", lambda e, kv=kv: e.matmul(pst2[:, kv, :], lhsT=bd, rhs=qsq[:, kv * 128:(kv + 1) * 128], start=True, stop=True), reads=["bd", "qsq"], writes=[bkey(1, 0)])
        P.op("act", lambda e: e.activation(out=qrs[:, 0:256].rearrange("p (a n) -> p a n", a=2), in_=pst2, func=AF.Sqrt, bias=epsb[:], scale=1.0),
             reads=[bkey(1, 0), "epsb"], writes=["qrs"])
        P.op("dve", lambda e: e.reciprocal(out=qrs[:, 0:256], in_=qrs[:, 0:256]), reads=["qrs"], writes=["qrs"])
        for kv in range(2):
            sl = slice(kv * 64, (kv + 1) * 64)
            P.op("dve", lambda e, kv=kv, sl=sl: e.scalar_tensor_tensor(out=knT[sl, :], in0=pk2[sl, kv, :], scalar=gk[sl, 0:1], in1=qrs[sl, kv * 128:(kv + 1) * 128],
                                                                       op0=ALU.mult, op1=ALU.mult),
                 reads=[bkey(1, 0), "gk", "qrs"], writes=["knT"])
        P.op("pe", lambda e: e.transpose(p_tr, knT, ident[:]), reads=["knT", "ident"], writes=[bkey(1, 1)])
        P.op("act", lambda e: e.copy(out=kv32[:, 0, :], in_=p_tr), reads=[bkey(1, 1)], writes=["kv32"])
        for k in range(KC):
            P.op("pe", lambda e, k=k: e.matmul(p_v32, lhsT=hT[:, k, tc0:tc0 + 128], rhs=win[:, k, 640:768], start=(k == 0), stop=(k == KC - 1)),
                 reads=["win", "hT"], writes=[bkey(1, 1)])
        P.op("dve", lambda e: e.tensor_copy(out=kv32[:, 1, :], in_=p_v32), reads=[bkey(1, 1)], writes=["kv32"])
        P.dma("sp", lambda e: e.dma_start(out=kout, in_=kv32[:, 0, :]), reads=["kv32"], semkey="kvo", final=True)
        P.dma("sp", lambda e: e.dma_start(out=vout, in_=kv32[:, 1, :]), reads=["kv32"], semkey="kvo", final=True)

    if KSTOP >= 2:
        for kv in range(2):
            proj_fm(lambda k, kv=kv: wkd[:, k, kv, :], 128,
                    lambda ps_, key, kv=kv: qknorm(ps_, key, 128, gk, "gk", kdT[:, kv, 0:128], "kdT"), src=hTh)
        tok_vd(0, 0, src=hTh)

    NB = TOK_P // BT
    for b in range(NB if KSTOP >= 3 else 0):
        ln_block(xT[:, :, b * BT:(b + 1) * BT], ("xT", (b * BT) // 512), BT, False, 0)
        project_block(BT, 0)
        if KSTOP < 4:
            continue
        for i in range(TPB):
            tok_vd(i * 128, i + 1)
        for i in range(TPB):
            attention_tile(i, 0, 2 if (b == 0 and i == 0) else 1)
        if KSTOP < 5:
            continue
        for i in range(TPB):
            tok_kv(i * 128, 0)
            retention_tile(i, 0)
            state_update()
            P.op("act", lambda e: e.copy(out=Sb, in_=S), reads=["S"], writes=["Sb"])
        if b == NB - 1:
            window_kv(BT - 128, pk_out, pv_out)
        if KSTOP < 6:
            continue
        out_proj(b * BT, BT, False, lambda k, m: wout[:, k, m * 128:(m + 1) * 128], KC, lambda k: oT[:, k, :BT], 16, ["wout", "oT"])
        P.op("pool", lambda e: e.tensor_copy(out=kdT[:, :, 0:128], in_=kdT[:, :, BT:BT + 128]), reads=["kdT"], writes=["kdT"])
        P.op("pool", lambda e: e.tensor_copy(out=vd_tok[:, 0, :], in_=vd_tok[:, TPB, :]), reads=["vd_tok"], writes=["vd_tok"])
    P.dma("sp", lambda e: e.dma_start(out=pret_out.rearrange("h d e -> d h e"), in_=S), reads=["S"], semkey="pret", final=True)

    if KSTOP >= 7:
        P.dma("sp", lambda e: e.dma_start(out=rmask, in_=rmask_d[:, 1]), writes=["rmask"], semkey="ld_rmask", group="ld_rmask")
        P.dma("sp", lambda e: e.dma_start(out=dq, in_=dq_d[:, 1]), writes=["dq"], semkey="ld_dq", group="ld_dq")
        ln_block(xT[:, :, TOK_P:TOK_P + 128], ("xT", 4), 128, True, 0)
        project_block(128, 1)
        tok_vd(0, 1)
        tok_kv(0, 1)
        window_kv(0, sk_out[:, 120:128, :], sv_out[:, 120:128, :])
        P.dma("sp", lambda e: e.dma_start(out=sk_out[:, 0:120, :], in_=cache_k[:, 8:128, :]), semkey="cko", final=True)
        P.dma("sp", lambda e: e.dma_start(out=sv_out[:, 0:120, :], in_=cache_v[:, 8:128, :]), semkey="cko", final=True)
        attention_sample()
        retention_sample()
        out_proj(TOK_P, 128, True, lambda k, m: wout[:, k, m * 128:(m + 1) * 128], KC, lambda k: oT[:, k, :128], 16, ["wout", "oT"])

    if KSTOP >= 8:
        ffn(0)

    P.barrier()
    cur[0] = mark1
    yst = [carve([128, D]) for _ in range(2)]
    for t in range(17):
        s = t % 2
        for k in range(KC):
            P.op("pe", lambda e, k=k, t=t, s=s: e.transpose(pT[s][:, k, :], xT[:, k, t * 128:(t + 1) * 128], ident[:]),
                 reads=[("xT", t // 4), "ident"], writes=[bkey(s, 0), bkey(s, 1)])
        P.op("act", lambda e, s=s: e.copy(out=yst[s], in_=Q[s][:, :]), reads=[bkey(s, 0), bkey(s, 1)], writes=[f"yst{s}"])
        P.dma("sp", lambda e, t=t, s=s: e.dma_start(out=x1_out[t * 128:(t + 1) * 128, :], in_=yst[s]), reads=[f"yst{s}"], semkey=f"yst{s}", final=True)

    P.barrier()
    P.dma("sp", lambda e: e.dma_start(out=modT[:].rearrange("p m c -> p (m c)"), in_=modin1_d), writes=["modT"], semkey="ld_modT1")
    make_G(1, 0)
    s5_stage(False)

    stats = P.build()
    nc_allow.__exit__(None, None, None)
    es.close()
    return nc, stats


_CACHE = {}


def _tables(c):
    f32 = np.float32
    ident = np.eye(128, dtype=f32)
    bd = np.zeros((128, 128), f32)
    bd[:64, :64] = 1.0 / 64
    bd[64:, 64:] = 1.0 / 64
    j = np.arange(128)[:, None]
    i = np.arange(128)[None, :]
    NEG = -1e30
    dneg = np.full((128, 5, 128), NEG, f32)
    dneg[:, 0, :] = np.where(i >= j, -(i - j), NEG)
    dneg[:, 1, :] = np.where(i < j, -(i - j + 128), NEG)
    dneg[:, 2, :] = dneg[:, 1, :] if c > 0 else NEG
    same = (j // 8) == (i // 8)
    dneg[:, 3, :] = np.where(same & ((i % 8) >= (j % 8)), -((i % 8) - (j % 8)), NEG)
    dneg[:, 4, :] = np.where(j >= (i % 8) + 1, -(128 + (i % 8) - j), NEG)
    g = np.array(GAM, np.float64)
    rmask = np.zeros((128, 2, 4, 128), f32)
    dq = np.zeros((128, 2, 4, 128), f32)
    dk = np.zeros((128, 2, 4), f32)
    dc = np.zeros((128, 2, 4), f32)
    for h in range(4):
        rmask[:, 0, h, :] = np.where(i >= j, g[h] ** np.maximum(i - j, 0), 0.0)
        rmask[:, 1, h, :] = np.where(same & ((i % 8) >= (j % 8)), g[h] ** np.maximum((i % 8) - (j % 8), 0), 0.0)
        dq[:, 0, h, :] = g[h] ** (i + 1.0)
        dq[:, 1, h, :] = g[h] ** ((i % 8) + 1.0)
        dk[:, 0, h] = 128.0 ** -0.5 * g[h] ** (127.0 - j[:, 0])
        dk[:, 1, h] = 128.0 ** -0.5 * g[h] ** (7.0 - (j[:, 0] % 8))
        dc[:, 0, h] = g[h] ** 128.0
        dc[:, 1, h] = g[h] ** 8.0
    seqmask = (j // 8 == np.arange(16)[None, :]).astype(f32)
    wret = np.zeros((128, 8, 4), f32)
    for r in range(8):
        if r < c:
            for h in range(4):
                wret[:, r, h] = g[h] ** (2048.0 * (c - r - 1))
    return dict(ident=ident, bdones=bd, dneg=dneg, rmask=rmask, dq=dq, dk=dk, dc=dc, seqmask=seqmask, wret=wret)


def _get(stage):
    if stage not in _CACHE:
        _CACHE[stage] = build_program(stage)
    return _CACHE[stage]


def kernel(**inp):
    f32 = np.float32
    A = lambda k: np.ascontiguousarray(np.asarray(inp[k], f32))
    xp = A("x_prompt")[0]
    xs = A("x_sample")
    base = dict(norm_mix=A("norm_mix"), norm_ffn=A("norm_ffn"), even_w_in=A("even_w_in")[0])
    per_core = []
    for c in range(NCORES):
        x18 = np.zeros((18 * 128, D), f32)
        if c > 0:
            x18[0:128] = xp[c * TOK_P - 128:c * TOK_P]
        x18[128:128 + TOK_P] = xp[c * TOK_P:(c + 1) * TOK_P]
        x18[128 + TOK_P:] = xs[16 * c:16 * c + 16].reshape(128, D)
        cv = np.concatenate([A("c_prompt"), A("c_sample")[16 * c:16 * c + 16]], axis=0)
        t = _tables(c)
        m = dict(base)
        m.update(x=x18, cvec=np.ascontiguousarray(cv), ident=t["ident"], dk=t["dk"], dc=t["dc"])
        per_core.append((m, t))

    ncA, _ = _get("A")
    mapsA = []
    for m, _ in per_core:
        m = dict(m)
        m.update(ada_w=A("ada_w"), ada_b=A("ada_b"))
        mapsA.append(m)
    resA = run_bass_kernel_spmd(ncA, mapsA, core_ids=list(range(NCORES)))
    mods = [np.ascontiguousarray(resA.results[c]["modout"]) for c in range(NCORES)]
    lall = np.ascontiguousarray(np.stack([resA.results[c]["lret"] for c in range(NCORES)], axis=0))
    _CACHE["lall"] = lall

    ncB, statsB = _get("B")
    _CACHE["statsB"] = statsB
    in_maps = []
    for c in range(NCORES):
        m, t = per_core[c]
        m = dict(m)
        m.pop("cvec", None)
        m["modin"] = np.ascontiguousarray(mods[c][0])
        m["modin1"] = np.ascontiguousarray(mods[c][1])
        m.update(odd_A_re=A("odd_A_re")[0], odd_A_im=A("odd_A_im")[0], odd_log_dt=A("odd_log_dt"), odd_B_re=A("odd_B_re")[0], odd_B_im=A("odd_B_im")[0],
                 odd_C_re=A("odd_C_re")[0], odd_C_im=A("odd_C_im")[0], odd_D=A("odd_D"), **_tables_c())
        m.update(even_q_gain=A("even_q_gain"), even_k_gain=A("even_k_gain"), even_sinks=A("even_sinks"), even_ret_gain=A("even_ret_gain"),
                 even_w_out=A("even_w_out")[0], ffn_wg=A("ffn_wg"), ffn_wu=A("ffn_wu"), ffn_wd=A("ffn_wd"),
                 cache_k=np.ascontiguousarray(A("cache_win_k")[0, 16 * c:16 * c + 16].reshape(16, 128, 128)),
                 cache_v=np.ascontiguousarray(A("cache_win_v")[0, 16 * c:16 * c + 16].reshape(16, 128, 128)),
                 state_ret=np.ascontiguousarray(A("state_ret")[0, 16 * c:16 * c + 16]),
                 bdones=t["bdones"], dneg=t["dneg"], rmask=t["rmask"], dq=t["dq"], seqmask=t["seqmask"], wret=t["wret"], lall=lall)
        in_maps.append(m)
    resB = run_bass_kernel_spmd(ncB, in_maps, core_ids=list(range(NCORES)))
    R = resB.results
    _CACHE["R"] = R
    last = R[NCORES - 1]
    p_k = last["p_k"].reshape(1, 1, 128, 2, 64)
    p_v = last["p_v"].reshape(1, 1, 128, 2, 64)
    p_ret = last["p_ret"].reshape(1, 1, 4, 128, 128)
    s_k = np.concatenate([R[c]["s_k"] for c in range(NCORES)], axis=0).reshape(1, 128, 128, 2, 64)
    s_v = np.concatenate([R[c]["s_v"] for c in range(NCORES)], axis=0).reshape(1, 128, 128, 2, 64)
    s_ret = np.concatenate([R[c]["s_ret"] for c in range(NCORES)], axis=0)[None]

    tC = _tables_c()
    mapsC = []
    for c in range(NCORES):
        m0, t = per_core[c]
        x18 = np.zeros((18 * 128, D), f32)
        x18[128:] = R[c]["x1"]
        wsel = np.zeros((128, 8), f32)
        wsel[:, :c] = 1.0
        m = dict(base)
        m.update(x=x18, modin=np.ascontiguousarray(mods[c][1]), ident=t["ident"], dk=t["dk"], dc=t["dc"],
                 ffn_wg=A("ffn_wg"), ffn_wu=A("ffn_wu"), ffn_wd=A("ffn_wd"),
                 odd_A_re=A("odd_A_re")[0], odd_A_im=A("odd_A_im")[0], odd_log_dt=A("odd_log_dt"),
                 odd_B_re=A("odd_B_re")[0], odd_B_im=A("odd_B_im")[0], odd_C_re=A("odd_C_re")[0], odd_C_im=A("odd_C_im")[0],
                 odd_D=A("odd_D"), odd_glu_a=A("odd_glu_a")[0], odd_glu_b=A("odd_glu_b")[0],
                 s5_re=np.ascontiguousarray(A("state_s5_re")[0, 16 * c:16 * c + 16]), s5_im=np.ascontiguousarray(A("state_s5_im")[0, 16 * c:16 * c + 16]),
                 fall=np.zeros((8, 128, 64), f32), wsel=wsel, **tC)
        mapsC.append(m)
    fall = np.ascontiguousarray(np.stack([R[c]["floc"] for c in range(NCORES)], axis=0))
    _CACHE["fall"] = fall
    for m in mapsC:
        m["fall"] = fall
    ncC2, statsC = _get("C2")
    _CACHE["statsC"] = statsC
    resC2 = run_bass_kernel_spmd(ncC2, mapsC, core_ids=list(range(NCORES)))
    RC = resC2.results
    _CACHE["RC"] = RC
    y_prompt = np.concatenate([RC[c]["y"][:TOK_P] for c in range(NCORES)], axis=0)[None]
    y_sample = np.concatenate([RC[c]["y"][TOK_P:].reshape(16, 8, D) for c in range(NCORES)], axis=0)
    lastC = RC[NCORES - 1]
    p_re = lastC["p_s5r"].reshape(1, 1, 64, 64)
    p_im = lastC["p_s5i"].reshape(1, 1, 64, 64)
    s_re = np.concatenate([RC[c]["s_s5r"] for c in range(NCORES)], axis=0)[None]
    s_im = np.concatenate([RC[c]["s_s5i"] for c in range(NCORES)], axis=0)[None]
    return (y_prompt.astype(f32), y_sample.astype(f32), p_k, p_v, p_ret, p_re, p_im, s_k, s_v, s_ret, s_re, s_im)


def _tables_c():
    f32 = np.float32
    t = np.arange(512)
    tal = np.broadcast_to((t // 64).astype(f32)[None, :], (128, 512)).copy()
    tbp = np.broadcast_to(((t % 64) + 1).astype(f32)[None, :], (128, 512)).copy()
    ts = np.arange(128)
    tbs = np.broadcast_to(((ts % 8) + 1).astype(f32)[None, :], (128, 128)).copy()
    amask = np.broadcast_to(((ts % 8) != 0).astype(f32)[None, :], (128, 128)).copy()
    maskg = (np.arange(128)[:, None] // 16 == np.arange(8)[None, :]).astype(f32)
    rott = np.zeros((128, 128), f32)
    for p in range(64):
        rott[64 + p, p] = -1.0
        rott[p, 64 + p] = 1.0
    return dict(tal=tal, tbp=tbp, tbs=tbs, amask=amask, maskg=maskg, rott=rott)
```

```python
import os
import numpy as np
from contextlib import ExitStack
import concourse.bass as bass
import concourse.mybir as mybir
from concourse.bass_utils import run_bass_kernel_spmd

F32 = mybir.dt.float32
BF16 = mybir.dt.bfloat16
I32 = mybir.dt.int32
ALU = mybir.AluOpType
AF = mybir.ActivationFunctionType

NCORES = 8
D = 1024
KC = 8
NPT = 16
TOK_P = NPT * 128
NTOK = TOK_P + 128
IN_W = 2816
DFF = 2816
FC = 22
BT = 128
TPB = BT // 128
EPS = 1e-6
ENGS = ("pe", "act", "dve", "pool", "sp")
GAM = [1.0 - 2.0 ** (-5.0 - h) for h in range(4)]
KSTOP = int(os.environ.get('KSTOP', '9'))
KSUB = int(os.environ.get('KSUB', '9'))


class Prog:
    def __init__(self, nc, same_engine_sync=("act", "dve", "pool")):
        self.nc = nc
        self.ins = []
        self.last_w = {}
        self.readers = {}
        self.same_sync = set(same_engine_sync)
        self.final_ids = []
        self.last_eng = {}
        self.last_dma = {}
        self.bar_deps = []
        self.bar_gen = 0
        self.eng_gen = {e: 0 for e in ENGS}

    def barrier(self):
        self.bar_deps = list(self.last_eng.values()) + list(self.last_dma.values())
        self.bar_gen += 1

    def _add(self, eng, fn, reads, writes, dma=False, semkey=None, group=None):
        iid = len(self.ins)
        deps = set()
        pk = tuple(k for k in reads if isinstance(k, str) and len(k) == 3 and k[0] == "Q" and k not in writes)
        writes = tuple(writes) + pk
        if self.eng_gen[eng] < self.bar_gen:
            deps.update(self.bar_deps)
            self.eng_gen[eng] = self.bar_gen
        for k in reads:
            w = self.last_w.get(k)
            if w is not None:
                deps.add(w)
        for k in writes:
            w = self.last_w.get(k)
            if w is not None:
                if group is not None and self.ins[w].get("group") == group:
                    deps.update(self.ins[w]["deps"])
                else:
                    deps.add(w)
            for r in self.readers.get(k, ()):
                deps.add(r)
        self.ins.append(dict(eng=eng, fn=fn, deps=sorted(deps), dma=dma, semkey=semkey, group=group))
        for k in reads:
            self.readers.setdefault(k, []).append(iid)
        for k in writes:
            self.last_w[k] = iid
            self.readers[k] = []
        self.last_eng[eng] = iid
        if dma:
            self.last_dma[semkey] = iid
        return iid

    capture = None

    def op(self, eng, fn, reads=(), writes=()):
        if self.capture is not None:
            self.capture.append((eng, fn, tuple(reads), tuple(writes)))
            return None
        return self._add(eng, fn, tuple(reads), tuple(writes))

    def dma(self, eng, fn, reads=(), writes=(), semkey=None, group=None, final=False):
        assert semkey is not None
        iid = self._add(eng, fn, tuple(reads), tuple(writes), dma=True, semkey=semkey, group=group)
        if final:
            self.final_ids.append(iid)
        return iid

    def build(self):
        nc = self.nc
        ins = self.ins
        n = len(ins)
        needed = [False] * n
        for i, it in enumerate(ins):
            nd = []
            for d in it["deps"]:
                de = ins[d]
                if (not de["dma"]) and (not it["dma"]) and de["eng"] == it["eng"] and it["eng"] not in self.same_sync:
                    continue
                nd.append(d)
            it["deps"] = nd
            for d in nd:
                needed[d] = True
        for f in self.final_ids:
            needed[f] = True
        semkeys = []
        for it in ins:
            if it["dma"] and it["semkey"] not in semkeys:
                semkeys.append(it["semkey"])
        sem_objs = {}
        ctxs = []
        for e in ENGS:
            c = nc.semaphore(f"s_{e}")
            sem_objs[("eng", e)] = c.__enter__()
            ctxs.append(c)
        for j, k in enumerate(semkeys):
            c = nc.semaphore(f"d{j}")
            sem_objs[("dma", k)] = c.__enter__()
            ctxs.append(c)
        cnt = {}
        for i, it in enumerate(ins):
            if it["dma"]:
                key = ("dma", it["semkey"])
                cnt[key] = cnt.get(key, 0) + 16
                it["sig"] = (key, cnt[key])
            elif needed[i]:
                key = ("eng", it["eng"])
                cnt[key] = cnt.get(key, 0) + 1
                it["sig"] = (key, cnt[key])
            else:
                it["sig"] = None
        per = {e: [] for e in ENGS}
        for i, it in enumerate(ins):
            per[it["eng"]].append(i)
        final_waits = {}
        for f in self.final_ids:
            key, val = ins[f]["sig"]
            final_waits[key] = max(final_waits.get(key, 0), val)
        with nc.Block() as block:
            def make(e):
                def body(eng):
                    waited = {}
                    for i in per[e]:
                        it = ins[i]
                        req = {}
                        for d in it["deps"]:
                            key, val = ins[d]["sig"]
                            if waited.get(key, 0) >= val:
                                continue
                            req[key] = max(req.get(key, 0), val)
                        for key, val in req.items():
                            eng.wait_ge(sem_objs[key], val)
                            waited[key] = val
                        r = it["fn"](eng)
                        if it["sig"] is not None:
                            key, val = it["sig"]
                            r.then_inc(sem_objs[key], 16 if it["dma"] else 1)
                    if e == "sp":
                        for key, val in final_waits.items():
                            eng.wait_ge(sem_objs[key], val)
                return body
            block.tensor(make("pe"))
            block.scalar(make("act"))
            block.vector(make("dve"))
            block.gpsimd(make("pool"))
            block.sync(make("sp"))
        for c in reversed(ctxs):
            c.__exit__(None, None, None)
        return dict(n=n, per={e: len(per[e]) for e in ENGS}, sems=len(sem_objs), maxcnt=max(cnt.values()), cnt={k[1]: v for k, v in cnt.items() if k[0] == 'eng'})


def build_program(stage):
    nc = bass.Bass("TRN2", target_bir_lowering=False)
    es = ExitStack()

    def din(name, shape):
        return nc.dram_tensor(name, list(shape), F32, kind="ExternalInput").ap()

    def dout(name, shape):
        return nc.dram_tensor(name, list(shape), F32, kind="ExternalOutput").ap()

    def sb(name, shape, dt=F32):
        return es.enter_context(nc.sbuf_tensor("sb_" + name, list(shape), dt))

    P = Prog(nc)
    nc_allow = nc.allow_non_contiguous_dma(reason="small parameter vectors laid out feature-major")
    nc_allow.__enter__()

    x_in = din("x", [18 * 128, D])
    if stage == "A":
        cvec = din("cvec", [17, D])
        ada_w = din("ada_w", [2, D, 6 * D])
        ada_b = din("ada_b", [2, 6 * D])
        mod_out = dout("modout", [2, 128, 48 * 17])
    else:
        modin_d = din("modin", [128, 48 * 17])
    norm_mix = din("norm_mix", [2, D])
    norm_ffn = din("norm_ffn", [2, D])
    w_in = din("even_w_in", [D, IN_W])
    ident_d = din("ident", [128, 128])
    dk_d = din("dk", [128, 2, 4])
    dc_d = din("dc", [128, 2, 4])
    if stage == "A":
        lret_out = dout("lret", [4, 128, 128])
    if stage in ("C1", "C2"):
        wg_d = din("ffn_wg", [2, D, DFF])
        wu_d = din("ffn_wu", [2, D, DFF])
        wd_d = din("ffn_wd", [2, DFF, D])
        A_re_d = din("odd_A_re", [64, 64])
        A_im_d = din("odd_A_im", [64, 64])
        ldt_d = din("odd_log_dt", [1, 64])
        B_re_d = din("odd_B_re", [64, 64, 16])
        B_im_d = din("odd_B_im", [64, 64, 16])
        C_re_d = din("odd_C_re", [64, 16, 64])
        C_im_d = din("odd_C_im", [64, 16, 64])
        Dsk_d = din("odd_D", [1, D])
        glua_d = din("odd_glu_a", [D, D])
        glub_d = din("odd_glu_b", [D, D])
        s5r_in = din("s5_re", [16, 64, 64])
        s5i_in = din("s5_im", [16, 64, 64])
        fall_d = din("fall", [8, 128, 64])
        wsel_d = din("wsel", [128, 8])
        tal_d = din("tal", [128, 512])
        tbp_d = din("tbp", [128, 512])
        tbs_d = din("tbs", [128, 128])
        amask_d = din("amask", [128, 128])
        maskg_d = din("maskg", [128, 8])
        rott_d = din("rott", [128, 128])
        if stage == "C1":
            floc_out = dout("floc", [128, 64])
        else:
            y_out = dout("y", [17 * 128, D])
            ps5r_out = dout("p_s5r", [64, 64])
            ps5i_out = dout("p_s5i", [64, 64])
            ss5r_out = dout("s_s5r", [16, 64, 64])
            ss5i_out = dout("s_s5i", [16, 64, 64])
    if stage == "B":
        A_re_d = din("odd_A_re", [64, 64])
        A_im_d = din("odd_A_im", [64, 64])
        ldt_d = din("odd_log_dt", [1, 64])
        B_re_d = din("odd_B_re", [64, 64, 16])
        B_im_d = din("odd_B_im", [64, 64, 16])
        C_re_d = din("odd_C_re", [64, 16, 64])
        C_im_d = din("odd_C_im", [64, 16, 64])
        Dsk_d = din("odd_D", [1, D])
        tal_d = din("tal", [128, 512])
        tbp_d = din("tbp", [128, 512])
        tbs_d = din("tbs", [128, 128])
        amask_d = din("amask", [128, 128])
        maskg_d = din("maskg", [128, 8])
        rott_d = din("rott", [128, 128])
        modin1_d = din("modin1", [128, 48 * 17])
        floc_out = dout("floc", [128, 64])
        qgain = din("even_q_gain", [1, 64])
        kgain = din("even_k_gain", [1, 64])
        sinks_d = din("even_sinks", [1, 8])
        retg_d = din("even_ret_gain", [1, 512])
        w_out = din("even_w_out", [D, D])
        wg_d = din("ffn_wg", [2, D, DFF])
        wu_d = din("ffn_wu", [2, D, DFF])
        wd_d = din("ffn_wd", [2, DFF, D])
        cache_k = din("cache_k", [16, 128, 128])
        cache_v = din("cache_v", [16, 128, 128])
        sret_in = din("state_ret", [16, 4, 128, 128])
        bd_d = din("bdones", [128, 128])
        dneg_d = din("dneg", [128, 5, 128])
        rmask_d = din("rmask", [128, 2, 4, 128])
        dq_d = din("dq", [128, 2, 4, 128])
        seqmask_d = din("seqmask", [128, 16])
        wret_d = din("wret", [128, 8, 4])
        lall_d = din("lall", [8, 4, 128, 128])
        x1_out = dout("x1", [17 * 128, D])
        pk_out = dout("p_k", [128, 128])
        pv_out = dout("p_v", [128, 128])
        pret_out = dout("p_ret", [4, 128, 128])
        sk_out = dout("s_k", [16, 128, 128])
        sv_out = dout("s_v", [16, 128, 128])
        sret_out = dout("s_ret", [16, 4, 128, 128])

    Q = [es.enter_context(nc.psum_tensor(f"ps_Q{i}", [128, 1024], F32)) for i in range(4)]

    def bank(i, h):
        return Q[i][:, h * 512:(h + 1) * 512]

    def bkey(i, h):
        return f"Q{i}{'ab'[h]}"

    ARENA_W = 33 * 1024
    arena = sb("arena", [128, ARENA_W])
    cur = [0]

    def carve(shape, dt=F32):
        n = int(np.prod(shape[1:]))
        words = n if dt in (F32, I32) else (n + 1) // 2
        words = (words + 7) // 8 * 8
        off = cur[0]
        assert off + words <= ARENA_W, ("arena overflow", off, words)
        cur[0] = off + words
        v = arena[:, off:off + words]
        if dt != F32:
            v = v.bitcast(dt)
        v = v[:, 0:n]
        if len(shape) == 3:
            v = v.rearrange("p (a b) -> p a b", a=shape[1])
        elif len(shape) == 4:
            v = v.rearrange("p (a b c) -> p a b c", a=shape[1], b=shape[2])
        elif len(shape) == 5:
            v = v.rearrange("p (a b c d) -> p a b c d", a=shape[1], b=shape[2], c=shape[3])
        return v

    ident = sb("ident", [128, 128])
    ones_m = sb("ones_m", [128, 128], BF16)
    ones_b = sb("ones_b", [128, 128], BF16)
    ones_g = sb("ones_g", [128, 128], BF16)
    epsb = sb("epsb", [128, 1])
    xT = sb("xT", [128, KC, NTOK])
    cT = sb("cT", [128, KC, 17], BF16)
    adab = sb("adab", [128, 2, 48])
    modT = sb("modT", [128, 48, 17])
    normg = sb("normg", [128, 2, 2, KC])
    G = sb("G", [128, KC, 17])
    dk = sb("dk", [128, 2, 4])
    dc = sb("dc", [128, 2, 4])
    P.dma("sp", lambda e: e.dma_start(out=ident[:], in_=ident_d), writes=["ident"], semkey="ld_ident", group="ld_ident")
    P.dma("sp", lambda e: e.dma_start(out=dk[:], in_=dk_d), writes=["dk"], semkey="ld_dk", group="ld_dk")
    P.dma("sp", lambda e: e.dma_start(out=dc[:], in_=dc_d), writes=["dc"], semkey="ld_dc", group="ld_dc")
    if stage == "A":
        P.dma("sp", lambda e: e.dma_start(out=adab[:], in_=ada_b.rearrange("l (m p) -> p l m", p=128)), writes=["adab"], semkey="ld_adab", group="ld_adab")
    P.dma("sp", lambda e: e.dma_start(out=normg[:, 0], in_=norm_mix.rearrange("l (k p) -> p l k", p=128)), writes=["normg"], semkey="ld_normg", group="ld_normg")
    P.dma("sp", lambda e: e.dma_start(out=normg[:, 1], in_=norm_ffn.rearrange("l (k p) -> p l k", p=128)), writes=["normg"], semkey="ld_normg", group="ld_normg")
    P.op("dve", lambda e: e.memset(ones_m[:], 1.0 / 1024.0), writes=["ones_m"])
    P.op("dve", lambda e: e.memset(ones_b[:], 1.0), writes=["ones_b"])
    P.op("dve", lambda e: e.memset(ones_g[:], 1.0 / 128.0), writes=["ones_g"])
    P.op("dve", lambda e: e.memset(epsb[:], EPS), writes=["epsb"])

    mark0 = cur[0]
    xst = [carve([128, D]) for _ in range(2)]
    c_sb = carve([128, D])
    adaw = [carve([128, KC, 768], BF16) for _ in range(2)]
    pT = [Q[i][:, :].rearrange("p (k n) -> p k n", k=KC) for i in range(2)]

    def load_tile(t, dst_ap, dst_key):
        s = t % 2
        P.dma("sp", lambda e: e.dma_start(out=xst[s], in_=x_in[t * 128:(t + 1) * 128, :]), writes=[f"xst{s}"], semkey=f"xst{s}")
        for k in range(KC):
            P.op("pe", lambda e, k=k: e.transpose(pT[s][:, k, :], xst[s][:, k * 128:(k + 1) * 128], ident[:]),
                 reads=[f"xst{s}", "ident"], writes=[bkey(s, 0), bkey(s, 1)])
        P.op("act", lambda e: e.copy(out=dst_ap, in_=pT[s]), reads=[bkey(s, 0), bkey(s, 1)], writes=[dst_key])

    for t in range(1, 18):
        c0 = (t - 1) * 128
        load_tile(t, xT[:, :, c0:c0 + 128], ("xT", (t - 1) // 4))

    if stage == "A":
        P.dma("sp", lambda e: e.dma_start(out=c_sb[0:17, :], in_=cvec), writes=["c_sb"], semkey="c_sb")
        P.op("act", lambda e: e.activation(out=c_sb[0:17, :], in_=c_sb[0:17, :], func=AF.Silu), reads=["c_sb"], writes=["c_sb"])
        for k in range(KC):
            P.op("pe", lambda e, k=k: e.transpose(pT[0][:, k, 0:17], c_sb[0:17, k * 128:(k + 1) * 128], ident[0:17, 0:17]),
                 reads=["c_sb", "ident"], writes=[bkey(0, 0), bkey(0, 1)])
        P.op("dve", lambda e: e.tensor_copy(out=cT[:], in_=pT[0][:, :, 0:17]), reads=[bkey(0, 0), bkey(0, 1)], writes=["cT"])

    pm = [bank(2, i)[:, 0:408].rearrange("p (m c) -> p m c", c=17) for i in range(2)]

    def modulation(layer):
        for cb in range(8):
            s = cb % 2
            P.dma("pool", lambda e, cb=cb, s=s: e.dma_start(out=adaw[s], in_=ada_w[layer, :, cb * 768:(cb + 1) * 768].rearrange("(k p) n -> p k n", p=128)),
                  writes=[f"adaw{s}"], semkey=f"adaw{s}")
            for mm in range(6):
                m = cb * 6 + mm
                for k in range(KC):
                    P.op("pe", lambda e, k=k, s=s, m=m, mm=mm: e.matmul(pm[m // 24][:, m % 24, :], lhsT=adaw[s][:, k, mm * 128:(mm + 1) * 128],
                                                                 rhs=cT[:, k, :], start=(k == 0), stop=(k == KC - 1)),
                         reads=[f"adaw{s}", "cT"], writes=[bkey(2, m // 24)])
        for h in range(2):
            P.op("dve", lambda e, h=h: e.tensor_tensor(out=modT[:, h * 24:(h + 1) * 24, :], in0=pm[h],
                                                       in1=adab[:, layer, h * 24:(h + 1) * 24].unsqueeze(2).to_broadcast([128, 24, 17]),
                                                       op=ALU.add),
                 reads=[bkey(2, h), "adab"], writes=["modT"])

    def make_G(layer, which):
        r0 = 8 if which == 0 else 32
        P.op("dve", lambda e: e.tensor_scalar(out=G[:], in0=modT[:, r0:r0 + 8, :], scalar1=1.0, scalar2=None, op0=ALU.add),
             reads=["modT"], writes=["G"])
        P.op("dve", lambda e: e.tensor_tensor(out=G[:], in0=G[:], in1=normg[:, which, layer, :].unsqueeze(2).to_broadcast([128, KC, 17]), op=ALU.mult),
             reads=["G", "normg"], writes=["G"])

    LAYER = 1 if stage in ("C1", "C2") else 0
    if stage == "A":
        for lay in (1, 0):
            modulation(lay)
            P.dma("sp", lambda e, lay=lay: e.dma_start(out=mod_out[lay], in_=modT[:].rearrange("p m c -> p (m c)")), reads=["modT"], semkey="modo", final=True)
    else:
        P.dma("sp", lambda e: e.dma_start(out=modT[:].rearrange("p m c -> p (m c)"), in_=modin_d), writes=["modT"], semkey="ld_modT")
    make_G(LAYER, 0)
    P.barrier()
    cur[0] = mark0

    sq = [carve([128, BT], BF16) for _ in range(2)]
    rstd = carve([128, BT])
    tn = [carve([128, BT]) for _ in range(2)]
    mark_ln = cur[0]
    hT = carve([128, KC, BT], BF16)
    p_ss = bank(3, 0)

    def ln_block(src_ap, src_key, ntok, sample, shift_row, out_ap=None, out_key="hT"):
        out_ap = hT if out_ap is None else out_ap
        for k in range(KC):
            s = k % 2
            P.op("act", lambda e, k=k, s=s: e.activation(out=sq[s][:, :ntok], in_=src_ap[:, k, :], func=AF.Square),
                 reads=[src_key], writes=[f"sq{s}"])
            P.op("pe", lambda e, k=k, s=s: e.matmul(p_ss[:, :ntok], lhsT=ones_m[:], rhs=sq[s][:, :ntok], start=(k == 0), stop=(k == KC - 1)),
                 reads=[f"sq{s}", "ones_m"], writes=[bkey(3, 0)])
        P.op("act", lambda e: e.activation(out=rstd[:, :ntok], in_=p_ss[:, :ntok], func=AF.Sqrt, bias=epsb[:], scale=1.0),
             reads=[bkey(3, 0), "epsb"], writes=["rstd"])
        P.op("dve", lambda e: e.reciprocal(out=rstd[:, :ntok], in_=rstd[:, :ntok]), reads=["rstd"], writes=["rstd"])
        for k in range(KC):
            s = k % 2
            P.op("dve", lambda e, k=k, s=s: e.tensor_tensor(out=tn[s][:, :ntok], in0=src_ap[:, k, :], in1=rstd[:, :ntok], op=ALU.mult),
                 reads=[src_key, "rstd"], writes=[f"tn{s}"])
            if not sample:
                P.op("act", lambda e, k=k, s=s: e.activation(out=out_ap[:, k, :ntok], in_=tn[s][:, :ntok], func=AF.Identity,
                                                             scale=G[:, k, 0:1], bias=modT[:, shift_row + k, 0:1]),
                     reads=[f"tn{s}", "G", "modT"], writes=[out_key])
            else:
                tv = tn[s][:, :128].rearrange("p (b t) -> p b t", t=8)
                P.op("dve", lambda e, k=k, tv=tv: e.tensor_tensor(out=tv, in0=tv, in1=G[:, k, 1:17].unsqueeze(2).to_broadcast([128, 16, 8]), op=ALU.mult),
                     reads=[f"tn{s}", "G"], writes=[f"tn{s}"])
                P.op("dve", lambda e, k=k, tv=tv: e.tensor_tensor(out=out_ap[:, k, :128].rearrange("p (b t) -> p b t", t=8), in0=tv,
                                                                  in1=modT[:, shift_row + k, 1:17].unsqueeze(2).to_broadcast([128, 16, 8]), op=ALU.add),
                     reads=[f"tn{s}", "modT"], writes=[out_key])

    win = None
    if stage in ("A", "B"):
        win = carve([128, KC, IN_W], BF16)
        for k in range(KC):
            P.dma("pool", lambda e, k=k: e.dma_start(out=win[:, k, :], in_=w_in[k * 128:(k + 1) * 128, :]), writes=["win"], semkey="win", group="win")

    S = carve([128, 4, 128])
    Sb = carve([128, 4, 128], BF16)
    kd_tok = carve([128, 512], BF16)
    vb_tok = carve([128, 512], BF16)
    p_kb = bank(3, 1)
    p_vb = bank(2, 0)
    p_su = bank(2, 1).rearrange("p (h e) -> p h e", h=4)

    def tok_kv(tc0, grp):
        for k in range(KC):
            P.op("pe", lambda e, k=k: e.matmul(p_kb, lhsT=hT[:, k, tc0:tc0 + 128], rhs=win[:, k, 1280:1792], start=(k == 0), stop=(k == KC - 1)),
                 reads=["hT", "win"], writes=[bkey(3, 1)])
        for k in range(KC):
            P.op("pe", lambda e, k=k: e.matmul(p_vb, lhsT=hT[:, k, tc0:tc0 + 128], rhs=win[:, k, 1792:2304], start=(k == 0), stop=(k == KC - 1)),
                 reads=["hT", "win"], writes=[bkey(2, 0)])
        P.op("dve", lambda e: e.tensor_tensor(out=kd_tok.rearrange("p (h d) -> p h d", h=4), in0=p_kb.rearrange("p (h d) -> p h d", h=4),
                                              in1=dk[:, grp, :].unsqueeze(2).to_broadcast([128, 4, 128]), op=ALU.mult),
             reads=[bkey(3, 1), "dk"], writes=["kd_tok"])
        P.op("act", lambda e: e.copy(out=vb_tok, in_=p_vb), reads=[bkey(2, 0)], writes=["vb_tok"])

    def state_update():
        for h in range(4):
            P.op("pe", lambda e, h=h: e.matmul(p_su[:, h, :], lhsT=kd_tok[:, h * 128:(h + 1) * 128], rhs=vb_tok[:, h * 128:(h + 1) * 128], start=True, stop=True),
                 reads=["kd_tok", "vb_tok"], writes=[bkey(2, 1)])
        P.op("dve", lambda e: e.tensor_tensor(out=S, in0=S, in1=dc[:, 0, :].unsqueeze(2).to_broadcast([128, 4, 128]), op=ALU.mult),
             reads=["S", "dc"], writes=["S"])
        P.op("dve", lambda e: e.tensor_tensor(out=S, in0=S, in1=p_su, op=ALU.add), reads=["S", bkey(2, 1)], writes=["S"])

    def ffn(layer):
        GC = 2
        NG = FC // GC
        P.barrier()
        cur[0] = mark_ln
        make_G(layer, 1)
        hT_all = carve([128, KC, NTOK], BF16)
        wslot = [(carve([128, KC, GC * 128], BF16), carve([128, KC, GC * 128], BF16), carve([128, GC, D], BF16)) for _ in range(3)]
        hid = [carve([128, GC, 512], BF16) for _ in range(2)]
        sgf = [carve([128, 512]) for _ in range(2)]
        gt2 = carve([128, 128])
        print("arena words used (ffn):", cur[0], "of", ARENA_W)
        for t in range(NPT):
            ln_block(xT[:, :, t * 128:(t + 1) * 128], ("xT", t // 4), 128, False, 24, out_ap=hT_all[:, :, t * 128:(t + 1) * 128], out_key="hT_all")
        ln_block(xT[:, :, TOK_P:TOK_P + 128], ("xT", 4), 128, True, 24, out_ap=hT_all[:, :, TOK_P:TOK_P + 128], out_key="hT_all")
        UP = [(bank(0, 0), bkey(0, 0), bank(0, 1), bkey(0, 1)), (bank(1, 0), bkey(1, 0), bank(1, 1), bkey(1, 1))]
        DN = [(bank(2, 0), bkey(2, 0)), (bank(2, 1), bkey(2, 1)), (bank(3, 0), bkey(3, 0)), (bank(3, 1), bkey(3, 1))]
        ui = 0
        di = 0
        blocks = [(b * 512, 512, False) for b in range(4)] + [(TOK_P, 128, True)]
        for g in range(NG):
            ws = g % 3
            wgs, wus, wds = wslot[ws]
            c0h = g * GC * 128
            P.dma("pool", lambda e, wgs=wgs, c0h=c0h: e.dma_start(out=wgs, in_=wg_d[layer, :, c0h:c0h + GC * 128].rearrange("(k p) n -> p k n", p=128)),
                  writes=[f"wg{ws}"], semkey=f"wg{ws}")
            P.dma("pool", lambda e, wus=wus, c0h=c0h: e.dma_start(out=wus, in_=wu_d[layer, :, c0h:c0h + GC * 128].rearrange("(k p) n -> p k n", p=128)),
                  writes=[f"wu{ws}"], semkey=f"wu{ws}")
            P.dma("pool", lambda e, wds=wds, c0h=c0h: e.dma_start(out=wds, in_=wd_d[layer, c0h:c0h + GC * 128, :].rearrange("(c p) n -> p c n", p=128)),
                  writes=[f"wd{ws}"], semkey=f"wd{ws}")
            for (t0, nt, smp) in blocks:
                hs = ui % 2
                for c in range(GC):
                    gps, gkey, ups, ukey = UP[ui % 2]
                    ui += 1
                    for k in range(KC):
                        P.op("pe", lambda e, k=k, c=c, gps=gps, wgs=wgs, nt=nt, t0=t0: e.matmul(gps[:, :nt], lhsT=wgs[:, k, c * 128:(c + 1) * 128], rhs=hT_all[:, k, t0:t0 + nt],
                                                                                  start=(k == 0), stop=(k == KC - 1)),
                             reads=[f"wg{ws}", "hT_all"], writes=[gkey])
                    for k in range(KC):
                        P.op("pe", lambda e, k=k, c=c, ups=ups, wus=wus, nt=nt, t0=t0: e.matmul(ups[:, :nt], lhsT=wus[:, k, c * 128:(c + 1) * 128], rhs=hT_all[:, k, t0:t0 + nt],
                                                                                  start=(k == 0), stop=(k == KC - 1)),
                             reads=[f"wu{ws}", "hT_all"], writes=[ukey])
                    sgs = sgf[c % 2]
                    P.op("act", lambda e, gps=gps, sgs=sgs, nt=nt: e.activation(out=sgs[:, :nt], in_=gps[:, :nt], func=AF.Silu), reads=[gkey], writes=[f"sgf{c % 2}"])
                    P.op("dve", lambda e, ups=ups, sgs=sgs, hs=hs, c=c, nt=nt: e.tensor_tensor(out=hid[hs][:, c, :nt], in0=ups[:, :nt], in1=sgs[:, :nt], op=ALU.mult),
                         reads=[ukey, f"sgf{c % 2}"], writes=[f"hid{hs}"])
                for m in range(KC):
                    dps, dkey = DN[di % 4]
                    di += 1
                    for c in range(GC):
                        P.op("pe", lambda e, m=m, c=c, dps=dps, wds=wds, hs=hs, nt=nt: e.matmul(dps[:, :nt], lhsT=wds[:, c, m * 128:(m + 1) * 128], rhs=hid[hs][:, c, :nt],
                                                                                         start=(c == 0), stop=(c == GC - 1)),
                             reads=[f"wd{ws}", f"hid{hs}"], writes=[dkey])
                    xs = xT[:, m, t0:t0 + nt]
                    xkey = ("xT", t0 // 512)
                    if not smp:
                        P.op("dve", lambda e, m=m, dps=dps, xs=xs, nt=nt: e.scalar_tensor_tensor(out=xs, in0=dps[:, :nt], scalar=modT[:, 40 + m, 0:1], in1=xs,
                                                                                         op0=ALU.mult, op1=ALU.add),
                             reads=[dkey, "modT", xkey], writes=[xkey])
                    else:
                        gv = gt2.rearrange("p (b t) -> p b t", t=8)
                        P.op("dve", lambda e, m=m, dps=dps, gv=gv: e.tensor_tensor(out=gv, in0=dps[:, :128].rearrange("p (b t) -> p b t", t=8),
                                                                                  in1=modT[:, 40 + m, 1:17].unsqueeze(2).to_broadcast([128, 16, 8]), op=ALU.mult),
                             reads=[dkey, "modT"], writes=["gt2"])
                        P.op("dve", lambda e, xs=xs: e.tensor_tensor(out=xs, in0=xs, in1=gt2, op=ALU.add), reads=["gt2", xkey], writes=[xkey])
        return hT_all

    if stage == "A":
        P.op("dve", lambda e: e.memset(S, 0.0), writes=["S"])
        for b in range(TOK_P // BT):
            ln_block(xT[:, :, b * BT:(b + 1) * BT], ("xT", (b * BT) // 512), BT, False, 0)
            for i in range(TPB):
                tok_kv(i * 128, 0)
                state_update()
        P.dma("sp", lambda e: e.dma_start(out=lret_out.rearrange("h d e -> d h e"), in_=S), reads=["S"], semkey="lret", final=True)
        stats = P.build()
        nc_allow.__exit__(None, None, None)
        es.close()
        return nc, stats

    def s5_stage(full):
        TWO_PI = 2.0 * np.pi
        cur[0] = mark_ln
        hT_all = carve([128, KC, NTOK], BF16)
        for t in range(NPT):
            ln_block(xT[:, :, t * 128:(t + 1) * 128], ("xT", t // 4), 128, False, 0, out_ap=hT_all[:, :, t * 128:(t + 1) * 128], out_key="hT_all")
        ln_block(xT[:, :, TOK_P:TOK_P + 128], ("xT", 4), 128, True, 0, out_ap=hT_all[:, :, TOK_P:TOK_P + 128], out_key="hT_all")

        def t64():
            return carve([128, 64])
        AreT, AimT, dtT, ar, ai, rho, thr, ph64, ph512, sinT, cosT, tmpa, tmpb, lre, lim, fre, fim, rden64 = [t64() for _ in range(18)]
        PHB = carve([128, 64, 4])
        pib = carve([128, 1])
        Glast = carve([128, 64])
        Alast = carve([128, 64])
        Blast = carve([128, 64])
        tal = carve([128, 512])
        tbp = carve([128, 512])
        tbs = carve([128, 128])
        amask = carve([128, 128])
        maskg = carve([128, 8])
        rott = carve([128, 128])
        Dsk = carve([128, KC])
        Mc = carve([128, KC, 128], BF16)
        Mcsw = carve([128, KC, 128], BF16)
        CA = carve([128, 64, 128], BF16)
        CB = carve([128, 64, 128], BF16)
        mark_s5 = cur[0]
        Bn_re = carve([128, 64, 16])
        Bn_im = carve([128, 64, 16])
        tB1 = carve([128, 64, 16])
        tB2 = carve([128, 64, 16])
        Cn = carve([128, 16, 2, 64])
        Cn2 = carve([128, 16, 2, 64])
        CsA = carve([128, 64, 16])
        CsB = carve([128, 64, 16])
        print("arena words used (s5 prep):", cur[0], "of", ARENA_W)
        P.op("dve", lambda e: e.memset(pib, float(np.pi / 2)), writes=["pib"])
        for half in range(2):
            hs_ = slice(half * 64, half * 64 + 64)
            P.dma("sp", lambda e, hs_=hs_: e.dma_start(out=AreT[hs_, :], in_=A_re_d.rearrange("g p -> p g")), writes=["AreT"], semkey="ld_AreT", group="ld_AreT")
            P.dma("sp", lambda e, hs_=hs_: e.dma_start(out=AimT[hs_, :], in_=A_im_d.rearrange("g p -> p g")), writes=["AimT"], semkey="ld_AimT", group="ld_AimT")
        P.dma("sp", lambda e: e.dma_start(out=dtT, in_=ldt_d.partition_broadcast(128)), writes=["dtT"], semkey="ld_dtT")
        for nm, tl, dd in (("tal", tal, tal_d), ("tbp", tbp, tbp_d), ("tbs", tbs, tbs_d), ("amask", amask, amask_d), ("maskg", maskg, maskg_d), ("rott", rott, rott_d)):
            P.dma("sp", lambda e, tl=tl, dd=dd: e.dma_start(out=tl, in_=dd), writes=[nm], semkey="ld_" + nm)
        P.dma("sp", lambda e: e.dma_start(out=Dsk, in_=Dsk_d.rearrange("o (k p) -> p (o k)", p=128)), writes=["Dsk"], semkey="ld_Dsk")
        P.dma("sp", lambda e: e.dma_start(out=Bn_re[0:64], in_=B_re_d.rearrange("g p j -> p g j")), writes=["Bn_re"], semkey="ld_Bn_re")
        P.dma("sp", lambda e: e.dma_start(out=Bn_im[0:64], in_=B_im_d.rearrange("g p j -> p g j")), writes=["Bn_im"], semkey="ld_Bn_im")
        if full:
            P.dma("sp", lambda e: e.dma_start(out=Cn[0:64, :, 0, :], in_=C_re_d), writes=["Cn"], semkey="ld_Cn", group="ld_Cn")
        if full:
            P.dma("sp", lambda e: e.dma_start(out=Cn[0:64, :, 1, :], in_=C_im_d), writes=["Cn"], semkey="ld_Cn", group="ld_Cn")
        if full:
            P.dma("sp", lambda e: e.dma_start(out=Cn2[0:64, :, 0, :], in_=C_im_d), writes=["Cn2"], semkey="ld_Cn2", group="ld_Cn2")
        if full:
            P.dma("sp", lambda e: e.dma_start(out=Cn2[0:64, :, 1, :], in_=C_re_d), writes=["Cn2"], semkey="ld_Cn2", group="ld_Cn2")

        def V(fn, reads, writes, eng="dve"):
            P.op(eng, fn, reads=reads, writes=writes)

        V(lambda e: e.activation(out=dtT, in_=dtT, func=AF.Exp), ["dtT"], ["dtT"], "act")
        V(lambda e: e.tensor_tensor(out=ar, in0=AreT, in1=dtT, op=ALU.mult), ["AreT", "dtT"], ["ar"])
        V(lambda e: e.tensor_tensor(out=ai, in0=AimT, in1=dtT, op=ALU.mult), ["AimT", "dtT"], ["ai"])
        V(lambda e: e.activation(out=rho, in_=ar, func=AF.Exp), ["ar"], ["rho"], "act")
        ti64 = carve([128, 64], I32)
        aiT = t64()

        def fracr(out, okey, inp, ikey):
            V(lambda e: e.tensor_copy(out=ti64, in_=inp), [ikey], ["ti64"])
            V(lambda e: e.tensor_copy(out=tmpb, in_=ti64), ["ti64"], ["tmpb"])
            V(lambda e: e.tensor_tensor(out=out, in0=inp, in1=tmpb, op=ALU.subtract), [ikey, "tmpb"], [okey])

        V(lambda e: e.tensor_scalar(out=aiT, in0=ai, scalar1=float(1.0 / TWO_PI), scalar2=None, op0=ALU.mult), ["ai"], ["aiT"])
        fracr(thr, "thr", aiT, "aiT")
        V(lambda e: e.tensor_scalar(out=tmpa, in0=aiT, scalar1=64.0, scalar2=None, op0=ALU.mult), ["aiT"], ["tmpa"])
        fracr(ph64, "ph64", tmpa, "tmpa")
        V(lambda e: e.tensor_scalar(out=tmpa, in0=aiT, scalar1=512.0, scalar2=None, op0=ALU.mult), ["aiT"], ["tmpa"])
        fracr(ph512, "ph512", tmpa, "tmpa")
        for blk in range(4):
            V(lambda e, blk=blk: e.tensor_scalar(out=tmpa, in0=ph512, scalar1=float(blk), scalar2=None, op0=ALU.mult), ["ph512"], ["tmpa"])
            fracr(PHB[:, :, blk], "PHB", tmpa, "tmpa")

        def sincos(ang, akey, s_out, skey, c_out, ckey, tmp, tkey):
            V(lambda e: e.activation(out=s_out, in_=ang, func=AF.Sin, scale=TWO_PI), [akey], [skey], "act")
            V(lambda e: e.activation(out=tmp, in_=ang, func=AF.Abs), [akey], [tkey], "act")
            V(lambda e: e.activation(out=c_out, in_=tmp, func=AF.Sin, scale=-TWO_PI, bias=pib[:]), [tkey, "pib"], [ckey], "act")

        sincos(thr, "thr", sinT, "sinT", cosT, "cosT", tmpa, "tmpa")
        V(lambda e: e.tensor_tensor(out=lre, in0=rho, in1=cosT, op=ALU.mult), ["rho", "cosT"], ["lre"])
        V(lambda e: e.tensor_tensor(out=lim, in0=rho, in1=sinT, op=ALU.mult), ["rho", "sinT"], ["lim"])
        V(lambda e: e.tensor_scalar(out=tmpa, in0=lre, scalar1=-1.0, scalar2=None, op0=ALU.add), ["lre"], ["tmpa"])
        V(lambda e: e.tensor_tensor(out=rden64, in0=AreT, in1=AreT, op=ALU.mult), ["AreT"], ["rden64"])
        V(lambda e: e.tensor_tensor(out=tmpb, in0=AimT, in1=AimT, op=ALU.mult), ["AimT"], ["tmpb"])
        V(lambda e: e.tensor_tensor(out=rden64, in0=rden64, in1=tmpb, op=ALU.add), ["rden64", "tmpb"], ["rden64"])
        V(lambda e: e.reciprocal(out=rden64, in_=rden64), ["rden64"], ["rden64"])
        V(lambda e: e.tensor_tensor(out=fre, in0=tmpa, in1=AreT, op=ALU.mult), ["tmpa", "AreT"], ["fre"])
        V(lambda e: e.tensor_tensor(out=tmpb, in0=lim, in1=AimT, op=ALU.mult), ["lim", "AimT"], ["tmpb"])
        V(lambda e: e.tensor_tensor(out=fre, in0=fre, in1=tmpb, op=ALU.add), ["fre", "tmpb"], ["fre"])
        V(lambda e: e.tensor_tensor(out=fre, in0=fre, in1=rden64, op=ALU.mult), ["fre", "rden64"], ["fre"])
        V(lambda e: e.tensor_tensor(out=fim, in0=lim, in1=AreT, op=ALU.mult), ["lim", "AreT"], ["fim"])
        V(lambda e: e.tensor_tensor(out=tmpb, in0=tmpa, in1=AimT, op=ALU.mult), ["tmpa", "AimT"], ["tmpb"])
        V(lambda e: e.tensor_tensor(out=fim, in0=fim, in1=tmpb, op=ALU.subtract), ["fim", "tmpb"], ["fim"])
        V(lambda e: e.tensor_tensor(out=fim, in0=fim, in1=rden64, op=ALU.mult), ["fim", "rden64"], ["fim"])
        h64 = slice(0, 64)
        frb = fre[h64, :].unsqueeze(2).to_broadcast([64, 64, 16])
        fib = fim[h64, :].unsqueeze(2).to_broadcast([64, 64, 16])
        V(lambda e: e.tensor_tensor(out=tB1[h64], in0=Bn_re[h64], in1=frb, op=ALU.mult), ["Bn_re", "fre"], ["tB1"])
        V(lambda e: e.tensor_tensor(out=tB2[h64], in0=Bn_im[h64], in1=fib, op=ALU.mult), ["Bn_im", "fim"], ["tB2"])
        V(lambda e: e.tensor_tensor(out=tB1[h64], in0=tB1[h64], in1=tB2[h64], op=ALU.subtract), ["tB1", "tB2"], ["tB1"])
        V(lambda e: e.tensor_tensor(out=tB2[h64], in0=Bn_im[h64], in1=frb, op=ALU.mult), ["Bn_im", "fre"], ["tB2"])
        V(lambda e: e.tensor_tensor(out=Bn_im[h64], in0=Bn_re[h64], in1=fib, op=ALU.mult), ["Bn_re", "fim", "tB2"], ["Bn_im"])
        V(lambda e: e.tensor_tensor(out=tB2[h64], in0=tB2[h64], in1=Bn_im[h64], op=ALU.add), ["tB2", "Bn_im"], ["tB2"])
        ptr = bank(3, 0)
        for F in range(KC):
            P.op("pe", lambda e, F=F: e.transpose(ptr[:, 0:64], tB1[h64, 8 * F:8 * F + 8, :].rearrange("p a b -> p (a b)"), ident[0:64, 0:64]),
                 reads=["tB1", "ident"], writes=[bkey(3, 0)])
            P.op("pe", lambda e, F=F: e.transpose(ptr[:, 64:128], tB2[h64, 8 * F:8 * F + 8, :].rearrange("p a b -> p (a b)"), ident[0:64, 0:64]),
                 reads=["tB2", "ident"], writes=[bkey(3, 0)])
            V(lambda e, F=F: e.copy(out=Mc[:, F, :], in_=ptr[:, 0:128]), [bkey(3, 0)], ["Mc"], "act")
            V(lambda e, F=F: e.copy(out=Mcsw[:, F, 0:64], in_=ptr[:, 64:128]), [bkey(3, 0)], ["Mcsw"], "act")
            V(lambda e, F=F: e.mul(out=Mcsw[:, F, 64:128], in_=ptr[:, 0:64], mul=-1.0), [bkey(3, 0)], ["Mcsw"], "act")
        if full:
            for (src, skey, dst, dkey) in ((Cn, "Cn", CsA, "CsA"), (Cn2, "Cn2", CsB, "CsB")):
                for ib in range(2):
                    for ii in range(8):
                        i_ = ib * 8 + ii
                        P.op("pe", lambda e, src=src, i_=i_, ii=ii: e.transpose(ptr[:, ii * 64:(ii + 1) * 64], src[h64, i_, :, :].rearrange("p c q -> p (c q)"), ident[0:64, 0:64]),
                             reads=[skey, "ident"], writes=[bkey(3, 0)])
                    V(lambda e, dst=dst, ib=ib: e.copy(out=dst[:, :, ib * 8:(ib + 1) * 8].rearrange("p g i -> p i g"), in_=ptr.rearrange("p (i g) -> p i g", i=8)),
                      [bkey(3, 0)], [dkey], "act")
            V(lambda e: e.tensor_scalar(out=CsA[64:128], in0=CsA[64:128], scalar1=-1.0, scalar2=None, op0=ALU.mult), ["CsA"], ["CsA"])
            V(lambda e: e.tensor_scalar(out=CsB, in0=CsB, scalar1=-1.0, scalar2=None, op0=ALU.mult), ["CsB"], ["CsB"])
            V(lambda e: e.memset(CA, 0.0), [], ["CA"], "pool")
            V(lambda e: e.memset(CB, 0.0), [], ["CB"], "pool")
            for gl in range(8):
                for (src, skey, dst, dkey) in ((CsA, "CsA", CA, "CA"), (CsB, "CsB", CB, "CB")):
                    V(lambda e, src=src, dst=dst, gl=gl: e.tensor_copy(out=dst.rearrange("p (f g) c -> p f g c", g=8)[:, :, gl, 16 * gl:16 * gl + 16],
                                                                       in_=src.rearrange("p (f g) i -> p f g i", g=8)[:, :, gl, :]),
                      [skey], [dkey])

        hin = carve([128, 64]) if False else None
        P.op("dve", lambda e: e.memset(Glast, 0.0), writes=["Glast"])
        if full:
            ang = tmpa
            V(lambda e: e.tensor_scalar(out=ang, in0=aiT, scalar1=2048.0, scalar2=None, op0=ALU.mult), ["aiT"], ["tmpa"])
            fracr(fim, "fim", ang, "tmpa")
            sincos(fim, "fim", sinT, "sinT", cosT, "cosT", fre, "fre")
            V(lambda e: e.activation(out=lre, in_=ar, func=AF.Exp, scale=2048.0), ["ar"], ["lre"], "act")
            V(lambda e: e.tensor_tensor(out=lim, in0=lre, in1=sinT, op=ALU.mult), ["lre", "sinT"], ["lim"])
            V(lambda e: e.tensor_tensor(out=lre, in0=lre, in1=cosT, op=ALU.mult), ["lre", "cosT"], ["lre"])
            wsel = carve([128, 8])
            fr = [carve([128, 64]) for _ in range(2)]
            P.dma("sp", lambda e: e.dma_start(out=wsel, in_=wsel_d), writes=["wsel"], semkey="ld_wsel")
            prot = bank(3, 1)[:, 0:64]
            for r in range(7):
                s_ = r % 2
                P.dma("sp", lambda e, r=r, s_=s_: e.dma_start(out=fr[s_], in_=fall_d[r]), writes=[f"fr{s_}"], semkey=f"fr{s_}")
                P.op("pe", lambda e: e.matmul(prot, lhsT=rott, rhs=Glast, start=True, stop=True), reads=["rott", "Glast"], writes=[bkey(3, 1)])
                V(lambda e: e.tensor_tensor(out=tmpa, in0=lre, in1=Glast, op=ALU.mult), ["lre", "Glast"], ["tmpa"])
                V(lambda e: e.tensor_tensor(out=tmpb, in0=lim, in1=prot, op=ALU.mult), ["lim", bkey(3, 1)], ["tmpb"])
                V(lambda e: e.tensor_tensor(out=tmpa, in0=tmpa, in1=tmpb, op=ALU.add), ["tmpa", "tmpb"], ["tmpa"])
                V(lambda e, s_=s_: e.tensor_tensor(out=tmpa, in0=tmpa, in1=fr[s_], op=ALU.add), ["tmpa", f"fr{s_}"], ["tmpa"])
                V(lambda e: e.tensor_tensor(out=tmpa, in0=tmpa, in1=Glast, op=ALU.subtract), ["tmpa", "Glast"], ["tmpa"])
                V(lambda e, r=r: e.scalar_tensor_tensor(out=Glast, in0=tmpa, scalar=wsel[:, r:r + 1], in1=Glast, op0=ALU.mult, op1=ALU.add),
                  ["tmpa", "wsel", "Glast"], ["Glast"])
        P.barrier()
        cur[0] = mark_s5
        NW = 2
        um = [carve([128, 512], BF16) for _ in range(NW)]
        xang = [carve([128, 512]) for _ in range(NW)]
        sang = [carve([128, 512]) for _ in range(NW)]
        SINt = [carve([128, 512]) for _ in range(NW)]
        COSt = [carve([128, 512]) for _ in range(NW)]
        Wt = [carve([128, 512]) for _ in range(NW)]
        Ab = [carve([128, 512], BF16) for _ in range(NW)]
        Bb = [carve([128, 512], BF16) for _ in range(NW)]
        aseq = carve([128, 128])
        ysb = carve([128, 512])
        y2 = carve([128, 512])
        H0T = carve([128, 64, 16])
        Asl = carve([128, 64, 16])
        Bsl = carve([128, 64, 16])
        h0n = carve([128, 64, 128]) if False else None
        print("arena words used (s5 main):", cur[0], "of", ARENA_W)
        BU = [(bank(0, 0), bkey(0, 0), bank(0, 1), bkey(0, 1)), (bank(1, 0), bkey(1, 0), bank(1, 1), bkey(1, 1))]
        YP = [(bank(2, 0), bkey(2, 0)), (bank(2, 1), bkey(2, 1))]
        wi = [0]

        def s5_T(F, gl, t0, nt, blk, sample):
            g = 8 * F + gl
            w = wi[0] % NW
            wi[0] += 1
            bu, bukey, bus, buskey = BU[w % 2]
            V(lambda e: e.activation(out=um[w][:, :nt], in_=hT_all[:, F, t0:t0 + nt], func=AF.Copy, scale=maskg[:, gl:gl + 1]),
              [("hT_all", F, t0), "maskg"], [f"um{w}"], "act")
            P.op("pe", lambda e: e.matmul(bu[:, :nt], lhsT=Mc[:, F, :], rhs=um[w][:, :nt], start=True, stop=True), reads=["Mc", f"um{w}"], writes=[bukey])
            P.op("pe", lambda e: e.matmul(bus[:, :nt], lhsT=Mcsw[:, F, :], rhs=um[w][:, :nt], start=True, stop=True), reads=["Mcsw", f"um{w}"], writes=[buskey])
            MAGIC = 12582912.0
            if not sample:
                V(lambda e: e.activation(out=xang[w], in_=tal, func=AF.Identity, scale=ph64[:, g:g + 1], bias=PHB[:, g, blk:blk + 1]),
                  ["tal", "ph64", "PHB"], [f"xang{w}"], "act")
                V(lambda e: e.scalar_tensor_tensor(out=xang[w], in0=tbp, scalar=thr[:, g:g + 1], in1=xang[w], op0=ALU.mult, op1=ALU.add),
                  ["tbp", "thr", f"xang{w}"], [f"xang{w}"])
            else:
                V(lambda e: e.activation(out=xang[w][:, :nt], in_=tbs, func=AF.Copy, scale=thr[:, g:g + 1]), ["tbs", "thr"], [f"xang{w}"], "act")
            V(lambda e: e.tensor_scalar(out=sang[w][:, :nt], in0=xang[w][:, :nt], scalar1=MAGIC, scalar2=MAGIC, op0=ALU.add, op1=ALU.subtract),
              [f"xang{w}"], [f"sang{w}"])
            V(lambda e: e.tensor_tensor(out=xang[w][:, :nt], in0=xang[w][:, :nt], in1=sang[w][:, :nt], op=ALU.subtract), [f"xang{w}", f"sang{w}"], [f"xang{w}"])
            V(lambda e: e.activation(out=sang[w][:, :nt], in_=xang[w][:, :nt], func=AF.Abs), [f"xang{w}"], [f"sang{w}"], "act")
            V(lambda e: e.activation(out=SINt[w][:, :nt], in_=xang[w][:, :nt], func=AF.Sin, scale=TWO_PI), [f"xang{w}"], [f"SINt{w}"], "act")
            V(lambda e: e.activation(out=COSt[w][:, :nt], in_=sang[w][:, :nt], func=AF.Sin, scale=-TWO_PI, bias=pib[:]), [f"sang{w}", "pib"], [f"COSt{w}"], "act")
            return dict(F=F, gl=gl, g=g, w=w, t0=t0, nt=nt, blk=blk, sample=sample, bu=bu, bukey=bukey, bus=bus, buskey=buskey)

        def s5_S(c, ypk, last_blk):
            F, gl, g, w, t0, nt, blk, sample = c['F'], c['gl'], c['g'], c['w'], c['t0'], c['nt'], c['blk'], c['sample']
            bu, bukey, bus, buskey = c['bu'], c['bukey'], c['bus'], c['buskey']
            yp, ykey = ypk
            V(lambda e: e.tensor_tensor(out=sang[w][:, :nt], in0=bu[:, :nt], in1=COSt[w][:, :nt], op=ALU.mult), [bukey, f"COSt{w}"], [f"sang{w}"])
            V(lambda e: e.tensor_tensor(out=Wt[w][:, :nt], in0=bus[:, :nt], in1=SINt[w][:, :nt], op=ALU.mult), [buskey, f"SINt{w}"], [f"Wt{w}"])
            V(lambda e: e.tensor_tensor(out=Wt[w][:, :nt], in0=Wt[w][:, :nt], in1=sang[w][:, :nt], op=ALU.add), [f"Wt{w}", f"sang{w}"], [f"Wt{w}"])
            if not sample:
                V(lambda e: e.tensor_tensor_scan(out=Wt[w], data0=rho[:, g:g + 1].to_broadcast([128, 512]), data1=Wt[w], initial=Glast[:, g:g + 1],
                                                 op0=ALU.mult, op1=ALU.add), [f"Wt{w}", "rho", "Glast"], [f"Wt{w}"])
                V(lambda e: e.copy(out=Glast[:, g:g + 1], in_=Wt[w][:, 511:512]), [f"Wt{w}"], ["Glast"], "act")
                if last_blk:
                    V(lambda e: e.tensor_tensor(out=Alast[:, g:g + 1], in0=Wt[w][:, 511:512], in1=COSt[w][:, 511:512], op=ALU.mult), [f"Wt{w}", f"COSt{w}"], ["Alast"])
                    V(lambda e: e.tensor_tensor(out=Blast[:, g:g + 1], in0=Wt[w][:, 511:512], in1=SINt[w][:, 511:512], op=ALU.mult), [f"Wt{w}", f"SINt{w}"], ["Blast"])
            else:
                wv = Wt[w][:, :128].rearrange("p (b t) -> p b t", t=8)
                V(lambda e: e.scalar_tensor_tensor(out=wv[:, :, 0], in0=H0T[:, g, :], scalar=rho[:, g:g + 1], in1=wv[:, :, 0], op0=ALU.mult, op1=ALU.add),
                  ["H0T", "rho", f"Wt{w}"], [f"Wt{w}"])
                V(lambda e: e.tensor_scalar(out=aseq, in0=amask, scalar1=rho[:, g:g + 1], scalar2=None, op0=ALU.mult), ["amask", "rho"], ["aseq"])
                V(lambda e: e.tensor_tensor_scan(out=Wt[w][:, :128], data0=aseq, data1=Wt[w][:, :128], initial=0.0, op0=ALU.mult, op1=ALU.add),
                  [f"Wt{w}", "aseq"], [f"Wt{w}"])
                cv = COSt[w][:, :128].rearrange("p (b t) -> p b t", t=8)
                sv = SINt[w][:, :128].rearrange("p (b t) -> p b t", t=8)
                V(lambda e: e.tensor_tensor(out=Asl[:, g, :], in0=wv[:, :, 7], in1=cv[:, :, 7], op=ALU.mult), [f"Wt{w}", f"COSt{w}"], ["Asl"])
                V(lambda e: e.tensor_tensor(out=Bsl[:, g, :], in0=wv[:, :, 7], in1=sv[:, :, 7], op=ALU.mult), [f"Wt{w}", f"SINt{w}"], ["Bsl"])
            if full:
                V(lambda e: e.tensor_tensor(out=Ab[w][:, :nt], in0=Wt[w][:, :nt], in1=COSt[w][:, :nt], op=ALU.mult), [f"Wt{w}", f"COSt{w}"], [f"Ab{w}"])
                V(lambda e: e.tensor_tensor(out=Bb[w][:, :nt], in0=Wt[w][:, :nt], in1=SINt[w][:, :nt], op=ALU.mult), [f"Wt{w}", f"SINt{w}"], [f"Bb{w}"], "pool")
                P.op("pe", lambda e: e.matmul(yp[:, :nt], lhsT=CA[:, g, :], rhs=Ab[w][:, :nt], start=(gl == 0), stop=False), reads=["CA", f"Ab{w}"], writes=[ykey])
                P.op("pe", lambda e: e.matmul(yp[:, :nt], lhsT=CB[:, g, :], rhs=Bb[w][:, :nt], start=False, stop=(gl == 7)), reads=["CB", f"Bb{w}"], writes=[ykey])

        def y_finish(F, t0, nt, ypk):
            yp, ykey = ypk
            V(lambda e: e.scalar_tensor_tensor(out=ysb[:, :nt], in0=hT_all[:, F, t0:t0 + nt], scalar=Dsk[:, F:F + 1], in1=yp[:, :nt], op0=ALU.mult, op1=ALU.add),
              [("hT_all", F, t0), "Dsk", ykey], ["ysb"])
            V(lambda e: e.tensor_tensor(out=y2[:, :nt], in0=ysb[:, :nt], in1=ysb[:, :nt], op=ALU.mult), ["ysb"], ["y2"], "pool")
            V(lambda e: e.tensor_scalar(out=y2[:, :nt], in0=y2[:, :nt], scalar1=0.044715, scalar2=1.0, op0=ALU.mult, op1=ALU.add), ["y2"], ["y2"], "pool")
            V(lambda e: e.tensor_tensor(out=y2[:, :nt], in0=y2[:, :nt], in1=ysb[:, :nt], op=ALU.mult), ["y2", "ysb"], ["y2"], "pool")
            V(lambda e: e.activation(out=y2[:, :nt], in_=y2[:, :nt], func=AF.Tanh, scale=float(np.sqrt(2.0 / np.pi))), ["y2"], ["y2"], "act")
            V(lambda e: e.tensor_scalar(out=y2[:, :nt], in0=y2[:, :nt], scalar1=0.5, scalar2=0.5, op0=ALU.mult, op1=ALU.add), ["y2"], ["y2"])
            V(lambda e: e.tensor_tensor(out=hT_all[:, F, t0:t0 + nt], in0=y2[:, :nt], in1=ysb[:, :nt], op=ALU.mult), ["y2", "ysb"], [("hT_all", F, t0)])

        if True:
            h0n = um
        s0n = carve([128, 64, 128]) if False else None
        hnat = carve([16, 64 * 128]) if False else None
        stg = xang[0]
        pth = bank(3, 0)
        for gq_ in range(16 if full else 0):
            P.dma("sp", lambda e, gq_=gq_: e.dma_start(out=stg[0:16, :].rearrange("p (g c) -> p g c", g=4)[:, :, 0:64], in_=s5r_in[:, 4 * gq_:4 * gq_ + 4, :]),
                  writes=["xang0"], semkey="stg", group=f"stg{gq_}")
            P.dma("sp", lambda e, gq_=gq_: e.dma_start(out=stg[0:16, :].rearrange("p (g c) -> p g c", g=4)[:, :, 64:128], in_=s5i_in[:, 4 * gq_:4 * gq_ + 4, :]),
                  writes=["xang0"], semkey="stg", group=f"stg{gq_}")
            for gg in range(4):
                P.op("pe", lambda e, gg=gg: e.transpose(pth[:, gg * 16:(gg + 1) * 16], stg[0:16, gg * 128:(gg + 1) * 128], ident[0:16, 0:16]),
                     reads=["xang0", "ident"], writes=[bkey(3, 0)])
            V(lambda e, gq_=gq_: e.copy(out=H0T[:, 4 * gq_:4 * gq_ + 4, :], in_=pth[:, 0:64].rearrange("p (g b) -> p g b", g=4)), [bkey(3, 0)], ["H0T"], "act")

        units = []
        yi = 0
        for F in range(KC):
            for blk in range(4):
                ypk = YP[yi % 2]
                yi += 1
                for gl in range(8):
                    units.append(dict(F=F, gl=gl, t0=blk * 512, nt=512, blk=blk, sample=False, ypk=ypk, last=(blk == 3), fin=(gl == 7)))
            if full:
                ypk = YP[yi % 2]
                yi += 1
                for gl in range(8):
                    units.append(dict(F=F, gl=gl, t0=TOK_P, nt=128, blk=0, sample=True, ypk=ypk, last=False, fin=(gl == 7)))
        def cap(fn_, *a_):
            P.capture = []
            r_ = fn_(*a_)
            ops_ = P.capture
            P.capture = None
            return ops_, r_

        def emit(ops_):
            for o_ in ops_:
                P.op(*o_)

        u0 = units[0]
        opsT, ctx = cap(s5_T, u0["F"], u0["gl"], u0["t0"], u0["nt"], u0["blk"], u0["sample"])
        emit(opsT)
        for n, u in enumerate(units):
            opsS, _ = cap(s5_S, ctx, u["ypk"], u["last"])
            if n + 1 < len(units):
                v = units[n + 1]
                opsT, nxt = cap(s5_T, v["F"], v["gl"], v["t0"], v["nt"], v["blk"], v["sample"])
            else:
                opsT, nxt = [], None
            pre = opsT[:4]
            rest = opsT[4:]
            k_ = 0
            while k_ < len(rest) and rest[k_][0] == "dve":
                k_ += 1
            t_dve, t_tail = rest[:k_], rest[k_:]
            seq = list(pre)
            for j_ in range(3):
                seq.append(opsS[j_])
                if j_ < len(t_dve):
                    seq.append(t_dve[j_])
            seq += t_dve[3:] + t_tail + opsS[3:]
            assert len(seq) == len(opsT) + len(opsS)
            emit(seq)
            if full and u["fin"]:
                y_finish(u["F"], u["t0"], u["nt"], u["ypk"])
            ctx = nxt

        pfin = bank(3, 1)[:, 0:64]
        P.op("pe", lambda e: e.matmul(pfin, lhsT=rott, rhs=Blast, start=True, stop=True), reads=["rott", "Blast"], writes=[bkey(3, 1)])
        V(lambda e: e.tensor_tensor(out=Alast, in0=Alast, in1=pfin, op=ALU.add), ["Alast", bkey(3, 1)], ["Alast"])
        if not full:
            P.dma("sp", lambda e: e.dma_start(out=floc_out, in_=Alast), reads=["Alast"], semkey="floc", final=True)
            return None
        ptp = bank(3, 0)[0:64, 0:128]
        P.op("pe", lambda e: e.transpose(ptp, Alast, ident[:]), reads=["Alast", "ident"], writes=[bkey(3, 0)])
        V(lambda e: e.copy(out=ysb[0:64, 0:128], in_=ptp), [bkey(3, 0)], ["ysb"], "act")
        P.dma("sp", lambda e: e.dma_start(out=ps5r_out, in_=ysb[0:64, 0:64]), reads=["ysb"], semkey="ps5", final=True)
        P.dma("sp", lambda e: e.dma_start(out=ps5i_out, in_=ysb[0:64, 64:128]), reads=["ysb"], semkey="ps5", final=True)
        for hf in range(2):
            pf2 = bank(3, 1)
            P.op("pe", lambda e, hf=hf: e.matmul(pf2, lhsT=rott, rhs=Bsl[:, 32 * hf:32 * hf + 32, :].rearrange("p g b -> p (g b)"), start=True, stop=True),
                 reads=["rott", "Bsl"], writes=[bkey(3, 1)])
            V(lambda e, hf=hf: e.tensor_tensor(out=Asl[:, 32 * hf:32 * hf + 32, :].rearrange("p g b -> p (g b)"), in0=Asl[:, 32 * hf:32 * hf + 32, :].rearrange("p g b -> p (g b)"),
                                               in1=pf2, op=ALU.add), ["Asl", bkey(3, 1)], ["Asl"])
        for gq_ in range(16):
            pto = bank(3, 0)[0:16, :]
            for gg in range(4):
                P.op("pe", lambda e, gq_=gq_, gg=gg: e.transpose(pto[:, gg * 128:(gg + 1) * 128], Asl[:, 4 * gq_ + gg, :], ident[:]),
                     reads=["Asl", "ident"], writes=[bkey(3, 0)])
            V(lambda e: e.copy(out=stg[0:16, :], in_=pto), [bkey(3, 0)], ["xang0"], "act")
            P.dma("sp", lambda e, gq_=gq_: e.dma_start(out=ss5r_out[:, 4 * gq_:4 * gq_ + 4, :], in_=stg[0:16, :].rearrange("p (g c) -> p g c", g=4)[:, :, 0:64]),
                  reads=["xang0"], semkey="ss5o", final=True)
            P.dma("sp", lambda e, gq_=gq_: e.dma_start(out=ss5i_out[:, 4 * gq_:4 * gq_ + 4, :], in_=stg[0:16, :].rearrange("p (g c) -> p g c", g=4)[:, :, 64:128]),
                  reads=["xang0"], semkey="ss5o", final=True)

        P.barrier()
        cur[0] = mark_ln + NTOK * KC // 2
        glua = carve([128, KC, D], BF16)
        glub = carve([128, KC, D], BF16)
        sgb = [carve([128, 512]) for _ in range(2)]
        prd = [carve([128, 512]) for _ in range(2)]
        gt3 = carve([128, 128])
        P.dma("pool", lambda e: e.dma_start(out=glua, in_=glua_d.rearrange("(k p) n -> p k n", p=128)), writes=["glua"], semkey="glua")
        P.dma("pool", lambda e: e.dma_start(out=glub, in_=glub_d.rearrange("(k p) n -> p k n", p=128)), writes=["glub"], semkey="glub")
        GA = [(bank(0, 0), bkey(0, 0), bank(0, 1), bkey(0, 1)), (bank(1, 0), bkey(1, 0), bank(1, 1), bkey(1, 1))]
        gi = 0
        for (t0, nt, smp) in [(b * 512, 512, False) for b in range(4)] + [(TOK_P, 128, True)]:
            for m in range(KC):
                pa, pak, pb, pbk = GA[gi % 2]
                w_ = gi % 2
                gi += 1
                for k in range(KC):
                    P.op("pe", lambda e, k=k, m=m, pa=pa, t0=t0, nt=nt: e.matmul(pa[:, :nt], lhsT=glua[:, k, m * 128:(m + 1) * 128], rhs=hT_all[:, k, t0:t0 + nt],
                                                                                  start=(k == 0), stop=(k == KC - 1)), reads=["glua", "hT_all"], writes=[pak])
                for k in range(KC):
                    P.op("pe", lambda e, k=k, m=m, pb=pb, t0=t0, nt=nt: e.matmul(pb[:, :nt], lhsT=glub[:, k, m * 128:(m + 1) * 128], rhs=hT_all[:, k, t0:t0 + nt],
                                                                                  start=(k == 0), stop=(k == KC - 1)), reads=["glub", "hT_all"], writes=[pbk])
                V(lambda e, pb=pb, w_=w_, nt=nt: e.activation(out=sgb[w_][:, :nt], in_=pb[:, :nt], func=AF.Sigmoid), [pbk], [f"sgb{w_}"], "act")
                V(lambda e, pa=pa, w_=w_, nt=nt: e.tensor_tensor(out=prd[w_][:, :nt], in0=pa[:, :nt], in1=sgb[w_][:, :nt], op=ALU.mult), [pak, f"sgb{w_}"], [f"prd{w_}"])
                xs = xT[:, m, t0:t0 + nt]
                xkey = ("xT", t0 // 512)
                if not smp:
                    V(lambda e, m=m, w_=w_, xs=xs, nt=nt: e.scalar_tensor_tensor(out=xs, in0=prd[w_][:, :nt], scalar=modT[:, 16 + m, 0:1], in1=xs, op0=ALU.mult, op1=ALU.add),
                      [f"prd{w_}", "modT", xkey], [xkey])
                else:
                    gv = gt3.rearrange("p (b t) -> p b t", t=8)
                    V(lambda e, m=m, w_=w_, gv=gv: e.tensor_tensor(out=gv, in0=prd[w_][:, :128].rearrange("p (b t) -> p b t", t=8),
                                                                  in1=modT[:, 16 + m, 1:17].unsqueeze(2).to_broadcast([128, 16, 8]), op=ALU.mult), [f"prd{w_}", "modT"], ["gt3"])
                    V(lambda e, xs=xs: e.tensor_tensor(out=xs, in0=xs, in1=gt3, op=ALU.add), ["gt3", xkey], [xkey])
        ffn(1)
        P.barrier()
        cur[0] = mark_ln
        yst = [carve([128, D]) for _ in range(2)]
        for t in range(17):
            s = t % 2
            for k in range(KC):
                P.op("pe", lambda e, k=k, t=t, s=s: e.transpose(pT[s][:, k, :], xT[:, k, t * 128:(t + 1) * 128], ident[:]),
                     reads=[("xT", t // 4), "ident"], writes=[bkey(s, 0), bkey(s, 1)])
            P.op("act", lambda e, s=s: e.copy(out=yst[s], in_=Q[s][:, :]), reads=[bkey(s, 0), bkey(s, 1)], writes=[f"yst{s}"])
            P.dma("sp", lambda e, t=t, s=s: e.dma_start(out=y_out[t * 128:(t + 1) * 128, :], in_=yst[s]), reads=[f"yst{s}"], semkey=f"yst{s}", final=True)
        stats = P.build()
        nc_allow.__exit__(None, None, None)
        es.close()
        return nc, stats

    if stage in ("C1", "C2"):
        r_ = s5_stage(stage == "C2")
        if r_ is not None:
            return r_
        stats = P.build()
        nc_allow.__exit__(None, None, None)
        es.close()
        return nc, stats

    hTh = carve([128, KC, 128], BF16)
    mark1 = cur[0]
    lst = [carve([128, 4, 128]) for _ in range(2)]
    wret = carve([128, 8, 4])
    xTh = carve([128, KC, 128])
    xst[0] = carve([128, D])
    load_tile(0, xTh, "xTh")
    P.dma("sp", lambda e: e.dma_start(out=wret, in_=wret_d), writes=["wret"], semkey="wret")
    P.op("dve", lambda e: e.memset(S, 0.0), writes=["S"])
    for r in range(8):
        s = r % 2
        P.dma("sp", lambda e, r=r, s=s: e.dma_start(out=lst[s], in_=lall_d[r].rearrange("h d e -> d h e")), writes=[f"lst{s}"], semkey=f"lst{s}")
        P.op("dve", lambda e, r=r, s=s: e.tensor_tensor(out=lst[s], in0=lst[s], in1=wret[:, r, :].unsqueeze(2).to_broadcast([128, 4, 128]), op=ALU.mult),
             reads=[f"lst{s}", "wret"], writes=[f"lst{s}"])
        P.op("dve", lambda e, s=s: e.tensor_tensor(out=S, in0=S, in1=lst[s], op=ALU.add), reads=["S", f"lst{s}"], writes=["S"])
    P.op("act", lambda e: e.copy(out=Sb, in_=S), reads=["S"], writes=["Sb"])
    ln_block(xTh, "xTh", 128, False, 0)
    P.op("pool", lambda e: e.tensor_copy(out=hTh, in_=hT[:, :, 0:128]), reads=["hT"], writes=["hTh"])
    P.barrier()
    cur[0] = mark1

    wkd = carve([128, KC, 2, 128], BF16)
    wvd = carve([128, KC, 2, 128], BF16)
    for kv in range(2):
        for half in range(2):
            P.op("pool", lambda e, kv=kv, half=half: e.tensor_copy(out=wkd[:, :, kv, half * 64:(half + 1) * 64], in_=win[:, :, 512 + 64 * kv:576 + 64 * kv]),
                 reads=["win"], writes=["wkd"])
            P.op("pool", lambda e, kv=kv, half=half: e.tensor_copy(out=wvd[:, :, kv, half * 64:(half + 1) * 64], in_=win[:, :, 640 + 64 * kv:704 + 64 * kv]),
                 reads=["win"], writes=["wvd"])
    wout = carve([128, KC, D], BF16)
    P.dma("pool", lambda e: e.dma_start(out=wout, in_=w_out.rearrange("(k p) n -> p k n", p=128)), writes=["wout"], semkey="wout")
    gq = carve([128, 1])
    gk = carve([128, 1])
    bd_f = carve([128, 128])
    bd = carve([128, 128], BF16)
    dneg = carve([128, 5, 128])
    rmask = carve([128, 4, 128])
    dq = carve([128, 4, 128])
    seqmask = carve([128, 16])
    esink = carve([128, 8])
    retg = carve([128, 4])
    for half in range(2):
        P.dma("sp", lambda e, half=half: e.dma_start(out=gq[half * 64:(half + 1) * 64, :], in_=qgain.rearrange("o d -> d o")), writes=["gq"], semkey="ld_gq", group="ld_gq")
        P.dma("sp", lambda e, half=half: e.dma_start(out=gk[half * 64:(half + 1) * 64, :], in_=kgain.rearrange("o d -> d o")), writes=["gk"], semkey="ld_gk", group="ld_gk")
    P.dma("sp", lambda e: e.dma_start(out=bd_f, in_=bd_d), writes=["bd_f"], semkey="ld_bd_f", group="ld_bd_f")
    P.dma("sp", lambda e: e.dma_start(out=dneg, in_=dneg_d), writes=["dneg"], semkey="ld_dneg", group="ld_dneg")
    P.dma("sp", lambda e: e.dma_start(out=rmask, in_=rmask_d[:, 0]), writes=["rmask"], semkey="ld_rmask", group="ld_rmask")
    P.dma("sp", lambda e: e.dma_start(out=dq, in_=dq_d[:, 0]), writes=["dq"], semkey="ld_dq", group="ld_dq")
    P.dma("sp", lambda e: e.dma_start(out=seqmask, in_=seqmask_d), writes=["seqmask"], semkey="ld_seqmask", group="ld_seqmask")
    P.dma("sp", lambda e: e.dma_start(out=esink, in_=sinks_d.partition_broadcast(128)), writes=["esink"], semkey="ld_esink", group="ld_esink")
    P.dma("sp", lambda e: e.dma_start(out=retg, in_=retg_d.rearrange("o (h e) -> e (o h)", h=4)), writes=["retg"], semkey="ld_retg", group="ld_retg")
    P.op("dve", lambda e: e.tensor_copy(out=bd, in_=bd_f), reads=["bd_f"], writes=["bd"])
    P.op("act", lambda e: e.activation(out=esink, in_=esink, func=AF.Exp), reads=["esink"], writes=["esink"])
    P.op("dve", lambda e: e.tensor_scalar(out=gq, in0=gq, scalar1=0.125, scalar2=None, op0=ALU.mult), reads=["gq"], writes=["gq"])

    qnT = carve([128, 4, BT], BF16)
    kdT = carve([128, 2, 128 + BT], BF16)
    qbT = carve([128, 4, BT], BF16)
    qdT = carve([128, 4, BT], BF16)
    kbT = carve([128, 4, BT], BF16)
    sgT = carve([128, 4, BT], BF16)
    vd_tok = carve([128, 1 + TPB, 256], BF16)
    oT = carve([128, KC, BT], BF16)
    qsq = carve([128, max(BT, 256)], BF16)
    qrs = carve([128, max(BT, 256)])
    sc_sb = carve([128, 2, 2, 2, 128])
    pTt = carve([128, 2, 2, 2, 128], BF16)
    rden = carve([128, 2, 2, 128])
    innT = carve([128, 4, 128], BF16)
    o32 = carve([128, 4, 128])
    obf = carve([128, 4, 128], BF16)
    osq = carve([128, 4, 128], BF16)
    t1 = carve([128, 4, 128])
    t2 = o32
    kv32 = carve([128, 2, 128])
    knT = carve([128, 128])
    gtmp = carve([128, 128])
    cacheT = carve([128, 16, 128], BF16)
    vcache = carve([128, 16, 128], BF16)
    kst = carve([128, 4, 128])
    S0 = [carve([128, 4, 128]) for _ in range(2)]
    S0b = [carve([128, 4, 128], BF16) for _ in range(2)]
    kdm = [carve([128, 512], BF16) for _ in range(2)]
    mark2 = cur[0]
    print("arena words used (mixer):", cur[0], "of", ARENA_W)

    PJ = [bank(0, 0), bank(0, 1)]
    PJK = [bkey(0, 0), bkey(0, 1)]
    p_st = bank(1, 0)
    pj_i = [0]

    def proj_fm(lhs_fn, ntok, evac, src=None):
        s = pj_i[0] % 2
        pj_i[0] += 1
        src = hT if src is None else src
        for k in range(KC):
            P.op("pe", lambda e, k=k, s=s: e.matmul(PJ[s][:, :ntok], lhsT=lhs_fn(k), rhs=src[:, k, :ntok], start=(k == 0), stop=(k == KC - 1)),
                 reads=["hT", "hTh", "win", "wkd"], writes=[PJK[s]])
        evac(PJ[s][:, :ntok], PJK[s])

    def qknorm(psum, pkey, ntok, gain, gkey, out_ap, out_key):
        P.op("act", lambda e: e.activation(out=qsq[:, :ntok], in_=psum, func=AF.Square), reads=[pkey], writes=["qsq"])
        P.op("pe", lambda e: e.matmul(p_st[:, :ntok], lhsT=bd, rhs=qsq[:, :ntok], start=True, stop=True), reads=["bd", "qsq"], writes=[bkey(1, 0)])
        P.op("act", lambda e: e.activation(out=qrs[:, :ntok], in_=p_st[:, :ntok], func=AF.Sqrt, bias=epsb[:], scale=1.0), reads=[bkey(1, 0), "epsb"], writes=["qrs"])
        P.op("dve", lambda e: e.reciprocal(out=qrs[:, :ntok], in_=qrs[:, :ntok]), reads=["qrs"], writes=["qrs"])
        P.op("dve", lambda e: e.scalar_tensor_tensor(out=out_ap, in0=psum, scalar=gain[:, 0:1], in1=qrs[:, :ntok], op0=ALU.mult, op1=ALU.mult),
             reads=[pkey, "qrs", gkey], writes=[out_key])

    def project_block(ntok, grp):
        for j in range(4):
            proj_fm(lambda k, j=j: win[:, k, j * 128:(j + 1) * 128], ntok,
                    lambda ps_, key, j=j: qknorm(ps_, key, ntok, gq, "gq", qnT[:, j, :ntok], "qnT"))
        if KSUB < 2:
            return
        for kv in range(2):
            proj_fm(lambda k, kv=kv: wkd[:, k, kv, :], ntok,
                    lambda ps_, key, kv=kv: qknorm(ps_, key, ntok, gk, "gk", kdT[:, kv, 128:128 + ntok], "kdT"))
        if KSUB < 3:
            return
        for h in range(4):
            def ev_q(ps_, key, h=h):
                P.op("act", lambda e: e.copy(out=qbT[:, h, :ntok], in_=ps_), reads=[key], writes=["qbT"])
                P.op("dve", lambda e: e.tensor_tensor(out=qdT[:, h, :ntok].rearrange("p (t i) -> p t i", i=128), in0=ps_.rearrange("p (t i) -> p t i", i=128),
                                                      in1=dq[:, h, :].unsqueeze(1).to_broadcast([128, ntok // 128, 128]), op=ALU.mult),
                     reads=[key, "dq"], writes=["qdT"])
            proj_fm(lambda k, h=h: win[:, k, 768 + h * 128:768 + (h + 1) * 128], ntok, ev_q)
        if KSUB < 4:
            return
        for h in range(4):
            proj_fm(lambda k, h=h: win[:, k, 1280 + h * 128:1280 + (h + 1) * 128], ntok,
                    lambda ps_, key, h=h: P.op("act", lambda e: e.mul(out=kbT[:, h, :ntok], in_=ps_, mul=128.0 ** -0.5), reads=[key], writes=["kbT"]))
        if KSUB < 5:
            return
        for h in range(4):
            proj_fm(lambda k, h=h: win[:, k, 2304 + h * 128:2304 + (h + 1) * 128], ntok,
                    lambda ps_, key, h=h: P.op("act", lambda e: e.activation(out=sgT[:, h, :ntok], in_=ps_, func=AF.Silu), reads=[key], writes=["sgT"]))

    p_vd = bank(1, 1)[:, 0:256]

    def tok_vd(tc0, slot, src=None):
        src = hT if src is None else src
        for k in range(KC):
            P.op("pe", lambda e, k=k: e.matmul(p_vd, lhsT=src[:, k, tc0:tc0 + 128], rhs=wvd[:, k, :, :].rearrange("p a b -> p (a b)"), start=(k == 0), stop=(k == KC - 1)),
                 reads=["hT", "hTh", "wvd"], writes=[bkey(1, 1)])
        P.op("act", lambda e: e.copy(out=vd_tok[:, slot, :], in_=p_vd), reads=[bkey(1, 1)], writes=["vd_tok"])

    SLOPE = [2.0 ** (-(h + 1)) for h in range(8)]
    p_sc = Q[3][:, :].rearrange("p (a c t q) -> p a c t q", a=2, c=2, t=2)
    p_num = bank(2, 0).rearrange("p (a c q) -> p a c q", a=2, c=2)
    p_den = bank(2, 1).rearrange("p (a c q) -> p a c q", a=2, c=2)
    SCK = [bkey(3, 0), bkey(3, 1)]

    def attn_softmax(kv, dn_own, dn_prev):
        for half in range(2):
            for c in range(2):
                h0 = 4 * kv + 2 * c + half
                P.op("dve", lambda e, half=half, c=c, h0=h0: e.scalar_tensor_tensor(out=sc_sb[:, half, c, 0, :], in0=dneg[:, dn_prev, :], scalar=SLOPE[h0],
                                                                                  in1=p_sc[:, half, c, 0, :], op0=ALU.mult, op1=ALU.add),
                     reads=["dneg"] + SCK, writes=["sc_sb"])
                P.op("dve", lambda e, half=half, c=c, h0=h0: e.scalar_tensor_tensor(out=sc_sb[:, half, c, 1, :], in0=dneg[:, dn_own, :], scalar=SLOPE[h0],
                                                                                  in1=p_sc[:, half, c, 1, :], op0=ALU.mult, op1=ALU.add),
                     reads=["dneg"] + SCK, writes=["sc_sb"])
        P.op("act", lambda e: e.activation(out=pTt, in_=sc_sb, func=AF.Exp), reads=["sc_sb"], writes=["pTt"])

    def attn_finish(kv, c0):
        for part in range(2):
            P.op("pe", lambda e, part=part: e.matmul(p_den, lhsT=ones_b[:], rhs=pTt[:, :, :, part, :], start=(part == 0), stop=(part == 1)),
                 reads=["ones_b", "pTt"], writes=[bkey(2, 1)])
        es_v = esink[:, 4 * kv:4 * kv + 4].rearrange("p (c a) -> p a c", a=2)
        P.op("dve", lambda e, es_v=es_v: e.tensor_tensor(out=rden, in0=p_den, in1=es_v.unsqueeze(3).to_broadcast([128, 2, 2, 128]), op=ALU.add),
             reads=[bkey(2, 1), "esink"], writes=["rden"])
        P.op("dve", lambda e: e.reciprocal(out=rden, in_=rden), reads=["rden"], writes=["rden"])
        for half in range(2):
            sl = slice(half * 64, half * 64 + 64)
            P.op("dve", lambda e, half=half, sl=sl, kv=kv: e.tensor_tensor(out=oT[sl, 2 * kv:2 * kv + 2, c0:c0 + 128], in0=p_num[sl, half, :, :],
                                                                         in1=rden[sl, half, :, :], op=ALU.mult),
                 reads=[bkey(2, 0), "rden"], writes=["oT"])

    def attention_tile(i, dn_own, dn_prev):
        c0 = i * 128
        for kv in range(2):
            for half in range(2):
                sl = slice(half * 64, half * 64 + 64)
                P.op("pe", lambda e, kv=kv, half=half, sl=sl: e.matmul(p_sc[:, half, :, 1, :], lhsT=kdT[sl, kv, 128 + c0:256 + c0],
                                                                       rhs=qnT[sl, 2 * kv:2 * kv + 2, c0:c0 + 128], start=True, stop=True),
                     reads=["kdT", "qnT"], writes=SCK)
                P.op("pe", lambda e, kv=kv, half=half, sl=sl: e.matmul(p_sc[:, half, :, 0, :], lhsT=kdT[sl, kv, c0:128 + c0],
                                                                       rhs=qnT[sl, 2 * kv:2 * kv + 2, c0:c0 + 128], start=True, stop=True),
                     reads=["kdT", "qnT"], writes=SCK)
            attn_softmax(kv, dn_own, dn_prev)
            parts = [(0, vd_tok[:, i, kv * 128:(kv + 1) * 128]), (1, vd_tok[:, i + 1, kv * 128:(kv + 1) * 128])]
            for n_, (part, lh) in enumerate(parts):
                P.op("pe", lambda e, part=part, lh=lh, n_=n_: e.matmul(p_num, lhsT=lh, rhs=pTt[:, :, :, part, :], start=(n_ == 0), stop=(n_ == 1)),
                     reads=["vd_tok", "pTt"], writes=[bkey(2, 0)])
            attn_finish(kv, c0)

    def attention_sample():
        p_ct = bank(1, 1)[:, 0:128]
        for kv in range(2):
            for half in range(2):
                P.dma("pool", lambda e, kv=kv, half=half: e.dma_start(out=vcache[:, :, half * 64:(half + 1) * 64],
                                                                      in_=cache_v[:, :, kv * 64:(kv + 1) * 64].rearrange("b w d -> w b d")),
                      writes=["vcache"], semkey="vcache", group=f"vc{kv}")
            for g4 in range(4):
                for half in range(2):
                    P.dma("sp", lambda e, kv=kv, half=half, g4=g4: e.dma_start(out=kst[:, :, half * 64:(half + 1) * 64],
                                                                              in_=cache_k[4 * g4:4 * g4 + 4, :, kv * 64:(kv + 1) * 64].rearrange("b w d -> w b d")),
                          writes=["kst"], semkey="kst", group=f"kst{kv}_{g4}")
                for bb in range(4):
                    b = 4 * g4 + bb
                    P.op("pe", lambda e, bb=bb: e.transpose(p_ct, kst[:, bb, :], ident[:]), reads=["kst", "ident"], writes=[bkey(1, 1)])
                    P.op("act", lambda e, b=b: e.copy(out=cacheT[:, b, :], in_=p_ct), reads=[bkey(1, 1)], writes=["cacheT"])
            for half in range(2):
                sl = slice(half * 64, half * 64 + 64)
                P.op("pe", lambda e, kv=kv, half=half, sl=sl: e.matmul(p_sc[:, half, :, 1, :], lhsT=kdT[sl, kv, 128:256],
                                                                       rhs=qnT[sl, 2 * kv:2 * kv + 2, 0:128], start=True, stop=True),
                     reads=["kdT", "qnT"], writes=SCK)
                for b in range(16):
                    P.op("pe", lambda e, kv=kv, half=half, sl=sl, b=b: e.matmul(p_sc[:, half, :, 0, 8 * b:8 * b + 8], lhsT=cacheT[sl, b, :],
                                                                                 rhs=qnT[sl, 2 * kv:2 * kv + 2, 8 * b:8 * b + 8], start=True, stop=True),
                         reads=["cacheT", "qnT"], writes=SCK)
            attn_softmax(kv, 3, 4)
            P.op("pe", lambda e, kv=kv: e.matmul(p_num, lhsT=vd_tok[:, 1, kv * 128:(kv + 1) * 128], rhs=pTt[:, :, :, 1, :], start=True, stop=False),
                 reads=["vd_tok", "pTt"], writes=[bkey(2, 0)])
            for b in range(16):
                P.op("pe", lambda e, b=b: e.matmul(p_num[:, :, :, 8 * b:8 * b + 8], lhsT=vcache[:, b, :], rhs=pTt[:, :, :, 0, 8 * b:8 * b + 8],
                                                   start=False, stop=(b == 15)),
                     reads=["vcache", "pTt"], writes=[bkey(2, 0)])
            attn_finish(kv, 0)

    p_in = bank(1, 1).rearrange("p (h i) -> p h i", h=4)
    p_o = bank(0, 0).rearrange("p (h i) -> p h i", h=4)
    p_mu = bank(0, 1).rearrange("p (h i) -> p h i", h=4)
    p_e2 = bank(1, 0).rearrange("p (h i) -> p h i", h=4)

    def ret_norm(c0):
        P.op("dve", lambda e: e.tensor_copy(out=obf, in_=o32), reads=["o32"], writes=["obf"])
        P.op("act", lambda e: e.activation(out=osq, in_=o32, func=AF.Square), reads=["o32"], writes=["osq"])
        for h in range(4):
            P.op("pe", lambda e, h=h: e.matmul(p_mu[:, h, :], lhsT=ones_g[:], rhs=obf[:, h, :], start=True, stop=True), reads=["ones_g", "obf"], writes=[bkey(0, 1)])
        for h in range(4):
            P.op("pe", lambda e, h=h: e.matmul(p_e2[:, h, :], lhsT=ones_g[:], rhs=osq[:, h, :], start=True, stop=True), reads=["ones_g", "osq"], writes=[bkey(1, 0)])
        P.op("act", lambda e: e.activation(out=t1, in_=p_mu, func=AF.Square), reads=[bkey(0, 1)], writes=["t1"])
        P.op("dve", lambda e: e.tensor_tensor(out=t1, in0=p_e2, in1=t1, op=ALU.subtract), reads=[bkey(1, 0), "t1"], writes=["t1"])
        P.op("dve", lambda e: e.tensor_scalar(out=t1, in0=t1, scalar1=0.0, scalar2=None, op0=ALU.max), reads=["t1"], writes=["t1"])
        P.op("act", lambda e: e.activation(out=t1, in_=t1, func=AF.Sqrt, bias=epsb[:], scale=1.0), reads=["t1", "epsb"], writes=["t1"])
        P.op("dve", lambda e: e.reciprocal(out=t1, in_=t1), reads=["t1"], writes=["t1"])
        P.op("dve", lambda e: e.tensor_tensor(out=o32, in0=o32, in1=p_mu, op=ALU.subtract), reads=["o32", bkey(0, 1)], writes=["o32"])
        P.op("dve", lambda e: e.tensor_tensor(out=o32, in0=o32, in1=t1, op=ALU.mult), reads=["o32", "t1"], writes=["o32"])
        P.op("dve", lambda e: e.tensor_tensor(out=o32, in0=o32, in1=retg.unsqueeze(2).to_broadcast([128, 4, 128]), op=ALU.mult), reads=["o32", "retg"], writes=["o32"])
        P.op("dve", lambda e: e.tensor_tensor(out=oT[:, 4:8, c0:c0 + 128], in0=o32, in1=sgT[:, :, c0:c0 + 128], op=ALU.mult), reads=["o32", "sgT"], writes=["oT"])

    def ret_inner(c0):
        for h in range(4):
            P.op("pe", lambda e, h=h: e.matmul(p_in[:, h, :], lhsT=kbT[:, h, c0:c0 + 128], rhs=qbT[:, h, c0:c0 + 128], start=True, stop=True),
                 reads=["kbT", "qbT"], writes=[bkey(1, 1)])
        P.op("dve", lambda e: e.tensor_tensor(out=innT, in0=p_in, in1=rmask, op=ALU.mult), reads=[bkey(1, 1), "rmask"], writes=["innT"])

    def retention_tile(i, grp):
        c0 = i * 128
        ret_inner(c0)
        for h in range(4):
            P.op("pe", lambda e, h=h: e.matmul(p_o[:, h, :], lhsT=vb_tok[:, h * 128:(h + 1) * 128], rhs=innT[:, h, :], start=True, stop=False),
                 reads=["vb_tok", "innT"], writes=[bkey(0, 0)])
            P.op("pe", lambda e, h=h: e.matmul(p_o[:, h, :], lhsT=Sb[:, h, :], rhs=qdT[:, h, c0:c0 + 128], start=False, stop=True),
                 reads=["Sb", "qdT"], writes=[bkey(0, 0)])
        P.op("act", lambda e: e.copy(out=o32, in_=p_o), reads=[bkey(0, 0)], writes=["o32"])
        ret_norm(c0)

    def retention_sample():
        ret_inner(0)
        poh = [bank(0, 0)[:, 0:128], bank(0, 1)[:, 0:128], bank(1, 0)[:, 0:128], bank(3, 0)[:, 0:128]]
        pok = [bkey(0, 0), bkey(0, 1), bkey(1, 0), bkey(3, 0)]
        for h in range(4):
            P.op("pe", lambda e, h=h: e.matmul(poh[h], lhsT=vb_tok[:, h * 128:(h + 1) * 128], rhs=innT[:, h, :], start=True, stop=False),
                 reads=["vb_tok", "innT"], writes=[pok[h]])
        for b in range(16):
            s_ = b % 2
            P.dma("sp", lambda e, b=b, s_=s_: e.dma_start(out=S0[s_], in_=sret_in[b].rearrange("h d e -> d h e")), writes=[f"S0{s_}"], semkey=f"S0{s_}")
            P.op("pool", lambda e, s_=s_: e.tensor_copy(out=S0b[s_], in_=S0[s_]), reads=[f"S0{s_}"], writes=[f"S0b{s_}"])
            for h in range(4):
                P.op("pe", lambda e, h=h, b=b, s_=s_: e.matmul(poh[h][:, 8 * b:8 * b + 8], lhsT=S0b[s_][:, h, :], rhs=qdT[:, h, 8 * b:8 * b + 8],
                                                              start=False, stop=(b == 15)),
                     reads=[f"S0b{s_}", "qdT"], writes=[pok[h]])
            P.op("dve", lambda e, b=b, s_=s_: e.tensor_scalar(out=kdm[s_], in0=kd_tok, scalar1=seqmask[:, b:b + 1], scalar2=None, op0=ALU.mult),
                 reads=["kd_tok", "seqmask"], writes=[f"kdm{s_}"])
            for h in range(4):
                P.op("pe", lambda e, h=h, s_=s_: e.matmul(p_su[:, h, :], lhsT=kdm[s_][:, h * 128:(h + 1) * 128], rhs=vb_tok[:, h * 128:(h + 1) * 128], start=True, stop=True),
                     reads=[f"kdm{s_}", "vb_tok"], writes=[bkey(2, 1)])
            P.op("pool", lambda e, s_=s_: e.tensor_tensor(out=S0[s_], in0=S0[s_], in1=dc[:, 1, :].unsqueeze(2).to_broadcast([128, 4, 128]), op=ALU.mult),
                 reads=[f"S0{s_}", f"S0b{s_}", "dc"], writes=[f"S0{s_}"])
            P.op("dve", lambda e, s_=s_: e.tensor_tensor(out=S0[s_], in0=S0[s_], in1=p_su, op=ALU.add), reads=[f"S0{s_}", bkey(2, 1)], writes=[f"S0{s_}"])
            P.dma("sp", lambda e, b=b, s_=s_: e.dma_start(out=sret_out[b].rearrange("h d e -> d h e"), in_=S0[s_]), reads=[f"S0{s_}"], semkey=f"S0o{s_}", final=True)
        for h in range(4):
            P.op("act", lambda e, h=h: e.copy(out=o32[:, h, :], in_=poh[h]), reads=[pok[h]], writes=["o32"])
        ret_norm(0)

    PO = [bank(0, 0), bank(0, 1)]
    POK = [bkey(0, 0), bkey(0, 1)]

    def out_proj(blk_c0, ntok, sample, wfn, nk, rhs_fn, gate_row, rkeys):
        for m in range(KC):
            s = m % 2
            for k in range(nk):
                P.op("pe", lambda e, m=m, k=k, s=s: e.matmul(PO[s][:, :ntok], lhsT=wfn(k, m), rhs=rhs_fn(k), start=(k == 0), stop=(k == nk - 1)),
                     reads=rkeys, writes=[POK[s]])
            xs = xT[:, m, blk_c0:blk_c0 + ntok]
            xkey = ("xT", blk_c0 // 512)
            if not sample:
                P.op("dve", lambda e, m=m, s=s, xs=xs: e.scalar_tensor_tensor(out=xs, in0=PO[s][:, :ntok], scalar=modT[:, gate_row + m, 0:1], in1=xs,
                                                                             op0=ALU.mult, op1=ALU.add),
                     reads=[POK[s], "modT", xkey], writes=[xkey])
            else:
                gv = gtmp[:, :128].rearrange("p (b t) -> p b t", t=8)
                P.op("dve", lambda e, m=m, s=s, gv=gv: e.tensor_tensor(out=gv, in0=PO[s][:, :128].rearrange("p (b t) -> p b t", t=8),
                                                                      in1=modT[:, gate_row + m, 1:17].unsqueeze(2).to_broadcast([128, 16, 8]), op=ALU.mult),
                     reads=[POK[s], "modT"], writes=["gtmp"])
                P.op("dve", lambda e, xs=xs: e.tensor_tensor(out=xs, in0=xs, in1=gtmp[:, :128], op=ALU.add), reads=["gtmp", xkey], writes=[xkey])

    p_tr = bank(1, 1)[:, 256:384]
    p_v32 = bank(1, 1)[:, 384:512]

    def window_kv(tc0, kout, vout):
        pk2 = bank(1, 0)[:, 0:256].rearrange("p (a n) -> p a n", a=2)
        pst2 = bank(1, 0)[:, 256:512].rearrange("p (a n) -> p a n", a=2)
        for kv in range(2):
            for k in range(KC):
                P.op("pe", lambda e, kv=kv, k=k: e.matmul(pk2[:, kv, :], lhsT=wkd[:, k, kv, :], rhs=hT[:, k, tc0:tc0 + 128], start=(k == 0), stop=(k == KC - 1)),
                     reads=["wkd", "hT"], writes=[bkey(1, 0)])
        P.op("act", lambda e: e.activation(out=qsq[:, 0:256].rearrange("p (a n) -> p a n", a=2), in_=pk2, func=AF.Square), reads=[bkey(1, 0)], writes=["qsq"])
        for kv in range(2):
            P.op("pe", lambda e, kv=kv: e.matmul(pst2[:, kv, :], lhsT=bd, rhs=qsq[:, kv * 128:(kv + 1) * 128], start=True, stop=True), reads=["bd", "qsq"], writes=[bkey(1, 0)])
        P.op("act", lambda e: e.activation(out=qrs[:, 0:256].rearrange("p (a n) -> p a n", a=2), in_=pst2, func=AF.Sqrt, bias=epsb[:], scale=1.0),
             reads=[bkey(1, 0), "epsb"], writes=["qrs"])
        P.op("dve", lambda e: e.reciprocal(out=qrs[:, 0:256], in_=qrs[:, 0:256]), reads=["qrs"], writes=["qrs"])
        for kv in range(2):
            sl = slice(kv * 64, (kv + 1) * 64)
            P.op("dve", lambda e, kv=kv, sl=sl: e.scalar_tensor_tensor(out=knT[sl, :], in0=pk2[sl, kv, :], scalar=gk[sl, 0:1], in1=qrs[sl, kv * 128:(kv + 1) * 128],
                                                                       op0=ALU.mult, op1=ALU.mult),
                 reads=[bkey(1, 0), "gk", "qrs"], writes=["knT"])
        P.op("pe", lambda e: e.transpose(p_tr, knT, ident[:]), reads=["knT", "ident"], writes=[bkey(1, 1)])
        P.op("act", lambda e: e.copy(out=kv32[:, 0, :], in_=p_tr), reads=[bkey(1, 1)], writes=["kv32"])
        for k in range(KC):
            P.op("pe", lambda e, k=k: e.matmul(p_v32, lhsT=hT[:, k, tc0:tc0 + 128], rhs=win[:, k, 640:768], start=(k == 0), stop=(k == KC - 1)),
                 reads=["win", "hT"], writes=[bkey(1, 1)])
        P.op("dve", lambda e: e.tensor_copy(out=kv32[:, 1, :], in_=p_v32), reads=[bkey(1, 1)], writes=["kv32"])
        P.dma("sp", lambda e: e.dma_start(out=kout, in_=kv32[:, 0, :]), reads=["kv32"], semkey="kvo", final=True)
        P.dma("sp", lambda e: e.dma_start(out=vout, in_=kv32[:, 1, :]), reads=["kv32"], semkey="kvo", final=True)

    if KSTOP >= 2:
        for kv in range(2):
            proj_fm(lambda k, kv=kv: wkd[:, k, kv, :], 128,
                    lambda ps_, key, kv=kv: qknorm(ps_, key, 128, gk, "gk", kdT[:, kv, 0:128], "kdT"), src=hTh)
        tok_vd(0, 0, src=hTh)

    NB = TOK_P // BT
    for b in range(NB if KSTOP >= 3 else 0):
        ln_block(xT[:, :, b * BT:(b + 1) * BT], ("xT", (b * BT) // 512), BT, False, 0)
        project_block(BT, 0)
        if KSTOP < 4:
            continue
        for i in range(TPB):
            tok_vd(i * 128, i + 1)
        for i in range(TPB):
            attention_tile(i, 0, 2 if (b == 0 and i == 0) else 1)
        if KSTOP < 5:
            continue
        for i in range(TPB):
            tok_kv(i * 128, 0)
            retention_tile(i, 0)
            state_update()
            P.op("act", lambda e: e.copy(out=Sb, in_=S), reads=["S"], writes=["Sb"])
        if b == NB - 1:
            window_kv(BT - 128, pk_out, pv_out)
        if KSTOP < 6:
            continue
        out_proj(b * BT, BT, False, lambda k, m: wout[:, k, m * 128:(m + 1) * 128], KC, lambda k: oT[:, k, :BT], 16, ["wout", "oT"])
        P.op("pool", lambda e: e.tensor_copy(out=kdT[:, :, 0:128], in_=kdT[:, :, BT:BT + 128]), reads=["kdT"], writes=["kdT"])
        P.op("pool", lambda e: e.tensor_copy(out=vd_tok[:, 0, :], in_=vd_tok[:, TPB, :]), reads=["vd_tok"], writes=["vd_tok"])
    P.dma("sp", lambda e: e.dma_start(out=pret_out.rearrange("h d e -> d h e"), in_=S), reads=["S"], semkey="pret", final=True)

    if KSTOP >= 7:
        P.dma("sp", lambda e: e.dma_start(out=rmask, in_=rmask_d[:, 1]), writes=["rmask"], semkey="ld_rmask", group="ld_rmask")
        P.dma("sp", lambda e: e.dma_start(out=dq, in_=dq_d[:, 1]), writes=["dq"], semkey="ld_dq", group="ld_dq")
        ln_block(xT[:, :, TOK_P:TOK_P + 128], ("xT", 4), 128, True, 0)
        project_block(128, 1)
        tok_vd(0, 1)
        tok_kv(0, 1)
        window_kv(0, sk_out[:, 120:128, :], sv_out[:, 120:128, :])
        P.dma("sp", lambda e: e.dma_start(out=sk_out[:, 0:120, :], in_=cache_k[:, 8:128, :]), semkey="cko", final=True)
        P.dma("sp", lambda e: e.dma_start(out=sv_out[:, 0:120, :], in_=cache_v[:, 8:128, :]), semkey="cko", final=True)
        attention_sample()
        retention_sample()
        out_proj(TOK_P, 128, True, lambda k, m: wout[:, k, m * 128:(m + 1) * 128], KC, lambda k: oT[:, k, :128], 16, ["wout", "oT"])

    if KSTOP >= 8:
        ffn(0)

    P.barrier()
    cur[0] = mark1
    yst = [carve([128, D]) for _ in range(2)]
    for t in range(17):
        s = t % 2
        for k in range(KC):
            P.op("pe", lambda e, k=k, t=t, s=s: e.transpose(pT[s][:, k, :], xT[:, k, t * 128:(t + 1) * 128], ident[:]),
                 reads=[("xT", t // 4), "ident"], writes=[bkey(s, 0), bkey(s, 1)])
        P.op("act", lambda e, s=s: e.copy(out=yst[s], in_=Q[s][:, :]), reads=[bkey(s, 0), bkey(s, 1)], writes=[f"yst{s}"])
        P.dma("sp", lambda e, t=t, s=s: e.dma_start(out=x1_out[t * 128:(t + 1) * 128, :], in_=yst[s]), reads=[f"yst{s}"], semkey=f"yst{s}", final=True)

    P.barrier()
    P.dma("sp", lambda e: e.dma_start(out=modT[:].rearrange("p m c -> p (m c)"), in_=modin1_d), writes=["modT"], semkey="ld_modT1")
    make_G(1, 0)
    s5_stage(False)

    stats = P.build()
    nc_allow.__exit__(None, None, None)
    es.close()
    return nc, stats


_CACHE = {}


def _tables(c):
    f32 = np.float32
    ident = np.eye(128, dtype=f32)
    bd = np.zeros((128, 128), f32)
    bd[:64, :64] = 1.0 / 64
    bd[64:, 64:] = 1.0 / 64
    j = np.arange(128)[:, None]
    i = np.arange(128)[None, :]
    NEG = -1e30
    dneg = np.full((128, 5, 128), NEG, f32)
    dneg[:, 0, :] = np.where(i >= j, -(i - j), NEG)
    dneg[:, 1, :] = np.where(i < j, -(i - j + 128), NEG)
    dneg[:, 2, :] = dneg[:, 1, :] if c > 0 else NEG
    same = (j // 8) == (i // 8)
    dneg[:, 3, :] = np.where(same & ((i % 8) >= (j % 8)), -((i % 8) - (j % 8)), NEG)
    dneg[:, 4, :] = np.where(j >= (i % 8) + 1, -(128 + (i % 8) - j), NEG)
    g = np.array(GAM, np.float64)
    rmask = np.zeros((128, 2, 4, 128), f32)
    dq = np.zeros((128, 2, 4, 128), f32)
    dk = np.zeros((128, 2, 4), f32)
    dc = np.zeros((128, 2, 4), f32)
    for h in range(4):
        rmask[:, 0, h, :] = np.where(i >= j, g[h] ** np.maximum(i - j, 0), 0.0)
        rmask[:, 1, h, :] = np.where(same & ((i % 8) >= (j % 8)), g[h] ** np.maximum((i % 8) - (j % 8), 0), 0.0)
        dq[:, 0, h, :] = g[h] ** (i + 1.0)
        dq[:, 1, h, :] = g[h] ** ((i % 8) + 1.0)
        dk[:, 0, h] = 128.0 ** -0.5 * g[h] ** (127.0 - j[:, 0])
        dk[:, 1, h] = 128.0 ** -0.5 * g[h] ** (7.0 - (j[:, 0] % 8))
        dc[:, 0, h] = g[h] ** 128.0
        dc[:, 1, h] = g[h] ** 8.0
    seqmask = (j // 8 == np.arange(16)[None, :]).astype(f32)
    wret = np.zeros((128, 8, 4), f32)
    for r in range(8):
        if r < c:
            for h in range(4):
                wret[:, r, h] = g[h] ** (2048.0 * (c - r - 1))
    return dict(ident=ident, bdones=bd, dneg=dneg, rmask=rmask, dq=dq, dk=dk, dc=dc, seqmask=seqmask, wret=wret)


def _get(stage):
    if stage not in _CACHE:
        _CACHE[stage] = build_program(stage)
    return _CACHE[stage]


def kernel(**inp):
    f32 = np.float32
    A = lambda k: np.ascontiguousarray(np.asarray(inp[k], f32))
    xp = A("x_prompt")[0]
    xs = A("x_sample")
    base = dict(norm_mix=A("norm_mix"), norm_ffn=A("norm_ffn"), even_w_in=A("even_w_in")[0])
    per_core = []
    for c in range(NCORES):
        x18 = np.zeros((18 * 128, D), f32)
        if c > 0:
            x18[0:128] = xp[c * TOK_P - 128:c * TOK_P]
        x18[128:128 + TOK_P] = xp[c * TOK_P:(c + 1) * TOK_P]
        x18[128 + TOK_P:] = xs[16 * c:16 * c + 16].reshape(128, D)
        cv = np.concatenate([A("c_prompt"), A("c_sample")[16 * c:16 * c + 16]], axis=0)
        t = _tables(c)
        m = dict(base)
        m.update(x=x18, cvec=np.ascontiguousarray(cv), ident=t["ident"], dk=t["dk"], dc=t["dc"])
        per_core.append((m, t))

    ncA, _ = _get("A")
    mapsA = []
    for m, _ in per_core:
        m = dict(m)
        m.update(ada_w=A("ada_w"), ada_b=A("ada_b"))
        mapsA.append(m)
    resA = run_bass_kernel_spmd(ncA, mapsA, core_ids=list(range(NCORES)))
    mods = [np.ascontiguousarray(resA.results[c]["modout"]) for c in range(NCORES)]
    lall = np.ascontiguousarray(np.stack([resA.results[c]["lret"] for c in range(NCORES)], axis=0))
    _CACHE["lall"] = lall

    ncB, statsB = _get("B")
    _CACHE["statsB"] = statsB
    in_maps = []
    for c in range(NCORES):
        m, t = per_core[c]
        m = dict(m)
        m.pop("cvec", None)
        m["modin"] = np.ascontiguousarray(mods[c][0])
        m["modin1"] = np.ascontiguousarray(mods[c][1])
        m.update(odd_A_re=A("odd_A_re")[0], odd_A_im=A("odd_A_im")[0], odd_log_dt=A("odd_log_dt"), odd_B_re=A("odd_B_re")[0], odd_B_im=A("odd_B_im")[0],
                 odd_C_re=A("odd_C_re")[0], odd_C_im=A("odd_C_im")[0], odd_D=A("odd_D"), **_tables_c())
        m.update(even_q_gain=A("even_q_gain"), even_k_gain=A("even_k_gain"), even_sinks=A("even_sinks"), even_ret_gain=A("even_ret_gain"),
                 even_w_out=A("even_w_out")[0], ffn_wg=A("ffn_wg"), ffn_wu=A("ffn_wu"), ffn_wd=A("ffn_wd"),
                 cache_k=np.ascontiguousarray(A("cache_win_k")[0, 16 * c:16 * c + 16].reshape(16, 128, 128)),
                 cache_v=np.ascontiguousarray(A("cache_win_v")[0, 16 * c:16 * c + 16].reshape(16, 128, 128)),
                 state_ret=np.ascontiguousarray(A("state_ret")[0, 16 * c:16 * c + 16]),
                 bdones=t["bdones"], dneg=t["dneg"], rmask=t["rmask"], dq=t["dq"], seqmask=t["seqmask"], wret=t["wret"], lall=lall)
        in_maps.append(m)
    resB = run_bass_kernel_spmd(ncB, in_maps, core_ids=list(range(NCORES)))
    R = resB.results
    _CACHE["R"] = R
    last = R[NCORES - 1]
    p_k = last["p_k"].reshape(1, 1, 128, 2, 64)
    p_v = last["p_v"].reshape(1, 1, 128, 2, 64)
    p_ret = last["p_ret"].reshape(1, 1, 4, 128, 128)
    s_k = np.concatenate([R[c]["s_k"] for c in range(NCORES)], axis=0).reshape(1, 128, 128, 2, 64)
    s_v = np.concatenate([R[c]["s_v"] for c in range(NCORES)], axis=0).reshape(1, 128, 128, 2, 64)
    s_ret = np.concatenate([R[c]["s_ret"] for c in range(NCORES)], axis=0)[None]

    tC = _tables_c()
    mapsC = []
    for c in range(NCORES):
        m0, t = per_core[c]
        x18 = np.zeros((18 * 128, D), f32)
        x18[128:] = R[c]["x1"]
        wsel = np.zeros((128, 8), f32)
        wsel[:, :c] = 1.0
        m = dict(base)
        m.update(x=x18, modin=np.ascontiguousarray(mods[c][1]), ident=t["ident"], dk=t["dk"], dc=t["dc"],
                 ffn_wg=A("ffn_wg"), ffn_wu=A("ffn_wu"), ffn_wd=A("ffn_wd"),
                 odd_A_re=A("odd_A_re")[0], odd_A_im=A("odd_A_im")[0], odd_log_dt=A("odd_log_dt"),
                 odd_B_re=A("odd_B_re")[0], odd_B_im=A("odd_B_im")[0], odd_C_re=A("odd_C_re")[0], odd_C_im=A("odd_C_im")[0],
                 odd_D=A("odd_D"), odd_glu_a=A("odd_glu_a")[0], odd_glu_b=A("odd_glu_b")[0],
                 s5_re=np.ascontiguousarray(A("state_s5_re")[0, 16 * c:16 * c + 16]), s5_im=np.ascontiguousarray(A("state_s5_im")[0, 16 * c:16 * c + 16]),
                 fall=np.zeros((8, 128, 64), f32), wsel=wsel, **tC)
        mapsC.append(m)
    fall = np.ascontiguousarray(np.stack([R[c]["floc"] for c in range(NCORES)], axis=0))
    _CACHE["fall"] = fall
    for m in mapsC:
        m["fall"] = fall
    ncC2, statsC = _get("C2")
    _CACHE["statsC"] = statsC
    resC2 = run_bass_kernel_spmd(ncC2, mapsC, core_ids=list(range(NCORES)))
    RC = resC2.results
    _CACHE["RC"] = RC
    y_prompt = np.concatenate([RC[c]["y"][:TOK_P] for c in range(NCORES)], axis=0)[None]
    y_sample = np.concatenate([RC[c]["y"][TOK_P:].reshape(16, 8, D) for c in range(NCORES)], axis=0)
    lastC = RC[NCORES - 1]
    p_re = lastC["p_s5r"].reshape(1, 1, 64, 64)
    p_im = lastC["p_s5i"].reshape(1, 1, 64, 64)
    s_re = np.concatenate([RC[c]["s_s5r"] for c in range(NCORES)], axis=0)[None]
    s_im = np.concatenate([RC[c]["s_s5i"] for c in range(NCORES)], axis=0)[None]
    return (y_prompt.astype(f32), y_sample.astype(f32), p_k, p_v, p_ret, p_re, p_im, s_k, s_v, s_ret, s_re, s_im)


def _tables_c():
    f32 = np.float32
    t = np.arange(512)
    tal = np.broadcast_to((t // 64).astype(f32)[None, :], (128, 512)).copy()
    tbp = np.broadcast_to(((t % 64) + 1).astype(f32)[None, :], (128, 512)).copy()
    ts = np.arange(128)
    tbs = np.broadcast_to(((ts % 8) + 1).astype(f32)[None, :], (128, 128)).copy()
    amask = np.broadcast_to(((ts % 8) != 0).astype(f32)[None, :], (128, 128)).copy()
    maskg = (np.arange(128)[:, None] // 16 == np.arange(8)[None, :]).astype(f32)
    rott = np.zeros((128, 128), f32)
    for p in range(64):
        rott[64 + p, p] = -1.0
        rott[p, 64 + p] = 1.0
    return dict(tal=tal, tbp=tbp, tbs=tbs, amask=amask, maskg=maskg, rott=rott)
```

```python
import os
import numpy as np
from contextlib import ExitStack
import concourse.bass as bass
import concourse.mybir as mybir
from concourse.bass_utils import run_bass_kernel_spmd

F32 = mybir.dt.float32
BF16 = mybir.dt.bfloat16
I32 = mybir.dt.int32
ALU = mybir.AluOpType
AF = mybir.ActivationFunctionType

NCORES = 8
D = 1024
KC = 8
NPT = 16
TOK_P = NPT * 128
NTOK = TOK_P + 128
IN_W = 2816
DFF = 2816
FC = 22
BT = 128
TPB = BT // 128
EPS = 1e-6
ENGS = ("pe", "act", "dve", "pool", "sp")
GAM = [1.0 - 2.0 ** (-5.0 - h) for h in range(4)]
KSTOP = int(os.environ.get('KSTOP', '9'))
KSUB = int(os.environ.get('KSUB', '9'))


class Prog:
    def __init__(self, nc, same_engine_sync=("act", "dve", "pool")):
        self.nc = nc
        self.ins = []
        self.last_w = {}
        self.readers = {}
        self.same_sync = set(same_engine_sync)
        self.final_ids = []
        self.last_eng = {}
        self.last_dma = {}
        self.bar_deps = []
        self.bar_gen = 0
        self.eng_gen = {e: 0 for e in ENGS}

    def barrier(self):
        self.bar_deps = list(self.last_eng.values()) + list(self.last_dma.values())
        self.bar_gen += 1

    def _add(self, eng, fn, reads, writes, dma=False, semkey=None, group=None):
        iid = len(self.ins)
        deps = set()
        pk = tuple(k for k in reads if isinstance(k, str) and len(k) == 3 and k[0] == "Q" and k not in writes)
        writes = tuple(writes) + pk
        if self.eng_gen[eng] < self.bar_gen:
            deps.update(self.bar_deps)
            self.eng_gen[eng] = self.bar_gen
        for k in reads:
            w = self.last_w.get(k)
            if w is not None:
                deps.add(w)
        for k in writes:
            w = self.last_w.get(k)
            if w is not None:
                if group is not None and self.ins[w].get("group") == group:
                    deps.update(self.ins[w]["deps"])
                else:
                    deps.add(w)
            for r in self.readers.get(k, ()):
                deps.add(r)
        self.ins.append(dict(eng=eng, fn=fn, deps=sorted(deps), dma=dma, semkey=semkey, group=group))
        for k in reads:
            self.readers.setdefault(k, []).append(iid)
        for k in writes:
            self.last_w[k] = iid
            self.readers[k] = []
        self.last_eng[eng] = iid
        if dma:
            self.last_dma[semkey] = iid
        return iid

    capture = None

    def op(self, eng, fn, reads=(), writes=()):
        if self.capture is not None:
            self.capture.append((eng, fn, tuple(reads), tuple(writes)))
            return None
        return self._add(eng, fn, tuple(reads), tuple(writes))

    def dma(self, eng, fn, reads=(), writes=(), semkey=None, group=None, final=False):
        assert semkey is not None
        iid = self._add(eng, fn, tuple(reads), tuple(writes), dma=True, semkey=semkey, group=group)
        if final:
            self.final_ids.append(iid)
        return iid

    def build(self):
        nc = self.nc
        ins = self.ins
        n = len(ins)
        needed = [False] * n
        for i, it in enumerate(ins):
            nd = []
            for d in it["deps"]:
                de = ins[d]
                if (not de["dma"]) and (not it["dma"]) and de["eng"] == it["eng"] and it["eng"] not in self.same_sync:
                    continue
                nd.append(d)
            it["deps"] = nd
            for d in nd:
                needed[d] = True
        for f in self.final_ids:
            needed[f] = True
        semkeys = []
        for it in ins:
            if it["dma"] and it["semkey"] not in semkeys:
                semkeys.append(it["semkey"])
        sem_objs = {}
        ctxs = []
        for e in ENGS:
            c = nc.semaphore(f"s_{e}")
            sem_objs[("eng", e)] = c.__enter__()
            ctxs.append(c)
        for j, k in enumerate(semkeys):
            c = nc.semaphore(f"d{j}")
            sem_objs[("dma", k)] = c.__enter__()
            ctxs.append(c)
        cnt = {}
        for i, it in enumerate(ins):
            if it["dma"]:
                key = ("dma", it["semkey"])
                cnt[key] = cnt.get(key, 0) + 16
                it["sig"] = (key, cnt[key])
            elif needed[i]:
                key = ("eng", it["eng"])
                cnt[key] = cnt.get(key, 0) + 1
                it["sig"] = (key, cnt[key])
            else:
                it["sig"] = None
        per = {e: [] for e in ENGS}
        for i, it in enumerate(ins):
            per[it["eng"]].append(i)
        final_waits = {}
        for f in self.final_ids:
            key, val = ins[f]["sig"]
            final_waits[key] = max(final_waits.get(key, 0), val)
        with nc.Block() as block:
            def make(e):
                def body(eng):
                    waited = {}
                    for i in per[e]:
                        it = ins[i]
                        req = {}
                        for d in it["deps"]:
                            key, val = ins[d]["sig"]
                            if waited.get(key, 0) >= val:
                                continue
                            req[key] = max(req.get(key, 0), val)
                        for key, val in req.items():
                            eng.wait_ge(sem_objs[key], val)
                            waited[key] = val
                        r = it["fn"](eng)
                        if it["sig"] is not None:
                            key, val = it["sig"]
                            r.then_inc(sem_objs[key], 16 if it["dma"] else 1)
                    if e == "sp":
                        for key, val in final_waits.items():
                            eng.wait_ge(sem_objs[key], val)
                return body
            block.tensor(make("pe"))
            block.scalar(make("act"))
            block.vector(make("dve"))
            block.gpsimd(make("pool"))
            block.sync(make("sp"))
        for c in reversed(ctxs):
            c.__exit__(None, None, None)
        return dict(n=n, per={e: len(per[e]) for e in ENGS}, sems=len(sem_objs), maxcnt=max(cnt.values()), cnt={k[1]: v for k, v in cnt.items() if k[0] == 'eng'})


def build_program(stage):
    nc = bass.Bass("TRN2", target_bir_lowering=False)
    es = ExitStack()

    def din(name, shape):
        return nc.dram_tensor(name, list(shape), F32, kind="ExternalInput").ap()

    def dout(name, shape):
        return nc.dram_tensor(name, list(shape), F32, kind="ExternalOutput").ap()

    def sb(name, shape, dt=F32):
        return es.enter_context(nc.sbuf_tensor("sb_" + name, list(shape), dt))

    P = Prog(nc)
    nc_allow = nc.allow_non_contiguous_dma(reason="small parameter vectors laid out feature-major")
    nc_allow.__enter__()

    x_in = din("x", [18 * 128, D]) if stage != "C2" else None
    if stage == "A":
        cvec = din("cvec", [17, D])
        ada_w = din("ada_w", [2, D, 6 * D])
        ada_b = din("ada_b", [2, 6 * D])
        mod_out = dout("modout", [2, 128, 48 * 17])
    else:
        modin_d = din("modin", [128, 48 * 17])
    norm_mix = din("norm_mix", [2, D])
    norm_ffn = din("norm_ffn", [2, D])
    w_in = din("even_w_in", [D, IN_W])
    ident_d = din("ident", [128, 128])
    dk_d = din("dk", [128, 2, 4])
    dc_d = din("dc", [128, 2, 4])
    if stage == "A":
        lret_out = dout("lret", [4, 128, 128])
    if stage in ("C1", "C2"):
        wg_d = din("ffn_wg", [2, D, DFF])
        wu_d = din("ffn_wu", [2, D, DFF])
        wd_d = din("ffn_wd", [2, DFF, D])
        A_re_d = din("odd_A_re", [64, 64])
        A_im_d = din("odd_A_im", [64, 64])
        ldt_d = din("odd_log_dt", [1, 64])
        B_re_d = din("odd_B_re", [64, 64, 16])
        B_im_d = din("odd_B_im", [64, 64, 16])
        C_re_d = din("odd_C_re", [64, 16, 64])
        C_im_d = din("odd_C_im", [64, 16, 64])
        Dsk_d = din("odd_D", [1, D])
        glua_d = din("odd_glu_a", [D, D])
        glub_d = din("odd_glu_b", [D, D])
        s5r_in = din("s5_re", [16, 64, 64])
        s5i_in = din("s5_im", [16, 64, 64])
        fall_d = din("fall", [8, 128, 64])
        wsel_d = din("wsel", [128, 8])
        tal_d = din("tal", [128, 512])
        tbp_d = din("tbp", [128, 512])
        tbs_d = din("tbs", [128, 128])
        amask_d = din("amask", [128, 128])
        maskg_d = din("maskg", [128, 8])
        rott_d = din("rott", [128, 128])
        if stage == "C1":
            floc_out = dout("floc", [128, 64])
        else:
            x1T_in = din("x1T", [128, KC * NTOK])
            y_out = dout("y", [17 * 128, D])
            ps5r_out = dout("p_s5r", [64, 64])
            ps5i_out = dout("p_s5i", [64, 64])
            ss5r_out = dout("s_s5r", [16, 64, 64])
            ss5i_out = dout("s_s5i", [16, 64, 64])
    if stage == "B":
        A_re_d = din("odd_A_re", [64, 64])
        A_im_d = din("odd_A_im", [64, 64])
        ldt_d = din("odd_log_dt", [1, 64])
        B_re_d = din("odd_B_re", [64, 64, 16])
        B_im_d = din("odd_B_im", [64, 64, 16])
        C_re_d = din("odd_C_re", [64, 16, 64])
        C_im_d = din("odd_C_im", [64, 16, 64])
        Dsk_d = din("odd_D", [1, D])
        tal_d = din("tal", [128, 512])
        tbp_d = din("tbp", [128, 512])
        tbs_d = din("tbs", [128, 128])
        amask_d = din("amask", [128, 128])
        maskg_d = din("maskg", [128, 8])
        rott_d = din("rott", [128, 128])
        modin1_d = din("modin1", [128, 48 * 17])
        floc_out = dout("floc", [128, 64])
        qgain = din("even_q_gain", [1, 64])
        kgain = din("even_k_gain", [1, 64])
        sinks_d = din("even_sinks", [1, 8])
        retg_d = din("even_ret_gain", [1, 512])
        w_out = din("even_w_out", [D, D])
        wg_d = din("ffn_wg", [2, D, DFF])
        wu_d = din("ffn_wu", [2, D, DFF])
        wd_d = din("ffn_wd", [2, DFF, D])
        cache_k = din("cache_k", [16, 128, 128])
        cache_v = din("cache_v", [16, 128, 128])
        sret_in = din("state_ret", [16, 4, 128, 128])
        bd_d = din("bdones", [128, 128])
        dneg_d = din("dneg", [128, 5, 128])
        rmask_d = din("rmask", [128, 2, 4, 128])
        dq_d = din("dq", [128, 2, 4, 128])
        seqmask_d = din("seqmask", [128, 16])
        wret_d = din("wret", [128, 8, 4])
        lall_d = din("lall", [8, 4, 128, 128])
        x1_out = dout("x1T", [128, KC * NTOK])
        pk_out = dout("p_k", [128, 128])
        pv_out = dout("p_v", [128, 128])
        pret_out = dout("p_ret", [4, 128, 128])
        sk_out = dout("s_k", [16, 128, 128])
        sv_out = dout("s_v", [16, 128, 128])
        sret_out = dout("s_ret", [16, 4, 128, 128])

    Q = [es.enter_context(nc.psum_tensor(f"ps_Q{i}", [128, 1024], F32)) for i in range(4)]

    def bank(i, h):
        return Q[i][:, h * 512:(h + 1) * 512]

    def bkey(i, h):
        return f"Q{i}{'ab'[h]}"

    ARENA_W = 33 * 1024
    arena = sb("arena", [128, ARENA_W])
    cur = [0]

    def carve(shape, dt=F32):
        n = int(np.prod(shape[1:]))
        words = n if dt in (F32, I32) else (n + 1) // 2
        words = (words + 7) // 8 * 8
        off = cur[0]
        assert off + words <= ARENA_W, ("arena overflow", off, words)
        cur[0] = off + words
        v = arena[:, off:off + words]
        if dt != F32:
            v = v.bitcast(dt)
        v = v[:, 0:n]
        if len(shape) == 3:
            v = v.rearrange("p (a b) -> p a b", a=shape[1])
        elif len(shape) == 4:
            v = v.rearrange("p (a b c) -> p a b c", a=shape[1], b=shape[2])
        elif len(shape) == 5:
            v = v.rearrange("p (a b c d) -> p a b c d", a=shape[1], b=shape[2], c=shape[3])
        return v

    ident = sb("ident", [128, 128])
    ones_m = sb("ones_m", [128, 128], BF16)
    ones_b = sb("ones_b", [128, 128], BF16)
    ones_g = sb("ones_g", [128, 128], BF16)
    epsb = sb("epsb", [128, 1])
    xT = sb("xT", [128, KC, NTOK])
    cT = sb("cT", [128, KC, 17], BF16)
    adab = sb("adab", [128, 2, 48])
    modT = sb("modT", [128, 48, 17])
    normg = sb("normg", [128, 2, 2, KC])
    G = sb("G", [128, KC, 17])
    dk = sb("dk", [128, 2, 4])
    dc = sb("dc", [128, 2, 4])
    P.dma("sp", lambda e: e.dma_start(out=ident[:], in_=ident_d), writes=["ident"], semkey="ld_ident", group="ld_ident")
    P.dma("sp", lambda e: e.dma_start(out=dk[:], in_=dk_d), writes=["dk"], semkey="ld_dk", group="ld_dk")
    P.dma("sp", lambda e: e.dma_start(out=dc[:], in_=dc_d), writes=["dc"], semkey="ld_dc", group="ld_dc")
    if stage == "A":
        P.dma("sp", lambda e: e.dma_start(out=adab[:], in_=ada_b.rearrange("l (m p) -> p l m", p=128)), writes=["adab"], semkey="ld_adab", group="ld_adab")
    P.dma("sp", lambda e: e.dma_start(out=normg[:, 0], in_=norm_mix.rearrange("l (k p) -> p l k", p=128)), writes=["normg"], semkey="ld_normg", group="ld_normg")
    P.dma("sp", lambda e: e.dma_start(out=normg[:, 1], in_=norm_ffn.rearrange("l (k p) -> p l k", p=128)), writes=["normg"], semkey="ld_normg", group="ld_normg")
    P.op("dve", lambda e: e.memset(ones_m[:], 1.0 / 1024.0), writes=["ones_m"])
    P.op("dve", lambda e: e.memset(ones_b[:], 1.0), writes=["ones_b"])
    P.op("dve", lambda e: e.memset(ones_g[:], 1.0 / 128.0), writes=["ones_g"])
    P.op("dve", lambda e: e.memset(epsb[:], EPS), writes=["epsb"])

    mark0 = cur[0]
    xst = [carve([128, D]) for _ in range(2)]
    c_sb = carve([128, D])
    adaw = [carve([128, KC, 768], BF16) for _ in range(2)]
    pT = [Q[i][:, :].rearrange("p (k n) -> p k n", k=KC) for i in range(2)]

    def load_tile(t, dst_ap, dst_key):
        s = t % 2
        P.dma("sp", lambda e: e.dma_start(out=xst[s], in_=x_in[t * 128:(t + 1) * 128, :]), writes=[f"xst{s}"], semkey=f"xst{s}")
        for k in range(KC):
            P.op("pe", lambda e, k=k: e.transpose(pT[s][:, k, :], xst[s][:, k * 128:(k + 1) * 128], ident[:]),
                 reads=[f"xst{s}", "ident"], writes=[bkey(s, 0), bkey(s, 1)])
        P.op("act", lambda e: e.copy(out=dst_ap, in_=pT[s]), reads=[bkey(s, 0), bkey(s, 1)], writes=[dst_key])

    if stage == "C2":
        for k in range(KC):
            P.dma("sp", lambda e, k=k: e.dma_start(out=xT[:, k, :], in_=x1T_in[:, k * NTOK:(k + 1) * NTOK]),
                  writes=[("xT", b_) for b_ in range(5)], semkey="ld_xT", group="ld_xT")
    else:
        for t in range(1, 18):
            c0 = (t - 1) * 128
            load_tile(t, xT[:, :, c0:c0 + 128], ("xT", (t - 1) // 4))

    if stage == "A":
        P.dma("sp", lambda e: e.dma_start(out=c_sb[0:17, :], in_=cvec), writes=["c_sb"], semkey="c_sb")
        P.op("act", lambda e: e.activation(out=c_sb[0:17, :], in_=c_sb[0:17, :], func=AF.Silu), reads=["c_sb"], writes=["c_sb"])
        for k in range(KC):
            P.op("pe", lambda e, k=k: e.transpose(pT[0][:, k, 0:17], c_sb[0:17, k * 128:(k + 1) * 128], ident[0:17, 0:17]),
                 reads=["c_sb", "ident"], writes=[bkey(0, 0), bkey(0, 1)])
        P.op("dve", lambda e: e.tensor_copy(out=cT[:], in_=pT[0][:, :, 0:17]), reads=[bkey(0, 0), bkey(0, 1)], writes=["cT"])

    pm = [bank(2, i)[:, 0:408].rearrange("p (m c) -> p m c", c=17) for i in range(2)]

    def modulation(layer):
        for cb in range(8):
            s = cb % 2
            P.dma("pool", lambda e, cb=cb, s=s: e.dma_start(out=adaw[s], in_=ada_w[layer, :, cb * 768:(cb + 1) * 768].rearrange("(k p) n -> p k n", p=128)),
                  writes=[f"adaw{s}"], semkey=f"adaw{s}")
            for mm in range(6):
                m = cb * 6 + mm
                for k in range(KC):
                    P.op("pe", lambda e, k=k, s=s, m=m, mm=mm: e.matmul(pm[m // 24][:, m % 24, :], lhsT=adaw[s][:, k, mm * 128:(mm + 1) * 128],
                                                                 rhs=cT[:, k, :], start=(k == 0), stop=(k == KC - 1)),
                         reads=[f"adaw{s}", "cT"], writes=[bkey(2, m // 24)])
        for h in range(2):
            P.op("dve", lambda e, h=h: e.tensor_tensor(out=modT[:, h * 24:(h + 1) * 24, :], in0=pm[h],
                                                       in1=adab[:, layer, h * 24:(h + 1) * 24].unsqueeze(2).to_broadcast([128, 24, 17]),
                                                       op=ALU.add),
                 reads=[bkey(2, h), "adab"], writes=["modT"])

    def make_G(layer, which):
        r0 = 8 if which == 0 else 32
        P.op("dve", lambda e: e.tensor_scalar(out=G[:], in0=modT[:, r0:r0 + 8, :], scalar1=1.0, scalar2=None, op0=ALU.add),
             reads=["modT"], writes=["G"])
        P.op("dve", lambda e: e.tensor_tensor(out=G[:], in0=G[:], in1=normg[:, which, layer, :].unsqueeze(2).to_broadcast([128, KC, 17]), op=ALU.mult),
             reads=["G", "normg"], writes=["G"])

    LAYER = 1 if stage in ("C1", "C2") else 0
    if stage == "A":
        for lay in (1, 0):
            modulation(lay)
            P.dma("sp", lambda e, lay=lay: e.dma_start(out=mod_out[lay], in_=modT[:].rearrange("p m c -> p (m c)")), reads=["modT"], semkey="modo", final=True)
    else:
        P.dma("sp", lambda e: e.dma_start(out=modT[:].rearrange("p m c -> p (m c)"), in_=modin_d), writes=["modT"], semkey="ld_modT")
    make_G(LAYER, 0)
    P.barrier()
    cur[0] = mark0

    sq = [carve([128, BT], BF16) for _ in range(2)]
    rstd = carve([128, BT])
    tn = [carve([128, BT]) for _ in range(2)]
    mark_ln = cur[0]
    hT = carve([128, KC, BT], BF16)
    p_ss = bank(3, 0)

    def ln_block(src_ap, src_key, ntok, sample, shift_row, out_ap=None, out_key="hT"):
        out_ap = hT if out_ap is None else out_ap
        for k in range(KC):
            s = k % 2
            P.op("act", lambda e, k=k, s=s: e.activation(out=sq[s][:, :ntok], in_=src_ap[:, k, :], func=AF.Square),
                 reads=[src_key], writes=[f"sq{s}"])
            P.op("pe", lambda e, k=k, s=s: e.matmul(p_ss[:, :ntok], lhsT=ones_m[:], rhs=sq[s][:, :ntok], start=(k == 0), stop=(k == KC - 1)),
                 reads=[f"sq{s}", "ones_m"], writes=[bkey(3, 0)])
        P.op("act", lambda e: e.activation(out=rstd[:, :ntok], in_=p_ss[:, :ntok], func=AF.Sqrt, bias=epsb[:], scale=1.0),
             reads=[bkey(3, 0), "epsb"], writes=["rstd"])
        P.op("dve", lambda e: e.reciprocal(out=rstd[:, :ntok], in_=rstd[:, :ntok]), reads=["rstd"], writes=["rstd"])
        for k in range(KC):
            s = k % 2
            P.op("dve", lambda e, k=k, s=s: e.tensor_tensor(out=tn[s][:, :ntok], in0=src_ap[:, k, :], in1=rstd[:, :ntok], op=ALU.mult),
                 reads=[src_key, "rstd"], writes=[f"tn{s}"])
            if not sample:
                P.op("act", lambda e, k=k, s=s: e.activation(out=out_ap[:, k, :ntok], in_=tn[s][:, :ntok], func=AF.Identity,
                                                             scale=G[:, k, 0:1], bias=modT[:, shift_row + k, 0:1]),
                     reads=[f"tn{s}", "G", "modT"], writes=[out_key])
            else:
                tv = tn[s][:, :128].rearrange("p (b t) -> p b t", t=8)
                P.op("dve", lambda e, k=k, tv=tv: e.tensor_tensor(out=tv, in0=tv, in1=G[:, k, 1:17].unsqueeze(2).to_broadcast([128, 16, 8]), op=ALU.mult),
                     reads=[f"tn{s}", "G"], writes=[f"tn{s}"])
                P.op("dve", lambda e, k=k, tv=tv: e.tensor_tensor(out=out_ap[:, k, :128].rearrange("p (b t) -> p b t", t=8), in0=tv,
                                                                  in1=modT[:, shift_row + k, 1:17].unsqueeze(2).to_broadcast([128, 16, 8]), op=ALU.add),
                     reads=[f"tn{s}", "modT"], writes=[out_key])

    win = None
    if stage in ("A", "B"):
        win = carve([128, KC, IN_W], BF16)
        for k in range(KC):
            P.dma("pool", lambda e, k=k: e.dma_start(out=win[:, k, :], in_=w_in[k * 128:(k + 1) * 128, :]), writes=["win"], semkey="win", group="win")

    S = carve([128, 4, 128])
    Sb = carve([128, 4, 128], BF16)
    kd_tok = carve([128, 512], BF16)
    vb_tok = carve([128, 512], BF16)
    p_kb = bank(3, 1)
    p_vb = bank(2, 0)
    p_su = bank(2, 1).rearrange("p (h e) -> p h e", h=4)

    def tok_kv(tc0, grp):
        for k in range(KC):
            P.op("pe", lambda e, k=k: e.matmul(p_kb, lhsT=hT[:, k, tc0:tc0 + 128], rhs=win[:, k, 1280:1792], start=(k == 0), stop=(k == KC - 1)),
                 reads=["hT", "win"], writes=[bkey(3, 1)])
        for k in range(KC):
            P.op("pe", lambda e, k=k: e.matmul(p_vb, lhsT=hT[:, k, tc0:tc0 + 128], rhs=win[:, k, 1792:2304], start=(k == 0), stop=(k == KC - 1)),
                 reads=["hT", "win"], writes=[bkey(2, 0)])
        P.op("dve", lambda e: e.tensor_tensor(out=kd_tok.rearrange("p (h d) -> p h d", h=4), in0=p_kb.rearrange("p (h d) -> p h d", h=4),
                                              in1=dk[:, grp, :].unsqueeze(2).to_broadcast([128, 4, 128]), op=ALU.mult),
             reads=[bkey(3, 1), "dk"], writes=["kd_tok"])
        P.op("act", lambda e: e.copy(out=vb_tok, in_=p_vb), reads=[bkey(2, 0)], writes=["vb_tok"])

    def state_update():
        for h in range(4):
            P.op("pe", lambda e, h=h: e.matmul(p_su[:, h, :], lhsT=kd_tok[:, h * 128:(h + 1) * 128], rhs=vb_tok[:, h * 128:(h + 1) * 128], start=True, stop=True),
                 reads=["kd_tok", "vb_tok"], writes=[bkey(2, 1)])
        P.op("dve", lambda e: e.tensor_tensor(out=S, in0=S, in1=dc[:, 0, :].unsqueeze(2).to_broadcast([128, 4, 128]), op=ALU.mult),
             reads=["S", "dc"], writes=["S"])
        P.op("dve", lambda e: e.tensor_tensor(out=S, in0=S, in1=p_su, op=ALU.add), reads=["S", bkey(2, 1)], writes=["S"])

    def ffn(layer):
        GC = 2
        NG = FC // GC
        P.barrier()
        cur[0] = mark_ln
        make_G(layer, 1)
        hT_all = carve([128, KC, NTOK], BF16)
        wslot = [(carve([128, KC, GC * 128], BF16), carve([128, KC, GC * 128], BF16), carve([128, GC, D], BF16)) for _ in range(3)]
        hid = [carve([128, GC, 512], BF16) for _ in range(2)]
        sgf = [carve([128, 512]) for _ in range(2)]
        gt2 = carve([128, 128])
        print("arena words used (ffn):", cur[0], "of", ARENA_W)
        for t in range(NPT):
            ln_block(xT[:, :, t * 128:(t + 1) * 128], ("xT", t // 4), 128, False, 24, out_ap=hT_all[:, :, t * 128:(t + 1) * 128], out_key="hT_all")
        ln_block(xT[:, :, TOK_P:TOK_P + 128], ("xT", 4), 128, True, 24, out_ap=hT_all[:, :, TOK_P:TOK_P + 128], out_key="hT_all")
        UP = [(bank(0, 0), bkey(0, 0), bank(0, 1), bkey(0, 1)), (bank(1, 0), bkey(1, 0), bank(1, 1), bkey(1, 1))]
        DN = [(bank(2, 0), bkey(2, 0)), (bank(2, 1), bkey(2, 1)), (bank(3, 0), bkey(3, 0)), (bank(3, 1), bkey(3, 1))]
        ui = 0
        di = 0
        blocks = [(b * 512, 512, False) for b in range(4)] + [(TOK_P, 128, True)]
        for g in range(NG):
            ws = g % 3
            wgs, wus, wds = wslot[ws]
            c0h = g * GC * 128
            P.dma("pool", lambda e, wgs=wgs, c0h=c0h: e.dma_start(out=wgs, in_=wg_d[layer, :, c0h:c0h + GC * 128].rearrange("(k p) n -> p k n", p=128)),
                  writes=[f"wg{ws}"], semkey=f"wg{ws}")
            P.dma("pool", lambda e, wus=wus, c0h=c0h: e.dma_start(out=wus, in_=wu_d[layer, :, c0h:c0h + GC * 128].rearrange("(k p) n -> p k n", p=128)),
                  writes=[f"wu{ws}"], semkey=f"wu{ws}")
            P.dma("pool", lambda e, wds=wds, c0h=c0h: e.dma_start(out=wds, in_=wd_d[layer, c0h:c0h + GC * 128, :].rearrange("(c p) n -> p c n", p=128)),
                  writes=[f"wd{ws}"], semkey=f"wd{ws}")
            for (t0, nt, smp) in blocks:
                hs = ui % 2
                for c in range(GC):
                    gps, gkey, ups, ukey = UP[ui % 2]
                    ui += 1
                    for k in range(KC):
                        P.op("pe", lambda e, k=k, c=c, gps=gps, wgs=wgs, nt=nt, t0=t0: e.matmul(gps[:, :nt], lhsT=wgs[:, k, c * 128:(c + 1) * 128], rhs=hT_all[:, k, t0:t0 + nt],
                                                                                  start=(k == 0), stop=(k == KC - 1)),
                             reads=[f"wg{ws}", "hT_all"], writes=[gkey])
                    for k in range(KC):
                        P.op("pe", lambda e, k=k, c=c, ups=ups, wus=wus, nt=nt, t0=t0: e.matmul(ups[:, :nt], lhsT=wus[:, k, c * 128:(c + 1) * 128], rhs=hT_all[:, k, t0:t0 + nt],
                                                                                  start=(k == 0), stop=(k == KC - 1)),
                             reads=[f"wu{ws}", "hT_all"], writes=[ukey])
                    sgs = sgf[c % 2]
                    P.op("act", lambda e, gps=gps, sgs=sgs, nt=nt: e.activation(out=sgs[:, :nt], in_=gps[:, :nt], func=AF.Silu), reads=[gkey], writes=[f"sgf{c % 2}"])
                    P.op("dve", lambda e, ups=ups, sgs=sgs, hs=hs, c=c, nt=nt: e.tensor_tensor(out=hid[hs][:, c, :nt], in0=ups[:, :nt], in1=sgs[:, :nt], op=ALU.mult),
                         reads=[ukey, f"sgf{c % 2}"], writes=[f"hid{hs}"])
                for m in range(KC):
                    dps, dkey = DN[di % 4]
                    di += 1
                    for c in range(GC):
                        P.op("pe", lambda e, m=m, c=c, dps=dps, wds=wds, hs=hs, nt=nt: e.matmul(dps[:, :nt], lhsT=wds[:, c, m * 128:(m + 1) * 128], rhs=hid[hs][:, c, :nt],
                                                                                         start=(c == 0), stop=(c == GC - 1)),
                             reads=[f"wd{ws}", f"hid{hs}"], writes=[dkey])
                    xs = xT[:, m, t0:t0 + nt]
                    xkey = ("xT", t0 // 512)
                    if not smp:
                        P.op("dve", lambda e, m=m, dps=dps, xs=xs, nt=nt: e.scalar_tensor_tensor(out=xs, in0=dps[:, :nt], scalar=modT[:, 40 + m, 0:1], in1=xs,
                                                                                         op0=ALU.mult, op1=ALU.add),
                             reads=[dkey, "modT", xkey], writes=[xkey])
                    else:
                        gv = gt2.rearrange("p (b t) -> p b t", t=8)
                        P.op("dve", lambda e, m=m, dps=dps, gv=gv: e.tensor_tensor(out=gv, in0=dps[:, :128].rearrange("p (b t) -> p b t", t=8),
                                                                                  in1=modT[:, 40 + m, 1:17].unsqueeze(2).to_broadcast([128, 16, 8]), op=ALU.mult),
                             reads=[dkey, "modT"], writes=["gt2"])
                        P.op("dve", lambda e, xs=xs: e.tensor_tensor(out=xs, in0=xs, in1=gt2, op=ALU.add), reads=["gt2", xkey], writes=[xkey])
        return hT_all

    if stage == "A":
        P.op("dve", lambda e: e.memset(S, 0.0), writes=["S"])
        for b in range(TOK_P // BT):
            ln_block(xT[:, :, b * BT:(b + 1) * BT], ("xT", (b * BT) // 512), BT, False, 0)
            for i in range(TPB):
                tok_kv(i * 128, 0)
                state_update()
        P.dma("sp", lambda e: e.dma_start(out=lret_out.rearrange("h d e -> d h e"), in_=S), reads=["S"], semkey="lret", final=True)
        stats = P.build()
        nc_allow.__exit__(None, None, None)
        es.close()
        return nc, stats

    def s5_stage(full):
        TWO_PI = 2.0 * np.pi
        cur[0] = mark_ln
        hT_all = carve([128, KC, NTOK], BF16)
        for t in range(NPT):
            ln_block(xT[:, :, t * 128:(t + 1) * 128], ("xT", t // 4), 128, False, 0, out_ap=hT_all[:, :, t * 128:(t + 1) * 128], out_key="hT_all")
        ln_block(xT[:, :, TOK_P:TOK_P + 128], ("xT", 4), 128, True, 0, out_ap=hT_all[:, :, TOK_P:TOK_P + 128], out_key="hT_all")

        def t64():
            return carve([128, 64])
        AreT, AimT, dtT, ar, ai, rho, thr, ph64, ph512, sinT, cosT, tmpa, tmpb, lre, lim, fre, fim, rden64 = [t64() for _ in range(18)]
        PHB = carve([128, 64, 4])
        pib = carve([128, 1])
        Glast = carve([128, 64])
        Alast = carve([128, 64])
        Blast = carve([128, 64])
        tal = carve([128, 512])
        tbp = carve([128, 512])
        tbs = carve([128, 128])
        amask = carve([128, 128])
        maskg = carve([128, 8])
        rott = carve([128, 128])
        Dsk = carve([128, KC])
        Mc = carve([128, KC, 128], BF16)
        Mcsw = carve([128, KC, 128], BF16)
        CA = carve([128, 64, 128], BF16)
        CB = carve([128, 64, 128], BF16)
        mark_s5 = cur[0]
        Bn_re = carve([128, 64, 16])
        Bn_im = carve([128, 64, 16])
        tB1 = carve([128, 64, 16])
        tB2 = carve([128, 64, 16])
        Cn = carve([128, 16, 2, 64])
        Cn2 = carve([128, 16, 2, 64])
        CsA = carve([128, 64, 16])
        CsB = carve([128, 64, 16])
        print("arena words used (s5 prep):", cur[0], "of", ARENA_W)
        P.op("dve", lambda e: e.memset(pib, float(np.pi / 2)), writes=["pib"])
        for half in range(2):
            hs_ = slice(half * 64, half * 64 + 64)
            P.dma("sp", lambda e, hs_=hs_: e.dma_start(out=AreT[hs_, :], in_=A_re_d.rearrange("g p -> p g")), writes=["AreT"], semkey="ld_AreT", group="ld_AreT")
            P.dma("sp", lambda e, hs_=hs_: e.dma_start(out=AimT[hs_, :], in_=A_im_d.rearrange("g p -> p g")), writes=["AimT"], semkey="ld_AimT", group="ld_AimT")
        P.dma("sp", lambda e: e.dma_start(out=dtT, in_=ldt_d.partition_broadcast(128)), writes=["dtT"], semkey="ld_dtT")
        for nm, tl, dd in (("tal", tal, tal_d), ("tbp", tbp, tbp_d), ("tbs", tbs, tbs_d), ("amask", amask, amask_d), ("maskg", maskg, maskg_d), ("rott", rott, rott_d)):
            P.dma("sp", lambda e, tl=tl, dd=dd: e.dma_start(out=tl, in_=dd), writes=[nm], semkey="ld_" + nm)
        P.dma("sp", lambda e: e.dma_start(out=Dsk, in_=Dsk_d.rearrange("o (k p) -> p (o k)", p=128)), writes=["Dsk"], semkey="ld_Dsk")
        P.dma("sp", lambda e: e.dma_start(out=Bn_re[0:64], in_=B_re_d.rearrange("g p j -> p g j")), writes=["Bn_re"], semkey="ld_Bn_re")
        P.dma("sp", lambda e: e.dma_start(out=Bn_im[0:64], in_=B_im_d.rearrange("g p j -> p g j")), writes=["Bn_im"], semkey="ld_Bn_im")
        if full:
            P.dma("sp", lambda e: e.dma_start(out=Cn[0:64, :, 0, :], in_=C_re_d), writes=["Cn"], semkey="ld_Cn", group="ld_Cn")
        if full:
            P.dma("sp", lambda e: e.dma_start(out=Cn[0:64, :, 1, :], in_=C_im_d), writes=["Cn"], semkey="ld_Cn", group="ld_Cn")
        if full:
            P.dma("sp", lambda e: e.dma_start(out=Cn2[0:64, :, 0, :], in_=C_im_d), writes=["Cn2"], semkey="ld_Cn2", group="ld_Cn2")
        if full:
            P.dma("sp", lambda e: e.dma_start(out=Cn2[0:64, :, 1, :], in_=C_re_d), writes=["Cn2"], semkey="ld_Cn2", group="ld_Cn2")

        def V(fn, reads, writes, eng="dve"):
            P.op(eng, fn, reads=reads, writes=writes)

        V(lambda e: e.activation(out=dtT, in_=dtT, func=AF.Exp), ["dtT"], ["dtT"], "act")
        V(lambda e: e.tensor_tensor(out=ar, in0=AreT, in1=dtT, op=ALU.mult), ["AreT", "dtT"], ["ar"])
        V(lambda e: e.tensor_tensor(out=ai, in0=AimT, in1=dtT, op=ALU.mult), ["AimT", "dtT"], ["ai"])
        V(lambda e: e.activation(out=rho, in_=ar, func=AF.Exp), ["ar"], ["rho"], "act")
        ti64 = carve([128, 64], I32)
        aiT = t64()

        def fracr(out, okey, inp, ikey):
            V(lambda e: e.tensor_copy(out=ti64, in_=inp), [ikey], ["ti64"])
            V(lambda e: e.tensor_copy(out=tmpb, in_=ti64), ["ti64"], ["tmpb"])
            V(lambda e: e.tensor_tensor(out=out, in0=inp, in1=tmpb, op=ALU.subtract), [ikey, "tmpb"], [okey])

        V(lambda e: e.tensor_scalar(out=aiT, in0=ai, scalar1=float(1.0 / TWO_PI), scalar2=None, op0=ALU.mult), ["ai"], ["aiT"])
        fracr(thr, "thr", aiT, "aiT")
        V(lambda e: e.tensor_scalar(out=tmpa, in0=aiT, scalar1=64.0, scalar2=None, op0=ALU.mult), ["aiT"], ["tmpa"])
        fracr(ph64, "ph64", tmpa, "tmpa")
        V(lambda e: e.tensor_scalar(out=tmpa, in0=aiT, scalar1=512.0, scalar2=None, op0=ALU.mult), ["aiT"], ["tmpa"])
        fracr(ph512, "ph512", tmpa, "tmpa")
        for blk in range(4):
            V(lambda e, blk=blk: e.tensor_scalar(out=tmpa, in0=ph512, scalar1=float(blk), scalar2=None, op0=ALU.mult), ["ph512"], ["tmpa"])
            fracr(PHB[:, :, blk], "PHB", tmpa, "tmpa")

        def sincos(ang, akey, s_out, skey, c_out, ckey, tmp, tkey):
            V(lambda e: e.activation(out=s_out, in_=ang, func=AF.Sin, scale=TWO_PI), [akey], [skey], "act")
            V(lambda e: e.activation(out=tmp, in_=ang, func=AF.Abs), [akey], [tkey], "act")
            V(lambda e: e.activation(out=c_out, in_=tmp, func=AF.Sin, scale=-TWO_PI, bias=pib[:]), [tkey, "pib"], [ckey], "act")

        sincos(thr, "thr", sinT, "sinT", cosT, "cosT", tmpa, "tmpa")
        V(lambda e: e.tensor_tensor(out=lre, in0=rho, in1=cosT, op=ALU.mult), ["rho", "cosT"], ["lre"])
        V(lambda e: e.tensor_tensor(out=lim, in0=rho, in1=sinT, op=ALU.mult), ["rho", "sinT"], ["lim"])
        V(lambda e: e.tensor_scalar(out=tmpa, in0=lre, scalar1=-1.0, scalar2=None, op0=ALU.add), ["lre"], ["tmpa"])
        V(lambda e: e.tensor_tensor(out=rden64, in0=AreT, in1=AreT, op=ALU.mult), ["AreT"], ["rden64"])
        V(lambda e: e.tensor_tensor(out=tmpb, in0=AimT, in1=AimT, op=ALU.mult), ["AimT"], ["tmpb"])
        V(lambda e: e.tensor_tensor(out=rden64, in0=rden64, in1=tmpb, op=ALU.add), ["rden64", "tmpb"], ["rden64"])
        V(lambda e: e.reciprocal(out=rden64, in_=rden64), ["rden64"], ["rden64"])
        V(lambda e: e.tensor_tensor(out=fre, in0=tmpa, in1=AreT, op=ALU.mult), ["tmpa", "AreT"], ["fre"])
        V(lambda e: e.tensor_tensor(out=tmpb, in0=lim, in1=AimT, op=ALU.mult), ["lim", "AimT"], ["tmpb"])
        V(lambda e: e.tensor_tensor(out=fre, in0=fre, in1=tmpb, op=ALU.add), ["fre", "tmpb"], ["fre"])
        V(lambda e: e.tensor_tensor(out=fre, in0=fre, in1=rden64, op=ALU.mult), ["fre", "rden64"], ["fre"])
        V(lambda e: e.tensor_tensor(out=fim, in0=lim, in1=AreT, op=ALU.mult), ["lim", "AreT"], ["fim"])
        V(lambda e: e.tensor_tensor(out=tmpb, in0=tmpa, in1=AimT, op=ALU.mult), ["tmpa", "AimT"], ["tmpb"])
        V(lambda e: e.tensor_tensor(out=fim, in0=fim, in1=tmpb, op=ALU.subtract), ["fim", "tmpb"], ["fim"])
        V(lambda e: e.tensor_tensor(out=fim, in0=fim, in1=rden64, op=ALU.mult), ["fim", "rden64"], ["fim"])
        h64 = slice(0, 64)
        frb = fre[h64, :].unsqueeze(2).to_broadcast([64, 64, 16])
        fib = fim[h64, :].unsqueeze(2).to_broadcast([64, 64, 16])
        V(lambda e: e.tensor_tensor(out=tB1[h64], in0=Bn_re[h64], in1=frb, op=ALU.mult), ["Bn_re", "fre"], ["tB1"])
        V(lambda e: e.tensor_tensor(out=tB2[h64], in0=Bn_im[h64], in1=fib, op=ALU.mult), ["Bn_im", "fim"], ["tB2"])
        V(lambda e: e.tensor_tensor(out=tB1[h64], in0=tB1[h64], in1=tB2[h64], op=ALU.subtract), ["tB1", "tB2"], ["tB1"])
        V(lambda e: e.tensor_tensor(out=tB2[h64], in0=Bn_im[h64], in1=frb, op=ALU.mult), ["Bn_im", "fre"], ["tB2"])
        V(lambda e: e.tensor_tensor(out=Bn_im[h64], in0=Bn_re[h64], in1=fib, op=ALU.mult), ["Bn_re", "fim", "tB2"], ["Bn_im"])
        V(lambda e: e.tensor_tensor(out=tB2[h64], in0=tB2[h64], in1=Bn_im[h64], op=ALU.add), ["tB2", "Bn_im"], ["tB2"])
        ptr = bank(3, 0)
        for F in range(KC):
            P.op("pe", lambda e, F=F: e.transpose(ptr[:, 0:64], tB1[h64, 8 * F:8 * F + 8, :].rearrange("p a b -> p (a b)"), ident[0:64, 0:64]),
                 reads=["tB1", "ident"], writes=[bkey(3, 0)])
            P.op("pe", lambda e, F=F: e.transpose(ptr[:, 64:128], tB2[h64, 8 * F:8 * F + 8, :].rearrange("p a b -> p (a b)"), ident[0:64, 0:64]),
                 reads=["tB2", "ident"], writes=[bkey(3, 0)])
            V(lambda e, F=F: e.copy(out=Mc[:, F, :], in_=ptr[:, 0:128]), [bkey(3, 0)], ["Mc"], "act")
            V(lambda e, F=F: e.copy(out=Mcsw[:, F, 0:64], in_=ptr[:, 64:128]), [bkey(3, 0)], ["Mcsw"], "act")
            V(lambda e, F=F: e.mul(out=Mcsw[:, F, 64:128], in_=ptr[:, 0:64], mul=-1.0), [bkey(3, 0)], ["Mcsw"], "act")
        if full:
            for (src, skey, dst, dkey) in ((Cn, "Cn", CsA, "CsA"), (Cn2, "Cn2", CsB, "CsB")):
                for ib in range(2):
                    for ii in range(8):
                        i_ = ib * 8 + ii
                        P.op("pe", lambda e, src=src, i_=i_, ii=ii: e.transpose(ptr[:, ii * 64:(ii + 1) * 64], src[h64, i_, :, :].rearrange("p c q -> p (c q)"), ident[0:64, 0:64]),
                             reads=[skey, "ident"], writes=[bkey(3, 0)])
                    V(lambda e, dst=dst, ib=ib: e.copy(out=dst[:, :, ib * 8:(ib + 1) * 8].rearrange("p g i -> p i g"), in_=ptr.rearrange("p (i g) -> p i g", i=8)),
                      [bkey(3, 0)], [dkey], "act")
            V(lambda e: e.tensor_scalar(out=CsA[64:128], in0=CsA[64:128], scalar1=-1.0, scalar2=None, op0=ALU.mult), ["CsA"], ["CsA"])
            V(lambda e: e.tensor_scalar(out=CsB, in0=CsB, scalar1=-1.0, scalar2=None, op0=ALU.mult), ["CsB"], ["CsB"])
            V(lambda e: e.memset(CA, 0.0), [], ["CA"], "pool")
            V(lambda e: e.memset(CB, 0.0), [], ["CB"], "pool")
            for gl in range(8):
                for (src, skey, dst, dkey) in ((CsA, "CsA", CA, "CA"), (CsB, "CsB", CB, "CB")):
                    V(lambda e, src=src, dst=dst, gl=gl: e.tensor_copy(out=dst.rearrange("p (f g) c -> p f g c", g=8)[:, :, gl, 16 * gl:16 * gl + 16],
                                                                       in_=src.rearrange("p (f g) i -> p f g i", g=8)[:, :, gl, :]),
                      [skey], [dkey])

        hin = carve([128, 64]) if False else None
        P.op("dve", lambda e: e.memset(Glast, 0.0), writes=["Glast"])
        if full:
            ang = tmpa
            V(lambda e: e.tensor_scalar(out=ang, in0=aiT, scalar1=2048.0, scalar2=None, op0=ALU.mult), ["aiT"], ["tmpa"])
            fracr(fim, "fim", ang, "tmpa")
            sincos(fim, "fim", sinT, "sinT", cosT, "cosT", fre, "fre")
            V(lambda e: e.activation(out=lre, in_=ar, func=AF.Exp, scale=2048.0), ["ar"], ["lre"], "act")
            V(lambda e: e.tensor_tensor(out=lim, in0=lre, in1=sinT, op=ALU.mult), ["lre", "sinT"], ["lim"])
            V(lambda e: e.tensor_tensor(out=lre, in0=lre, in1=cosT, op=ALU.mult), ["lre", "cosT"], ["lre"])
            wsel = carve([128, 8])
            fr = [carve([128, 64]) for _ in range(2)]
            P.dma("sp", lambda e: e.dma_start(out=wsel, in_=wsel_d), writes=["wsel"], semkey="ld_wsel")
            prot = bank(3, 1)[:, 0:64]
            for r in range(7):
                s_ = r % 2
                P.dma("sp", lambda e, r=r, s_=s_: e.dma_start(out=fr[s_], in_=fall_d[r]), writes=[f"fr{s_}"], semkey=f"fr{s_}")
                P.op("pe", lambda e: e.matmul(prot, lhsT=rott, rhs=Glast, start=True, stop=True), reads=["rott", "Glast"], writes=[bkey(3, 1)])
                V(lambda e: e.tensor_tensor(out=tmpa, in0=lre, in1=Glast, op=ALU.mult), ["lre", "Glast"], ["tmpa"])
                V(lambda e: e.tensor_tensor(out=tmpb, in0=lim, in1=prot, op=ALU.mult), ["lim", bkey(3, 1)], ["tmpb"])
                V(lambda e: e.tensor_tensor(out=tmpa, in0=tmpa, in1=tmpb, op=ALU.add), ["tmpa", "tmpb"], ["tmpa"])
                V(lambda e, s_=s_: e.tensor_tensor(out=tmpa, in0=tmpa, in1=fr[s_], op=ALU.add), ["tmpa", f"fr{s_}"], ["tmpa"])
                V(lambda e: e.tensor_tensor(out=tmpa, in0=tmpa, in1=Glast, op=ALU.subtract), ["tmpa", "Glast"], ["tmpa"])
                V(lambda e, r=r: e.scalar_tensor_tensor(out=Glast, in0=tmpa, scalar=wsel[:, r:r + 1], in1=Glast, op0=ALU.mult, op1=ALU.add),
                  ["tmpa", "wsel", "Glast"], ["Glast"])
        P.barrier()
        cur[0] = mark_s5
        NW = 2
        um = [carve([128, 512], BF16) for _ in range(NW)]
        xang = [carve([128, 512]) for _ in range(NW)]
        sang = [carve([128, 512]) for _ in range(NW)]
        SINt = [carve([128, 512]) for _ in range(NW)]
        COSt = [carve([128, 512]) for _ in range(NW)]
        Wt = [carve([128, 512]) for _ in range(NW)]
        Ab = [carve([128, 512], BF16) for _ in range(NW)]
        Bb = [carve([128, 512], BF16) for _ in range(NW)]
        aseq = carve([128, 128])
        ysb = carve([128, 512])
        y2 = carve([128, 512])
        H0T = carve([128, 64, 16])
        Asl = carve([128, 64, 16])
        Bsl = carve([128, 64, 16])
        h0n = carve([128, 64, 128]) if False else None
        print("arena words used (s5 main):", cur[0], "of", ARENA_W)
        BU = [(bank(0, 0), bkey(0, 0), bank(0, 1), bkey(0, 1)), (bank(1, 0), bkey(1, 0), bank(1, 1), bkey(1, 1))]
        YP = [(bank(2, 0), bkey(2, 0)), (bank(2, 1), bkey(2, 1))]
        wi = [0]

        def s5_T(F, gl, t0, nt, blk, sample):
            g = 8 * F + gl
            w = wi[0] % NW
            wi[0] += 1
            bu, bukey, bus, buskey = BU[w % 2]
            V(lambda e: e.activation(out=um[w][:, :nt], in_=hT_all[:, F, t0:t0 + nt], func=AF.Copy, scale=maskg[:, gl:gl + 1]),
              [("hT_all", F, t0), "maskg"], [f"um{w}"], "act")
            P.op("pe", lambda e: e.matmul(bu[:, :nt], lhsT=Mc[:, F, :], rhs=um[w][:, :nt], start=True, stop=True), reads=["Mc", f"um{w}"], writes=[bukey])
            P.op("pe", lambda e: e.matmul(bus[:, :nt], lhsT=Mcsw[:, F, :], rhs=um[w][:, :nt], start=True, stop=True), reads=["Mcsw", f"um{w}"], writes=[buskey])
            MAGIC = 12582912.0
            if not sample:
                V(lambda e: e.activation(out=xang[w], in_=tal, func=AF.Identity, scale=ph64[:, g:g + 1], bias=PHB[:, g, blk:blk + 1]),
                  ["tal", "ph64", "PHB"], [f"xang{w}"], "act")
                V(lambda e: e.scalar_tensor_tensor(out=xang[w], in0=tbp, scalar=thr[:, g:g + 1], in1=xang[w], op0=ALU.mult, op1=ALU.add),
                  ["tbp", "thr", f"xang{w}"], [f"xang{w}"])
            else:
                V(lambda e: e.activation(out=xang[w][:, :nt], in_=tbs, func=AF.Copy, scale=thr[:, g:g + 1]), ["tbs", "thr"], [f"xang{w}"], "act")
            V(lambda e: e.tensor_scalar(out=sang[w][:, :nt], in0=xang[w][:, :nt], scalar1=MAGIC, scalar2=MAGIC, op0=ALU.add, op1=ALU.subtract),
              [f"xang{w}"], [f"sang{w}"])
            V(lambda e: e.tensor_tensor(out=xang[w][:, :nt], in0=xang[w][:, :nt], in1=sang[w][:, :nt], op=ALU.subtract), [f"xang{w}", f"sang{w}"], [f"xang{w}"])
            V(lambda e: e.activation(out=sang[w][:, :nt], in_=xang[w][:, :nt], func=AF.Abs), [f"xang{w}"], [f"sang{w}"], "act")
            V(lambda e: e.activation(out=SINt[w][:, :nt], in_=xang[w][:, :nt], func=AF.Sin, scale=TWO_PI), [f"xang{w}"], [f"SINt{w}"], "act")
            V(lambda e: e.activation(out=COSt[w][:, :nt], in_=sang[w][:, :nt], func=AF.Sin, scale=-TWO_PI, bias=pib[:]), [f"sang{w}", "pib"], [f"COSt{w}"], "act")
            return dict(F=F, gl=gl, g=g, w=w, t0=t0, nt=nt, blk=blk, sample=sample, bu=bu, bukey=bukey, bus=bus, buskey=buskey)

        def s5_S(c, ypk, last_blk):
            F, gl, g, w, t0, nt, blk, sample = c['F'], c['gl'], c['g'], c['w'], c['t0'], c['nt'], c['blk'], c['sample']
            bu, bukey, bus, buskey = c['bu'], c['bukey'], c['bus'], c['buskey']
            yp, ykey = ypk
            V(lambda e: e.tensor_tensor(out=sang[w][:, :nt], in0=bu[:, :nt], in1=COSt[w][:, :nt], op=ALU.mult), [bukey, f"COSt{w}"], [f"sang{w}"])
            V(lambda e: e.tensor_tensor(out=Wt[w][:, :nt], in0=bus[:, :nt], in1=SINt[w][:, :nt], op=ALU.mult), [buskey, f"SINt{w}"], [f"Wt{w}"])
            V(lambda e: e.tensor_tensor(out=Wt[w][:, :nt], in0=Wt[w][:, :nt], in1=sang[w][:, :nt], op=ALU.add), [f"Wt{w}", f"sang{w}"], [f"Wt{w}"])
            if not sample:
                V(lambda e: e.tensor_tensor_scan(out=Wt[w], data0=rho[:, g:g + 1].to_broadcast([128, 512]), data1=Wt[w], initial=Glast[:, g:g + 1],
                                                 op0=ALU.mult, op1=ALU.add), [f"Wt{w}", "rho", "Glast"], [f"Wt{w}"])
                V(lambda e: e.copy(out=Glast[:, g:g + 1], in_=Wt[w][:, 511:512]), [f"Wt{w}"], ["Glast"], "act")
                if last_blk:
                    V(lambda e: e.tensor_tensor(out=Alast[:, g:g + 1], in0=Wt[w][:, 511:512], in1=COSt[w][:, 511:512], op=ALU.mult), [f"Wt{w}", f"COSt{w}"], ["Alast"])
                    V(lambda e: e.tensor_tensor(out=Blast[:, g:g + 1], in0=Wt[w][:, 511:512], in1=SINt[w][:, 511:512], op=ALU.mult), [f"Wt{w}", f"SINt{w}"], ["Blast"])
            else:
                wv = Wt[w][:, :128].rearrange("p (b t) -> p b t", t=8)
                V(lambda e: e.scalar_tensor_tensor(out=wv[:, :, 0], in0=H0T[:, g, :], scalar=rho[:, g:g + 1], in1=wv[:, :, 0], op0=ALU.mult, op1=ALU.add),
                  ["H0T", "rho", f"Wt{w}"], [f"Wt{w}"])
                V(lambda e: e.tensor_scalar(out=aseq, in0=amask, scalar1=rho[:, g:g + 1], scalar2=None, op0=ALU.mult), ["amask", "rho"], ["aseq"])
                V(lambda e: e.tensor_tensor_scan(out=Wt[w][:, :128], data0=aseq, data1=Wt[w][:, :128], initial=0.0, op0=ALU.mult, op1=ALU.add),
                  [f"Wt{w}", "aseq"], [f"Wt{w}"])
                cv = COSt[w][:, :128].rearrange("p (b t) -> p b t", t=8)
                sv = SINt[w][:, :128].rearrange("p (b t) -> p b t", t=8)
                V(lambda e: e.tensor_tensor(out=Asl[:, g, :], in0=wv[:, :, 7], in1=cv[:, :, 7], op=ALU.mult), [f"Wt{w}", f"COSt{w}"], ["Asl"])
                V(lambda e: e.tensor_tensor(out=Bsl[:, g, :], in0=wv[:, :, 7], in1=sv[:, :, 7], op=ALU.mult), [f"Wt{w}", f"SINt{w}"], ["Bsl"])
            if full:
                V(lambda e: e.tensor_tensor(out=Ab[w][:, :nt], in0=Wt[w][:, :nt], in1=COSt[w][:, :nt], op=ALU.mult), [f"Wt{w}", f"COSt{w}"], [f"Ab{w}"])
                V(lambda e: e.tensor_tensor(out=Bb[w][:, :nt], in0=Wt[w][:, :nt], in1=SINt[w][:, :nt], op=ALU.mult), [f"Wt{w}", f"SINt{w}"], [f"Bb{w}"], "pool")
                P.op("pe", lambda e: e.matmul(yp[:, :nt], lhsT=CA[:, g, :], rhs=Ab[w][:, :nt], start=(gl == 0), stop=False), reads=["CA", f"Ab{w}"], writes=[ykey])
                P.op("pe", lambda e: e.matmul(yp[:, :nt], lhsT=CB[:, g, :], rhs=Bb[w][:, :nt], start=False, stop=(gl == 7)), reads=["CB", f"Bb{w}"], writes=[ykey])

        def y_finish(F, t0, nt, ypk):
            yp, ykey = ypk
            V(lambda e: e.scalar_tensor_tensor(out=ysb[:, :nt], in0=hT_all[:, F, t0:t0 + nt], scalar=Dsk[:, F:F + 1], in1=yp[:, :nt], op0=ALU.mult, op1=ALU.add),
              [("hT_all", F, t0), "Dsk", ykey], ["ysb"])
            V(lambda e: e.tensor_tensor(out=y2[:, :nt], in0=ysb[:, :nt], in1=ysb[:, :nt], op=ALU.mult), ["ysb"], ["y2"], "pool")
            V(lambda e: e.tensor_scalar(out=y2[:, :nt], in0=y2[:, :nt], scalar1=0.044715, scalar2=1.0, op0=ALU.mult, op1=ALU.add), ["y2"], ["y2"], "pool")
            V(lambda e: e.tensor_tensor(out=y2[:, :nt], in0=y2[:, :nt], in1=ysb[:, :nt], op=ALU.mult), ["y2", "ysb"], ["y2"], "pool")
            V(lambda e: e.activation(out=y2[:, :nt], in_=y2[:, :nt], func=AF.Tanh, scale=float(np.sqrt(2.0 / np.pi))), ["y2"], ["y2"], "act")
            V(lambda e: e.tensor_scalar(out=y2[:, :nt], in0=y2[:, :nt], scalar1=0.5, scalar2=0.5, op0=ALU.mult, op1=ALU.add), ["y2"], ["y2"])
            V(lambda e: e.tensor_tensor(out=hT_all[:, F, t0:t0 + nt], in0=y2[:, :nt], in1=ysb[:, :nt], op=ALU.mult), ["y2", "ysb"], [("hT_all", F, t0)])

        if True:
            h0n = um
        s0n = carve([128, 64, 128]) if False else None
        hnat = carve([16, 64 * 128]) if False else None
        stg = xang[0]
        pth = bank(3, 0)
        for gq_ in range(16 if full else 0):
            P.dma("sp", lambda e, gq_=gq_: e.dma_start(out=stg[0:16, :].rearrange("p (g c) -> p g c", g=4)[:, :, 0:64], in_=s5r_in[:, 4 * gq_:4 * gq_ + 4, :]),
                  writes=["xang0"], semkey="stg", group=f"stg{gq_}")
            P.dma("sp", lambda e, gq_=gq_: e.dma_start(out=stg[0:16, :].rearrange("p (g c) -> p g c", g=4)[:, :, 64:128], in_=s5i_in[:, 4 * gq_:4 * gq_ + 4, :]),
                  writes=["xang0"], semkey="stg", group=f"stg{gq_}")
            for gg in range(4):
                P.op("pe", lambda e, gg=gg: e.transpose(pth[:, gg * 16:(gg + 1) * 16], stg[0:16, gg * 128:(gg + 1) * 128], ident[0:16, 0:16]),
                     reads=["xang0", "ident"], writes=[bkey(3, 0)])
            V(lambda e, gq_=gq_: e.copy(out=H0T[:, 4 * gq_:4 * gq_ + 4, :], in_=pth[:, 0:64].rearrange("p (g b) -> p g b", g=4)), [bkey(3, 0)], ["H0T"], "act")

        units = []
        yi = 0
        for F in range(KC):
            for blk in range(4):
                ypk = YP[yi % 2]
                yi += 1
                for gl in range(8):
                    units.append(dict(F=F, gl=gl, t0=blk * 512, nt=512, blk=blk, sample=False, ypk=ypk, last=(blk == 3), fin=(gl == 7)))
            if full:
                ypk = YP[yi % 2]
                yi += 1
                for gl in range(8):
                    units.append(dict(F=F, gl=gl, t0=TOK_P, nt=128, blk=0, sample=True, ypk=ypk, last=False, fin=(gl == 7)))
        def cap(fn_, *a_):
            P.capture = []
            r_ = fn_(*a_)
            ops_ = P.capture
            P.capture = None
            return ops_, r_

        def emit(ops_):
            for o_ in ops_:
                P.op(*o_)

        u0 = units[0]
        opsT, ctx = cap(s5_T, u0["F"], u0["gl"], u0["t0"], u0["nt"], u0["blk"], u0["sample"])
        emit(opsT)
        for n, u in enumerate(units):
            opsS, _ = cap(s5_S, ctx, u["ypk"], u["last"])
            if n + 1 < len(units):
                v = units[n + 1]
                opsT, nxt = cap(s5_T, v["F"], v["gl"], v["t0"], v["nt"], v["blk"], v["sample"])
            else:
                opsT, nxt = [], None
            pre = opsT[:4]
            rest = opsT[4:]
            k_ = 0
            while k_ < len(rest) and rest[k_][0] == "dve":
                k_ += 1
            t_dve, t_tail = rest[:k_], rest[k_:]
            seq = list(pre)
            for j_ in range(3):
                seq.append(opsS[j_])
                if j_ < len(t_dve):
                    seq.append(t_dve[j_])
            seq += t_dve[3:] + t_tail + opsS[3:]
            assert len(seq) == len(opsT) + len(opsS)
            emit(seq)
            if full and u["fin"]:
                y_finish(u["F"], u["t0"], u["nt"], u["ypk"])
            ctx = nxt

        pfin = bank(3, 1)[:, 0:64]
        P.op("pe", lambda e: e.matmul(pfin, lhsT=rott, rhs=Blast, start=True, stop=True), reads=["rott", "Blast"], writes=[bkey(3, 1)])
        V(lambda e: e.tensor_tensor(out=Alast, in0=Alast, in1=pfin, op=ALU.add), ["Alast", bkey(3, 1)], ["Alast"])
        if not full:
            P.dma("sp", lambda e: e.dma_start(out=floc_out, in_=Alast), reads=["Alast"], semkey="floc", final=True)
            return None
        ptp = bank(3, 0)[0:64, 0:128]
        P.op("pe", lambda e: e.transpose(ptp, Alast, ident[:]), reads=["Alast", "ident"], writes=[bkey(3, 0)])
        V(lambda e: e.copy(out=ysb[0:64, 0:128], in_=ptp), [bkey(3, 0)], ["ysb"], "act")
        P.dma("sp", lambda e: e.dma_start(out=ps5r_out, in_=ysb[0:64, 0:64]), reads=["ysb"], semkey="ps5", final=True)
        P.dma("sp", lambda e: e.dma_start(out=ps5i_out, in_=ysb[0:64, 64:128]), reads=["ysb"], semkey="ps5", final=True)
        for hf in range(2):
            pf2 = bank(3, 1)
            P.op("pe", lambda e, hf=hf: e.matmul(pf2, lhsT=rott, rhs=Bsl[:, 32 * hf:32 * hf + 32, :].rearrange("p g b -> p (g b)"), start=True, stop=True),
                 reads=["rott", "Bsl"], writes=[bkey(3, 1)])
            V(lambda e, hf=hf: e.tensor_tensor(out=Asl[:, 32 * hf:32 * hf + 32, :].rearrange("p g b -> p (g b)"), in0=Asl[:, 32 * hf:32 * hf + 32, :].rearrange("p g b -> p (g b)"),
                                               in1=pf2, op=ALU.add), ["Asl", bkey(3, 1)], ["Asl"])
        for gq_ in range(16):
            pto = bank(3, 0)[0:16, :]
            for gg in range(4):
                P.op("pe", lambda e, gq_=gq_, gg=gg: e.transpose(pto[:, gg * 128:(gg + 1) * 128], Asl[:, 4 * gq_ + gg, :], ident[:]),
                     reads=["Asl", "ident"], writes=[bkey(3, 0)])
            V(lambda e: e.copy(out=stg[0:16, :], in_=pto), [bkey(3, 0)], ["xang0"], "act")
            P.dma("sp", lambda e, gq_=gq_: e.dma_start(out=ss5r_out[:, 4 * gq_:4 * gq_ + 4, :], in_=stg[0:16, :].rearrange("p (g c) -> p g c", g=4)[:, :, 0:64]),
                  reads=["xang0"], semkey="ss5o", final=True)
            P.dma("sp", lambda e, gq_=gq_: e.dma_start(out=ss5i_out[:, 4 * gq_:4 * gq_ + 4, :], in_=stg[0:16, :].rearrange("p (g c) -> p g c", g=4)[:, :, 64:128]),
                  reads=["xang0"], semkey="ss5o", final=True)

        P.barrier()
        cur[0] = mark_ln + NTOK * KC // 2
        glua = carve([128, KC, D], BF16)
        glub = carve([128, KC, D], BF16)
        sgb = [carve([128, 512]) for _ in range(2)]
        prd = [carve([128, 512]) for _ in range(2)]
        gt3 = carve([128, 128])
        P.dma("pool", lambda e: e.dma_start(out=glua, in_=glua_d.rearrange("(k p) n -> p k n", p=128)), writes=["glua"], semkey="glua")
        P.dma("pool", lambda e: e.dma_start(out=glub, in_=glub_d.rearrange("(k p) n -> p k n", p=128)), writes=["glub"], semkey="glub")
        GA = [(bank(0, 0), bkey(0, 0), bank(0, 1), bkey(0, 1)), (bank(1, 0), bkey(1, 0), bank(1, 1), bkey(1, 1))]
        gi = 0
        for (t0, nt, smp) in [(b * 512, 512, False) for b in range(4)] + [(TOK_P, 128, True)]:
            for m in range(KC):
                pa, pak, pb, pbk = GA[gi % 2]
                w_ = gi % 2
                gi += 1
                for k in range(KC):
                    P.op("pe", lambda e, k=k, m=m, pa=pa, t0=t0, nt=nt: e.matmul(pa[:, :nt], lhsT=glua[:, k, m * 128:(m + 1) * 128], rhs=hT_all[:, k, t0:t0 + nt],
                                                                                  start=(k == 0), stop=(k == KC - 1)), reads=["glua", "hT_all"], writes=[pak])
                for k in range(KC):
                    P.op("pe", lambda e, k=k, m=m, pb=pb, t0=t0, nt=nt: e.matmul(pb[:, :nt], lhsT=glub[:, k, m * 128:(m + 1) * 128], rhs=hT_all[:, k, t0:t0 + nt],
                                                                                  start=(k == 0), stop=(k == KC - 1)), reads=["glub", "hT_all"], writes=[pbk])
                V(lambda e, pb=pb, w_=w_, nt=nt: e.activation(out=sgb[w_][:, :nt], in_=pb[:, :nt], func=AF.Sigmoid), [pbk], [f"sgb{w_}"], "act")
                V(lambda e, pa=pa, w_=w_, nt=nt: e.tensor_tensor(out=prd[w_][:, :nt], in0=pa[:, :nt], in1=sgb[w_][:, :nt], op=ALU.mult), [pak, f"sgb{w_}"], [f"prd{w_}"])
                xs = xT[:, m, t0:t0 + nt]
                xkey = ("xT", t0 // 512)
                if not smp:
                    V(lambda e, m=m, w_=w_, xs=xs, nt=nt: e.scalar_tensor_tensor(out=xs, in0=prd[w_][:, :nt], scalar=modT[:, 16 + m, 0:1], in1=xs, op0=ALU.mult, op1=ALU.add),
                      [f"prd{w_}", "modT", xkey], [xkey])
                else:
                    gv = gt3.rearrange("p (b t) -> p b t", t=8)
                    V(lambda e, m=m, w_=w_, gv=gv: e.tensor_tensor(out=gv, in0=prd[w_][:, :128].rearrange("p (b t) -> p b t", t=8),
                                                                  in1=modT[:, 16 + m, 1:17].unsqueeze(2).to_broadcast([128, 16, 8]), op=ALU.mult), [f"prd{w_}", "modT"], ["gt3"])
                    V(lambda e, xs=xs: e.tensor_tensor(out=xs, in0=xs, in1=gt3, op=ALU.add), ["gt3", xkey], [xkey])
        ffn(1)
        P.barrier()
        cur[0] = mark_ln
        yst = [carve([128, D]) for _ in range(2)]
        for t in range(17):
            s = t % 2
            for k in range(KC):
                P.op("pe", lambda e, k=k, t=t, s=s: e.transpose(pT[s][:, k, :], xT[:, k, t * 128:(t + 1) * 128], ident[:]),
                     reads=[("xT", t // 4), "ident"], writes=[bkey(s, 0), bkey(s, 1)])
            P.op("act", lambda e, s=s: e.copy(out=yst[s], in_=Q[s][:, :]), reads=[bkey(s, 0), bkey(s, 1)], writes=[f"yst{s}"])
            P.dma("sp", lambda e, t=t, s=s: e.dma_start(out=y_out[t * 128:(t + 1) * 128, :], in_=yst[s]), reads=[f"yst{s}"], semkey=f"yst{s}", final=True)
        stats = P.build()
        nc_allow.__exit__(None, None, None)
        es.close()
        return nc, stats

    if stage in ("C1", "C2"):
        r_ = s5_stage(stage == "C2")
        if r_ is not None:
            return r_
        stats = P.build()
        nc_allow.__exit__(None, None, None)
        es.close()
        return nc, stats

    hTh = carve([128, KC, 128], BF16)
    mark1 = cur[0]
    lst = [carve([128, 4, 128]) for _ in range(2)]
    wret = carve([128, 8, 4])
    xTh = carve([128, KC, 128])
    xst[0] = carve([128, D])
    load_tile(0, xTh, "xTh")
    P.dma("sp", lambda e: e.dma_start(out=wret, in_=wret_d), writes=["wret"], semkey="wret")
    P.op("dve", lambda e: e.memset(S, 0.0), writes=["S"])
    for r in range(8):
        s = r % 2
        P.dma("sp", lambda e, r=r, s=s: e.dma_start(out=lst[s], in_=lall_d[r].rearrange("h d e -> d h e")), writes=[f"lst{s}"], semkey=f"lst{s}")
        P.op("dve", lambda e, r=r, s=s: e.tensor_tensor(out=lst[s], in0=lst[s], in1=wret[:, r, :].unsqueeze(2).to_broadcast([128, 4, 128]), op=ALU.mult),
             reads=[f"lst{s}", "wret"], writes=[f"lst{s}"])
        P.op("dve", lambda e, s=s: e.tensor_tensor(out=S, in0=S, in1=lst[s], op=ALU.add), reads=["S", f"lst{s}"], writes=["S"])
    P.op("act", lambda e: e.copy(out=Sb, in_=S), reads=["S"], writes=["Sb"])
    ln_block(xTh, "xTh", 128, False, 0)
    P.op("pool", lambda e: e.tensor_copy(out=hTh, in_=hT[:, :, 0:128]), reads=["hT"], writes=["hTh"])
    P.barrier()
    cur[0] = mark1

    wkd = carve([128, KC, 2, 128], BF16)
    wvd = carve([128, KC, 2, 128], BF16)
    for kv in range(2):
        for half in range(2):
            P.op("pool", lambda e, kv=kv, half=half: e.tensor_copy(out=wkd[:, :, kv, half * 64:(half + 1) * 64], in_=win[:, :, 512 + 64 * kv:576 + 64 * kv]),
                 reads=["win"], writes=["wkd"])
            P.op("pool", lambda e, kv=kv, half=half: e.tensor_copy(out=wvd[:, :, kv, half * 64:(half + 1) * 64], in_=win[:, :, 640 + 64 * kv:704 + 64 * kv]),
                 reads=["win"], writes=["wvd"])
    wout = carve([128, KC, D], BF16)
    P.dma("pool", lambda e: e.dma_start(out=wout, in_=w_out.rearrange("(k p) n -> p k n", p=128)), writes=["wout"], semkey="wout")
    gq = carve([128, 1])
    gk = carve([128, 1])
    bd_f = carve([128, 128])
    bd = carve([128, 128], BF16)
    dneg = carve([128, 5, 128])
    rmask = carve([128, 4, 128])
    dq = carve([128, 4, 128])
    seqmask = carve([128, 16])
    esink = carve([128, 8])
    retg = carve([128, 4])
    for half in range(2):
        P.dma("sp", lambda e, half=half: e.dma_start(out=gq[half * 64:(half + 1) * 64, :], in_=qgain.rearrange("o d -> d o")), writes=["gq"], semkey="ld_gq", group="ld_gq")
        P.dma("sp", lambda e, half=half: e.dma_start(out=gk[half * 64:(half + 1) * 64, :], in_=kgain.rearrange("o d -> d o")), writes=["gk"], semkey="ld_gk", group="ld_gk")
    P.dma("sp", lambda e: e.dma_start(out=bd_f, in_=bd_d), writes=["bd_f"], semkey="ld_bd_f", group="ld_bd_f")
    P.dma("sp", lambda e: e.dma_start(out=dneg, in_=dneg_d), writes=["dneg"], semkey="ld_dneg", group="ld_dneg")
    P.dma("sp", lambda e: e.dma_start(out=rmask, in_=rmask_d[:, 0]), writes=["rmask"], semkey="ld_rmask", group="ld_rmask")
    P.dma("sp", lambda e: e.dma_start(out=dq, in_=dq_d[:, 0]), writes=["dq"], semkey="ld_dq", group="ld_dq")
    P.dma("sp", lambda e: e.dma_start(out=seqmask, in_=seqmask_d), writes=["seqmask"], semkey="ld_seqmask", group="ld_seqmask")
    P.dma("sp", lambda e: e.dma_start(out=esink, in_=sinks_d.partition_broadcast(128)), writes=["esink"], semkey="ld_esink", group="ld_esink")
    P.dma("sp", lambda e: e.dma_start(out=retg, in_=retg_d.rearrange("o (h e) -> e (o h)", h=4)), writes=["retg"], semkey="ld_retg", group="ld_retg")
    P.op("dve", lambda e: e.tensor_copy(out=bd, in_=bd_f), reads=["bd_f"], writes=["bd"])
    P.op("act", lambda e: e.activation(out=esink, in_=esink, func=AF.Exp), reads=["esink"], writes=["esink"])
    P.op("dve", lambda e: e.tensor_scalar(out=gq, in0=gq, scalar1=0.125, scalar2=None, op0=ALU.mult), reads=["gq"], writes=["gq"])

    qnT = carve([128, 4, BT], BF16)
    kdT = carve([128, 2, 128 + BT], BF16)
    qbT = carve([128, 4, BT], BF16)
    qdT = carve([128, 4, BT], BF16)
    kbT = carve([128, 4, BT], BF16)
    sgT = carve([128, 4, BT], BF16)
    vd_tok = carve([128, 1 + TPB, 256], BF16)
    oT = carve([128, KC, BT], BF16)
    qsq = carve([128, max(BT, 256)], BF16)
    qrs = carve([128, max(BT, 256)])
    sc_sb = carve([128, 2, 2, 2, 128])
    pTt = carve([128, 2, 2, 2, 128], BF16)
    rden = carve([128, 2, 2, 128])
    innT = carve([128, 4, 128], BF16)
    o32 = carve([128, 4, 128])
    obf = carve([128, 4, 128], BF16)
    osq = carve([128, 4, 128], BF16)
    t1 = carve([128, 4, 128])
    t2 = o32
    kv32 = carve([128, 2, 128])
    knT = carve([128, 128])
    gtmp = carve([128, 128])
    cacheT = carve([128, 16, 128], BF16)
    vcache = carve([128, 16, 128], BF16)
    kst = carve([128, 4, 128])
    S0 = [carve([128, 4, 128]) for _ in range(2)]
    S0b = [carve([128, 4, 128], BF16) for _ in range(2)]
    kdm = [carve([128, 512], BF16) for _ in range(2)]
    mark2 = cur[0]
    print("arena words used (mixer):", cur[0], "of", ARENA_W)

    PJ = [bank(0, 0), bank(0, 1)]
    PJK = [bkey(0, 0), bkey(0, 1)]
    p_st = bank(1, 0)
    pj_i = [0]

    def proj_fm(lhs_fn, ntok, evac, src=None):
        s = pj_i[0] % 2
        pj_i[0] += 1
        src = hT if src is None else src
        for k in range(KC):
            P.op("pe", lambda e, k=k, s=s: e.matmul(PJ[s][:, :ntok], lhsT=lhs_fn(k), rhs=src[:, k, :ntok], start=(k == 0), stop=(k == KC - 1)),
                 reads=["hT", "hTh", "win", "wkd"], writes=[PJK[s]])
        evac(PJ[s][:, :ntok], PJK[s])

    def qknorm(psum, pkey, ntok, gain, gkey, out_ap, out_key):
        P.op("act", lambda e: e.activation(out=qsq[:, :ntok], in_=psum, func=AF.Square), reads=[pkey], writes=["qsq"])
        P.op("pe", lambda e: e.matmul(p_st[:, :ntok], lhsT=bd, rhs=qsq[:, :ntok], start=True, stop=True), reads=["bd", "qsq"], writes=[bkey(1, 0)])
        P.op("act", lambda e: e.activation(out=qrs[:, :ntok], in_=p_st[:, :ntok], func=AF.Sqrt, bias=epsb[:], scale=1.0), reads=[bkey(1, 0), "epsb"], writes=["qrs"])
        P.op("dve", lambda e: e.reciprocal(out=qrs[:, :ntok], in_=qrs[:, :ntok]), reads=["qrs"], writes=["qrs"])
        P.op("dve", lambda e: e.scalar_tensor_tensor(out=out_ap, in0=psum, scalar=gain[:, 0:1], in1=qrs[:, :ntok], op0=ALU.mult, op1=ALU.mult),
             reads=[pkey, "qrs", gkey], writes=[out_key])

    def project_block(ntok, grp):
        for j in range(4):
            proj_fm(lambda k, j=j: win[:, k, j * 128:(j + 1) * 128], ntok,
                    lambda ps_, key, j=j: qknorm(ps_, key, ntok, gq, "gq", qnT[:, j, :ntok], "qnT"))
        if KSUB < 2:
            return
        for kv in range(2):
            proj_fm(lambda k, kv=kv: wkd[:, k, kv, :], ntok,
                    lambda ps_, key, kv=kv: qknorm(ps_, key, ntok, gk, "gk", kdT[:, kv, 128:128 + ntok], "kdT"))
        if KSUB < 3:
            return
        for h in range(4):
            def ev_q(ps_, key, h=h):
                P.op("act", lambda e: e.copy(out=qbT[:, h, :ntok], in_=ps_), reads=[key], writes=["qbT"])
                P.op("dve", lambda e: e.tensor_tensor(out=qdT[:, h, :ntok].rearrange("p (t i) -> p t i", i=128), in0=ps_.rearrange("p (t i) -> p t i", i=128),
                                                      in1=dq[:, h, :].unsqueeze(1).to_broadcast([128, ntok // 128, 128]), op=ALU.mult),
                     reads=[key, "dq"], writes=["qdT"])
            proj_fm(lambda k, h=h: win[:, k, 768 + h * 128:768 + (h + 1) * 128], ntok, ev_q)
        if KSUB < 4:
            return
        for h in range(4):
            proj_fm(lambda k, h=h: win[:, k, 1280 + h * 128:1280 + (h + 1) * 128], ntok,
                    lambda ps_, key, h=h: P.op("act", lambda e: e.mul(out=kbT[:, h, :ntok], in_=ps_, mul=128.0 ** -0.5), reads=[key], writes=["kbT"]))
        if KSUB < 5:
            return
        for h in range(4):
            proj_fm(lambda k, h=h: win[:, k, 2304 + h * 128:2304 + (h + 1) * 128], ntok,
                    lambda ps_, key, h=h: P.op("act", lambda e: e.activation(out=sgT[:, h, :ntok], in_=ps_, func=AF.Silu), reads=[key], writes=["sgT"]))

    p_vd = bank(1, 1)[:, 0:256]

    def tok_vd(tc0, slot, src=None):
        src = hT if src is None else src
        for k in range(KC):
            P.op("pe", lambda e, k=k: e.matmul(p_vd, lhsT=src[:, k, tc0:tc0 + 128], rhs=wvd[:, k, :, :].rearrange("p a b -> p (a b)"), start=(k == 0), stop=(k == KC - 1)),
                 reads=["hT", "hTh", "wvd"], writes=[bkey(1, 1)])
        P.op("act", lambda e: e.copy(out=vd_tok[:, slot, :], in_=p_vd), reads=[bkey(1, 1)], writes=["vd_tok"])

    SLOPE = [2.0 ** (-(h + 1)) for h in range(8)]
    p_sc = Q[3][:, :].rearrange("p (a c t q) -> p a c t q", a=2, c=2, t=2)
    p_num = bank(2, 0).rearrange("p (a c q) -> p a c q", a=2, c=2)
    p_den = bank(2, 1).rearrange("p (a c q) -> p a c q", a=2, c=2)
    SCK = [bkey(3, 0), bkey(3, 1)]

    def attn_softmax(kv, dn_own, dn_prev):
        for half in range(2):
            for c in range(2):
                h0 = 4 * kv + 2 * c + half
                P.op("dve", lambda e, half=half, c=c, h0=h0: e.scalar_tensor_tensor(out=sc_sb[:, half, c, 0, :], in0=dneg[:, dn_prev, :], scalar=SLOPE[h0],
                                                                                  in1=p_sc[:, half, c, 0, :], op0=ALU.mult, op1=ALU.add),
                     reads=["dneg"] + SCK, writes=["sc_sb"])
                P.op("dve", lambda e, half=half, c=c, h0=h0: e.scalar_tensor_tensor(out=sc_sb[:, half, c, 1, :], in0=dneg[:, dn_own, :], scalar=SLOPE[h0],
                                                                                  in1=p_sc[:, half, c, 1, :], op0=ALU.mult, op1=ALU.add),
                     reads=["dneg"] + SCK, writes=["sc_sb"])
        P.op("act", lambda e: e.activation(out=pTt, in_=sc_sb, func=AF.Exp), reads=["sc_sb"], writes=["pTt"])

    def attn_finish(kv, c0):
        for part in range(2):
            P.op("pe", lambda e, part=part: e.matmul(p_den, lhsT=ones_b[:], rhs=pTt[:, :, :, part, :], start=(part == 0), stop=(part == 1)),
                 reads=["ones_b", "pTt"], writes=[bkey(2, 1)])
        es_v = esink[:, 4 * kv:4 * kv + 4].rearrange("p (c a) -> p a c", a=2)
        P.op("dve", lambda e, es_v=es_v: e.tensor_tensor(out=rden, in0=p_den, in1=es_v.unsqueeze(3).to_broadcast([128, 2, 2, 128]), op=ALU.add),
             reads=[bkey(2, 1), "esink"], writes=["rden"])
        P.op("dve", lambda e: e.reciprocal(out=rden, in_=rden), reads=["rden"], writes=["rden"])
        for half in range(2):
            sl = slice(half * 64, half * 64 + 64)
            P.op("dve", lambda e, half=half, sl=sl, kv=kv: e.tensor_tensor(out=oT[sl, 2 * kv:2 * kv + 2, c0:c0 + 128], in0=p_num[sl, half, :, :],
                                                                         in1=rden[sl, half, :, :], op=ALU.mult),
                 reads=[bkey(2, 0), "rden"], writes=["oT"])

    def attention_tile(i, dn_own, dn_prev):
        c0 = i * 128
        for kv in range(2):
            for half in range(2):
                sl = slice(half * 64, half * 64 + 64)
                P.op("pe", lambda e, kv=kv, half=half, sl=sl: e.matmul(p_sc[:, half, :, 1, :], lhsT=kdT[sl, kv, 128 + c0:256 + c0],
                                                                       rhs=qnT[sl, 2 * kv:2 * kv + 2, c0:c0 + 128], start=True, stop=True),
                     reads=["kdT", "qnT"], writes=SCK)
                P.op("pe", lambda e, kv=kv, half=half, sl=sl: e.matmul(p_sc[:, half, :, 0, :], lhsT=kdT[sl, kv, c0:128 + c0],
                                                                       rhs=qnT[sl, 2 * kv:2 * kv + 2, c0:c0 + 128], start=True, stop=True),
                     reads=["kdT", "qnT"], writes=SCK)
            attn_softmax(kv, dn_own, dn_prev)
            parts = [(0, vd_tok[:, i, kv * 128:(kv + 1) * 128]), (1, vd_tok[:, i + 1, kv * 128:(kv + 1) * 128])]
            for n_, (part, lh) in enumerate(parts):
                P.op("pe", lambda e, part=part, lh=lh, n_=n_: e.matmul(p_num, lhsT=lh, rhs=pTt[:, :, :, part, :], start=(n_ == 0), stop=(n_ == 1)),
                     reads=["vd_tok", "pTt"], writes=[bkey(2, 0)])
            attn_finish(kv, c0)

    def attention_sample():
        p_ct = bank(1, 1)[:, 0:128]
        for kv in range(2):
            for half in range(2):
                P.dma("pool", lambda e, kv=kv, half=half: e.dma_start(out=vcache[:, :, half * 64:(half + 1) * 64],
                                                                      in_=cache_v[:, :, kv * 64:(kv + 1) * 64].rearrange("b w d -> w b d")),
                      writes=["vcache"], semkey="vcache", group=f"vc{kv}")
            for g4 in range(4):
                for half in range(2):
                    P.dma("sp", lambda e, kv=kv, half=half, g4=g4: e.dma_start(out=kst[:, :, half * 64:(half + 1) * 64],
                                                                              in_=cache_k[4 * g4:4 * g4 + 4, :, kv * 64:(kv + 1) * 64].rearrange("b w d -> w b d")),
                          writes=["kst"], semkey="kst", group=f"kst{kv}_{g4}")
                for bb in range(4):
                    b = 4 * g4 + bb
                    P.op("pe", lambda e, bb=bb: e.transpose(p_ct, kst[:, bb, :], ident[:]), reads=["kst", "ident"], writes=[bkey(1, 1)])
                    P.op("act", lambda e, b=b: e.copy(out=cacheT[:, b, :], in_=p_ct), reads=[bkey(1, 1)], writes=["cacheT"])
            for half in range(2):
                sl = slice(half * 64, half * 64 + 64)
                P.op("pe", lambda e, kv=kv, half=half, sl=sl: e.matmul(p_sc[:, half, :, 1, :], lhsT=kdT[sl, kv, 128:256],
                                                                       rhs=qnT[sl, 2 * kv:2 * kv + 2, 0:128], start=True, stop=True),
                     reads=["kdT", "qnT"], writes=SCK)
                for b in range(16):
                    P.op("pe", lambda e, kv=kv, half=half, sl=sl, b=b: e.matmul(p_sc[:, half, :, 0, 8 * b:8 * b + 8], lhsT=cacheT[sl, b, :],
                                                                                 rhs=qnT[sl, 2 * kv:2 * kv + 2, 8 * b:8 * b + 8], start=True, stop=True),
                         reads=["cacheT", "qnT"], writes=SCK)
            attn_softmax(kv, 3, 4)
            P.op("pe", lambda e, kv=kv: e.matmul(p_num, lhsT=vd_tok[:, 1, kv * 128:(kv + 1) * 128], rhs=pTt[:, :, :, 1, :], start=True, stop=False),
                 reads=["vd_tok", "pTt"], writes=[bkey(2, 0)])
            for b in range(16):
                P.op("pe", lambda e, b=b: e.matmul(p_num[:, :, :, 8 * b:8 * b + 8], lhsT=vcache[:, b, :], rhs=pTt[:, :, :, 0, 8 * b:8 * b + 8],
                                                   start=False, stop=(b == 15)),
                     reads=["vcache", "pTt"], writes=[bkey(2, 0)])
            attn_finish(kv, 0)

    p_in = bank(1, 1).rearrange("p (h i) -> p h i", h=4)
    p_o = bank(0, 0).rearrange("p (h i) -> p h i", h=4)
    p_mu = bank(0, 1).rearrange("p (h i) -> p h i", h=4)
    p_e2 = bank(1, 0).rearrange("p (h i) -> p h i", h=4)

    def ret_norm(c0):
        P.op("dve", lambda e: e.tensor_copy(out=obf, in_=o32), reads=["o32"], writes=["obf"])
        P.op("act", lambda e: e.activation(out=osq, in_=o32, func=AF.Square), reads=["o32"], writes=["osq"])
        for h in range(4):
            P.op("pe", lambda e, h=h: e.matmul(p_mu[:, h, :], lhsT=ones_g[:], rhs=obf[:, h, :], start=True, stop=True), reads=["ones_g", "obf"], writes=[bkey(0, 1)])
        for h in range(4):
            P.op("pe", lambda e, h=h: e.matmul(p_e2[:, h, :], lhsT=ones_g[:], rhs=osq[:, h, :], start=True, stop=True), reads=["ones_g", "osq"], writes=[bkey(1, 0)])
        P.op("act", lambda e: e.activation(out=t1, in_=p_mu, func=AF.Square), reads=[bkey(0, 1)], writes=["t1"])
        P.op("dve", lambda e: e.tensor_tensor(out=t1, in0=p_e2, in1=t1, op=ALU.subtract), reads=[bkey(1, 0), "t1"], writes=["t1"])
        P.op("dve", lambda e: e.tensor_scalar(out=t1, in0=t1, scalar1=0.0, scalar2=None, op0=ALU.max), reads=["t1"], writes=["t1"])
        P.op("act", lambda e: e.activation(out=t1, in_=t1, func=AF.Sqrt, bias=epsb[:], scale=1.0), reads=["t1", "epsb"], writes=["t1"])
        P.op("dve", lambda e: e.reciprocal(out=t1, in_=t1), reads=["t1"], writes=["t1"])
        P.op("dve", lambda e: e.tensor_tensor(out=o32, in0=o32, in1=p_mu, op=ALU.subtract), reads=["o32", bkey(0, 1)], writes=["o32"])
        P.op("dve", lambda e: e.tensor_tensor(out=o32, in0=o32, in1=t1, op=ALU.mult), reads=["o32", "t1"], writes=["o32"])
        P.op("dve", lambda e: e.tensor_tensor(out=o32, in0=o32, in1=retg.unsqueeze(2).to_broadcast([128, 4, 128]), op=ALU.mult), reads=["o32", "retg"], writes=["o32"])
        P.op("dve", lambda e: e.tensor_tensor(out=oT[:, 4:8, c0:c0 + 128], in0=o32, in1=sgT[:, :, c0:c0 + 128], op=ALU.mult), reads=["o32", "sgT"], writes=["oT"])

    def ret_inner(c0):
        for h in range(4):
            P.op("pe", lambda e, h=h: e.matmul(p_in[:, h, :], lhsT=kbT[:, h, c0:c0 + 128], rhs=qbT[:, h, c0:c0 + 128], start=True, stop=True),
                 reads=["kbT", "qbT"], writes=[bkey(1, 1)])
        P.op("dve", lambda e: e.tensor_tensor(out=innT, in0=p_in, in1=rmask, op=ALU.mult), reads=[bkey(1, 1), "rmask"], writes=["innT"])

    def retention_tile(i, grp):
        c0 = i * 128
        ret_inner(c0)
        for h in range(4):
            P.op("pe", lambda e, h=h: e.matmul(p_o[:, h, :], lhsT=vb_tok[:, h * 128:(h + 1) * 128], rhs=innT[:, h, :], start=True, stop=False),
                 reads=["vb_tok", "innT"], writes=[bkey(0, 0)])
            P.op("pe", lambda e, h=h: e.matmul(p_o[:, h, :], lhsT=Sb[:, h, :], rhs=qdT[:, h, c0:c0 + 128], start=False, stop=True),
                 reads=["Sb", "qdT"], writes=[bkey(0, 0)])
        P.op("act", lambda e: e.copy(out=o32, in_=p_o), reads=[bkey(0, 0)], writes=["o32"])
        ret_norm(c0)

    def retention_sample():
        ret_inner(0)
        poh = [bank(0, 0)[:, 0:128], bank(0, 1)[:, 0:128], bank(1, 0)[:, 0:128], bank(3, 0)[:, 0:128]]
        pok = [bkey(0, 0), bkey(0, 1), bkey(1, 0), bkey(3, 0)]
        for h in range(4):
            P.op("pe", lambda e, h=h: e.matmul(poh[h], lhsT=vb_tok[:, h * 128:(h + 1) * 128], rhs=innT[:, h, :], start=True, stop=False),
                 reads=["vb_tok", "innT"], writes=[pok[h]])
        for b in range(16):
            s_ = b % 2
            P.dma("sp", lambda e, b=b, s_=s_: e.dma_start(out=S0[s_], in_=sret_in[b].rearrange("h d e -> d h e")), writes=[f"S0{s_}"], semkey=f"S0{s_}")
            P.op("pool", lambda e, s_=s_: e.tensor_copy(out=S0b[s_], in_=S0[s_]), reads=[f"S0{s_}"], writes=[f"S0b{s_}"])
            for h in range(4):
                P.op("pe", lambda e, h=h, b=b, s_=s_: e.matmul(poh[h][:, 8 * b:8 * b + 8], lhsT=S0b[s_][:, h, :], rhs=qdT[:, h, 8 * b:8 * b + 8],
                                                              start=False, stop=(b == 15)),
                     reads=[f"S0b{s_}", "qdT"], writes=[pok[h]])
            P.op("dve", lambda e, b=b, s_=s_: e.tensor_scalar(out=kdm[s_], in0=kd_tok, scalar1=seqmask[:, b:b + 1], scalar2=None, op0=ALU.mult),
                 reads=["kd_tok", "seqmask"], writes=[f"kdm{s_}"])
            for h in range(4):
                P.op("pe", lambda e, h=h, s_=s_: e.matmul(p_su[:, h, :], lhsT=kdm[s_][:, h * 128:(h + 1) * 128], rhs=vb_tok[:, h * 128:(h + 1) * 128], start=True, stop=True),
                     reads=[f"kdm{s_}", "vb_tok"], writes=[bkey(2, 1)])
            P.op("pool", lambda e, s_=s_: e.tensor_tensor(out=S0[s_], in0=S0[s_], in1=dc[:, 1, :].unsqueeze(2).to_broadcast([128, 4, 128]), op=ALU.mult),
                 reads=[f"S0{s_}", f"S0b{s_}", "dc"], writes=[f"S0{s_}"])
            P.op("dve", lambda e, s_=s_: e.tensor_tensor(out=S0[s_], in0=S0[s_], in1=p_su, op=ALU.add), reads=[f"S0{s_}", bkey(2, 1)], writes=[f"S0{s_}"])
            P.dma("sp", lambda e, b=b, s_=s_: e.dma_start(out=sret_out[b].rearrange("h d e -> d h e"), in_=S0[s_]), reads=[f"S0{s_}"], semkey=f"S0o{s_}", final=True)
        for h in range(4):
            P.op("act", lambda e, h=h: e.copy(out=o32[:, h, :], in_=poh[h]), reads=[pok[h]], writes=["o32"])
        ret_norm(0)

    PO = [bank(0, 0), bank(0, 1)]
    POK = [bkey(0, 0), bkey(0, 1)]

    def out_proj(blk_c0, ntok, sample, wfn, nk, rhs_fn, gate_row, rkeys):
        for m in range(KC):
            s = m % 2
            for k in range(nk):
                P.op("pe", lambda e, m=m, k=k, s=s: e.matmul(PO[s][:, :ntok], lhsT=wfn(k, m), rhs=rhs_fn(k), start=(k == 0), stop=(k == nk - 1)),
                     reads=rkeys, writes=[POK[s]])
            xs = xT[:, m, blk_c0:blk_c0 + ntok]
            xkey = ("xT", blk_c0 // 512)
            if not sample:
                P.op("dve", lambda e, m=m, s=s, xs=xs: e.scalar_tensor_tensor(out=xs, in0=PO[s][:, :ntok], scalar=modT[:, gate_row + m, 0:1], in1=xs,
                                                                             op0=ALU.mult, op1=ALU.add),
                     reads=[POK[s], "modT", xkey], writes=[xkey])
            else:
                gv = gtmp[:, :128].rearrange("p (b t) -> p b t", t=8)
                P.op("dve", lambda e, m=m, s=s, gv=gv: e.tensor_tensor(out=gv, in0=PO[s][:, :128].rearrange("p (b t) -> p b t", t=8),
                                                                      in1=modT[:, gate_row + m, 1:17].unsqueeze(2).to_broadcast([128, 16, 8]), op=ALU.mult),
                     reads=[POK[s], "modT"], writes=["gtmp"])
                P.op("dve", lambda e, xs=xs: e.tensor_tensor(out=xs, in0=xs, in1=gtmp[:, :128], op=ALU.add), reads=["gtmp", xkey], writes=[xkey])

    p_tr = bank(1, 1)[:, 256:384]
    p_v32 = bank(1, 1)[:, 384:512]

    def window_kv(tc0, kout, vout):
        pk2 = bank(1, 0)[:, 0:256].rearrange("p (a n) -> p a n", a=2)
        pst2 = bank(1, 0)[:, 256:512].rearrange("p (a n) -> p a n", a=2)
        for kv in range(2):
            for k in range(KC):
                P.op("pe", lambda e, kv=kv, k=k: e.matmul(pk2[:, kv, :], lhsT=wkd[:, k, kv, :], rhs=hT[:, k, tc0:tc0 + 128], start=(k == 0), stop=(k == KC - 1)),
                     reads=["wkd", "hT"], writes=[bkey(1, 0)])
        P.op("act", lambda e: e.activation(out=qsq[:, 0:256].rearrange("p (a n) -> p a n", a=2), in_=pk2, func=AF.Square), reads=[bkey(1, 0)], writes=["qsq"])
        for kv in range(2):
            P.op("pe", lambda e, kv=kv: e.matmul(pst2[:, kv, :], lhsT=bd, rhs=qsq[:, kv * 128:(kv + 1) * 128], start=True, stop=True), reads=["bd", "qsq"], writes=[bkey(1, 0)])
        P.op("act", lambda e: e.activation(out=qrs[:, 0:256].rearrange("p (a n) -> p a n", a=2), in_=pst2, func=AF.Sqrt, bias=epsb[:], scale=1.0),
             reads=[bkey(1, 0), "epsb"], writes=["qrs"])
        P.op("dve", lambda e: e.reciprocal(out=qrs[:, 0:256], in_=qrs[:, 0:256]), reads=["qrs"], writes=["qrs"])
        for kv in range(2):
            sl = slice(kv * 64, (kv + 1) * 64)
            P.op("dve", lambda e, kv=kv, sl=sl: e.scalar_tensor_tensor(out=knT[sl, :], in0=pk2[sl, kv, :], scalar=gk[sl, 0:1], in1=qrs[sl, kv * 128:(kv + 1) * 128],
                                                                       op0=ALU.mult, op1=ALU.mult),
                 reads=[bkey(1, 0), "gk", "qrs"], writes=["knT"])
        P.op("pe", lambda e: e.transpose(p_tr, knT, ident[:]), reads=["knT", "ident"], writes=[bkey(1, 1)])
        P.op("act", lambda e: e.copy(out=kv32[:, 0, :], in_=p_tr), reads=[bkey(1, 1)], writes=["kv32"])
        for k in range(KC):
            P.op("pe", lambda e, k=k: e.matmul(p_v32, lhsT=hT[:, k, tc0:tc0 + 128], rhs=win[:, k, 640:768], start=(k == 0), stop=(k == KC - 1)),
                 reads=["win", "hT"], writes=[bkey(1, 1)])
        P.op("dve", lambda e: e.tensor_copy(out=kv32[:, 1, :], in_=p_v32), reads=[bkey(1, 1)], writes=["kv32"])
        P.dma("sp", lambda e: e.dma_start(out=kout, in_=kv32[:, 0, :]), reads=["kv32"], semkey="kvo", final=True)
        P.dma("sp", lambda e: e.dma_start(out=vout, in_=kv32[:, 1, :]), reads=["kv32"], semkey="kvo", final=True)

    if KSTOP >= 2:
        for kv in range(2):
            proj_fm(lambda k, kv=kv: wkd[:, k, kv, :], 128,
                    lambda ps_, key, kv=kv: qknorm(ps_, key, 128, gk, "gk", kdT[:, kv, 0:128], "kdT"), src=hTh)
        tok_vd(0, 0, src=hTh)

    NB = TOK_P // BT
    for b in range(NB if KSTOP >= 3 else 0):
        ln_block(xT[:, :, b * BT:(b + 1) * BT], ("xT", (b * BT) // 512), BT, False, 0)
        project_block(BT, 0)
        if KSTOP < 4:
            continue
        for i in range(TPB):
            tok_vd(i * 128, i + 1)
        for i in range(TPB):
            attention_tile(i, 0, 2 if (b == 0 and i == 0) else 1)
        if KSTOP < 5:
            continue
        for i in range(TPB):
            tok_kv(i * 128, 0)
            retention_tile(i, 0)
            state_update()
            P.op("act", lambda e: e.copy(out=Sb, in_=S), reads=["S"], writes=["Sb"])
        if b == NB - 1:
            window_kv(BT - 128, pk_out, pv_out)
        if KSTOP < 6:
            continue
        out_proj(b * BT, BT, False, lambda k, m: wout[:, k, m * 128:(m + 1) * 128], KC, lambda k: oT[:, k, :BT], 16, ["wout", "oT"])
        P.op("pool", lambda e: e.tensor_copy(out=kdT[:, :, 0:128], in_=kdT[:, :, BT:BT + 128]), reads=["kdT"], writes=["kdT"])
        P.op("pool", lambda e: e.tensor_copy(out=vd_tok[:, 0, :], in_=vd_tok[:, TPB, :]), reads=["vd_tok"], writes=["vd_tok"])
    P.dma("sp", lambda e: e.dma_start(out=pret_out.rearrange("h d e -> d h e"), in_=S), reads=["S"], semkey="pret", final=True)

    if KSTOP >= 7:
        P.dma("sp", lambda e: e.dma_start(out=rmask, in_=rmask_d[:, 1]), writes=["rmask"], semkey="ld_rmask", group="ld_rmask")
        P.dma("sp", lambda e: e.dma_start(out=dq, in_=dq_d[:, 1]), writes=["dq"], semkey="ld_dq", group="ld_dq")
        ln_block(xT[:, :, TOK_P:TOK_P + 128], ("xT", 4), 128, True, 0)
        project_block(128, 1)
        tok_vd(0, 1)
        tok_kv(0, 1)
        window_kv(0, sk_out[:, 120:128, :], sv_out[:, 120:128, :])
        P.dma("sp", lambda e: e.dma_start(out=sk_out[:, 0:120, :], in_=cache_k[:, 8:128, :]), semkey="cko", final=True)
        P.dma("sp", lambda e: e.dma_start(out=sv_out[:, 0:120, :], in_=cache_v[:, 8:128, :]), semkey="cko", final=True)
        attention_sample()
        retention_sample()
        out_proj(TOK_P, 128, True, lambda k, m: wout[:, k, m * 128:(m + 1) * 128], KC, lambda k: oT[:, k, :128], 16, ["wout", "oT"])

    if KSTOP >= 8:
        ffn(0)

    for k in range(KC):
        P.dma("sp", lambda e, k=k: e.dma_start(out=x1_out[:, k * NTOK:(k + 1) * NTOK], in_=xT[:, k, :]),
              reads=[("xT", b_) for b_ in range(5)], semkey="x1o", final=True)

    P.barrier()
    P.dma("sp", lambda e: e.dma_start(out=modT[:].rearrange("p m c -> p (m c)"), in_=modin1_d), writes=["modT"], semkey="ld_modT1")
    make_G(1, 0)
    s5_stage(False)

    stats = P.build()
    nc_allow.__exit__(None, None, None)
    es.close()
    return nc, stats


_CACHE = {}


def _tables(c):
    f32 = np.float32
    ident = np.eye(128, dtype=f32)
    bd = np.zeros((128, 128), f32)
    bd[:64, :64] = 1.0 / 64
    bd[64:, 64:] = 1.0 / 64
    j = np.arange(128)[:, None]
    i = np.arange(128)[None, :]
    NEG = -1e30
    dneg = np.full((128, 5, 128), NEG, f32)
    dneg[:, 0, :] = np.where(i >= j, -(i - j), NEG)
    dneg[:, 1, :] = np.where(i < j, -(i - j + 128), NEG)
    dneg[:, 2, :] = dneg[:, 1, :] if c > 0 else NEG
    same = (j // 8) == (i // 8)
    dneg[:, 3, :] = np.where(same & ((i % 8) >= (j % 8)), -((i % 8) - (j % 8)), NEG)
    dneg[:, 4, :] = np.where(j >= (i % 8) + 1, -(128 + (i % 8) - j), NEG)
    g = np.array(GAM, np.float64)
    rmask = np.zeros((128, 2, 4, 128), f32)
    dq = np.zeros((128, 2, 4, 128), f32)
    dk = np.zeros((128, 2, 4), f32)
    dc = np.zeros((128, 2, 4), f32)
    for h in range(4):
        rmask[:, 0, h, :] = np.where(i >= j, g[h] ** np.maximum(i - j, 0), 0.0)
        rmask[:, 1, h, :] = np.where(same & ((i % 8) >= (j % 8)), g[h] ** np.maximum((i % 8) - (j % 8), 0), 0.0)
        dq[:, 0, h, :] = g[h] ** (i + 1.0)
        dq[:, 1, h, :] = g[h] ** ((i % 8) + 1.0)
        dk[:, 0, h] = 128.0 ** -0.5 * g[h] ** (127.0 - j[:, 0])
        dk[:, 1, h] = 128.0 ** -0.5 * g[h] ** (7.0 - (j[:, 0] % 8))
        dc[:, 0, h] = g[h] ** 128.0
        dc[:, 1, h] = g[h] ** 8.0
    seqmask = (j // 8 == np.arange(16)[None, :]).astype(f32)
    wret = np.zeros((128, 8, 4), f32)
    for r in range(8):
        if r < c:
            for h in range(4):
                wret[:, r, h] = g[h] ** (2048.0 * (c - r - 1))
    return dict(ident=ident, bdones=bd, dneg=dneg, rmask=rmask, dq=dq, dk=dk, dc=dc, seqmask=seqmask, wret=wret)


def _get(stage):
    if stage not in _CACHE:
        _CACHE[stage] = build_program(stage)
    return _CACHE[stage]


def kernel(**inp):
    f32 = np.float32
    A = lambda k: np.ascontiguousarray(np.asarray(inp[k], f32))
    xp = A("x_prompt")[0]
    xs = A("x_sample")
    base = dict(norm_mix=A("norm_mix"), norm_ffn=A("norm_ffn"), even_w_in=A("even_w_in")[0])
    per_core = []
    for c in range(NCORES):
        x18 = np.zeros((18 * 128, D), f32)
        if c > 0:
            x18[0:128] = xp[c * TOK_P - 128:c * TOK_P]
        x18[128:128 + TOK_P] = xp[c * TOK_P:(c + 1) * TOK_P]
        x18[128 + TOK_P:] = xs[16 * c:16 * c + 16].reshape(128, D)
        cv = np.concatenate([A("c_prompt"), A("c_sample")[16 * c:16 * c + 16]], axis=0)
        t = _tables(c)
        m = dict(base)
        m.update(x=x18, cvec=np.ascontiguousarray(cv), ident=t["ident"], dk=t["dk"], dc=t["dc"])
        per_core.append((m, t))

    ncA, _ = _get("A")
    mapsA = []
    for m, _ in per_core:
        m = dict(m)
        m.update(ada_w=A("ada_w"), ada_b=A("ada_b"))
        mapsA.append(m)
    resA = run_bass_kernel_spmd(ncA, mapsA, core_ids=list(range(NCORES)))
    mods = [np.ascontiguousarray(resA.results[c]["modout"]) for c in range(NCORES)]
    lall = np.ascontiguousarray(np.stack([resA.results[c]["lret"] for c in range(NCORES)], axis=0))
    _CACHE["lall"] = lall

    ncB, statsB = _get("B")
    _CACHE["statsB"] = statsB
    in_maps = []
    for c in range(NCORES):
        m, t = per_core[c]
        m = dict(m)
        m.pop("cvec", None)
        m["modin"] = np.ascontiguousarray(mods[c][0])
        m["modin1"] = np.ascontiguousarray(mods[c][1])
        m.update(odd_A_re=A("odd_A_re")[0], odd_A_im=A("odd_A_im")[0], odd_log_dt=A("odd_log_dt"), odd_B_re=A("odd_B_re")[0], odd_B_im=A("odd_B_im")[0],
                 odd_C_re=A("odd_C_re")[0], odd_C_im=A("odd_C_im")[0], odd_D=A("odd_D"), **_tables_c())
        m.update(even_q_gain=A("even_q_gain"), even_k_gain=A("even_k_gain"), even_sinks=A("even_sinks"), even_ret_gain=A("even_ret_gain"),
                 even_w_out=A("even_w_out")[0], ffn_wg=A("ffn_wg"), ffn_wu=A("ffn_wu"), ffn_wd=A("ffn_wd"),
                 cache_k=np.ascontiguousarray(A("cache_win_k")[0, 16 * c:16 * c + 16].reshape(16, 128, 128)),
                 cache_v=np.ascontiguousarray(A("cache_win_v")[0, 16 * c:16 * c + 16].reshape(16, 128, 128)),
                 state_ret=np.ascontiguousarray(A("state_ret")[0, 16 * c:16 * c + 16]),
                 bdones=t["bdones"], dneg=t["dneg"], rmask=t["rmask"], dq=t["dq"], seqmask=t["seqmask"], wret=t["wret"], lall=lall)
        in_maps.append(m)
    resB = run_bass_kernel_spmd(ncB, in_maps, core_ids=list(range(NCORES)))
    R = resB.results
    _CACHE["R"] = R
    last = R[NCORES - 1]
    p_k = last["p_k"].reshape(1, 1, 128, 2, 64)
    p_v = last["p_v"].reshape(1, 1, 128, 2, 64)
    p_ret = last["p_ret"].reshape(1, 1, 4, 128, 128)
    s_k = np.concatenate([R[c]["s_k"] for c in range(NCORES)], axis=0).reshape(1, 128, 128, 2, 64)
    s_v = np.concatenate([R[c]["s_v"] for c in range(NCORES)], axis=0).reshape(1, 128, 128, 2, 64)
    s_ret = np.concatenate([R[c]["s_ret"] for c in range(NCORES)], axis=0)[None]

    tC = _tables_c()
    mapsC = []
    for c in range(NCORES):
        m0, t = per_core[c]
        wsel = np.zeros((128, 8), f32)
        wsel[:, :c] = 1.0
        m = dict(base)
        m.update(x1T=np.ascontiguousarray(R[c]["x1T"]), modin=np.ascontiguousarray(mods[c][1]), ident=t["ident"], dk=t["dk"], dc=t["dc"],
                 ffn_wg=A("ffn_wg"), ffn_wu=A("ffn_wu"), ffn_wd=A("ffn_wd"),
                 odd_A_re=A("odd_A_re")[0], odd_A_im=A("odd_A_im")[0], odd_log_dt=A("odd_log_dt"),
                 odd_B_re=A("odd_B_re")[0], odd_B_im=A("odd_B_im")[0], odd_C_re=A("odd_C_re")[0], odd_C_im=A("odd_C_im")[0],
                 odd_D=A("odd_D"), odd_glu_a=A("odd_glu_a")[0], odd_glu_b=A("odd_glu_b")[0],
                 s5_re=np.ascontiguousarray(A("state_s5_re")[0, 16 * c:16 * c + 16]), s5_im=np.ascontiguousarray(A("state_s5_im")[0, 16 * c:16 * c + 16]),
                 fall=np.zeros((8, 128, 64), f32), wsel=wsel, **tC)
        mapsC.append(m)
    fall = np.ascontiguousarray(np.stack([R[c]["floc"] for c in range(NCORES)], axis=0))
    _CACHE["fall"] = fall
    for m in mapsC:
        m["fall"] = fall
    ncC2, statsC = _get("C2")
    _CACHE["statsC"] = statsC
    resC2 = run_bass_kernel_spmd(ncC2, mapsC, core_ids=list(range(NCORES)))
    RC = resC2.results
    _CACHE["RC"] = RC
    y_prompt = np.concatenate([RC[c]["y"][:TOK_P] for c in range(NCORES)], axis=0)[None]
    y_sample = np.concatenate([RC[c]["y"][TOK_P:].reshape(16, 8, D) for c in range(NCORES)], axis=0)
    lastC = RC[NCORES - 1]
    p_re = lastC["p_s5r"].reshape(1, 1, 64, 64)
    p_im = lastC["p_s5i"].reshape(1, 1, 64, 64)
    s_re = np.concatenate([RC[c]["s_s5r"] for c in range(NCORES)], axis=0)[None]
    s_im = np.concatenate([RC[c]["s_s5i"] for c in range(NCORES)], axis=0)[None]
    return (y_prompt.astype(f32), y_sample.astype(f32), p_k, p_v, p_ret, p_re, p_im, s_k, s_v, s_ret, s_re, s_im)


def _tables_c():
    f32 = np.float32
    t = np.arange(512)
    tal = np.broadcast_to((t // 64).astype(f32)[None, :], (128, 512)).copy()
    tbp = np.broadcast_to(((t % 64) + 1).astype(f32)[None, :], (128, 512)).copy()
    ts = np.arange(128)
    tbs = np.broadcast_to(((ts % 8) + 1).astype(f32)[None, :], (128, 128)).copy()
    amask = np.broadcast_to(((ts % 8) != 0).astype(f32)[None, :], (128, 128)).copy()
    maskg = (np.arange(128)[:, None] // 16 == np.arange(8)[None, :]).astype(f32)
    rott = np.zeros((128, 128), f32)
    for p in range(64):
        rott[64 + p, p] = -1.0
        rott[p, 64 + p] = 1.0
    return dict(tal=tal, tbp=tbp, tbs=tbs, amask=amask, maskg=maskg, rott=rott)
```
